# Optimizing a Trainium2 kernel written in Bass

```python
import jax, jax.numpy as jnp
from jax import lax
import numpy as np

D_MODEL = 1024
BATCH = 2
SEQ = 8192
DEPTH = 2

HEAD_DIM = 64
GROUP_WIDTH = D_MODEL // 2
CONV_CH = GROUP_WIDTH
CONV_WIDTH = 31
NSA_HEADS = GROUP_WIDTH // HEAD_DIM
NSA_KV_GROUPS = 2
NSA_CMP_BLOCK = 32
NSA_CMP_STRIDE = 16
NSA_CMP_HIDDEN = 128
NSA_SEL_BLOCK = 64
NSA_TOP_N = 16
NSA_WINDOW = 512
NSA_N_BRANCH = 3
SC_CH = GROUP_WIDTH
SC_WIDTH = 3
SB_HEADS = GROUP_WIDTH // HEAD_DIM
D_FF = 2816
Q_BLOCK = 128
RMS_EPS = 1e-6
LN_EPS = 1e-5
NEG_INF = -1e30
SEL_FORCE = 1e4

kernel_name = "hybrid_conformer_nsa_shortconv_stickbreak"


def _rmsnorm(x, g):
    x32 = x.astype(jnp.float32)
    y = x32 * lax.rsqrt(jnp.mean(x32 * x32, axis=-1, keepdims=True) + RMS_EPS)
    return (y * g.astype(jnp.float32)).astype(x.dtype)


def _layernorm(x, g, b):
    x32 = x.astype(jnp.float32)
    mu = jnp.mean(x32, axis=-1, keepdims=True)
    var = jnp.mean(jnp.square(x32 - mu), axis=-1, keepdims=True)
    y = (x32 - mu) * lax.rsqrt(var + LN_EPS)
    return (y * g.astype(jnp.float32) + b.astype(jnp.float32)).astype(x.dtype)


def _swiglu(x, w_in, w_out):
    gate, up = jnp.split(x @ w_in, 2, axis=-1)
    return (jax.nn.silu(gate) * up) @ w_out


def _split(u, sizes):
    cuts = [int(c) for c in np.cumsum(sizes)[:-1]]
    return jnp.split(u, cuts, axis=-1)


def _causal_dwconv(x, w):
    k = w.shape[0]
    return lax.conv_general_dilated(
        x, w[:, None, :].astype(x.dtype), window_strides=(1,), padding=[(k - 1, 0)],
        dimension_numbers=('NWC', 'WIO', 'NWC'), feature_group_count=x.shape[-1])


def _alibi_slopes(n):
    return jnp.asarray(np.power(2.0, -8.0 * np.arange(1, n + 1) / n).astype(np.float32))


def _masked_softmax(s, mask):
    p = jax.nn.softmax(jnp.where(mask, s, NEG_INF), axis=-1)
    return jnp.where(mask, p, 0.0)


def _nsa_compress(kx, pe, w1, w2):
    b, t, g, d = kx.shape
    chunks = kx.reshape(b, t // NSA_CMP_STRIDE, NSA_CMP_STRIDE, g, d)
    blocks = jnp.concatenate([chunks[:, :-1], chunks[:, 1:]], axis=2)
    h = jax.nn.gelu(jnp.einsum('bnlgd,ldf->bngf', blocks + pe[:, None, :], w1))
    return jnp.einsum('bngf,fe->bnge', h, w2)


def _nsa(q, kc, vc, ks, vs, kw, vw, gates, pe_k, w1_k, w2_k, pe_v, w1_v, w2_v):
    B, T, H, D = q.shape
    G = kc.shape[2]
    R = H // G
    scale = D ** -0.5
    k_cmp = _nsa_compress(kc, pe_k, w1_k, w2_k)
    v_cmp = _nsa_compress(vc, pe_v, w1_v, w2_v)
    NC = k_cmp.shape[1]
    cmp_start = jnp.arange(NC) * NSA_CMP_STRIDE
    cmp_end = cmp_start + NSA_CMP_BLOCK - 1
    cmp_centre = cmp_start.astype(jnp.float32) + (NSA_CMP_BLOCK - 1) / 2
    NS = T // NSA_SEL_BLOCK
    n_top = min(NSA_TOP_N, NS)
    ratio = NSA_SEL_BLOCK // NSA_CMP_STRIDE
    blk = jnp.arange(NS)
    slopes = _alibi_slopes(H).reshape(G, R)[None, None, :, :, None]
    ks_t = ks.transpose(0, 2, 1, 3)
    vs_t = vs.transpose(0, 2, 1, 3)
    kw_pad = jnp.pad(kw, ((0, 0), (NSA_WINDOW, 0), (0, 0), (0, 0)))
    vw_pad = jnp.pad(vw, ((0, 0), (NSA_WINDOW, 0), (0, 0), (0, 0)))
    b_idx = jnp.arange(B)[:, None, None]
    g_idx = jnp.arange(G)[None, :, None]
    win_off = jnp.arange(NSA_WINDOW + Q_BLOCK) - NSA_WINDOW
    sel_off = jnp.arange(NSA_SEL_BLOCK)

    def block(i):
        q0 = i * Q_BLOCK
        t = q0 + jnp.arange(Q_BLOCK)
        tf = t.astype(jnp.float32)
        qg = lax.dynamic_slice_in_dim(q, q0, Q_BLOCK, axis=1).reshape(B, Q_BLOCK, G, R, D)
        gq = lax.dynamic_slice_in_dim(gates, q0, Q_BLOCK, axis=1).reshape(B, Q_BLOCK, G, R, NSA_N_BRANCH)
        s_c = jnp.einsum('bqgrd,bngd->bqgrn', qg, k_cmp).astype(jnp.float32) * scale
        s_c = s_c - slopes * (tf[:, None] - cmp_centre[None, :])[None, :, None, None, :]
        p_c = _masked_softmax(s_c, (cmp_end[None, :] <= t[:, None])[None, :, None, None, :])
        o_c = jnp.einsum('bqgrn,bngd->bqgrd', p_c.astype(v_cmp.dtype), v_cmp)
        imp = jnp.pad(p_c.sum(axis=3), ((0, 0), (0, 0), (0, 0), (1, 1)))
        imp = (imp[..., :ratio * NS].reshape(B, Q_BLOCK, G, NS, ratio).sum(-1)
               + imp[..., ratio::ratio][..., :NS])
        cur = t // NSA_SEL_BLOCK
        visible = blk[None, :] * NSA_SEL_BLOCK <= t[:, None]
        forced = (blk[None, :] == 0) | (blk[None, :] == cur[:, None]) | (blk[None, :] == cur[:, None] - 1)
        score = jnp.where(visible[None, :, None, :],
                          jnp.where(forced[None, :, None, :], SEL_FORCE, imp), -1.0)
        top_val, top_idx = lax.top_k(score, n_top)
        tok = (top_idx[..., None] * NSA_SEL_BLOCK + sel_off).reshape(B, Q_BLOCK, G, n_top * NSA_SEL_BLOCK)
        tok_valid = jnp.repeat(top_val >= 0, NSA_SEL_BLOCK, axis=-1) & (tok <= t[None, :, None, None])
        tok_g = tok.transpose(0, 2, 1, 3).reshape(B, G, -1)
        k_sel = ks_t[b_idx, g_idx, tok_g].reshape(B, G, Q_BLOCK, -1, D)
        v_sel = vs_t[b_idx, g_idx, tok_g].reshape(B, G, Q_BLOCK, -1, D)
        s_s = jnp.einsum('bqgrd,bgqnd->bqgrn', qg, k_sel).astype(jnp.float32) * scale
        s_s = s_s - slopes * (tf[None, :, None, None] - tok.astype(jnp.float32))[:, :, :, None, :]
        p_s = _masked_softmax(s_s, tok_valid[:, :, :, None, :])
        o_s = jnp.einsum('bqgrn,bgqnd->bqgrd', p_s.astype(v_sel.dtype), v_sel)
        kwb = lax.dynamic_slice_in_dim(kw_pad, q0, NSA_WINDOW + Q_BLOCK, axis=1)
        vwb = lax.dynamic_slice_in_dim(vw_pad, q0, NSA_WINDOW + Q_BLOCK, axis=1)
        spos = q0 + win_off
        dist = t[:, None] - spos[None, :]
        mask_w = (spos[None, :] >= 0) & (dist >= 0) & (dist < NSA_WINDOW)
        s_w = jnp.einsum('bqgrd,bsgd->bqgrs', qg, kwb).astype(jnp.float32) * scale
        s_w = s_w - slopes * dist.astype(jnp.float32)[None, :, None, None, :]
        p_w = _masked_softmax(s_w, mask_w[None, :, None, None, :])
        o_w = jnp.einsum('bqgrs,bsgd->bqgrd', p_w.astype(vwb.dtype), vwb)
        o = gq[..., 0:1] * o_c + gq[..., 1:2] * o_s + gq[..., 2:3] * o_w
        return o.reshape(B, Q_BLOCK, H, D)

    out = lax.map(block, jnp.arange(T // Q_BLOCK))
    return out.transpose(1, 0, 2, 3, 4).reshape(B, T, H * D)


def _stick_breaking(q, k, v):
    B, T, H, D = q.shape
    scale = D ** -0.5
    s_pos = jnp.arange(T)

    def block(i):
        q0 = i * Q_BLOCK
        t = q0 + jnp.arange(Q_BLOCK)
        qb = lax.dynamic_slice_in_dim(q, q0, Q_BLOCK, axis=1)
        z = jnp.einsum('bqhd,bshd->bhqs', qb, k).astype(jnp.float32) * scale
        mask = (s_pos[None, :] < t[:, None])[None, None]
        log_keep = jnp.where(mask, jax.nn.log_sigmoid(-z), 0.0)
        log_a = jax.nn.log_sigmoid(z) + lax.cumsum(log_keep, axis=3, reverse=True) - log_keep
        a = jnp.where(mask, jnp.exp(log_a), 0.0)
        return jnp.einsum('bhqs,bshd->bqhd', a.astype(v.dtype), v)

    out = lax.map(block, jnp.arange(T // Q_BLOCK))
    return out.transpose(1, 0, 2, 3, 4).reshape(B, T, H * D)


def _mixer_conv_nsa(h, w_in, dw_w, dw_b, ln_g, ln_b, pe_k, w1_k, w2_k, pe_v, w1_v, w2_v, w_out):
    B, T, _ = h.shape
    kv = NSA_KV_GROUPS * HEAD_DIM
    sizes = [CONV_CH, CONV_CH, NSA_HEADS * HEAD_DIM] + [kv] * 6 + [NSA_HEADS * NSA_N_BRANCH]
    a_val, a_gate, q, kc, vc, ks, vs, kw, vw, g = _split(h @ w_in, sizes)
    a = a_val * jax.nn.sigmoid(a_gate)
    a = _causal_dwconv(a, dw_w) + dw_b
    a = jax.nn.silu(_layernorm(a, ln_g, ln_b))
    heads = lambda u, n: u.reshape(B, T, n, HEAD_DIM)
    G = NSA_KV_GROUPS
    o_nsa = _nsa(heads(q, NSA_HEADS), heads(kc, G), heads(vc, G), heads(ks, G), heads(vs, G),
                 heads(kw, G), heads(vw, G),
                 jax.nn.sigmoid(g).reshape(B, T, NSA_HEADS, NSA_N_BRANCH),
                 pe_k, w1_k, w2_k, pe_v, w1_v, w2_v)
    return jnp.concatenate([a, o_nsa], axis=-1) @ w_out


def _mixer_shortconv_sb(h, w_in, sc_w, w_out):
    B, T, _ = h.shape
    sizes = [SC_CH] * 3 + [SB_HEADS * HEAD_DIM] * 3
    bg, cg, u, q, k, v = _split(h @ w_in, sizes)
    c_out = bg * _causal_dwconv(cg * u, sc_w)
    heads = lambda z: z.reshape(B, T, SB_HEADS, HEAD_DIM)
    o_sb = _stick_breaking(heads(q), heads(k), heads(v))
    return jnp.concatenate([c_out, o_sb], axis=-1) @ w_out


def setup_inputs(seed: int = 0) -> dict:
    key = jax.random.key(seed)
    keys = iter(jax.random.split(key, 40))
    n_even = (DEPTH + 1) // 2
    n_odd = DEPTH // 2
    def nrm(shape, fan_in):
        return jax.random.normal(next(keys), shape, jnp.float32) * (fan_in ** -0.5)
    def gain(shape):
        return 1.0 + 0.01 * jax.random.normal(next(keys), shape, jnp.float32)
    def small(shape, s=0.01):
        return s * jax.random.normal(next(keys), shape, jnp.float32)
    ab_in = 2 * CONV_CH + NSA_HEADS * HEAD_DIM + 6 * NSA_KV_GROUPS * HEAD_DIM + NSA_HEADS * NSA_N_BRANCH
    ab_out = CONV_CH + NSA_HEADS * HEAD_DIM
    cd_in = 3 * SC_CH + 3 * SB_HEADS * HEAD_DIM
    cd_out = SC_CH + SB_HEADS * HEAD_DIM
    cmp_fan = NSA_CMP_BLOCK * HEAD_DIM
    return {
        'x': jax.random.normal(next(keys), (BATCH, SEQ, D_MODEL), jnp.float32),
        'ffn1_norm': gain((DEPTH, D_MODEL)),
        'ffn1_w_in': nrm((DEPTH, D_MODEL, 2 * D_FF), D_MODEL),
        'ffn1_w_out': nrm((DEPTH, D_FF, D_MODEL), D_FF),
        'mix_norm': gain((DEPTH, D_MODEL)),
        'ffn2_norm': gain((DEPTH, D_MODEL)),
        'ffn2_w_in': nrm((DEPTH, D_MODEL, 2 * D_FF), D_MODEL),
        'ffn2_w_out': nrm((DEPTH, D_FF, D_MODEL), D_FF),
        'ab_w_in': nrm((n_even, D_MODEL, ab_in), D_MODEL),
        'conv_dw_w': nrm((n_even, CONV_WIDTH, CONV_CH), CONV_WIDTH),
        'conv_dw_b': small((n_even, CONV_CH)),
        'conv_ln_g': gain((n_even, CONV_CH)),
        'conv_ln_b': small((n_even, CONV_CH)),
        'nsa_pe_k': small((n_even, NSA_CMP_BLOCK, HEAD_DIM), 0.02),
        'nsa_w1_k': nrm((n_even, NSA_CMP_BLOCK, HEAD_DIM, NSA_CMP_HIDDEN), cmp_fan),
        'nsa_w2_k': nrm((n_even, NSA_CMP_HIDDEN, HEAD_DIM), NSA_CMP_HIDDEN),
        'nsa_pe_v': small((n_even, NSA_CMP_BLOCK, HEAD_DIM), 0.02),
        'nsa_w1_v': nrm((n_even, NSA_CMP_BLOCK, HEAD_DIM, NSA_CMP_HIDDEN), cmp_fan),
        'nsa_w2_v': nrm((n_even, NSA_CMP_HIDDEN, HEAD_DIM), NSA_CMP_HIDDEN),
        'ab_w_out': nrm((n_even, ab_out, D_MODEL), ab_out),
        'cd_w_in': nrm((n_odd, D_MODEL, cd_in), D_MODEL),
        'sc_conv_w': nrm((n_odd, SC_WIDTH, SC_CH), SC_WIDTH),
        'cd_w_out': nrm((n_odd, cd_out, D_MODEL), cd_out),
        'final_norm': gain((D_MODEL,)),
    }


def reference(x, ffn1_norm, ffn1_w_in, ffn1_w_out, mix_norm, ffn2_norm, ffn2_w_in, ffn2_w_out,
              ab_w_in, conv_dw_w, conv_dw_b, conv_ln_g, conv_ln_b,
              nsa_pe_k, nsa_w1_k, nsa_w2_k, nsa_pe_v, nsa_w1_v, nsa_w2_v, ab_w_out,
              cd_w_in, sc_conv_w, cd_w_out, final_norm):
    for layer in range(DEPTH):
        x = x + 0.5 * _swiglu(_rmsnorm(x, ffn1_norm[layer]), ffn1_w_in[layer], ffn1_w_out[layer])
        h = _rmsnorm(x, mix_norm[layer])
        if layer % 2 == 0:
            e = layer // 2
            y = _mixer_conv_nsa(h, ab_w_in[e], conv_dw_w[e], conv_dw_b[e], conv_ln_g[e], conv_ln_b[e],
                                nsa_pe_k[e], nsa_w1_k[e], nsa_w2_k[e],
                                nsa_pe_v[e], nsa_w1_v[e], nsa_w2_v[e], ab_w_out[e])
        else:
            o = layer // 2
            y = _mixer_shortconv_sb(h, cd_w_in[o], sc_conv_w[o], cd_w_out[o])
        x = x + y
        x = x + 0.5 * _swiglu(_rmsnorm(x, ffn2_norm[layer]), ffn2_w_in[layer], ffn2_w_out[layer])
    return _rmsnorm(x, final_norm)
```

```python
import numpy as np
import math
from contextlib import ExitStack
import concourse.bass as bass
import concourse.mybir as mybir
from concourse.bass_utils import run_bass_kernel_spmd
import ml_dtypes

F32 = mybir.dt.float32
BF16 = mybir.dt.bfloat16
AF = mybir.ActivationFunctionType
ALU = mybir.AluOpType
AX = mybir.AxisListType

SEM_EPOCH = 30000


class Buf:
    __slots__ = ("name", "w", "r", "dsem", "dcnt")

    def __init__(self, name):
        self.name = name
        self.w = []
        self.r = []
        self.dsem = None
        self.dcnt = 0


class Prog:
    def __init__(self, nc, es):
        self.nc = nc
        self.es = es
        self.eng = {"pe": nc.tensor, "act": nc.scalar, "dve": nc.vector, "pool": nc.gpsimd, "sp": nc.sync}
        self.sem = {}
        self.cnt = {}
        self.waited = {k: {} for k in self.eng}
        self.nsem = 0
        for k in self.eng:
            self._new_eng_sem(k)
        self.n_inst = 0
        self.n_wait = 0
        self.q = {k: [] for k in self.eng}

    def _new_sem(self, name):
        self.nsem += 1
        return self.es.enter_context(self.nc.semaphore(f"{name}_{self.nsem}"))

    def _new_eng_sem(self, k):
        self.sem[k] = self._new_sem("e" + k)
        self.cnt[k] = 0

    def buf(self, name):
        return Buf(name)

    def sb(self, name, shape, dtype):
        t = self.es.enter_context(self.nc.sbuf_tensor(name, list(shape), dtype))
        return t

    def ps(self, name, shape, dtype):
        t = self.es.enter_context(self.nc.psum_tensor(name, list(shape), dtype))
        return t

    def _wait(self, e, conds, skip_self=False):
        eng = self.eng[e]
        wd = self.waited[e]
        best = {}
        for (s, v, owner) in conds:
            if skip_self and owner == e:
                continue
            key = id(s)
            if wd.get(key, 0) >= v:
                continue
            if key not in best or best[key][1] < v:
                best[key] = (s, v)
        for key, (s, v) in best.items():
            self.q[e].append(("w", s, v))
            wd[key] = v
            self.n_wait += 1

    def op(self, e, fn, reads=(), writes=(), skip_self=False):
        conds = []
        for b in reads:
            conds += b.w
        for b in writes:
            conds += b.w
            conds += b.r
        self._wait(e, conds, skip_self=skip_self)
        if self.cnt[e] >= SEM_EPOCH:
            self._new_eng_sem(e)
        self.cnt[e] += 1
        self.q[e].append(("i", fn, self.sem[e], 1))
        c = (self.sem[e], self.cnt[e], e)
        for b in reads:
            b.r = [x for x in b.r if x[0] is not c[0]] + [c]
        for b in writes:
            b.w = [c]
            b.r = []
        self.n_inst += 1

    def dma(self, e, out, in_, reads=(), writes=(), **kw):
        conds = []
        for b in reads:
            conds += b.w
        for b in writes:
            conds += b.w
            conds += b.r
        self._wait(e, conds)
        tgt = writes[0] if writes else reads[0]
        if tgt.dsem is None:
            tgt.dsem = self._new_sem("d" + tgt.name)
        tgt.dcnt += 1
        eng = self.eng[e]
        self.q[e].append(("i", (lambda: eng.dma_start(out=out, in_=in_, **kw)), tgt.dsem, 16))
        c = (tgt.dsem, 16 * tgt.dcnt, "dma")
        for b in reads:
            b.r = [x for x in b.r if x[0] is not c[0]] + [c]
        for b in writes:
            b.w = [x for x in b.w if x[0] is not c[0]] + [c]
            b.r = []
        self.n_inst += 1

    def dma_fn(self, e, fn, reads=(), writes=()):
        conds = []
        for b in reads:
            conds += b.w
        for b in writes:
            conds += b.w
            conds += b.r
        self._wait(e, conds)
        tgt = writes[0] if writes else reads[0]
        if tgt.dsem is None:
            tgt.dsem = self._new_sem("d" + tgt.name)
        tgt.dcnt += 1
        self.q[e].append(("i", fn, tgt.dsem, 16))
        c = (tgt.dsem, 16 * tgt.dcnt, "dma")
        for b in reads:
            b.r = [x for x in b.r if x[0] is not c[0]] + [c]
        for b in writes:
            b.w = [x for x in b.w if x[0] is not c[0]] + [c]
            b.r = []
        self.n_inst += 1

    def cc(self, kind, in_ap, out_ap, groups, reads=(), writes=()):
        nc = self.nc
        fn = lambda: nc.gpsimd.collective_compute(kind, mybir.AluOpType.bypass, replica_groups=groups, ins=[in_ap], outs=[out_ap])
        self.dma_fn("pool", fn, reads=reads, writes=writes)

    def finish(self, e, bufs):
        conds = []
        for b in bufs:
            conds += b.w
        self._wait(e, conds)

    def emit(self):
        nc = self.nc
        with nc.Block() as block:
            def run(e):
                eng = self.eng[e]
                for it in self.q[e]:
                    if it[0] == "w":
                        eng.wait_ge(it[1], it[2])
                    else:
                        it[1]().then_inc(it[2], it[3])

            @block.tensor
            def _(x):
                run("pe")

            @block.scalar
            def _(x):
                run("act")

            @block.vector
            def _(x):
                run("dve")

            @block.gpsimd
            def _(x):
                run("pool")

            @block.sync
            def _(x):
                run("sp")


D = 1024
DFF = 2816
NFC = DFF // 128


def make_ident(nc, P, name="ident"):
    ident = P.sb(name, [128, 128], BF16)
    b = P.buf(name)
    P.op("pool", lambda: nc.gpsimd.memset(ident[:], 0.0), writes=[b])
    P.op("pool", lambda: nc.gpsimd.affine_select(out=ident[:], in_=ident[:], pattern=[[-1, 128]],
                                                   compare_op=ALU.not_equal, fill=1.0, base=0,
                                                   channel_multiplier=1), reads=[b], writes=[b])
    return ident, b


class Dense:
    def __init__(self, nc, P, NT):
        self.nc, self.P, self.NT = nc, P, NT
        self.NTILE = NT // 128
        self.NST = NT // 512
        nt = self.NTILE
        self.x = P.sb("x_res", [128, nt, D], F32)
        self.b_x = [P.buf(f"x{t}") for t in range(nt)]
        self.xnT = P.sb("xnT", [128, 8, NT], BF16)
        self.b_xnT = [P.buf(f"xnT{t}") for t in range(nt)]
        self.ident, self.b_id = make_ident(nc, P)
        self.sq = P.sb("sq", [128, D], F32); self.b_sq = P.buf("sq")
        self.ss = P.sb("ss", [128, nt], F32); self.b_ss = P.buf("ss")
        self.rstd = P.sb("rstd", [128, nt], F32); self.b_rstd = P.buf("rstd")
        self.xs = [P.sb(f"xs{i}", [128, D], BF16) for i in range(2)]
        self.b_xs = [P.buf(f"xs{i}") for i in range(2)]
        self.gt = P.sb("gt", [128, 8], F32); self.b_gt = P.buf("gt")
        self.NWB = 6
        self.wb = [P.sb(f"wb{i}", [128, 8 * 512], BF16) for i in range(self.NWB)]
        self.b_wb = [P.buf(f"wb{i}") for i in range(self.NWB)]
        self.wi = 0
        self.tp = P.ps("tp", [128, 8, 128], BF16); self.b_tp = P.buf("tp")
        self.pg = [P.ps(f"pg{i}", [128, 512], F32) for i in range(2)]; self.b_pg = [P.buf(f"pg{i}") for i in range(2)]
        self.pu = [P.ps(f"pu{i}", [128, 512], F32) for i in range(2)]; self.b_pu = [P.buf(f"pu{i}") for i in range(2)]
        self.py = [P.ps(f"py{i}", [128, 512], F32) for i in range(2)]; self.b_py = [P.buf(f"py{i}") for i in range(2)]
        self.ipg = 0
        self.ipy = 0
        self.sg = [P.sb(f"sg{i}", [128, 512], F32) for i in range(2)]; self.b_sg = [P.buf(f"sg{i}") for i in range(2)]
        self.act = [P.sb(f"actT{i}", [128, 4, 512], BF16) for i in range(2)]
        self.b_act = [P.buf(f"actT{i}") for i in range(2)]
        self.iact = 0
        self.stg = [P.sb(f"stg{i}", [128, 512], F32) for i in range(3)]
        self.b_stg = [P.buf(f"stg{i}") for i in range(3)]
        self.istg = 0
        self.gfull = None

    def next_wb(self):
        i = self.wi % self.NWB
        self.wi += 1
        return self.wb[i], self.b_wb[i]

    def load_w(self, src_ap, rc, cols):
        wb, b = self.next_wb()
        view = wb[:, 0:rc * cols].rearrange("p (c n) -> p c n", c=rc)
        self.P.dma("pool", view, src_ap.rearrange("(c p) n -> p c n", p=128), writes=[b])
        return view, b

    def load_x(self, x_dram):
        for t in range(self.NTILE):
            self.P.dma("sp", self.x[:, t, :], x_dram[t * 128:(t + 1) * 128, :], writes=[self.b_x[t]])

    def store_x(self, out_dram, b_out):
        for t in range(self.NTILE):
            self.P.dma("sp", out_dram[t * 128:(t + 1) * 128, :], self.x[:, t, :], reads=[self.b_x[t]], writes=[b_out])

    def stats(self):
        nc, P = self.nc, self.P
        for t in range(self.NTILE):
            P.op("act", lambda t=t: nc.scalar.activation(out=self.sq[:], in_=self.x[:, t, :], func=AF.Square,
                                                           accum_out=self.ss[:, t:t + 1]),
                 reads=[self.b_x[t]], writes=[self.b_sq, self.b_ss])
        P.op("act", lambda: nc.scalar.activation(out=self.rstd[:], in_=self.ss[:], func=AF.Sqrt, scale=1.0 / D, bias=1e-6),
             reads=[self.b_ss], writes=[self.b_rstd])
        P.op("dve", lambda: nc.vector.reciprocal(out=self.rstd[:], in_=self.rstd[:]), reads=[self.b_rstd], writes=[self.b_rstd])

    def norm_T(self, g_dram):
        nc, P = self.nc, self.P
        P.dma("sp", self.gt[:], g_dram.rearrange("(c p) -> p c", p=128), writes=[self.b_gt], allow_slow_non_contiguous=True)
        self.stats()
        for t in range(self.NTILE):
            xs, b_xs = self.xs[t % 2], self.b_xs[t % 2]
            P.op("dve", lambda t=t, xs=xs: nc.vector.tensor_scalar(out=xs[:], in0=self.x[:, t, :], scalar1=self.rstd[:, t:t + 1],
                                                                    scalar2=None, op0=ALU.mult),
                 reads=[self.b_x[t], self.b_rstd], writes=[b_xs])
            for c in range(8):
                P.op("pe", lambda c=c, xs=xs: nc.tensor.transpose(out=self.tp[:, c, :], in_=xs[:, c * 128:(c + 1) * 128],
                                                                   identity=self.ident[:]),
                     reads=[b_xs, self.b_id], writes=[self.b_tp], skip_self=True)
            P.op("dve", lambda t=t: nc.vector.tensor_tensor(out=self.xnT[:, :, t * 128:(t + 1) * 128], in0=self.tp[:],
                                                              in1=self.gt[:].unsqueeze(2).to_broadcast([128, 8, 128]), op=ALU.mult),
                 reads=[self.b_tp, self.b_gt], writes=[self.b_xnT[t]])

    def ffn(self, g_dram, w_in, w_out):
        nc, P = self.nc, self.P
        self.norm_T(g_dram)
        groups = [(s, min(4, NFC - s)) for s in range(0, NFC, 4)]
        for (fc0, nfc) in groups:
            ncol = nfc * 128
            wg, b_wg = self.load_w(w_in[:, fc0 * 128: fc0 * 128 + ncol], 8, ncol)
            wu, b_wu = self.load_w(w_in[:, DFF + fc0 * 128: DFF + fc0 * 128 + ncol], 8, ncol)
            wo, b_wo = self.load_w(w_out[fc0 * 128: fc0 * 128 + ncol, :], nfc, D)
            for st in range(self.NST):
                tiles = list(range(st * 4, st * 4 + 4))
                xb = [self.b_xnT[t] for t in tiles]
                act, b_act = self.act[self.iact % 2], self.b_act[self.iact % 2]
                self.iact += 1
                for j in range(nfc):
                    i = self.ipg % 2
                    self.ipg += 1
                    pg, b_pg, pu, b_pu = self.pg[i], self.b_pg[i], self.pu[i], self.b_pu[i]
                    sg, b_sg = self.sg[i], self.b_sg[i]
                    for k in range(8):
                        P.op("pe", lambda k=k, j=j, pg=pg, wg=wg, st=st: nc.tensor.matmul(
                            pg[:], lhsT=wg[:, k, j * 128:(j + 1) * 128], rhs=self.xnT[:, k, st * 512:(st + 1) * 512],
                            start=(k == 0), stop=(k == 7)), reads=xb + [b_wg], writes=[b_pg], skip_self=True)
                    for k in range(8):
                        P.op("pe", lambda k=k, j=j, pu=pu, wu=wu, st=st: nc.tensor.matmul(
                            pu[:], lhsT=wu[:, k, j * 128:(j + 1) * 128], rhs=self.xnT[:, k, st * 512:(st + 1) * 512],
                            start=(k == 0), stop=(k == 7)), reads=xb + [b_wu], writes=[b_pu], skip_self=True)
                    P.op("act", lambda pg=pg, sg=sg: nc.scalar.activation(out=sg[:], in_=pg[:], func=AF.Silu),
                         reads=[b_pg], writes=[b_sg])
                    P.op("dve", lambda j=j, pu=pu, sg=sg, act=act: nc.vector.tensor_tensor(out=act[:, j, :], in0=pu[:], in1=sg[:], op=ALU.mult),
                         reads=[b_pu, b_sg], writes=[b_act])
                for sub in range(4):
                    t = st * 4 + sub
                    for dh in range(2):
                        i = self.ipy % 2
                        self.ipy += 1
                        py, b_py = self.py[i], self.b_py[i]
                        for j in range(nfc):
                            P.op("pe", lambda j=j, py=py, act=act, wo=wo, sub=sub, dh=dh, nfc=nfc: nc.tensor.matmul(
                                py[:], lhsT=act[:, j, sub * 128:(sub + 1) * 128], rhs=wo[:, j, dh * 512:(dh + 1) * 512],
                                start=(j == 0), stop=(j == nfc - 1)), reads=[b_act, b_wo], writes=[b_py], skip_self=True)
                        P.op("dve", lambda t=t, dh=dh, py=py: nc.vector.scalar_tensor_tensor(
                            out=self.x[:, t, dh * 512:(dh + 1) * 512], in0=py[:], scalar=0.5, in1=self.x[:, t, dh * 512:(dh + 1) * 512],
                            op0=ALU.mult, op1=ALU.add), reads=[b_py, self.b_x[t]], writes=[self.b_x[t]])

    def outproj(self, oT_dram, w_dram):
        nc, P = self.nc, self.P
        for t in range(self.NTILE):
            P.dma("sp", self.xnT[:, :, t * 128:(t + 1) * 128],
                  oT_dram[:, t * 128:(t + 1) * 128].rearrange("(c p) n -> p c n", p=128), writes=[self.b_xnT[t]])
        for dh in range(2):
            w, b_w = self.load_w(w_dram[:, dh * 512:(dh + 1) * 512], 8, 512)
            for t in range(self.NTILE):
                i = self.ipy % 2
                self.ipy += 1
                py, b_py = self.py[i], self.b_py[i]
                for k in range(8):
                    P.op("pe", lambda k=k, py=py, w=w, t=t: nc.tensor.matmul(
                        py[:], lhsT=self.xnT[:, k, t * 128:(t + 1) * 128], rhs=w[:, k, :], start=(k == 0), stop=(k == 7)),
                        reads=[self.b_xnT[t], b_w], writes=[b_py], skip_self=True)
                P.op("dve", lambda t=t, dh=dh, py=py: nc.vector.tensor_tensor(
                    out=self.x[:, t, dh * 512:(dh + 1) * 512], in0=py[:], in1=self.x[:, t, dh * 512:(dh + 1) * 512], op=ALU.add),
                    reads=[b_py, self.b_x[t]], writes=[self.b_x[t]])

    def proj(self, g_dram, w_dram, outs):
        nc, P = self.nc, self.P
        self.norm_T(g_dram)
        for (c0, c1, layout, o_ap, b_o) in outs:
            for cs in range(c0, c1, 512):
                ce = min(cs + 512, c1)
                ncol = ce - cs
                w, b_w = self.load_w(w_dram[:, cs:ce], 8, ncol)
                if layout == "F":
                    assert ncol % 128 == 0
                    for j in range(ncol // 128):
                        for st in range(self.NST):
                            i = self.ipy % 2
                            self.ipy += 1
                            py, b_py = self.py[i], self.b_py[i]
                            xb = [self.b_xnT[t] for t in range(st * 4, st * 4 + 4)]
                            for k in range(8):
                                P.op("pe", lambda k=k, j=j, py=py, w=w, st=st: nc.tensor.matmul(
                                    py[:], lhsT=w[:, k, j * 128:(j + 1) * 128], rhs=self.xnT[:, k, st * 512:(st + 1) * 512],
                                    start=(k == 0), stop=(k == 7)), reads=xb + [b_w], writes=[b_py], skip_self=True)
                            si = self.istg % 3
                            self.istg += 1
                            stg, b_stg = self.stg[si], self.b_stg[si]
                            if o_ap.dtype == BF16:
                                sv = stg[:].bitcast(BF16)[:, 0:512]
                            else:
                                sv = stg[:]
                            eng = "act" if (self.istg % 2) else "dve"
                            if eng == "act":
                                P.op("act", lambda sv=sv, py=py: nc.scalar.copy(out=sv, in_=py[:]), reads=[b_py], writes=[b_stg])
                            else:
                                P.op("dve", lambda sv=sv, py=py: nc.vector.tensor_copy(out=sv, in_=py[:]), reads=[b_py], writes=[b_stg])
                            r0 = cs - c0 + j * 128
                            P.dma("sp", o_ap[r0:r0 + 128, st * 512:(st + 1) * 512], sv, reads=[b_stg], writes=[b_o])
                else:
                    for t in range(self.NTILE):
                        i = self.ipy % 2
                        self.ipy += 1
                        py, b_py = self.py[i], self.b_py[i]
                        for k in range(8):
                            P.op("pe", lambda k=k, py=py, w=w, t=t, ncol=ncol: nc.tensor.matmul(
                                py[:, 0:ncol], lhsT=self.xnT[:, k, t * 128:(t + 1) * 128], rhs=w[:, k, :],
                                start=(k == 0), stop=(k == 7)), reads=[self.b_xnT[t], b_w], writes=[b_py], skip_self=True)
                        si = self.istg % 3
                        self.istg += 1
                        stg, b_stg = self.stg[si], self.b_stg[si]
                        if o_ap.dtype == BF16:
                            sv = stg[:].bitcast(BF16)[:, 0:ncol]
                        else:
                            sv = stg[:, 0:ncol]
                        P.op("dve", lambda sv=sv, py=py, ncol=ncol: nc.vector.tensor_copy(out=sv, in_=py[:, 0:ncol]), reads=[b_py], writes=[b_stg])
                        P.dma("sp", o_ap[t * 128:(t + 1) * 128, cs - c0:ce - c0], sv, reads=[b_stg], writes=[b_o])

    def final(self, g_dram, out_dram, b_out):
        nc, P = self.nc, self.P
        gfull = P.sb("gfull", [128, D], F32)
        b_g = P.buf("gfull")
        P.dma("sp", gfull[:], g_dram.partition_broadcast(128), writes=[b_g])
        self.stats()
        for t in range(self.NTILE):
            si = t % 2
            o = self.fin[si]
            b_o = self.b_fin[si]
            P.op("dve", lambda t=t, o=o: nc.vector.scalar_tensor_tensor(out=o[:], in0=self.x[:, t, :], scalar=self.rstd[:, t:t + 1],
                                                                        in1=gfull[:], op0=ALU.mult, op1=ALU.mult),
                 reads=[self.b_x[t], self.b_rstd, b_g], writes=[b_o])
            P.dma("sp", out_dram[t * 128:(t + 1) * 128, :], o[:], reads=[b_o], writes=[b_out])

    def alloc_final(self):
        P = self.P
        self.fin = [self.sq, self.sq]
        self.b_fin = [self.b_sq, self.b_sq]


def tri_consts(nc, P):
    triu = P.sb("triu", [128, 128], BF16); b_u = P.buf("triu")
    tril = P.sb("tril", [128, 128], BF16); b_l = P.buf("tril")
    P.op("pool", lambda: nc.gpsimd.memset(triu[:], 1.0), writes=[b_u])
    P.op("pool", lambda: nc.gpsimd.affine_select(out=triu[:], in_=triu[:], pattern=[[-1, 128]], compare_op=ALU.is_ge,
                                                   fill=0.0, base=0, channel_multiplier=1), reads=[b_u], writes=[b_u])
    P.op("pool", lambda: nc.gpsimd.memset(tril[:], 0.0), writes=[b_l])
    P.op("pool", lambda: nc.gpsimd.affine_select(out=tril[:], in_=tril[:], pattern=[[-1, 128]], compare_op=ALU.is_ge,
                                                   fill=1.0, base=0, channel_multiplier=1), reads=[b_l], writes=[b_l])
    return triu, b_u, tril, b_l


def causal_masks(nc, P, strict=True, dtype=BF16, name="cm"):
    m = P.sb(name, [128, 4, 512], dtype); b = P.buf(name)
    P.op("pool", lambda: nc.gpsimd.memset(m[:], 1.0), writes=[b])
    for o in range(4):
        P.op("pool", lambda o=o: nc.gpsimd.affine_select(out=m[:, o, :], in_=m[:, o, :], pattern=[[1, 512]],
                                                          compare_op=(ALU.is_gt if strict else ALU.is_ge), fill=0.0,
                                                          base=-128 * o, channel_multiplier=-1), reads=[b], writes=[b])
    return m, b


def build_mixer1(nc, P, T, NTC, d):
    scale = 64 ** -0.5
    NQB = T // 512
    NKT = T // 128
    qT = P.sb("qT_sb", [64, 2, T], BF16); b_q = P.buf("qT")
    kT = P.sb("kT_sb", [64, 2, T], BF16); b_k = P.buf("kT")
    v = P.sb("v_sb", [128, NKT, 2, 64], BF16); b_v = P.buf("v")
    for h in range(2):
        P.dma("sp", qT[:, h, :], d["qT"][h], writes=[b_q])
        P.dma("sp", kT[:, h, :], d["kT"][h], writes=[b_k])
    P.dma("sp", v[:], d["v"].rearrange("(n p) h e -> p n h e", p=128), writes=[b_v])
    triu, b_u, tril, b_l = tri_consts(nc, P)
    cm, b_cm = causal_masks(nc, P, strict=True)
    b_out = P.buf("o_sb_out")
    b_cout = P.buf("cout_out")

    N = NTC
    wT = P.sb("scw_sb", [128, 4, 3], F32); b_w = P.buf("scw")
    P.dma("sp", wT[:], d["scw"].rearrange("p (c k) -> p c k", c=4), writes=[b_w])
    cin = [P.sb(f"cin{i}", [128, 3, N + 2], F32) for i in range(2)]
    b_cin = [P.buf(f"cin{i}") for i in range(2)]
    vv = P.sb("cvv", [128, N + 2], F32); b_vv = P.buf("cvv")
    yy = P.sb("cyy", [128, N], F32); b_yy = P.buf("cyy")
    yo = [P.sb(f"cyo{i}", [128, N], BF16) for i in range(2)]
    b_yo = [P.buf(f"cyo{i}") for i in range(2)]
    for c in range(4):
        ci, b_ci = cin[c % 2], b_cin[c % 2]
        P.dma("sp", ci[:], d["convin"][:, c * 128:(c + 1) * 128, :].rearrange("k p n -> p k n"), writes=[b_ci])
        P.op("pool", lambda ci=ci: nc.gpsimd.tensor_tensor(out=vv[:], in0=ci[:, 1, :], in1=ci[:, 2, :], op=ALU.mult),
             reads=[b_ci], writes=[b_vv])
        P.op("dve", lambda c=c: nc.vector.tensor_scalar(out=yy[:], in0=vv[:, 0:N], scalar1=wT[:, c, 0:1], scalar2=None, op0=ALU.mult),
             reads=[b_vv, b_w], writes=[b_yy])
        for k in (1, 2):
            P.op("dve", lambda c=c, k=k: nc.vector.scalar_tensor_tensor(out=yy[:], in0=vv[:, k:N + k], scalar=wT[:, c, k:k + 1],
                                                                        in1=yy[:], op0=ALU.mult, op1=ALU.add),
                 reads=[b_vv, b_w, b_yy], writes=[b_yy])
        o, b_o = yo[c % 2], b_yo[c % 2]
        P.op("dve", lambda ci=ci, o=o: nc.vector.tensor_tensor(out=o[:], in0=yy[:], in1=ci[:, 0, 2:N + 2], op=ALU.mult),
             reads=[b_yy, b_ci], writes=[b_o])
        P.dma("sp", d["coutT"][c * 128:(c + 1) * 128, :], o[:], reads=[b_o], writes=[b_cout])

    S_ps = [P.ps(f"S{i}", [128, 512], F32) for i in range(2)]
    b_S = [P.buf(f"S{i}") for i in range(2)]
    D_ps = [P.ps(f"D{h}", [128, 512], F32) for h in range(2)]
    b_D = [P.buf(f"D{h}") for h in range(2)]
    O_ps = [P.ps(f"O{h}", [64, 512], F32) for h in range(2)]
    b_O = [P.buf(f"O{h}") for h in range(2)]
    NE, NF, NA = 5, 4, 4
    e_sb = [P.sb(f"e{i}", [128, 512], F32) for i in range(NE)]; b_e = [P.buf(f"e{i}") for i in range(NE)]
    sp_sb = [P.sb(f"sp{i}", [128, 512], BF16) for i in range(NE)]; b_sp = [P.buf(f"sp{i}") for i in range(NE)]
    f_sb = [P.sb(f"f{i}", [128, 512], F32) for i in range(NF)]; b_f = [P.buf(f"f{i}") for i in range(NF)]
    a_sb = [P.sb(f"a{i}", [128, 512], BF16) for i in range(NA)]; b_a = [P.buf(f"a{i}") for i in range(NA)]
    oo = [P.sb(f"oo{h}", [64, 512], BF16) for h in range(2)]
    b_oo = [P.buf(f"oo{h}") for h in range(2)]
    zz = P.sb("zz", [128, 512], BF16); b_zz = P.buf("zz")
    P.op("pool", lambda: nc.gpsimd.memset(zz[:], 0.0), writes=[b_zz])
    items = []
    for qb in range(NQB):
        kmax = 4 * qb + 3
        for kb in range(kmax, -1, -1):
            for h in range(2):
                items.append(dict(qb=qb, kb=kb, h=h, diag=(kb >= 4 * qb), o=kb - 4 * qb, first=(kb == kmax), last=(kb == 0)))
    NI = len(items)

    def stA1(i):
        it = items[i]; h, kb, qb = it["h"], it["kb"], it["qb"]
        Sp, bS = S_ps[i % 2], b_S[i % 2]
        e, be = e_sb[i % NE], b_e[i % NE]
        P.op("pe", lambda: nc.tensor.matmul(Sp[:], lhsT=kT[:, h, kb * 128:(kb + 1) * 128], rhs=qT[:, h, qb * 512:(qb + 1) * 512],
                                            start=True, stop=True), reads=[b_k, b_q], writes=[bS], skip_self=True)
        P.op("act", lambda: nc.scalar.activation(out=e[:], in_=Sp[:], func=AF.Exp, scale=scale), reads=[bS], writes=[be])

    def stA2(i):
        it = items[i]; o = it["o"]
        e, be = e_sb[i % NE], b_e[i % NE]
        sp, bsp = sp_sb[i % NE], b_sp[i % NE]
        P.op("act", lambda: nc.scalar.activation(out=sp[:], in_=e[:], func=AF.Ln, bias=1.0), reads=[be], writes=[bsp])
        if it["diag"]:
            P.op("pool", lambda: nc.gpsimd.tensor_tensor(out=sp[:], in0=sp[:], in1=cm[:, o, :], op=ALU.mult), reads=[bsp, b_cm], writes=[bsp])
            P.op("pool", lambda: nc.gpsimd.tensor_tensor(out=e[:], in0=e[:], in1=cm[:, o, :], op=ALU.mult), reads=[be, b_cm], writes=[be])

    def stB1(i):
        it = items[i]; h = it["h"]
        sp, bsp = sp_sb[i % NE], b_sp[i % NE]
        P.op("pe", lambda: nc.tensor.matmul(D_ps[h][:], lhsT=triu[:], rhs=sp[:], start=it["first"], stop=True, skip_group_check=True),
             reads=[bsp, b_u], writes=[b_D[h]], skip_self=True)

    def stB2(i):
        it = items[i]; h = it["h"]
        f, bf_ = f_sb[i % NF], b_f[i % NF]
        P.op("act", lambda: nc.scalar.activation(out=f[:], in_=D_ps[h][:], func=AF.Exp, scale=-1.0), reads=[b_D[h]], writes=[bf_])

    def stC(i):
        it = items[i]; h = it["h"]
        sp, bsp = sp_sb[i % NE], b_sp[i % NE]
        e, be = e_sb[i % NE], b_e[i % NE]
        f, bf_ = f_sb[i % NF], b_f[i % NF]
        a, ba = a_sb[i % NA], b_a[i % NA]
        if not it["last"]:
            P.op("pe", lambda: nc.tensor.matmul(D_ps[h][:], lhsT=tril[:], rhs=sp[:], start=False, stop=True, skip_group_check=True),
                 reads=[bsp, b_l], writes=[b_D[h]], skip_self=True)
        P.op("dve", lambda: nc.vector.tensor_tensor(out=a[:], in0=e[:], in1=f[:], op=ALU.mult), reads=[be, bf_], writes=[ba])

    def stD(i):
        it = items[i]; h, kb, qb = it["h"], it["kb"], it["qb"]
        a, ba = a_sb[i % NA], b_a[i % NA]
        if it["first"]:
            P.op("pe", lambda: nc.tensor.matmul(O_ps[h][:], lhsT=zz[:, 0:64], rhs=zz[:], start=True, stop=True),
                 reads=[b_zz], writes=[b_O[h]], skip_self=True)
        P.op("pe", lambda: nc.tensor.matmul(O_ps[h][:], lhsT=v[:, kb, h, :], rhs=a[:], start=False, stop=True, skip_group_check=True),
             reads=[ba, b_v], writes=[b_O[h]], skip_self=True)
        if it["last"]:
            P.op("dve", lambda: nc.vector.tensor_copy(out=oo[h][:], in_=O_ps[h][:]), reads=[b_O[h]], writes=[b_oo[h]])
            P.dma("sp", d["o_sbT"][h, :, qb * 512:(qb + 1) * 512], oo[h][:], reads=[b_oo[h]], writes=[b_out])

    for s_ in range(-2, NI + 1):
        if 0 <= s_ + 2 < NI:
            stA1(s_ + 2)
        if 0 <= s_ + 1 < NI:
            stB1(s_ + 1)
            stB2(s_ + 1)
        if 0 <= s_ + 2 < NI:
            stA2(s_ + 2)
        if 0 <= s_ < NI:
            stC(s_)
        if 0 <= s_ - 1 < NI:
            stD(s_ - 1)
    return [b_out, b_cout]


def build_nsa(nc, P, T, qbs, d, banks, NOWN=4):
    scale = 64 ** -0.5
    NCP = T // 16
    NC = NCP - 1
    NCT = NCP // 128
    NKT = T // 128
    ident, b_id = make_ident(nc, P, "ident0")
    b_out = P.buf("nsa_out")

    QAq = [P.sb(f"QAq{i}", [68, 4, 512], BF16) for i in range(2)]; b_QAq = [P.buf(f"QAq{i}") for i in range(2)]
    cur = {}
    KSA = P.sb("KSA_sb", [68, T], BF16); b_KSA = P.buf("KSA")
    KWA = P.sb("KWA_sb", [68, T], BF16); b_KWA = P.buf("KWA")
    P.dma("sp", KSA[:], d["KSA"], writes=[b_KSA])
    P.dma("sp", KWA[:], d["KWA"], writes=[b_KWA])
    VSA = P.sb("VSA_sb", [128, NKT, 65], BF16); b_VSA = P.buf("VSA")
    VWA = P.sb("VWA_sb", [128, NKT, 65], BF16); b_VWA = P.buf("VWA")
    P.dma("sp", VSA[:], d["VSA"].rearrange("(n p) e -> p n e", p=128), writes=[b_VSA])
    P.dma("sp", VWA[:], d["VWA"].rearrange("(n p) e -> p n e", p=128), writes=[b_VWA])
    Wsel = P.sb("Wsel", [128, T], BF16); b_Wsel = P.buf("Wsel")
    P.op("pool", lambda: nc.gpsimd.memset(Wsel[:], 1.0), writes=[b_Wsel])
    P.op("pool", lambda: nc.gpsimd.affine_select(out=Wsel[:], in_=Wsel[:], pattern=[[1, T]], compare_op=ALU.is_ge, fill=0.0,
                                                   base=0, channel_multiplier=-64), reads=[b_Wsel], writes=[b_Wsel])
    P.op("pool", lambda: nc.gpsimd.affine_select(out=Wsel[:], in_=Wsel[:], pattern=[[-1, T]], compare_op=ALU.is_ge, fill=0.0,
                                                   base=63, channel_multiplier=64), reads=[b_Wsel], writes=[b_Wsel])

    S_ps = [banks[i][0] for i in range(2)]; b_S = [banks[i][1] for i in range(2)]
    MK, b_MK = banks[2]
    A_ps = [banks[3 + i][0] for i in range(4)]; b_A = [banks[3 + i][1] for i in range(4)]
    X, b_X = banks[7]
    zz = P.sb("nzz", [128, 512], BF16); b_zz = P.buf("nzz")
    P.op("pool", lambda: nc.gpsimd.memset(zz[:], 0.0), writes=[b_zz])

    def zero_bank(i):
        P.op("pe", lambda i=i: nc.tensor.matmul(A_ps[i][:], lhsT=zz[:, 0:128], rhs=zz[:], start=True, stop=True),
             reads=[b_zz], writes=[b_A[i]], skip_self=True)

    KcA = P.sb("KcA", [68, NCP], BF16); b_KcA = P.buf("KcA")
    VcX = P.sb("VcX", [128, NCT, 193], BF16); b_VcX = P.buf("VcX")
    P.dma("sp", KcA[64:68, :], d["kaugc"], writes=[b_KcA])
    P.dma("sp", VcX[:, :, 65:193], d["poolm"].rearrange("(n p) j -> p n j", p=128), writes=[b_VcX])
    P.op("pool", lambda: nc.gpsimd.memset(VcX[:, :, 64:65], 1.0), reads=[], writes=[b_VcX])
    w1 = P.sb("w1_sb", [128, 16, 128], BF16); b_w1 = P.buf("w1")
    pe2 = P.sb("pe2_sb", [128, 16], BF16); b_pe2 = P.buf("pe2")
    w2 = P.sb("w2_sb", [128, 64], BF16); b_w2 = P.buf("w2")
    c2 = P.sb("c2_sb", [128, T], BF16); b_c2 = P.buf("c2")
    pb = P.sb("pb_sb", [128, 1], F32); b_pb = P.buf("pb")
    xh = P.sb("xh_sb", [128, NCP], F32); b_xh = P.buf("xh")
    uh = P.sb("uh_sb", [128, NCP], F32); b_uh = P.buf("uh")
    hT = P.sb("hT_sb", [128, NCP], BF16); b_hT = P.buf("hT")
    P.op("pool", lambda: nc.gpsimd.memset(hT[:], 0.0), writes=[b_hT])
    for kv in ("k", "v"):
        P.dma("pool", w1[:], d["w1" + kv], writes=[b_w1])
        P.dma("pool", pe2[:], d["pe2" + kv], writes=[b_pe2])
        P.dma("pool", w2[:], d["w2" + kv], writes=[b_w2])
        P.dma("sp", c2[:], d["c2" + kv], writes=[b_c2])
        c2v = c2[:].rearrange("p (n s) -> p n s", s=16)
        for c in range(16):
            if 2 * c < 16:
                rhs = c2v[:, 0:NC, 2 * c]
            else:
                rhs = c2v[:, 1:NC + 1, 2 * c - 16]
            P.op("pe", lambda c=c, rhs=rhs: nc.tensor.matmul(X[:, 0:NC], lhsT=w1[:, c, :], rhs=rhs, start=(c == 0), stop=(c == 15)),
                 reads=[b_w1, b_c2], writes=[b_X], skip_self=True)
        P.op("act", lambda: nc.scalar.copy(out=xh[:, 0:NC], in_=X[:, 0:NC]), reads=[b_X], writes=[b_xh])
        for c in range(16):
            P.op("pe", lambda c=c: nc.tensor.matmul(X[:, 0:1], lhsT=w1[:, c, :], rhs=pe2[:, c:c + 1], start=(c == 0), stop=(c == 15)),
                 reads=[b_w1, b_pe2], writes=[b_X], skip_self=True)
        P.op("dve", lambda: nc.vector.tensor_copy(out=pb[:], in_=X[:, 0:1]), reads=[b_X], writes=[b_pb])
        P.op("dve", lambda: nc.vector.tensor_scalar(out=xh[:, 0:NC], in0=xh[:, 0:NC], scalar1=pb[:, 0:1], scalar2=None, op0=ALU.add),
             reads=[b_xh, b_pb], writes=[b_xh])
        P.op("dve", lambda: nc.vector.tensor_tensor(out=uh[:, 0:NC], in0=xh[:, 0:NC], in1=xh[:, 0:NC], op=ALU.mult),
             reads=[b_xh], writes=[b_uh])
        P.op("dve", lambda: nc.vector.tensor_scalar(out=uh[:, 0:NC], in0=uh[:, 0:NC], scalar1=0.044715, scalar2=1.0, op0=ALU.mult, op1=ALU.add),
             reads=[b_uh], writes=[b_uh])
        P.op("dve", lambda: nc.vector.tensor_tensor(out=uh[:, 0:NC], in0=uh[:, 0:NC], in1=xh[:, 0:NC], op=ALU.mult),
             reads=[b_uh, b_xh], writes=[b_uh])
        P.op("act", lambda: nc.scalar.activation(out=uh[:, 0:NC], in_=uh[:, 0:NC], func=AF.Sigmoid, scale=2.0 * math.sqrt(2.0 / math.pi)),
             reads=[b_uh], writes=[b_uh])
        P.op("dve", lambda: nc.vector.tensor_tensor(out=hT[:, 0:NC], in0=uh[:, 0:NC], in1=xh[:, 0:NC], op=ALU.mult),
             reads=[b_uh, b_xh], writes=[b_hT])
        if kv == "k":
            P.op("pe", lambda: nc.tensor.matmul(X[0:64, 0:NCP], lhsT=w2[:], rhs=hT[:], start=True, stop=True),
                 reads=[b_w2, b_hT], writes=[b_X], skip_self=True)
            P.op("dve", lambda: nc.vector.tensor_copy(out=KcA[0:64, :], in_=X[0:64, 0:NCP]), reads=[b_X], writes=[b_KcA])
        else:
            for n in range(NCT):
                P.op("pe", lambda n=n: nc.tensor.matmul(X[:, n * 64:(n + 1) * 64], lhsT=hT[:, n * 128:(n + 1) * 128], rhs=w2[:],
                                                        start=True, stop=True), reads=[b_w2, b_hT], writes=[b_X], skip_self=True)
            P.op("dve", lambda: nc.vector.tensor_copy(out=VcX[:, :, 0:64], in_=X[:, 0:NCT * 64].rearrange("p (n e) -> p n e", e=64)),
                 reads=[b_X], writes=[b_VcX])

    NE = 5
    e_sb = [P.sb(f"ne{i}", [128, 512], F32) for i in range(NE)]; b_e = [P.buf(f"ne{i}") for i in range(NE)]
    p_sb = [P.sb(f"np{i}", [128, 512], BF16) for i in range(NE)]; b_p = [P.buf(f"np{i}") for i in range(NE)]
    NMK = 3
    mk_sb = [P.sb(f"nmk{i}", [128, 512], BF16) for i in range(NMK)]; b_mk = [P.buf(f"nmk{i}") for i in range(NMK)]
    accC = [P.sb(f"accC{r}", [128, 4, 193], F32) for r in range(4)]; b_accC = [P.buf(f"accC{r}") for r in range(4)]
    accS = [P.sb(f"accS{r}", [128, 4, 65], F32) for r in range(NOWN)]; b_accS = [P.buf(f"accS{r}") for r in range(NOWN)]
    accW = [P.sb(f"accW{r}", [128, 4, 65], F32) for r in range(NOWN)]; b_accW = [P.buf(f"accW{r}") for r in range(NOWN)]
    rden = P.sb("rden", [128, 3, 4, 4], F32)
    b_rdenC = [P.buf(f"rdenC{r}") for r in range(4)]
    b_rdenS = [P.buf(f"rdenS{r}") for r in range(4)]
    b_rdenW = [P.buf(f"rdenW{r}") for r in range(4)]
    imp = P.sb("imp", [128, 4, 128], F32); b_imp = P.buf("imp")
    M1 = P.sb("M1_sb", [128, 4, 128], F32); b_M1 = P.buf("M1")
    A1 = P.sb("A1_sb", [128, 4, 128], F32); b_A1 = P.buf("A1")
    score = [P.sb(f"score{i}", [128, 128], F32) for i in range(2)]; b_score = [P.buf(f"score{i}") for i in range(2)]
    work = [P.sb(f"work{i}", [128, 128], F32) for i in range(2)]; b_work = [P.buf(f"work{i}") for i in range(2)]
    m8 = [P.sb(f"m8{i}", [128, 16], F32) for i in range(2)]; b_m8 = [P.buf(f"m8{i}") for i in range(2)]
    sel = P.sb("sel", [128, 4, 128], BF16); b_sel = P.buf("sel")
    selT = P.sb("selT", [128, 512], BF16); b_selT = P.buf("selT")
    graw2 = [P.sb(f"graw_sb{i}", [128, 4, 3 * NOWN], F32) for i in range(2)]; b_graw2 = [P.buf(f"graw{i}") for i in range(2)]
    gsig2 = [P.sb(f"gsig{i}", [128, 4, 3 * NOWN], F32) for i in range(2)]; b_gsig2 = [P.buf(f"gsig{i}") for i in range(2)]
    wgt = P.sb("wgt", [128, 3, 4, 4], F32); b_wgt = P.buf("wgt")
    oacc = P.sb("oacc", [128, 4, 64 * NOWN], F32); b_oacc = P.buf("oacc")
    obf = P.sb("obf", [128, 4, 64 * NOWN], BF16); b_obf = P.buf("obf")

    items = []
    for qb in qbs:
        nct = min(NCT, (32 * (qb + 1) + 127) // 128)
        for r in range(4):
            for kt in range(nct):
                items.append(dict(kind="C", qb=qb, kt=kt, r=r, first=(kt == 0), last=(kt == nct - 1), qbstart=(r == 0 and kt == 0)))
        kw0 = max(0, 4 * qb - 4)
        for kt in range(kw0, 4 * qb + 4):
            for r in range(NOWN):
                items.append(dict(kind="W", qb=qb, kt=kt, r=r, first=(kt == kw0), last=(kt == 4 * qb + 3), qbstart=False))
        for kt in range(0, 4 * qb + 4):
            for r in range(NOWN):
                items.append(dict(kind="S", qb=qb, kt=kt, r=r, first=(kt == 0), last=(kt == 4 * qb + 3), qbstart=False))
    NI = len(items)

    def banks_of(it):
        r = it["r"]
        if it["kind"] == "C":
            return [2 * (r % 2), 2 * (r % 2) + 1]
        if it["kind"] == "W":
            return [r]
        return [2 + r] if NOWN == 2 else [r]

    def load_qb(qb_):
        q0_ = qb_ * 512
        qi_ = qb_ % 2
        for rr in range(4):
            P.dma("sp", QAq[qi_][:, rr, :], d["QA"][rr][:, q0_:q0_ + 512], writes=[b_QAq[qi_]])
        P.dma("sp", M1[:], d["M1"][q0_:q0_ + 512, :].rearrange("(c p) j -> p c j", p=128), writes=[b_M1])
        P.dma("sp", A1[:], d["A1"][q0_:q0_ + 512, :].rearrange("(c p) j -> p c j", p=128), writes=[b_A1])
        graw, b_graw, gsig, b_gsig = graw2[qi_], b_graw2[qi_], gsig2[qi_], b_gsig2[qi_]
        P.dma("sp", graw[:], d["graw"][q0_:q0_ + 512, :].rearrange("(c p) j -> p c j", p=128), writes=[b_graw])
        P.op("act", lambda: nc.scalar.activation(out=gsig[:], in_=graw[:], func=AF.Sigmoid), reads=[b_graw], writes=[b_gsig])

    def stA(i):
        it = items[i]; kind, qb, kt, r = it["kind"], it["qb"], it["kt"], it["r"]
        q0 = qb * 512
        qi = qb % 2
        if it["qbstart"] and qb == qbs[0]:
            load_qb(qb)
        diag = kt >= 4 * qb
        if kind == "S" and r == 0 and kt == 0:
            selection_pe()
            nxt = qbs.index(qb) + 1
            if nxt < len(qbs):
                load_qb(qbs[nxt])
        if kind == "S" and r == 0:
            mi = kt % NMK
            P.op("pe", lambda: nc.tensor.matmul(MK[:], lhsT=Wsel[:, kt * 128:(kt + 1) * 128], rhs=selT[:], start=True, stop=True),
                 reads=[b_Wsel, b_selT], writes=[b_MK], skip_self=True)
            P.op("act", lambda: nc.scalar.copy(out=mk_sb[mi][:], in_=MK[:]), reads=[b_MK], writes=[b_mk[mi]])
        KA, bKA = {"C": (KcA, b_KcA), "W": (KWA, b_KWA), "S": (KSA, b_KSA)}[kind]
        clamp = True if kind == "C" else diag
        si = i % 2
        ei = i % NE
        QAc, bQAc = QAq[qi], b_QAq[qi]
        P.op("pe", lambda: nc.tensor.matmul(S_ps[si][:], lhsT=KA[:, kt * 128:(kt + 1) * 128], rhs=QAc[:, r, :], start=True, stop=True),
             reads=[bKA, bQAc], writes=[b_S[si]], skip_self=True)
        if clamp:
            P.op("dve", lambda: nc.vector.tensor_scalar(out=e_sb[ei][:], in0=S_ps[si][:], scalar1=40.0 / scale, scalar2=None, op0=ALU.min),
                 reads=[b_S[si]], writes=[b_e[ei]])
            P.op("act", lambda: nc.scalar.activation(out=e_sb[ei][:], in_=e_sb[ei][:], func=AF.Exp, scale=scale), reads=[b_e[ei]], writes=[b_e[ei]])
        else:
            P.op("act", lambda: nc.scalar.activation(out=e_sb[ei][:], in_=S_ps[si][:], func=AF.Exp, scale=scale), reads=[b_S[si]], writes=[b_e[ei]])

    def stB(i):
        it = items[i]; kind, qb, kt, r = it["kind"], it["qb"], it["kt"], it["r"]
        q0 = qb * 512
        ei = i % NE
        diag = kt >= 4 * qb
        e, be, p, bp = e_sb[ei], b_e[ei], p_sb[ei], b_p[ei]
        if kind == "C":
            bs = -(2048 * kt + 31 - q0)
            P.op("pool", lambda: nc.gpsimd.affine_select(out=p[:], in_=e[:], pattern=[[1, 512]], compare_op=ALU.is_ge, fill=0.0,
                                                          base=bs, channel_multiplier=-16), reads=[be], writes=[bp])
        elif kind == "W":
            if diag:
                bs = -(128 * kt - q0)
                P.op("pool", lambda: nc.gpsimd.affine_select(out=p[:], in_=e[:], pattern=[[1, 512]], compare_op=ALU.is_ge, fill=0.0,
                                                              base=bs, channel_multiplier=-1), reads=[be], writes=[bp])
            else:
                bs = 511 - (q0 - 128 * kt)
                P.op("pool", lambda: nc.gpsimd.affine_select(out=p[:], in_=e[:], pattern=[[-1, 512]], compare_op=ALU.is_ge, fill=0.0,
                                                              base=bs, channel_multiplier=1), reads=[be], writes=[bp])
        else:
            mi = kt % NMK
            if diag:
                bs = -(128 * kt - q0)
                P.op("pool", lambda: nc.gpsimd.affine_select(out=e[:], in_=e[:], pattern=[[1, 512]], compare_op=ALU.is_ge, fill=0.0,
                                                              base=bs, channel_multiplier=-1), reads=[be], writes=[be])
            if r % 2 == 0:
                P.op("dve", lambda: nc.vector.tensor_tensor(out=p[:], in0=e[:], in1=mk_sb[mi][:], op=ALU.mult), reads=[be, b_mk[mi]], writes=[bp])
            else:
                P.op("pool", lambda: nc.gpsimd.tensor_tensor(out=p[:], in0=e[:], in1=mk_sb[mi][:], op=ALU.mult), reads=[be, b_mk[mi]], writes=[bp])

    import collections as _col
    pending = _col.deque()

    def flush(n=None):
        while pending and (n is None or n > 0):
            pending.popleft()()
            if n is not None:
                n -= 1

    def selection():
        for c in range(4):
            pending.append(lambda c=c: sel_chunk(c))

    def sel_chunk(c):
        if True:
            k = c % 2
            sc, bsc, wk, bwk, mm, bmm = score[k], b_score[k], work[k], b_work[k], m8[k], b_m8[k]
            P.op("dve", lambda c=c, sc=sc: nc.vector.tensor_tensor(out=sc[:], in0=imp[:, c, :], in1=M1[:, c, :], op=ALU.mult),
                 reads=[b_imp, b_M1], writes=[bsc])
            P.op("dve", lambda c=c, sc=sc: nc.vector.tensor_tensor(out=sc[:], in0=sc[:], in1=A1[:, c, :], op=ALU.add),
                 reads=[bsc, b_A1], writes=[bsc])
            P.op("dve", lambda sc=sc, mm=mm: nc.vector.max(out=mm[:, 0:8], in_=sc[:]), reads=[bsc], writes=[bmm])
            P.op("dve", lambda sc=sc, mm=mm, wk=wk: nc.vector.match_replace(out=wk[:], in_to_replace=mm[:, 0:8], in_values=sc[:], imm_value=-1e9),
                 reads=[bsc, bmm], writes=[bwk])
            P.op("dve", lambda mm=mm, wk=wk: nc.vector.max(out=mm[:, 8:16], in_=wk[:]), reads=[bwk], writes=[bmm])
            P.op("dve", lambda mm=mm: nc.vector.tensor_scalar(out=mm[:, 15:16], in0=mm[:, 15:16], scalar1=0.0, scalar2=None, op0=ALU.max),
                 reads=[bmm], writes=[bmm])
            P.op("dve", lambda c=c, sc=sc, mm=mm: nc.vector.tensor_scalar(out=sel[:, c, :], in0=sc[:], scalar1=mm[:, 15:16], scalar2=None, op0=ALU.is_ge),
                 reads=[bsc, bmm], writes=[b_sel])

    def selection_pe():
        flush()
        for c in range(4):
            P.op("pe", lambda c=c: nc.tensor.matmul(X[:, c * 128:(c + 1) * 128], lhsT=sel[:, c, :], rhs=ident[:], start=True, stop=True),
                 reads=[b_sel, b_id], writes=[b_X], skip_self=True)
        P.op("act", lambda: nc.scalar.copy(out=selT[:], in_=X[:]), reads=[b_X], writes=[b_selT])

    def combine(qb):
        pending.append(lambda: combine_w(qb))
        for r in range(NOWN):
            pending.append(lambda r=r: combine_r(qb, r))
        pending.append(lambda: combine_out(qb))

    def combine_w(qb):
        gsig, b_gsig = gsig2[qb % 2], b_gsig2[qb % 2]
        for br, brd in ((0, b_rdenC), (1, b_rdenS), (2, b_rdenW)):
            P.op("dve", lambda br=br: nc.vector.tensor_tensor(
                out=wgt[:, br, 0:NOWN, :], in0=rden[:, br, 0:NOWN, :],
                in1=gsig[:].rearrange("p c (r b) -> p r c b", b=3)[:, :, :, br], op=ALU.mult),
                reads=brd[0:NOWN] + [b_gsig], writes=[b_wgt])

    def combine_r(qb, r):
        if True:
            for c in range(4):
                P.op("dve", lambda r=r, c=c: nc.vector.tensor_scalar(out=oacc[:, c, r * 64:(r + 1) * 64], in0=accC[r][:, c, 0:64],
                                                                      scalar1=wgt[:, 0, r, c:c + 1], scalar2=None, op0=ALU.mult),
                     reads=[b_accC[r], b_wgt], writes=[b_oacc])
                P.op("dve", lambda r=r, c=c: nc.vector.scalar_tensor_tensor(out=oacc[:, c, r * 64:(r + 1) * 64], in0=accS[r][:, c, 0:64],
                                                                            scalar=wgt[:, 1, r, c:c + 1], in1=oacc[:, c, r * 64:(r + 1) * 64],
                                                                            op0=ALU.mult, op1=ALU.add),
                     reads=[b_accS[r], b_wgt, b_oacc], writes=[b_oacc])
                P.op("dve", lambda r=r, c=c: nc.vector.scalar_tensor_tensor(out=oacc[:, c, r * 64:(r + 1) * 64], in0=accW[r][:, c, 0:64],
                                                                            scalar=wgt[:, 2, r, c:c + 1], in1=oacc[:, c, r * 64:(r + 1) * 64],
                                                                            op0=ALU.mult, op1=ALU.add),
                     reads=[b_accW[r], b_wgt, b_oacc], writes=[b_oacc])

    def combine_out(qb):
        q0 = qb * 512
        P.op("act", lambda: nc.scalar.copy(out=obf[:], in_=oacc[:]), reads=[b_oacc], writes=[b_obf])
        P.dma("sp", d["o_nsa"][q0:q0 + 512, :].rearrange("(c p) e -> p c e", p=128), obf[:], reads=[b_obf], writes=[b_out])

    def stC(i):
        it = items[i]; kind, qb, kt, r = it["kind"], it["qb"], it["kt"], it["r"]
        ei = i % NE
        p, bp = p_sb[ei], b_p[ei]
        bks = banks_of(it)
        diag = kt >= 4 * qb
        if it["first"]:
            for bk in bks:
                zero_bank(bk)
        if kind == "C":
            for c in range(4):
                bk, off = bks[c // 2], (c % 2) * 193
                P.op("pe", lambda c=c, bk=bk, off=off: nc.tensor.matmul(A_ps[bk][:, off:off + 193], lhsT=p[:, c * 128:(c + 1) * 128], rhs=VcX[:, kt, 0:193],
                                                                        start=False, stop=True, skip_group_check=True),
                     reads=[bp, b_VcX], writes=[b_A[bk]], skip_self=True)
        else:
            VA, bVA = (VWA, b_VWA) if kind == "W" else (VSA, b_VSA)
            bk = bks[0]
            for c in (range(kt - 4 * qb, 4) if diag else range(4)):
                P.op("pe", lambda c=c: nc.tensor.matmul(A_ps[bk][:, c * 65:(c + 1) * 65], lhsT=p[:, c * 128:(c + 1) * 128], rhs=VA[:, kt, 0:65],
                                                        start=False, stop=True, skip_group_check=True),
                     reads=[bp, bVA], writes=[b_A[bk]], skip_self=True)
        if not it["last"]:
            return
        if kind == "C":
            flush()
            P.op("act", lambda: nc.scalar.copy(out=accC[r][:, 0:2, :].rearrange("p c e -> p (c e)"), in_=A_ps[bks[0]][:, 0:386]),
                 reads=[b_A[bks[0]]], writes=[b_accC[r]])
            P.op("dve", lambda: nc.vector.tensor_copy(out=accC[r][:, 2:4, :].rearrange("p c e -> p (c e)"), in_=A_ps[bks[1]][:, 0:386]),
                 reads=[b_A[bks[1]]], writes=[b_accC[r]])
            P.op("dve", lambda: nc.vector.tensor_scalar(out=rden[:, 0, r, :], in0=accC[r][:, :, 64], scalar1=1e-30, scalar2=None, op0=ALU.max),
                 reads=[b_accC[r]], writes=[b_rdenC[r]])
            P.op("dve", lambda: nc.vector.reciprocal(out=rden[:, 0, r, :], in_=rden[:, 0, r, :]), reads=[b_rdenC[r]], writes=[b_rdenC[r]])
            for c in range(4):
                if r == 0:
                    P.op("dve", lambda c=c: nc.vector.tensor_scalar(out=imp[:, c, :], in0=accC[r][:, c, 65:193], scalar1=rden[:, 0, r, c:c + 1],
                                                                     scalar2=None, op0=ALU.mult), reads=[b_accC[r], b_rdenC[r]], writes=[b_imp])
                else:
                    P.op("dve", lambda c=c: nc.vector.scalar_tensor_tensor(out=imp[:, c, :], in0=accC[r][:, c, 65:193], scalar=rden[:, 0, r, c:c + 1],
                                                                           in1=imp[:, c, :], op0=ALU.mult, op1=ALU.add),
                         reads=[b_accC[r], b_rdenC[r], b_imp], writes=[b_imp])
            if r == 3:
                selection()
        else:
            acc, bacc, brd, bri = (accW, b_accW, b_rdenW, 2) if kind == "W" else (accS, b_accS, b_rdenS, 1)
            bk = bks[0]
            if r % 2 == 0:
                P.op("act", lambda: nc.scalar.copy(out=acc[r][:].rearrange("p c e -> p (c e)"), in_=A_ps[bk][:, 0:260]), reads=[b_A[bk]], writes=[bacc[r]])
            else:
                P.op("dve", lambda: nc.vector.tensor_copy(out=acc[r][:].rearrange("p c e -> p (c e)"), in_=A_ps[bk][:, 0:260]), reads=[b_A[bk]], writes=[bacc[r]])
            P.op("dve", lambda: nc.vector.reciprocal(out=rden[:, bri, r, :], in_=acc[r][:, :, 64]), reads=[bacc[r]], writes=[brd[r]])
            if kind == "S" and r == NOWN - 1:
                combine(qb)

    for s_ in range(-2, NI):
        if 0 <= s_ + 2 < NI:
            stA(s_ + 2)
        if 0 <= s_ + 1 < NI:
            stB(s_ + 1)
        if 0 <= s_ < NI:
            stC(s_)
        flush(1)
    flush()
    return [b_out]


def alloc_banks(P):
    return [(P.ps(f"bank{i}", [128, 512], F32), P.buf(f"bank{i}")) for i in range(8)]


def build_conv(nc, P, NTC, d, banks, ident, b_id):
    N = NTC
    NTT = N // 512
    b_out = P.buf("conv_out")
    dww = P.sb("dww_sb", [128, 4, 31], F32); b_dww = P.buf("dww")
    prm = P.sb("cprm_sb", [128, 3, 4], F32); b_prm = P.buf("cprm")
    P.dma("sp", dww[:], d["dww"], writes=[b_dww])
    P.dma("sp", prm[:, 0, :], d["dwb"], writes=[b_prm])
    P.dma("sp", prm[:, 1, :], d["lng"], writes=[b_prm])
    P.dma("sp", prm[:, 2, :], d["lnb"], writes=[b_prm])
    onesF = P.sb("onesF", [128, 128], F32); b_ones = P.buf("onesF")
    P.op("pool", lambda: nc.gpsimd.memset(onesF[:], 1.0 / 512.0), writes=[b_ones])
    ain = [P.sb(f"ain{i}", [128, 2, N + 30], F32) for i in range(2)]; b_ain = [P.buf(f"ain{i}") for i in range(2)]
    abf = [P.sb(f"abf{i}", [128, N + 30], BF16) for i in range(2)]; b_abf = [P.buf(f"abf{i}") for i in range(2)]
    diag = [P.sb(f"diag{i}", [128, 31, 128], BF16) for i in range(2)]; b_diag = [P.buf(f"diag{i}") for i in range(2)]
    y = [P.sb(f"cy{c}", [128, N], F32) for c in range(4)]; b_y = [P.buf(f"cy{c}") for c in range(4)]
    ysq = P.sb("cysq", [128, 512], F32); b_ysq = P.buf("cysq")
    for c in range(4):
        ai, b_ai = ain[c % 2], b_ain[c % 2]
        ab, b_ab = abf[c % 2], b_abf[c % 2]
        dg, b_dg = diag[c % 2], b_diag[c % 2]
        P.dma("sp", ai[:], d["aT"][:, c * 128:(c + 1) * 128, :].rearrange("k p n -> p k n"), writes=[b_ai])
        P.op("act", lambda ai=ai: nc.scalar.activation(out=ai[:, 1, :], in_=ai[:, 1, :], func=AF.Sigmoid), reads=[b_ai], writes=[b_ai])
        P.op("dve", lambda ai=ai, ab=ab: nc.vector.tensor_tensor(out=ab[:], in0=ai[:, 0, :], in1=ai[:, 1, :], op=ALU.mult),
             reads=[b_ai], writes=[b_ab])
        for k in range(31):
            P.op("pool", lambda k=k, c=c, dg=dg: nc.gpsimd.tensor_scalar(out=dg[:, k, :], in0=ident[:], scalar1=dww[:, c, k:k + 1], scalar2=None,
                                                                          op0=ALU.mult), reads=[b_id, b_dww], writes=[b_dg])
        for tt in range(NTT):
            ps, b_ps = banks[tt % 2]
            for k in range(31):
                P.op("pe", lambda k=k, tt=tt, ps=ps, dg=dg, ab=ab: nc.tensor.matmul(ps[:], lhsT=dg[:, k, :], rhs=ab[:, tt * 512 + k: tt * 512 + k + 512],
                                                                                      start=(k == 0), stop=(k == 30)),
                     reads=[b_dg, b_ab], writes=[b_ps], skip_self=True)
            P.op("act", lambda c=c, tt=tt, ps=ps: nc.scalar.activation(out=y[c][:, tt * 512:(tt + 1) * 512], in_=ps[:], func=AF.Identity,
                                                                        bias=prm[:, 0, c:c + 1]), reads=[b_ps, b_prm], writes=[b_y[c]])
    mean = P.sb("cmean", [128, 512], F32); b_mean = P.buf("cmean")
    rstd = P.sb("crstd", [128, 512], F32); b_rstd = P.buf("crstd")
    yn = [P.sb(f"cyn{i}", [128, 512], F32) for i in range(2)]; b_yn = [P.buf(f"cyn{i}") for i in range(2)]
    co = [P.sb(f"cco{i}", [128, 512], BF16) for i in range(2)]; b_co = [P.buf(f"cco{i}") for i in range(2)]
    it = 0
    for tt in range(NTT):
        sl = slice(tt * 512, (tt + 1) * 512)
        pm, b_pm = banks[2]
        pq, b_pq = banks[3]
        for c in range(4):
            P.op("pe", lambda c=c, sl=sl: nc.tensor.matmul(pm[:], lhsT=onesF[:], rhs=y[c][:, sl], start=(c == 0), stop=(c == 3)),
                 reads=[b_ones, b_y[c]], writes=[b_pm], skip_self=True)
        for c in range(4):
            P.op("act", lambda c=c, sl=sl: nc.scalar.activation(out=ysq[:], in_=y[c][:, sl], func=AF.Square), reads=[b_y[c]], writes=[b_ysq])
            P.op("pe", lambda c=c: nc.tensor.matmul(pq[:], lhsT=onesF[:], rhs=ysq[:], start=(c == 0), stop=(c == 3)),
                 reads=[b_ones, b_ysq], writes=[b_pq], skip_self=True)
        P.op("dve", lambda: nc.vector.tensor_copy(out=mean[:], in_=pm[:]), reads=[b_pm], writes=[b_mean])
        P.op("dve", lambda: nc.vector.tensor_tensor(out=rstd[:], in0=mean[:], in1=mean[:], op=ALU.mult), reads=[b_mean], writes=[b_rstd])
        P.op("dve", lambda: nc.vector.tensor_tensor(out=rstd[:], in0=pq[:], in1=rstd[:], op=ALU.subtract), reads=[b_pq, b_rstd], writes=[b_rstd])
        P.op("act", lambda: nc.scalar.activation(out=rstd[:], in_=rstd[:], func=AF.Sqrt, bias=1e-5), reads=[b_rstd], writes=[b_rstd])
        P.op("dve", lambda: nc.vector.reciprocal(out=rstd[:], in_=rstd[:]), reads=[b_rstd], writes=[b_rstd])
        for c in range(4):
            i = it % 2
            it += 1
            P.op("dve", lambda c=c, sl=sl, i=i: nc.vector.tensor_tensor(out=yn[i][:], in0=y[c][:, sl], in1=mean[:], op=ALU.subtract),
                 reads=[b_y[c], b_mean], writes=[b_yn[i]])
            P.op("dve", lambda i=i: nc.vector.tensor_tensor(out=yn[i][:], in0=yn[i][:], in1=rstd[:], op=ALU.mult),
                 reads=[b_yn[i], b_rstd], writes=[b_yn[i]])
            P.op("act", lambda c=c, i=i: nc.scalar.activation(out=co[i][:], in_=yn[i][:], func=AF.Silu, scale=prm[:, 1, c:c + 1],
                                                               bias=prm[:, 2, c:c + 1]), reads=[b_yn[i], b_prm], writes=[b_co[i]])
            P.dma("sp", d["coutT"][c * 128:(c + 1) * 128, sl], co[i][:], reads=[b_co[i]], writes=[b_out])
    return [b_out]

bf = ml_dtypes.bfloat16

def nsa_consts(T):
    t = np.arange(T)
    j = np.arange(128)
    vis = (j[None, :] * 64 <= t[:, None])
    cur = t // 64
    forced = (j[None, :] == 0) | (j[None, :] == cur[:, None]) | (j[None, :] == cur[:, None] - 1)
    M1 = (vis & ~forced).astype(np.float32)
    A1 = np.where(vis, np.where(forced, 1e4, 0.0), -1.0).astype(np.float32)
    NS = T // 64
    M1[:, NS:] = 0.0; A1[:, NS:] = -1.0
    n = np.arange(512)
    poolm = ((n[:, None] >= 4 * j[None, :] - 1) & (n[:, None] <= 4 * j[None, :] + 3)).astype(np.float32).astype(bf)
    kaug_tok = np.stack([t // 64, t % 64, np.ones(T), np.ones(T)]).astype(np.float32).astype(bf)
    kaugc = np.stack([n // 4, 16 * (n % 4) + 15.5, np.ones(512), np.ones(512)]).astype(np.float32).astype(bf)[:, :T // 16]
    return dict(M1=M1, A1=A1, poolm=poolm[:T // 16], kaug_tok=kaug_tok, kaugc=kaugc)

def q_aug(T, h):
    t = np.arange(T)
    c = (2.0 ** (-(h + 1))) * 8.0
    return np.stack([np.full(T, 64 * c), np.full(T, c), -64 * c * (t // 64), -c * (t % 64)]).astype(np.float32).astype(bf)

def nsa_inputs(T, g, qT, kcT, vcT, ksT, kwT, vs, vw, graw, w, consts, horder=(0, 1, 2, 3)):
    d = {}
    QA = np.zeros((4, 68, T), dtype=bf)
    for r in range(4):
        h = 4 * g + horder[r]
        QA[r, :64] = qT[h * 64:(h + 1) * 64]
        QA[r, 64:] = q_aug(T, h)
    d["QA"] = QA
    for nm, src in (("KSA", ksT), ("KWA", kwT)):
        a = np.zeros((68, T), dtype=bf)
        a[:64] = src[g * 64:(g + 1) * 64]
        a[64:] = consts["kaug_tok"]
        d[nm] = a
    for nm, src in (("VSA", vs), ("VWA", vw)):
        a = np.ones((T, 65), dtype=bf)
        a[:, :64] = src[:, g * 64:(g + 1) * 64]
        d[nm] = a
    for nm, src in (("c2k", kcT), ("c2v", vcT)):
        a = np.zeros((128, T), dtype=bf)
        a[:64] = src[g * 64:(g + 1) * 64]
        a[64:, :T - 1] = src[g * 64:(g + 1) * 64, 1:]
        d[nm] = a
    for kv in ("k", "v"):
        w1 = np.asarray(w["w1_" + kv], dtype=np.float32)
        d["w1" + kv] = np.ascontiguousarray(w1.reshape(16, 2, 64, 128).transpose(1, 2, 0, 3).reshape(128, 16, 128))
        pe = np.asarray(w["pe_" + kv], dtype=np.float32)
        d["pe2" + kv] = np.ascontiguousarray(pe.reshape(16, 2, 64).transpose(1, 2, 0).reshape(128, 16))
        d["w2" + kv] = np.ascontiguousarray(np.asarray(w["w2_" + kv], dtype=np.float32))
    d["kaugc"] = consts["kaugc"]
    d["poolm"] = consts["poolm"]
    d["M1"] = consts["M1"]; d["A1"] = consts["A1"]
    h0 = 4 * g + horder[0]
    d["graw"] = np.ascontiguousarray(graw[:, h0 * 3:(h0 + 2) * 3])
    return d


T_SEQ = 8192
NTOK = 2048
NCORE = 8


def _launch(nc, in_maps):
    res = run_bass_kernel_spmd(nc, in_maps, core_ids=list(range(NCORE)))
    return res.results


def _mk(nc, d, name, shape, dt, out=False):
    d[name] = nc.dram_tensor(name, list(shape), dt, kind="ExternalOutput" if out else "ExternalInput").ap()
    return d[name]


def _build_dense(kind):
    nc = bass.Bass("TRN2", target_bir_lowering=False)
    d = {}
    NT = NTOK
    _mk(nc, d, "x", [NT, 1024], F32)
    if kind in ("B", "C"):
        _mk(nc, d, "oT", [1024, NT], BF16)
        _mk(nc, d, "wo", [1024, 1024], F32)
    nffn = {"A": 1, "B": 2, "C": 1}[kind]
    for i in range(nffn):
        _mk(nc, d, f"fg{i}", [1024], F32)
        _mk(nc, d, f"fwi{i}", [1024, 5632], F32)
        _mk(nc, d, f"fwo{i}", [2816, 1024], F32)
    if kind == "A":
        _mk(nc, d, "pg", [1024], F32); _mk(nc, d, "pw", [1024, 2328], F32)
        _mk(nc, d, "xo", [NT, 1024], F32, True)
        _mk(nc, d, "aT", [1024, NT], F32, True); _mk(nc, d, "qT", [512, NT], BF16, True)
        for n in ("kcT", "vcT", "ksT", "kwT"):
            _mk(nc, d, n, [128, NT], BF16, True)
        _mk(nc, d, "vs", [NT, 128], BF16, True); _mk(nc, d, "vw", [NT, 128], BF16, True)
        _mk(nc, d, "gg", [NT, 24], F32, True)
    elif kind == "B":
        _mk(nc, d, "pg", [1024], F32); _mk(nc, d, "pw", [1024, 3072], F32)
        _mk(nc, d, "xo", [NT, 1024], F32, True)
        _mk(nc, d, "cT", [1536, NT], F32, True); _mk(nc, d, "qT", [512, NT], BF16, True)
        _mk(nc, d, "kT", [512, NT], BF16, True); _mk(nc, d, "v", [NT, 512], BF16, True)
    else:
        _mk(nc, d, "gfin", [1024], F32)
        _mk(nc, d, "out", [NT, 1024], F32, True)
    with ExitStack() as es:
        P = Prog(nc, es)
        dn = Dense(nc, P, NT)
        outs = []
        dn.load_x(d["x"])
        if kind in ("B", "C"):
            dn.outproj(d["oT"], d["wo"])
        for i in range(nffn):
            dn.ffn(d[f"fg{i}"], d[f"fwi{i}"], d[f"fwo{i}"])
        if kind == "A":
            names = [(0, 1024, "F", "aT"), (1024, 1536, "F", "qT"), (1536, 1664, "F", "kcT"), (1664, 1792, "F", "vcT"),
                     (1792, 1920, "F", "ksT"), (1920, 2048, "T", "vs"), (2048, 2176, "F", "kwT"), (2176, 2304, "T", "vw"),
                     (2304, 2328, "T", "gg")]
        elif kind == "B":
            names = [(0, 1536, "F", "cT"), (1536, 2048, "F", "qT"), (2048, 2560, "F", "kT"), (2560, 3072, "T", "v")]
        if kind in ("A", "B"):
            bx = P.buf("xo_out")
            dn.store_x(d["xo"], bx)
            outs.append(bx)
            specs = []
            for (c0, c1, lay, n) in names:
                b = P.buf("o_" + n)
                outs.append(b)
                specs.append((c0, c1, lay, d[n], b))
            dn.proj(d["pg"], d["pw"], specs)
        else:
            dn.alloc_final()
            bo = P.buf("out_out")
            dn.final(d["gfin"], d["out"], bo)
            outs.append(bo)
        P.finish("sp", outs)
        P.emit()
    return nc


def _build_conv0():
    nc = bass.Bass("TRN2", target_bir_lowering=False)
    d = {}
    _mk(nc, d, "aT", [2, 512, NTOK + 30], F32); _mk(nc, d, "dww", [128, 4, 31], F32)
    for n in ("dwb", "lng", "lnb"):
        _mk(nc, d, n, [128, 4], F32)
    _mk(nc, d, "coutT", [512, NTOK], BF16, True)
    with ExitStack() as es:
        P = Prog(nc, es)
        banks = alloc_banks(P)
        ident, b_id = make_ident(nc, P, "identc")
        outs = build_conv(nc, P, NTOK, d, banks, ident, b_id)
        P.finish("sp", outs)
        P.emit()
    return nc


def _build_nsa():
    nc = bass.Bass("TRN2", target_bir_lowering=False)
    d = {}
    T = T_SEQ
    _mk(nc, d, "QA", [4, 68, T], BF16); _mk(nc, d, "KSA", [68, T], BF16); _mk(nc, d, "KWA", [68, T], BF16)
    _mk(nc, d, "VSA", [T, 65], BF16); _mk(nc, d, "VWA", [T, 65], BF16)
    _mk(nc, d, "c2k", [128, T], BF16); _mk(nc, d, "c2v", [128, T], BF16)
    for kv in "kv":
        _mk(nc, d, "w1" + kv, [128, 16, 128], F32); _mk(nc, d, "pe2" + kv, [128, 16], F32); _mk(nc, d, "w2" + kv, [128, 64], F32)
    _mk(nc, d, "kaugc", [4, T // 16], BF16); _mk(nc, d, "poolm", [T // 16, 128], BF16)
    _mk(nc, d, "M1", [T, 128], F32); _mk(nc, d, "A1", [T, 128], F32); _mk(nc, d, "graw", [T, 6], F32)
    _mk(nc, d, "o_nsa", [T, 128], BF16, True)
    with ExitStack() as es:
        P = Prog(nc, es)
        outs = build_nsa(nc, P, T, list(range(T // 512)), d, alloc_banks(P), NOWN=2)
        P.finish("sp", outs)
        P.emit()
    return nc


def _build_m1():
    nc = bass.Bass("TRN2", target_bir_lowering=False)
    d = {}
    T = T_SEQ
    _mk(nc, d, "qT", [2, 64, T], BF16); _mk(nc, d, "kT", [2, 64, T], BF16); _mk(nc, d, "v", [T, 2, 64], BF16)
    _mk(nc, d, "convin", [3, 512, NTOK + 2], F32); _mk(nc, d, "scw", [128, 12], F32)
    _mk(nc, d, "o_sbT", [2, 64, T], BF16, True); _mk(nc, d, "coutT", [512, NTOK], BF16, True)
    with ExitStack() as es:
        P = Prog(nc, es)
        outs = build_mixer1(nc, P, T, NTOK, d)
        P.finish("sp", outs)
        P.emit()
    return nc


def _cat_tok(res, name, axis):
    return [np.concatenate([np.asarray(res[b * 4 + j][name]) for j in range(4)], axis=axis) for b in range(2)]


def kernel(x, ffn1_norm, ffn1_w_in, ffn1_w_out, mix_norm, ffn2_norm, ffn2_w_in, ffn2_w_out,
           ab_w_in, conv_dw_w, conv_dw_b, conv_ln_g, conv_ln_b,
           nsa_pe_k, nsa_w1_k, nsa_w2_k, nsa_pe_v, nsa_w1_v, nsa_w2_v, ab_w_out,
           cd_w_in, sc_conv_w, cd_w_out, final_norm):
    f32 = lambda a: np.ascontiguousarray(np.asarray(a, dtype=np.float32))
    x = f32(x)
    T = T_SEQ
    xs = [np.ascontiguousarray(x[c // 4, (c % 4) * NTOK:(c % 4 + 1) * NTOK]) for c in range(NCORE)]
    common = {"fg0": f32(ffn1_norm[0]), "fwi0": f32(ffn1_w_in[0]), "fwo0": f32(ffn1_w_out[0]), "pg": f32(mix_norm[0]), "pw": f32(ab_w_in[0])}
    rA = _launch(_build_dense("A"), [dict(common, x=xs[c]) for c in range(NCORE)])
    aT = _cat_tok(rA, "aT", 1); qT = _cat_tok(rA, "qT", 1)
    kcT = _cat_tok(rA, "kcT", 1); vcT = _cat_tok(rA, "vcT", 1); ksT = _cat_tok(rA, "ksT", 1); kwT = _cat_tok(rA, "kwT", 1)
    vs = _cat_tok(rA, "vs", 0); vw = _cat_tok(rA, "vw", 0); gg = _cat_tok(rA, "gg", 0)
    lay4 = lambda v: np.ascontiguousarray(f32(v).reshape(4, 128).T)
    cc = {"dww": np.ascontiguousarray(f32(conv_dw_w[0]).reshape(31, 4, 128).transpose(2, 1, 0)),
          "dwb": lay4(conv_dw_b[0]), "lng": lay4(conv_ln_g[0]), "lnb": lay4(conv_ln_b[0])}
    maps = []
    for c in range(NCORE):
        b, j = c // 4, c % 4
        a = np.zeros((2, 512, NTOK + 30), dtype=np.float32)
        lo = j * NTOK - 30
        src = aT[b].reshape(2, 512, T)
        if lo < 0:
            a[:, :, 30:] = src[:, :, 0:NTOK]
        else:
            a[:] = src[:, :, lo:lo + NTOK + 30]
        maps.append(dict(cc, aT=a))
    rC0 = _launch(_build_conv0(), maps)
    consts = nsa_consts(T)
    w = dict(pe_k=nsa_pe_k[0], w1_k=nsa_w1_k[0], w2_k=nsa_w2_k[0], pe_v=nsa_pe_v[0], w1_v=nsa_w1_v[0], w2_v=nsa_w2_v[0])
    maps = []
    for c in range(NCORE):
        b, g, hh = c // 4, (c % 4) // 2, c % 2
        horder = [2 * hh, 2 * hh + 1, 2 * (1 - hh), 2 * (1 - hh) + 1]
        dd = nsa_inputs(T, g, qT[b], kcT[b], vcT[b], ksT[b], kwT[b], vs[b], vw[b], gg[b], w, consts, horder)
        maps.append(dd)
    rN = _launch(_build_nsa(), maps)
    oT = []
    for c in range(NCORE):
        b, j = c // 4, c % 4
        o = np.zeros((1024, NTOK), dtype=bf)
        o[0:512] = np.asarray(rC0[c]["coutT"])
        for g in range(2):
            for hh in range(2):
                src = np.asarray(rN[b * 4 + g * 2 + hh]["o_nsa"])[j * NTOK:(j + 1) * NTOK]
                r0 = 512 + (4 * g + 2 * hh) * 64
                o[r0:r0 + 128] = src.T
        oT.append(o)
    common = {"wo": f32(ab_w_out[0]), "fg0": f32(ffn2_norm[0]), "fwi0": f32(ffn2_w_in[0]), "fwo0": f32(ffn2_w_out[0]),
              "fg1": f32(ffn1_norm[1]), "fwi1": f32(ffn1_w_in[1]), "fwo1": f32(ffn1_w_out[1]), "pg": f32(mix_norm[1]), "pw": f32(cd_w_in[0])}
    rB = _launch(_build_dense("B"), [dict(common, x=np.asarray(rA[c]["xo"]), oT=oT[c]) for c in range(NCORE)])
    cT = _cat_tok(rB, "cT", 1); q1 = _cat_tok(rB, "qT", 1); k1 = _cat_tok(rB, "kT", 1); v1 = _cat_tok(rB, "v", 0)
    scw = np.ascontiguousarray(f32(sc_conv_w[0]).reshape(3, 4, 128).transpose(2, 1, 0).reshape(128, 12))
    maps = []
    for c in range(NCORE):
        b, j = c // 4, c % 4
        ci = np.zeros((3, 512, NTOK + 2), dtype=np.float32)
        src = cT[b].reshape(3, 512, T)
        lo = j * NTOK - 2
        if lo < 0:
            ci[:, :, 2:] = src[:, :, 0:NTOK]
        else:
            ci[:] = src[:, :, lo:lo + NTOK + 2]
        hp = j
        maps.append({"qT": np.ascontiguousarray(q1[b][hp * 128:(hp + 1) * 128].reshape(2, 64, T)),
                     "kT": np.ascontiguousarray(k1[b][hp * 128:(hp + 1) * 128].reshape(2, 64, T)),
                     "v": np.ascontiguousarray(v1[b][:, hp * 128:(hp + 1) * 128].reshape(T, 2, 64)),
                     "convin": ci, "scw": scw})
    rM1 = _launch(_build_m1(), maps)
    oT = []
    for c in range(NCORE):
        b, j = c // 4, c % 4
        o = np.zeros((1024, NTOK), dtype=bf)
        o[0:512] = np.asarray(rM1[c]["coutT"])
        for hp in range(4):
            src = np.asarray(rM1[b * 4 + hp]["o_sbT"]).reshape(128, T)[:, j * NTOK:(j + 1) * NTOK]
            o[512 + hp * 128:512 + (hp + 1) * 128] = src
        oT.append(o)
    common = {"wo": f32(cd_w_out[0]), "fg0": f32(ffn2_norm[1]), "fwi0": f32(ffn2_w_in[1]), "fwo0": f32(ffn2_w_out[1]), "gfin": f32(final_norm)}
    rC = _launch(_build_dense("C"), [dict(common, x=np.asarray(rB[c]["xo"]), oT=oT[c]) for c in range(NCORE)])
    out = np.zeros((2, T, 1024), dtype=np.float32)
    for c in range(NCORE):
        out[c // 4, (c % 4) * NTOK:(c % 4 + 1) * NTOK] = np.asarray(rC[c]["out"])
    return out
```

```python
import numpy as np
import math
from contextlib import ExitStack
import concourse.bass as bass
import concourse.mybir as mybir
from concourse.bass_utils import run_bass_kernel_spmd
import ml_dtypes

F32 = mybir.dt.float32
BF16 = mybir.dt.bfloat16
AF = mybir.ActivationFunctionType
ALU = mybir.AluOpType
AX = mybir.AxisListType

SEM_EPOCH = 30000


class Buf:
    __slots__ = ("name", "w", "r", "dsem", "dcnt")

    def __init__(self, name):
        self.name = name
        self.w = []
        self.r = []
        self.dsem = None
        self.dcnt = 0


class Prog:
    def __init__(self, nc, es):
        self.nc = nc
        self.es = es
        self.eng = {"pe": nc.tensor, "act": nc.scalar, "dve": nc.vector, "pool": nc.gpsimd, "sp": nc.sync}
        self.sem = {}
        self.cnt = {}
        self.waited = {k: {} for k in self.eng}
        self.nsem = 0
        for k in self.eng:
            self._new_eng_sem(k)
        self.n_inst = 0
        self.n_wait = 0
        self.q = {k: [] for k in self.eng}

    def _new_sem(self, name):
        self.nsem += 1
        return self.es.enter_context(self.nc.semaphore(f"{name}_{self.nsem}"))

    def _new_eng_sem(self, k):
        self.sem[k] = self._new_sem("e" + k)
        self.cnt[k] = 0

    def buf(self, name):
        return Buf(name)

    def sb(self, name, shape, dtype):
        t = self.es.enter_context(self.nc.sbuf_tensor(name, list(shape), dtype))
        return t

    def ps(self, name, shape, dtype):
        t = self.es.enter_context(self.nc.psum_tensor(name, list(shape), dtype))
        return t

    def _wait(self, e, conds, skip_self=False):
        eng = self.eng[e]
        wd = self.waited[e]
        best = {}
        for (s, v, owner) in conds:
            if skip_self and owner == e:
                continue
            key = id(s)
            if wd.get(key, 0) >= v:
                continue
            if key not in best or best[key][1] < v:
                best[key] = (s, v)
        for key, (s, v) in best.items():
            self.q[e].append(("w", s, v))
            wd[key] = v
            self.n_wait += 1

    def op(self, e, fn, reads=(), writes=(), skip_self=False):
        conds = []
        for b in reads:
            conds += b.w
        for b in writes:
            conds += b.w
            conds += b.r
        self._wait(e, conds, skip_self=skip_self)
        if self.cnt[e] >= SEM_EPOCH:
            self._new_eng_sem(e)
        self.cnt[e] += 1
        self.q[e].append(("i", fn, self.sem[e], 1))
        c = (self.sem[e], self.cnt[e], e)
        for b in reads:
            b.r = [x for x in b.r if x[0] is not c[0]] + [c]
        for b in writes:
            b.w = [c]
            b.r = []
        self.n_inst += 1

    def dma(self, e, out, in_, reads=(), writes=(), **kw):
        conds = []
        for b in reads:
            conds += b.w
        for b in writes:
            conds += b.w
            conds += b.r
        self._wait(e, conds)
        tgt = writes[0] if writes else reads[0]
        if tgt.dsem is None:
            tgt.dsem = self._new_sem("d" + tgt.name)
        tgt.dcnt += 1
        eng = self.eng[e]
        self.q[e].append(("i", (lambda: eng.dma_start(out=out, in_=in_, **kw)), tgt.dsem, 16))
        c = (tgt.dsem, 16 * tgt.dcnt, "dma")
        for b in reads:
            b.r = [x for x in b.r if x[0] is not c[0]] + [c]
        for b in writes:
            b.w = [x for x in b.w if x[0] is not c[0]] + [c]
            b.r = []
        self.n_inst += 1

    def dma_fn(self, e, fn, reads=(), writes=()):
        conds = []
        for b in reads:
            conds += b.w
        for b in writes:
            conds += b.w
            conds += b.r
        self._wait(e, conds)
        tgt = writes[0] if writes else reads[0]
        if tgt.dsem is None:
            tgt.dsem = self._new_sem("d" + tgt.name)
        tgt.dcnt += 1
        self.q[e].append(("i", fn, tgt.dsem, 16))
        c = (tgt.dsem, 16 * tgt.dcnt, "dma")
        for b in reads:
            b.r = [x for x in b.r if x[0] is not c[0]] + [c]
        for b in writes:
            b.w = [x for x in b.w if x[0] is not c[0]] + [c]
            b.r = []
        self.n_inst += 1

    def cc(self, kind, in_ap, out_ap, groups, reads=(), writes=()):
        nc = self.nc
        fn = lambda: nc.gpsimd.collective_compute(kind, mybir.AluOpType.bypass, replica_groups=groups, ins=[in_ap], outs=[out_ap])
        self.dma_fn("pool", fn, reads=reads, writes=writes)

    def finish(self, e, bufs):
        conds = []
        for b in bufs:
            conds += b.w
        self._wait(e, conds)

    def emit(self):
        nc = self.nc
        with nc.Block() as block:
            def run(e):
                eng = self.eng[e]
                for it in self.q[e]:
                    if it[0] == "w":
                        eng.wait_ge(it[1], it[2])
                    else:
                        it[1]().then_inc(it[2], it[3])

            @block.tensor
            def _(x):
                run("pe")

            @block.scalar
            def _(x):
                run("act")

            @block.vector
            def _(x):
                run("dve")

            @block.gpsimd
            def _(x):
                run("pool")

            @block.sync
            def _(x):
                run("sp")


D = 1024
DFF = 2816
NFC = DFF // 128


def make_ident(nc, P, name="ident"):
    ident = P.sb(name, [128, 128], BF16)
    b = P.buf(name)
    P.op("pool", lambda: nc.gpsimd.memset(ident[:], 0.0), writes=[b])
    P.op("pool", lambda: nc.gpsimd.affine_select(out=ident[:], in_=ident[:], pattern=[[-1, 128]],
                                                   compare_op=ALU.not_equal, fill=1.0, base=0,
                                                   channel_multiplier=1), reads=[b], writes=[b])
    return ident, b


class Dense:
    def __init__(self, nc, P, NT):
        self.nc, self.P, self.NT = nc, P, NT
        self.NTILE = NT // 128
        self.NST = NT // 512
        nt = self.NTILE
        self.x = P.sb("x_res", [128, nt, D], F32)
        self.b_x = [P.buf(f"x{t}") for t in range(nt)]
        self.xnT = P.sb("xnT", [128, 8, NT], BF16)
        self.b_xnT = [P.buf(f"xnT{t}") for t in range(nt)]
        self.ident, self.b_id = make_ident(nc, P)
        self.sq = P.sb("sq", [128, D], F32); self.b_sq = P.buf("sq")
        self.ss = P.sb("ss", [128, nt], F32); self.b_ss = P.buf("ss")
        self.rstd = P.sb("rstd", [128, nt], F32); self.b_rstd = P.buf("rstd")
        self.xs = [P.sb(f"xs{i}", [128, D], BF16) for i in range(2)]
        self.b_xs = [P.buf(f"xs{i}") for i in range(2)]
        self.gt = P.sb("gt", [128, 8], F32); self.b_gt = P.buf("gt")
        self.NWB = 6
        self.wb = [P.sb(f"wb{i}", [128, 8 * 512], BF16) for i in range(self.NWB)]
        self.b_wb = [P.buf(f"wb{i}") for i in range(self.NWB)]
        self.wi = 0
        self.tp = P.ps("tp", [128, 8, 128], BF16); self.b_tp = P.buf("tp")
        self.pg = [P.ps(f"pg{i}", [128, 512], F32) for i in range(2)]; self.b_pg = [P.buf(f"pg{i}") for i in range(2)]
        self.pu = [P.ps(f"pu{i}", [128, 512], F32) for i in range(2)]; self.b_pu = [P.buf(f"pu{i}") for i in range(2)]
        self.py = [P.ps(f"py{i}", [128, 512], F32) for i in range(2)]; self.b_py = [P.buf(f"py{i}") for i in range(2)]
        self.ipg = 0
        self.ipy = 0
        self.sg = [P.sb(f"sg{i}", [128, 512], F32) for i in range(2)]; self.b_sg = [P.buf(f"sg{i}") for i in range(2)]
        self.act = [P.sb(f"actT{i}", [128, 4, 512], BF16) for i in range(2)]
        self.b_act = [P.buf(f"actT{i}") for i in range(2)]
        self.iact = 0
        self.stg = [P.sb(f"stg{i}", [128, 512], F32) for i in range(3)]
        self.b_stg = [P.buf(f"stg{i}") for i in range(3)]
        self.istg = 0
        self.gfull = None

    def next_wb(self):
        i = self.wi % self.NWB
        self.wi += 1
        return self.wb[i], self.b_wb[i]

    def load_w(self, src_ap, rc, cols):
        wb, b = self.next_wb()
        view = wb[:, 0:rc * cols].rearrange("p (c n) -> p c n", c=rc)
        self.P.dma("pool", view, src_ap.rearrange("(c p) n -> p c n", p=128), writes=[b])
        return view, b

    def load_x(self, x_dram):
        for t in range(self.NTILE):
            self.P.dma("sp", self.x[:, t, :], x_dram[t * 128:(t + 1) * 128, :], writes=[self.b_x[t]])

    def store_x(self, out_dram, b_out):
        for t in range(self.NTILE):
            self.P.dma("sp", out_dram[t * 128:(t + 1) * 128, :], self.x[:, t, :], reads=[self.b_x[t]], writes=[b_out])

    def stats(self):
        nc, P = self.nc, self.P
        for t in range(self.NTILE):
            P.op("act", lambda t=t: nc.scalar.activation(out=self.sq[:], in_=self.x[:, t, :], func=AF.Square,
                                                           accum_out=self.ss[:, t:t + 1]),
                 reads=[self.b_x[t]], writes=[self.b_sq, self.b_ss])
        P.op("act", lambda: nc.scalar.activation(out=self.rstd[:], in_=self.ss[:], func=AF.Sqrt, scale=1.0 / D, bias=1e-6),
             reads=[self.b_ss], writes=[self.b_rstd])
        P.op("dve", lambda: nc.vector.reciprocal(out=self.rstd[:], in_=self.rstd[:]), reads=[self.b_rstd], writes=[self.b_rstd])

    def norm_T(self, g_dram):
        nc, P = self.nc, self.P
        P.dma("sp", self.gt[:], g_dram.rearrange("(c p) -> p c", p=128), writes=[self.b_gt], allow_slow_non_contiguous=True)
        self.stats()
        for t in range(self.NTILE):
            xs, b_xs = self.xs[t % 2], self.b_xs[t % 2]
            P.op("dve", lambda t=t, xs=xs: nc.vector.tensor_scalar(out=xs[:], in0=self.x[:, t, :], scalar1=self.rstd[:, t:t + 1],
                                                                    scalar2=None, op0=ALU.mult),
                 reads=[self.b_x[t], self.b_rstd], writes=[b_xs])
            for c in range(8):
                P.op("pe", lambda c=c, xs=xs: nc.tensor.transpose(out=self.tp[:, c, :], in_=xs[:, c * 128:(c + 1) * 128],
                                                                   identity=self.ident[:]),
                     reads=[b_xs, self.b_id], writes=[self.b_tp], skip_self=True)
            P.op("dve", lambda t=t: nc.vector.tensor_tensor(out=self.xnT[:, :, t * 128:(t + 1) * 128], in0=self.tp[:],
                                                              in1=self.gt[:].unsqueeze(2).to_broadcast([128, 8, 128]), op=ALU.mult),
                 reads=[self.b_tp, self.b_gt], writes=[self.b_xnT[t]])

    def ffn(self, g_dram, w_in, w_out):
        nc, P = self.nc, self.P
        self.norm_T(g_dram)
        groups = [(s, min(4, NFC - s)) for s in range(0, NFC, 4)]
        for (fc0, nfc) in groups:
            ncol = nfc * 128
            wg, b_wg = self.load_w(w_in[:, fc0 * 128: fc0 * 128 + ncol], 8, ncol)
            wu, b_wu = self.load_w(w_in[:, DFF + fc0 * 128: DFF + fc0 * 128 + ncol], 8, ncol)
            wo, b_wo = self.load_w(w_out[fc0 * 128: fc0 * 128 + ncol, :], nfc, D)
            for st in range(self.NST):
                tiles = list(range(st * 4, st * 4 + 4))
                xb = [self.b_xnT[t] for t in tiles]
                act, b_act = self.act[self.iact % 2], self.b_act[self.iact % 2]
                self.iact += 1
                for j in range(nfc):
                    i = self.ipg % 2
                    self.ipg += 1
                    pg, b_pg, pu, b_pu = self.pg[i], self.b_pg[i], self.pu[i], self.b_pu[i]
                    sg, b_sg = self.sg[i], self.b_sg[i]
                    for k in range(8):
                        P.op("pe", lambda k=k, j=j, pg=pg, wg=wg, st=st: nc.tensor.matmul(
                            pg[:], lhsT=wg[:, k, j * 128:(j + 1) * 128], rhs=self.xnT[:, k, st * 512:(st + 1) * 512],
                            start=(k == 0), stop=(k == 7)), reads=xb + [b_wg], writes=[b_pg], skip_self=True)
                    for k in range(8):
                        P.op("pe", lambda k=k, j=j, pu=pu, wu=wu, st=st: nc.tensor.matmul(
                            pu[:], lhsT=wu[:, k, j * 128:(j + 1) * 128], rhs=self.xnT[:, k, st * 512:(st + 1) * 512],
                            start=(k == 0), stop=(k == 7)), reads=xb + [b_wu], writes=[b_pu], skip_self=True)
                    P.op("act", lambda pg=pg, sg=sg: nc.scalar.activation(out=sg[:], in_=pg[:], func=AF.Silu),
                         reads=[b_pg], writes=[b_sg])
                    P.op("dve", lambda j=j, pu=pu, sg=sg, act=act: nc.vector.tensor_tensor(out=act[:, j, :], in0=pu[:], in1=sg[:], op=ALU.mult),
                         reads=[b_pu, b_sg], writes=[b_act])
                for sub in range(4):
                    t = st * 4 + sub
                    for dh in range(2):
                        i = self.ipy % 2
                        self.ipy += 1
                        py, b_py = self.py[i], self.b_py[i]
                        for j in range(nfc):
                            P.op("pe", lambda j=j, py=py, act=act, wo=wo, sub=sub, dh=dh, nfc=nfc: nc.tensor.matmul(
                                py[:], lhsT=act[:, j, sub * 128:(sub + 1) * 128], rhs=wo[:, j, dh * 512:(dh + 1) * 512],
                                start=(j == 0), stop=(j == nfc - 1)), reads=[b_act, b_wo], writes=[b_py], skip_self=True)
                        P.op("dve", lambda t=t, dh=dh, py=py: nc.vector.scalar_tensor_tensor(
                            out=self.x[:, t, dh * 512:(dh + 1) * 512], in0=py[:], scalar=0.5, in1=self.x[:, t, dh * 512:(dh + 1) * 512],
                            op0=ALU.mult, op1=ALU.add), reads=[b_py, self.b_x[t]], writes=[self.b_x[t]])

    def outproj(self, oT_dram, w_dram):
        nc, P = self.nc, self.P
        for t in range(self.NTILE):
            P.dma("sp", self.xnT[:, :, t * 128:(t + 1) * 128],
                  oT_dram[:, t * 128:(t + 1) * 128].rearrange("(c p) n -> p c n", p=128), writes=[self.b_xnT[t]])
        for dh in range(2):
            w, b_w = self.load_w(w_dram[:, dh * 512:(dh + 1) * 512], 8, 512)
            for t in range(self.NTILE):
                i = self.ipy % 2
                self.ipy += 1
                py, b_py = self.py[i], self.b_py[i]
                for k in range(8):
                    P.op("pe", lambda k=k, py=py, w=w, t=t: nc.tensor.matmul(
                        py[:], lhsT=self.xnT[:, k, t * 128:(t + 1) * 128], rhs=w[:, k, :], start=(k == 0), stop=(k == 7)),
                        reads=[self.b_xnT[t], b_w], writes=[b_py], skip_self=True)
                P.op("dve", lambda t=t, dh=dh, py=py: nc.vector.tensor_tensor(
                    out=self.x[:, t, dh * 512:(dh + 1) * 512], in0=py[:], in1=self.x[:, t, dh * 512:(dh + 1) * 512], op=ALU.add),
                    reads=[b_py, self.b_x[t]], writes=[self.b_x[t]])

    def proj(self, g_dram, w_dram, outs):
        nc, P = self.nc, self.P
        self.norm_T(g_dram)
        for (c0, c1, layout, o_ap, b_o) in outs:
            for cs in range(c0, c1, 512):
                ce = min(cs + 512, c1)
                ncol = ce - cs
                w, b_w = self.load_w(w_dram[:, cs:ce], 8, ncol)
                if layout == "F":
                    assert ncol % 128 == 0
                    for j in range(ncol // 128):
                        for st in range(self.NST):
                            i = self.ipy % 2
                            self.ipy += 1
                            py, b_py = self.py[i], self.b_py[i]
                            xb = [self.b_xnT[t] for t in range(st * 4, st * 4 + 4)]
                            for k in range(8):
                                P.op("pe", lambda k=k, j=j, py=py, w=w, st=st: nc.tensor.matmul(
                                    py[:], lhsT=w[:, k, j * 128:(j + 1) * 128], rhs=self.xnT[:, k, st * 512:(st + 1) * 512],
                                    start=(k == 0), stop=(k == 7)), reads=xb + [b_w], writes=[b_py], skip_self=True)
                            si = self.istg % 3
                            self.istg += 1
                            stg, b_stg = self.stg[si], self.b_stg[si]
                            if o_ap.dtype == BF16:
                                sv = stg[:].bitcast(BF16)[:, 0:512]
                            else:
                                sv = stg[:]
                            eng = "act" if (self.istg % 2) else "dve"
                            if eng == "act":
                                P.op("act", lambda sv=sv, py=py: nc.scalar.copy(out=sv, in_=py[:]), reads=[b_py], writes=[b_stg])
                            else:
                                P.op("dve", lambda sv=sv, py=py: nc.vector.tensor_copy(out=sv, in_=py[:]), reads=[b_py], writes=[b_stg])
                            r0 = cs - c0 + j * 128
                            P.dma("sp", o_ap[r0:r0 + 128, st * 512:(st + 1) * 512], sv, reads=[b_stg], writes=[b_o])
                else:
                    for t in range(self.NTILE):
                        i = self.ipy % 2
                        self.ipy += 1
                        py, b_py = self.py[i], self.b_py[i]
                        for k in range(8):
                            P.op("pe", lambda k=k, py=py, w=w, t=t, ncol=ncol: nc.tensor.matmul(
                                py[:, 0:ncol], lhsT=self.xnT[:, k, t * 128:(t + 1) * 128], rhs=w[:, k, :],
                                start=(k == 0), stop=(k == 7)), reads=[self.b_xnT[t], b_w], writes=[b_py], skip_self=True)
                        si = self.istg % 3
                        self.istg += 1
                        stg, b_stg = self.stg[si], self.b_stg[si]
                        if o_ap.dtype == BF16:
                            sv = stg[:].bitcast(BF16)[:, 0:ncol]
                        else:
                            sv = stg[:, 0:ncol]
                        P.op("dve", lambda sv=sv, py=py, ncol=ncol: nc.vector.tensor_copy(out=sv, in_=py[:, 0:ncol]), reads=[b_py], writes=[b_stg])
                        P.dma("sp", o_ap[t * 128:(t + 1) * 128, cs - c0:ce - c0], sv, reads=[b_stg], writes=[b_o])

    def final(self, g_dram, out_dram, b_out):
        nc, P = self.nc, self.P
        gfull = P.sb("gfull", [128, D], F32)
        b_g = P.buf("gfull")
        P.dma("sp", gfull[:], g_dram.partition_broadcast(128), writes=[b_g])
        self.stats()
        for t in range(self.NTILE):
            si = t % 2
            o = self.fin[si]
            b_o = self.b_fin[si]
            P.op("dve", lambda t=t, o=o: nc.vector.scalar_tensor_tensor(out=o[:], in0=self.x[:, t, :], scalar=self.rstd[:, t:t + 1],
                                                                        in1=gfull[:], op0=ALU.mult, op1=ALU.mult),
                 reads=[self.b_x[t], self.b_rstd, b_g], writes=[b_o])
            P.dma("sp", out_dram[t * 128:(t + 1) * 128, :], o[:], reads=[b_o], writes=[b_out])

    def alloc_final(self):
        P = self.P
        self.fin = [self.sq, self.sq]
        self.b_fin = [self.b_sq, self.b_sq]


def tri_consts(nc, P):
    triu = P.sb("triu", [128, 128], BF16); b_u = P.buf("triu")
    tril = P.sb("tril", [128, 128], BF16); b_l = P.buf("tril")
    P.op("pool", lambda: nc.gpsimd.memset(triu[:], 1.0), writes=[b_u])
    P.op("pool", lambda: nc.gpsimd.affine_select(out=triu[:], in_=triu[:], pattern=[[-1, 128]], compare_op=ALU.is_ge,
                                                   fill=0.0, base=0, channel_multiplier=1), reads=[b_u], writes=[b_u])
    P.op("pool", lambda: nc.gpsimd.memset(tril[:], 0.0), writes=[b_l])
    P.op("pool", lambda: nc.gpsimd.affine_select(out=tril[:], in_=tril[:], pattern=[[-1, 128]], compare_op=ALU.is_ge,
                                                   fill=1.0, base=0, channel_multiplier=1), reads=[b_l], writes=[b_l])
    return triu, b_u, tril, b_l


def causal_masks(nc, P, strict=True, dtype=BF16, name="cm"):
    m = P.sb(name, [128, 4, 512], dtype); b = P.buf(name)
    P.op("pool", lambda: nc.gpsimd.memset(m[:], 1.0), writes=[b])
    for o in range(4):
        P.op("pool", lambda o=o: nc.gpsimd.affine_select(out=m[:, o, :], in_=m[:, o, :], pattern=[[1, 512]],
                                                          compare_op=(ALU.is_gt if strict else ALU.is_ge), fill=0.0,
                                                          base=-128 * o, channel_multiplier=-1), reads=[b], writes=[b])
    return m, b


def build_mixer1(nc, P, T, NTC, d):
    scale = 64 ** -0.5
    NQB = T // 512
    NKT = T // 128
    qT = P.sb("qT_sb", [64, 2, T], BF16); b_q = P.buf("qT")
    kT = P.sb("kT_sb", [64, 2, T], BF16); b_k = P.buf("kT")
    v = P.sb("v_sb", [128, NKT, 2, 64], BF16); b_v = P.buf("v")
    for h in range(2):
        P.dma("sp", qT[:, h, :], d["qT"][h], writes=[b_q])
        P.dma("sp", kT[:, h, :], d["kT"][h], writes=[b_k])
    P.dma("sp", v[:], d["v"].rearrange("(n p) h e -> p n h e", p=128), writes=[b_v])
    triu, b_u, tril, b_l = tri_consts(nc, P)
    cm, b_cm = causal_masks(nc, P, strict=True)
    b_out = P.buf("o_sb_out")
    b_cout = P.buf("cout_out")

    N = NTC
    wT = P.sb("scw_sb", [128, 4, 3], F32); b_w = P.buf("scw")
    P.dma("sp", wT[:], d["scw"].rearrange("p (c k) -> p c k", c=4), writes=[b_w])
    cin = [P.sb(f"cin{i}", [128, 3, N + 2], F32) for i in range(2)]
    b_cin = [P.buf(f"cin{i}") for i in range(2)]
    vv = P.sb("cvv", [128, N + 2], F32); b_vv = P.buf("cvv")
    yy = P.sb("cyy", [128, N], F32); b_yy = P.buf("cyy")
    yo = [P.sb(f"cyo{i}", [128, N], BF16) for i in range(2)]
    b_yo = [P.buf(f"cyo{i}") for i in range(2)]
    for c in range(4):
        ci, b_ci = cin[c % 2], b_cin[c % 2]
        P.dma("sp", ci[:], d["convin"][:, c * 128:(c + 1) * 128, :].rearrange("k p n -> p k n"), writes=[b_ci])
        P.op("pool", lambda ci=ci: nc.gpsimd.tensor_tensor(out=vv[:], in0=ci[:, 1, :], in1=ci[:, 2, :], op=ALU.mult),
             reads=[b_ci], writes=[b_vv])
        P.op("dve", lambda c=c: nc.vector.tensor_scalar(out=yy[:], in0=vv[:, 0:N], scalar1=wT[:, c, 0:1], scalar2=None, op0=ALU.mult),
             reads=[b_vv, b_w], writes=[b_yy])
        for k in (1, 2):
            P.op("dve", lambda c=c, k=k: nc.vector.scalar_tensor_tensor(out=yy[:], in0=vv[:, k:N + k], scalar=wT[:, c, k:k + 1],
                                                                        in1=yy[:], op0=ALU.mult, op1=ALU.add),
                 reads=[b_vv, b_w, b_yy], writes=[b_yy])
        o, b_o = yo[c % 2], b_yo[c % 2]
        P.op("dve", lambda ci=ci, o=o: nc.vector.tensor_tensor(out=o[:], in0=yy[:], in1=ci[:, 0, 2:N + 2], op=ALU.mult),
             reads=[b_yy, b_ci], writes=[b_o])
        P.dma("sp", d["coutT"][c * 128:(c + 1) * 128, :], o[:], reads=[b_o], writes=[b_cout])

    S_ps = [P.ps(f"S{i}", [128, 2, 512], F32) for i in range(2)]
    b_S = [P.buf(f"S{i}") for i in range(2)]
    D_ps = [P.ps(f"D{h}", [128, 512], F32) for h in range(2)]
    b_D = [P.buf(f"D{h}") for h in range(2)]
    O_ps = [P.ps(f"O{h}", [64, 512], F32) for h in range(2)]
    b_O = [P.buf(f"O{h}") for h in range(2)]
    NE, NF, NA = 3, 4, 4
    e_sb = [P.sb(f"e{i}", [128, 2, 512], F32) for i in range(NE)]; b_e = [P.buf(f"e{i}") for i in range(NE)]
    sp_sb = [P.sb(f"sp{i}", [128, 2, 512], BF16) for i in range(NE)]; b_sp = [P.buf(f"sp{i}") for i in range(NE)]
    f_sb = [P.sb(f"f{i}", [128, 512], F32) for i in range(NF)]; b_f = [P.buf(f"f{i}") for i in range(NF)]
    a_sb = [P.sb(f"a{i}", [128, 512], BF16) for i in range(NA)]; b_a = [P.buf(f"a{i}") for i in range(NA)]
    oo = [P.sb(f"oo{h}", [64, 512], BF16) for h in range(2)]
    b_oo = [P.buf(f"oo{h}") for h in range(2)]
    zz = P.sb("zz", [128, 512], BF16); b_zz = P.buf("zz")
    P.op("pool", lambda: nc.gpsimd.memset(zz[:], 0.0), writes=[b_zz])
    items = []
    for qb in range(NQB):
        kmax = 4 * qb + 3
        for kb in range(kmax, -1, -1):
            for h in range(2):
                items.append(dict(qb=qb, kb=kb, h=h, diag=(kb >= 4 * qb), o=kb - 4 * qb, first=(kb == kmax), last=(kb == 0)))
    NI = len(items)
    NP = NI // 2

    def stA1(p):
        it = items[2 * p]; kb, qb = it["kb"], it["qb"]
        Sp, bS = S_ps[p % 2], b_S[p % 2]
        e, be = e_sb[p % NE], b_e[p % NE]
        for h in range(2):
            P.op("pe", lambda h=h: nc.tensor.matmul(Sp[:, h, :], lhsT=kT[:, h, kb * 128:(kb + 1) * 128], rhs=qT[:, h, qb * 512:(qb + 1) * 512],
                                                    start=True, stop=True), reads=[b_k, b_q], writes=[bS], skip_self=True)
        P.op("act", lambda: nc.scalar.activation(out=e[:], in_=Sp[:], func=AF.Exp, scale=scale), reads=[bS], writes=[be])

    def stA2(p):
        it = items[2 * p]; o = it["o"]
        e, be = e_sb[p % NE], b_e[p % NE]
        sp, bsp = sp_sb[p % NE], b_sp[p % NE]
        P.op("act", lambda: nc.scalar.activation(out=sp[:], in_=e[:], func=AF.Ln, bias=1.0), reads=[be], writes=[bsp])
        if it["diag"]:
            mb = cm[:, o, :].unsqueeze(1).to_broadcast([128, 2, 512])
            P.op("pool", lambda: nc.gpsimd.tensor_tensor(out=sp[:], in0=sp[:], in1=mb, op=ALU.mult), reads=[bsp, b_cm], writes=[bsp])
            P.op("pool", lambda: nc.gpsimd.tensor_tensor(out=e[:], in0=e[:], in1=mb, op=ALU.mult), reads=[be, b_cm], writes=[be])

    def stB1(i):
        it = items[i]; h = it["h"]
        sp, bsp = sp_sb[(i // 2) % NE], b_sp[(i // 2) % NE]
        P.op("pe", lambda: nc.tensor.matmul(D_ps[h][:], lhsT=triu[:], rhs=sp[:, h, :], start=it["first"], stop=True, skip_group_check=True),
             reads=[bsp, b_u], writes=[b_D[h]], skip_self=True)

    def stB2(i):
        it = items[i]; h = it["h"]
        f, bf_ = f_sb[i % NF], b_f[i % NF]
        P.op("act", lambda: nc.scalar.activation(out=f[:], in_=D_ps[h][:], func=AF.Exp, scale=-1.0), reads=[b_D[h]], writes=[bf_])

    def stC(i):
        it = items[i]; h = it["h"]
        sp, bsp = sp_sb[(i // 2) % NE], b_sp[(i // 2) % NE]
        e, be = e_sb[(i // 2) % NE], b_e[(i // 2) % NE]
        f, bf_ = f_sb[i % NF], b_f[i % NF]
        a, ba = a_sb[i % NA], b_a[i % NA]
        if not it["last"]:
            P.op("pe", lambda: nc.tensor.matmul(D_ps[h][:], lhsT=tril[:], rhs=sp[:, h, :], start=False, stop=True, skip_group_check=True),
                 reads=[bsp, b_l], writes=[b_D[h]], skip_self=True)
        P.op("dve", lambda: nc.vector.tensor_tensor(out=a[:], in0=e[:, h, :], in1=f[:], op=ALU.mult), reads=[be, bf_], writes=[ba])

    def stD(i):
        it = items[i]; h, kb, qb = it["h"], it["kb"], it["qb"]
        a, ba = a_sb[i % NA], b_a[i % NA]
        if it["first"]:
            P.op("pe", lambda: nc.tensor.matmul(O_ps[h][:], lhsT=zz[:, 0:64], rhs=zz[:], start=True, stop=True),
                 reads=[b_zz], writes=[b_O[h]], skip_self=True)
        P.op("pe", lambda: nc.tensor.matmul(O_ps[h][:], lhsT=v[:, kb, h, :], rhs=a[:], start=False, stop=True, skip_group_check=True),
             reads=[ba, b_v], writes=[b_O[h]], skip_self=True)
        if it["last"]:
            P.op("dve", lambda: nc.vector.tensor_copy(out=oo[h][:], in_=O_ps[h][:]), reads=[b_O[h]], writes=[b_oo[h]])
            P.dma("sp", d["o_sbT"][h, :, qb * 512:(qb + 1) * 512], oo[h][:], reads=[b_oo[h]], writes=[b_out])

    for s_ in range(-4, NI + 1):
        if s_ % 2 == 0 and 0 <= (s_ + 4) // 2 < NP:
            stA1((s_ + 4) // 2)
        if 0 <= s_ + 1 < NI:
            stB1(s_ + 1)
            stB2(s_ + 1)
        if s_ % 2 == 1 and 0 <= (s_ + 3) // 2 < NP:
            stA2((s_ + 3) // 2)
        if 0 <= s_ < NI:
            stC(s_)
        if 0 <= s_ - 1 < NI:
            stD(s_ - 1)
    return [b_out, b_cout]


def build_nsa(nc, P, T, qbs, d, banks, NOWN=4):
    scale = 64 ** -0.5
    NCP = T // 16
    NC = NCP - 1
    NCT = NCP // 128
    NKT = T // 128
    ident, b_id = make_ident(nc, P, "ident0")
    b_out = P.buf("nsa_out")

    QAq = [P.sb(f"QAq{i}", [68, 4, 512], BF16) for i in range(2)]; b_QAq = [P.buf(f"QAq{i}") for i in range(2)]
    cur = {}
    KSA = P.sb("KSA_sb", [68, T], BF16); b_KSA = P.buf("KSA")
    KWA = P.sb("KWA_sb", [68, T], BF16); b_KWA = P.buf("KWA")
    P.dma("sp", KSA[:], d["KSA"], writes=[b_KSA])
    P.dma("sp", KWA[:], d["KWA"], writes=[b_KWA])
    VSA = P.sb("VSA_sb", [128, NKT, 65], BF16); b_VSA = P.buf("VSA")
    VWA = P.sb("VWA_sb", [128, NKT, 65], BF16); b_VWA = P.buf("VWA")
    P.dma("sp", VSA[:], d["VSA"].rearrange("(n p) e -> p n e", p=128), writes=[b_VSA])
    P.dma("sp", VWA[:], d["VWA"].rearrange("(n p) e -> p n e", p=128), writes=[b_VWA])
    Wsel = P.sb("Wsel", [128, T], BF16); b_Wsel = P.buf("Wsel")
    P.op("pool", lambda: nc.gpsimd.memset(Wsel[:], 1.0), writes=[b_Wsel])
    P.op("pool", lambda: nc.gpsimd.affine_select(out=Wsel[:], in_=Wsel[:], pattern=[[1, T]], compare_op=ALU.is_ge, fill=0.0,
                                                   base=0, channel_multiplier=-64), reads=[b_Wsel], writes=[b_Wsel])
    P.op("pool", lambda: nc.gpsimd.affine_select(out=Wsel[:], in_=Wsel[:], pattern=[[-1, T]], compare_op=ALU.is_ge, fill=0.0,
                                                   base=63, channel_multiplier=64), reads=[b_Wsel], writes=[b_Wsel])

    S_ps = [banks[i][0] for i in range(2)]; b_S = [banks[i][1] for i in range(2)]
    MK, b_MK = banks[2]
    A_ps = [banks[3 + i][0] for i in range(4)]; b_A = [banks[3 + i][1] for i in range(4)]
    X, b_X = banks[7]
    zz = P.sb("nzz", [128, 512], BF16); b_zz = P.buf("nzz")
    P.op("pool", lambda: nc.gpsimd.memset(zz[:], 0.0), writes=[b_zz])

    def zero_bank(i):
        P.op("pe", lambda i=i: nc.tensor.matmul(A_ps[i][:], lhsT=zz[:, 0:128], rhs=zz[:], start=True, stop=True),
             reads=[b_zz], writes=[b_A[i]], skip_self=True)

    KcA = P.sb("KcA", [68, NCP], BF16); b_KcA = P.buf("KcA")
    VcX = P.sb("VcX", [128, NCT, 193], BF16); b_VcX = P.buf("VcX")
    P.dma("sp", KcA[64:68, :], d["kaugc"], writes=[b_KcA])
    P.dma("sp", VcX[:, :, 65:193], d["poolm"].rearrange("(n p) j -> p n j", p=128), writes=[b_VcX])
    P.op("pool", lambda: nc.gpsimd.memset(VcX[:, :, 64:65], 1.0), reads=[], writes=[b_VcX])
    w1 = P.sb("w1_sb", [128, 16, 128], BF16); b_w1 = P.buf("w1")
    pe2 = P.sb("pe2_sb", [128, 16], BF16); b_pe2 = P.buf("pe2")
    w2 = P.sb("w2_sb", [128, 64], BF16); b_w2 = P.buf("w2")
    c2 = P.sb("c2_sb", [128, T], BF16); b_c2 = P.buf("c2")
    pb = P.sb("pb_sb", [128, 1], F32); b_pb = P.buf("pb")
    xh = P.sb("xh_sb", [128, NCP], F32); b_xh = P.buf("xh")
    uh = P.sb("uh_sb", [128, NCP], F32); b_uh = P.buf("uh")
    hT = P.sb("hT_sb", [128, NCP], BF16); b_hT = P.buf("hT")
    P.op("pool", lambda: nc.gpsimd.memset(hT[:], 0.0), writes=[b_hT])
    for kv in ("k", "v"):
        P.dma("pool", w1[:], d["w1" + kv], writes=[b_w1])
        P.dma("pool", pe2[:], d["pe2" + kv], writes=[b_pe2])
        P.dma("pool", w2[:], d["w2" + kv], writes=[b_w2])
        P.dma("sp", c2[:], d["c2" + kv], writes=[b_c2])
        c2v = c2[:].rearrange("p (n s) -> p n s", s=16)
        for c in range(16):
            if 2 * c < 16:
                rhs = c2v[:, 0:NC, 2 * c]
            else:
                rhs = c2v[:, 1:NC + 1, 2 * c - 16]
            P.op("pe", lambda c=c, rhs=rhs: nc.tensor.matmul(X[:, 0:NC], lhsT=w1[:, c, :], rhs=rhs, start=(c == 0), stop=(c == 15)),
                 reads=[b_w1, b_c2], writes=[b_X], skip_self=True)
        P.op("act", lambda: nc.scalar.copy(out=xh[:, 0:NC], in_=X[:, 0:NC]), reads=[b_X], writes=[b_xh])
        for c in range(16):
            P.op("pe", lambda c=c: nc.tensor.matmul(X[:, 0:1], lhsT=w1[:, c, :], rhs=pe2[:, c:c + 1], start=(c == 0), stop=(c == 15)),
                 reads=[b_w1, b_pe2], writes=[b_X], skip_self=True)
        P.op("dve", lambda: nc.vector.tensor_copy(out=pb[:], in_=X[:, 0:1]), reads=[b_X], writes=[b_pb])
        P.op("dve", lambda: nc.vector.tensor_scalar(out=xh[:, 0:NC], in0=xh[:, 0:NC], scalar1=pb[:, 0:1], scalar2=None, op0=ALU.add),
             reads=[b_xh, b_pb], writes=[b_xh])
        P.op("dve", lambda: nc.vector.tensor_tensor(out=uh[:, 0:NC], in0=xh[:, 0:NC], in1=xh[:, 0:NC], op=ALU.mult),
             reads=[b_xh], writes=[b_uh])
        P.op("dve", lambda: nc.vector.tensor_scalar(out=uh[:, 0:NC], in0=uh[:, 0:NC], scalar1=0.044715, scalar2=1.0, op0=ALU.mult, op1=ALU.add),
             reads=[b_uh], writes=[b_uh])
        P.op("dve", lambda: nc.vector.tensor_tensor(out=uh[:, 0:NC], in0=uh[:, 0:NC], in1=xh[:, 0:NC], op=ALU.mult),
             reads=[b_uh, b_xh], writes=[b_uh])
        P.op("act", lambda: nc.scalar.activation(out=uh[:, 0:NC], in_=uh[:, 0:NC], func=AF.Sigmoid, scale=2.0 * math.sqrt(2.0 / math.pi)),
             reads=[b_uh], writes=[b_uh])
        P.op("dve", lambda: nc.vector.tensor_tensor(out=hT[:, 0:NC], in0=uh[:, 0:NC], in1=xh[:, 0:NC], op=ALU.mult),
             reads=[b_uh, b_xh], writes=[b_hT])
        if kv == "k":
            P.op("pe", lambda: nc.tensor.matmul(X[0:64, 0:NCP], lhsT=w2[:], rhs=hT[:], start=True, stop=True),
                 reads=[b_w2, b_hT], writes=[b_X], skip_self=True)
            P.op("dve", lambda: nc.vector.tensor_copy(out=KcA[0:64, :], in_=X[0:64, 0:NCP]), reads=[b_X], writes=[b_KcA])
        else:
            for n in range(NCT):
                P.op("pe", lambda n=n: nc.tensor.matmul(X[:, n * 64:(n + 1) * 64], lhsT=hT[:, n * 128:(n + 1) * 128], rhs=w2[:],
                                                        start=True, stop=True), reads=[b_w2, b_hT], writes=[b_X], skip_self=True)
            P.op("dve", lambda: nc.vector.tensor_copy(out=VcX[:, :, 0:64], in_=X[:, 0:NCT * 64].rearrange("p (n e) -> p n e", e=64)),
                 reads=[b_X], writes=[b_VcX])

    NE = 5
    e_sb = [P.sb(f"ne{i}", [128, 512], F32) for i in range(NE)]; b_e = [P.buf(f"ne{i}") for i in range(NE)]
    p_sb = [P.sb(f"np{i}", [128, 512], BF16) for i in range(NE)]; b_p = [P.buf(f"np{i}") for i in range(NE)]
    NMK = 3
    mk_sb = [P.sb(f"nmk{i}", [128, 512], BF16) for i in range(NMK)]; b_mk = [P.buf(f"nmk{i}") for i in range(NMK)]
    accC = [P.sb(f"accC{r}", [128, 4, 193], F32) for r in range(4)]; b_accC = [P.buf(f"accC{r}") for r in range(4)]
    accS = [P.sb(f"accS{r}", [128, 4, 65], F32) for r in range(NOWN)]; b_accS = [P.buf(f"accS{r}") for r in range(NOWN)]
    accW = [P.sb(f"accW{r}", [128, 4, 65], F32) for r in range(NOWN)]; b_accW = [P.buf(f"accW{r}") for r in range(NOWN)]
    rden = P.sb("rden", [128, 3, 4, 4], F32)
    b_rdenC = [P.buf(f"rdenC{r}") for r in range(4)]
    b_rdenS = [P.buf(f"rdenS{r}") for r in range(4)]
    b_rdenW = [P.buf(f"rdenW{r}") for r in range(4)]
    imp = P.sb("imp", [128, 4, 128], F32); b_imp = P.buf("imp")
    M1 = P.sb("M1_sb", [128, 4, 128], F32); b_M1 = P.buf("M1")
    A1 = P.sb("A1_sb", [128, 4, 128], F32); b_A1 = P.buf("A1")
    score = [P.sb(f"score{i}", [128, 128], F32) for i in range(2)]; b_score = [P.buf(f"score{i}") for i in range(2)]
    work = [P.sb(f"work{i}", [128, 128], F32) for i in range(2)]; b_work = [P.buf(f"work{i}") for i in range(2)]
    m8 = [P.sb(f"m8{i}", [128, 16], F32) for i in range(2)]; b_m8 = [P.buf(f"m8{i}") for i in range(2)]
    sel = P.sb("sel", [128, 4, 128], BF16); b_sel = P.buf("sel")
    selT = P.sb("selT", [128, 512], BF16); b_selT = P.buf("selT")
    graw2 = [P.sb(f"graw_sb{i}", [128, 4, 3 * NOWN], F32) for i in range(2)]; b_graw2 = [P.buf(f"graw{i}") for i in range(2)]
    gsig2 = [P.sb(f"gsig{i}", [128, 4, 3 * NOWN], F32) for i in range(2)]; b_gsig2 = [P.buf(f"gsig{i}") for i in range(2)]
    wgt = P.sb("wgt", [128, 3, 4, 4], F32); b_wgt = P.buf("wgt")
    oacc = P.sb("oacc", [128, 4, 64 * NOWN], F32); b_oacc = P.buf("oacc")
    obf = P.sb("obf", [128, 4, 64 * NOWN], BF16); b_obf = P.buf("obf")

    items = []
    for qb in qbs:
        nct = min(NCT, (32 * (qb + 1) + 127) // 128)
        for r in range(4):
            for kt in range(nct):
                items.append(dict(kind="C", qb=qb, kt=kt, r=r, first=(kt == 0), last=(kt == nct - 1), qbstart=(r == 0 and kt == 0)))
        kw0 = max(0, 4 * qb - 4)
        for kt in range(kw0, 4 * qb + 4):
            for r in range(NOWN):
                items.append(dict(kind="W", qb=qb, kt=kt, r=r, first=(kt == kw0), last=(kt == 4 * qb + 3), qbstart=False))
        for kt in range(0, 4 * qb + 4):
            for r in range(NOWN):
                items.append(dict(kind="S", qb=qb, kt=kt, r=r, first=(kt == 0), last=(kt == 4 * qb + 3), qbstart=False))
    NI = len(items)

    def banks_of(it):
        r = it["r"]
        if it["kind"] == "C":
            return [2 * (r % 2), 2 * (r % 2) + 1]
        if it["kind"] == "W":
            return [r]
        return [2 + r] if NOWN == 2 else [r]

    def load_qb(qb_):
        q0_ = qb_ * 512
        qi_ = qb_ % 2
        for rr in range(4):
            P.dma("sp", QAq[qi_][:, rr, :], d["QA"][rr][:, q0_:q0_ + 512], writes=[b_QAq[qi_]])
        P.dma("sp", M1[:], d["M1"][q0_:q0_ + 512, :].rearrange("(c p) j -> p c j", p=128), writes=[b_M1])
        P.dma("sp", A1[:], d["A1"][q0_:q0_ + 512, :].rearrange("(c p) j -> p c j", p=128), writes=[b_A1])
        graw, b_graw, gsig, b_gsig = graw2[qi_], b_graw2[qi_], gsig2[qi_], b_gsig2[qi_]
        P.dma("sp", graw[:], d["graw"][q0_:q0_ + 512, :].rearrange("(c p) j -> p c j", p=128), writes=[b_graw])
        P.op("act", lambda: nc.scalar.activation(out=gsig[:], in_=graw[:], func=AF.Sigmoid), reads=[b_graw], writes=[b_gsig])

    def stA(i):
        it = items[i]; kind, qb, kt, r = it["kind"], it["qb"], it["kt"], it["r"]
        q0 = qb * 512
        qi = qb % 2
        if it["qbstart"] and qb == qbs[0]:
            load_qb(qb)
        diag = kt >= 4 * qb
        if kind == "S" and r == 0 and kt == 0:
            selection_pe()
            nxt = qbs.index(qb) + 1
            if nxt < len(qbs):
                load_qb(qbs[nxt])
        if kind == "S" and r == 0:
            mi = kt % NMK
            P.op("pe", lambda: nc.tensor.matmul(MK[:], lhsT=Wsel[:, kt * 128:(kt + 1) * 128], rhs=selT[:], start=True, stop=True),
                 reads=[b_Wsel, b_selT], writes=[b_MK], skip_self=True)
            P.op("act", lambda: nc.scalar.copy(out=mk_sb[mi][:], in_=MK[:]), reads=[b_MK], writes=[b_mk[mi]])
        KA, bKA = {"C": (KcA, b_KcA), "W": (KWA, b_KWA), "S": (KSA, b_KSA)}[kind]
        clamp = True if kind == "C" else diag
        si = i % 2
        ei = i % NE
        QAc, bQAc = QAq[qi], b_QAq[qi]
        P.op("pe", lambda: nc.tensor.matmul(S_ps[si][:], lhsT=KA[:, kt * 128:(kt + 1) * 128], rhs=QAc[:, r, :], start=True, stop=True),
             reads=[bKA, bQAc], writes=[b_S[si]], skip_self=True)
        if clamp:
            P.op("dve", lambda: nc.vector.tensor_scalar(out=e_sb[ei][:], in0=S_ps[si][:], scalar1=40.0 / scale, scalar2=None, op0=ALU.min),
                 reads=[b_S[si]], writes=[b_e[ei]])
            P.op("act", lambda: nc.scalar.activation(out=e_sb[ei][:], in_=e_sb[ei][:], func=AF.Exp, scale=scale), reads=[b_e[ei]], writes=[b_e[ei]])
        else:
            P.op("act", lambda: nc.scalar.activation(out=e_sb[ei][:], in_=S_ps[si][:], func=AF.Exp, scale=scale), reads=[b_S[si]], writes=[b_e[ei]])

    def stB(i):
        it = items[i]; kind, qb, kt, r = it["kind"], it["qb"], it["kt"], it["r"]
        q0 = qb * 512
        ei = i % NE
        diag = kt >= 4 * qb
        e, be, p, bp = e_sb[ei], b_e[ei], p_sb[ei], b_p[ei]
        if kind == "C":
            bs = -(2048 * kt + 31 - q0)
            P.op("pool", lambda: nc.gpsimd.affine_select(out=p[:], in_=e[:], pattern=[[1, 512]], compare_op=ALU.is_ge, fill=0.0,
                                                          base=bs, channel_multiplier=-16), reads=[be], writes=[bp])
        elif kind == "W":
            if diag:
                bs = -(128 * kt - q0)
                P.op("pool", lambda: nc.gpsimd.affine_select(out=p[:], in_=e[:], pattern=[[1, 512]], compare_op=ALU.is_ge, fill=0.0,
                                                              base=bs, channel_multiplier=-1), reads=[be], writes=[bp])
            else:
                bs = 511 - (q0 - 128 * kt)
                P.op("pool", lambda: nc.gpsimd.affine_select(out=p[:], in_=e[:], pattern=[[-1, 512]], compare_op=ALU.is_ge, fill=0.0,
                                                              base=bs, channel_multiplier=1), reads=[be], writes=[bp])
        else:
            mi = kt % NMK
            if diag:
                bs = -(128 * kt - q0)
                P.op("pool", lambda: nc.gpsimd.affine_select(out=e[:], in_=e[:], pattern=[[1, 512]], compare_op=ALU.is_ge, fill=0.0,
                                                              base=bs, channel_multiplier=-1), reads=[be], writes=[be])
            if r % 2 == 0:
                P.op("dve", lambda: nc.vector.tensor_tensor(out=p[:], in0=e[:], in1=mk_sb[mi][:], op=ALU.mult), reads=[be, b_mk[mi]], writes=[bp])
            else:
                P.op("pool", lambda: nc.gpsimd.tensor_tensor(out=p[:], in0=e[:], in1=mk_sb[mi][:], op=ALU.mult), reads=[be, b_mk[mi]], writes=[bp])

    import collections as _col
    pending = _col.deque()

    def flush(n=None):
        while pending and (n is None or n > 0):
            pending.popleft()()
            if n is not None:
                n -= 1

    def selection():
        for c in range(4):
            pending.append(lambda c=c: sel_chunk(c))

    def sel_chunk(c):
        if True:
            k = c % 2
            sc, bsc, wk, bwk, mm, bmm = score[k], b_score[k], work[k], b_work[k], m8[k], b_m8[k]
            P.op("dve", lambda c=c, sc=sc: nc.vector.tensor_tensor(out=sc[:], in0=imp[:, c, :], in1=M1[:, c, :], op=ALU.mult),
                 reads=[b_imp, b_M1], writes=[bsc])
            P.op("dve", lambda c=c, sc=sc: nc.vector.tensor_tensor(out=sc[:], in0=sc[:], in1=A1[:, c, :], op=ALU.add),
                 reads=[bsc, b_A1], writes=[bsc])
            P.op("dve", lambda sc=sc, mm=mm: nc.vector.max(out=mm[:, 0:8], in_=sc[:]), reads=[bsc], writes=[bmm])
            P.op("dve", lambda sc=sc, mm=mm, wk=wk: nc.vector.match_replace(out=wk[:], in_to_replace=mm[:, 0:8], in_values=sc[:], imm_value=-1e9),
                 reads=[bsc, bmm], writes=[bwk])
            P.op("dve", lambda mm=mm, wk=wk: nc.vector.max(out=mm[:, 8:16], in_=wk[:]), reads=[bwk], writes=[bmm])
            P.op("dve", lambda mm=mm: nc.vector.tensor_scalar(out=mm[:, 15:16], in0=mm[:, 15:16], scalar1=0.0, scalar2=None, op0=ALU.max),
                 reads=[bmm], writes=[bmm])
            P.op("dve", lambda c=c, sc=sc, mm=mm: nc.vector.tensor_scalar(out=sel[:, c, :], in0=sc[:], scalar1=mm[:, 15:16], scalar2=None, op0=ALU.is_ge),
                 reads=[bsc, bmm], writes=[b_sel])

    def selection_pe():
        flush()
        for c in range(4):
            P.op("pe", lambda c=c: nc.tensor.matmul(X[:, c * 128:(c + 1) * 128], lhsT=sel[:, c, :], rhs=ident[:], start=True, stop=True),
                 reads=[b_sel, b_id], writes=[b_X], skip_self=True)
        P.op("act", lambda: nc.scalar.copy(out=selT[:], in_=X[:]), reads=[b_X], writes=[b_selT])

    def combine(qb):
        pending.append(lambda: combine_w(qb))
        for r in range(NOWN):
            pending.append(lambda r=r: combine_r(qb, r))
        pending.append(lambda: combine_out(qb))

    def combine_w(qb):
        gsig, b_gsig = gsig2[qb % 2], b_gsig2[qb % 2]
        for br, brd in ((0, b_rdenC), (1, b_rdenS), (2, b_rdenW)):
            P.op("dve", lambda br=br: nc.vector.tensor_tensor(
                out=wgt[:, br, 0:NOWN, :], in0=rden[:, br, 0:NOWN, :],
                in1=gsig[:].rearrange("p c (r b) -> p r c b", b=3)[:, :, :, br], op=ALU.mult),
                reads=brd[0:NOWN] + [b_gsig], writes=[b_wgt])

    def combine_r(qb, r):
        if True:
            for c in range(4):
                P.op("dve", lambda r=r, c=c: nc.vector.tensor_scalar(out=oacc[:, c, r * 64:(r + 1) * 64], in0=accC[r][:, c, 0:64],
                                                                      scalar1=wgt[:, 0, r, c:c + 1], scalar2=None, op0=ALU.mult),
                     reads=[b_accC[r], b_wgt], writes=[b_oacc])
                P.op("dve", lambda r=r, c=c: nc.vector.scalar_tensor_tensor(out=oacc[:, c, r * 64:(r + 1) * 64], in0=accS[r][:, c, 0:64],
                                                                            scalar=wgt[:, 1, r, c:c + 1], in1=oacc[:, c, r * 64:(r + 1) * 64],
                                                                            op0=ALU.mult, op1=ALU.add),
                     reads=[b_accS[r], b_wgt, b_oacc], writes=[b_oacc])
                P.op("dve", lambda r=r, c=c: nc.vector.scalar_tensor_tensor(out=oacc[:, c, r * 64:(r + 1) * 64], in0=accW[r][:, c, 0:64],
                                                                            scalar=wgt[:, 2, r, c:c + 1], in1=oacc[:, c, r * 64:(r + 1) * 64],
                                                                            op0=ALU.mult, op1=ALU.add),
                     reads=[b_accW[r], b_wgt, b_oacc], writes=[b_oacc])

    def combine_out(qb):
        q0 = qb * 512
        P.op("act", lambda: nc.scalar.copy(out=obf[:], in_=oacc[:]), reads=[b_oacc], writes=[b_obf])
        P.dma("sp", d["o_nsa"][q0:q0 + 512, :].rearrange("(c p) e -> p c e", p=128), obf[:], reads=[b_obf], writes=[b_out])

    def stC(i):
        it = items[i]; kind, qb, kt, r = it["kind"], it["qb"], it["kt"], it["r"]
        ei = i % NE
        p, bp = p_sb[ei], b_p[ei]
        bks = banks_of(it)
        diag = kt >= 4 * qb
        if it["first"]:
            for bk in bks:
                zero_bank(bk)
        if kind == "C":
            for c in range(4):
                bk, off = bks[c // 2], (c % 2) * 193
                P.op("pe", lambda c=c, bk=bk, off=off: nc.tensor.matmul(A_ps[bk][:, off:off + 193], lhsT=p[:, c * 128:(c + 1) * 128], rhs=VcX[:, kt, 0:193],
                                                                        start=False, stop=True, skip_group_check=True),
                     reads=[bp, b_VcX], writes=[b_A[bk]], skip_self=True)
        else:
            VA, bVA = (VWA, b_VWA) if kind == "W" else (VSA, b_VSA)
            bk = bks[0]
            for c in (range(kt - 4 * qb, 4) if diag else range(4)):
                P.op("pe", lambda c=c: nc.tensor.matmul(A_ps[bk][:, c * 65:(c + 1) * 65], lhsT=p[:, c * 128:(c + 1) * 128], rhs=VA[:, kt, 0:65],
                                                        start=False, stop=True, skip_group_check=True),
                     reads=[bp, bVA], writes=[b_A[bk]], skip_self=True)
        if not it["last"]:
            return
        if kind == "C":
            flush()
            P.op("act", lambda: nc.scalar.copy(out=accC[r][:, 0:2, :].rearrange("p c e -> p (c e)"), in_=A_ps[bks[0]][:, 0:386]),
                 reads=[b_A[bks[0]]], writes=[b_accC[r]])
            P.op("dve", lambda: nc.vector.tensor_copy(out=accC[r][:, 2:4, :].rearrange("p c e -> p (c e)"), in_=A_ps[bks[1]][:, 0:386]),
                 reads=[b_A[bks[1]]], writes=[b_accC[r]])
            P.op("dve", lambda: nc.vector.tensor_scalar(out=rden[:, 0, r, :], in0=accC[r][:, :, 64], scalar1=1e-30, scalar2=None, op0=ALU.max),
                 reads=[b_accC[r]], writes=[b_rdenC[r]])
            P.op("dve", lambda: nc.vector.reciprocal(out=rden[:, 0, r, :], in_=rden[:, 0, r, :]), reads=[b_rdenC[r]], writes=[b_rdenC[r]])
            for c in range(4):
                if r == 0:
                    P.op("dve", lambda c=c: nc.vector.tensor_scalar(out=imp[:, c, :], in0=accC[r][:, c, 65:193], scalar1=rden[:, 0, r, c:c + 1],
                                                                     scalar2=None, op0=ALU.mult), reads=[b_accC[r], b_rdenC[r]], writes=[b_imp])
                else:
                    P.op("dve", lambda c=c: nc.vector.scalar_tensor_tensor(out=imp[:, c, :], in0=accC[r][:, c, 65:193], scalar=rden[:, 0, r, c:c + 1],
                                                                           in1=imp[:, c, :], op0=ALU.mult, op1=ALU.add),
                         reads=[b_accC[r], b_rdenC[r], b_imp], writes=[b_imp])
            if r == 3:
                selection()
        else:
            acc, bacc, brd, bri = (accW, b_accW, b_rdenW, 2) if kind == "W" else (accS, b_accS, b_rdenS, 1)
            bk = bks[0]
            if r % 2 == 0:
                P.op("act", lambda: nc.scalar.copy(out=acc[r][:].rearrange("p c e -> p (c e)"), in_=A_ps[bk][:, 0:260]), reads=[b_A[bk]], writes=[bacc[r]])
            else:
                P.op("dve", lambda: nc.vector.tensor_copy(out=acc[r][:].rearrange("p c e -> p (c e)"), in_=A_ps[bk][:, 0:260]), reads=[b_A[bk]], writes=[bacc[r]])
            P.op("dve", lambda: nc.vector.reciprocal(out=rden[:, bri, r, :], in_=acc[r][:, :, 64]), reads=[bacc[r]], writes=[brd[r]])
            if kind == "S" and r == NOWN - 1:
                combine(qb)

    for s_ in range(-2, NI):
        if 0 <= s_ + 2 < NI:
            stA(s_ + 2)
        if 0 <= s_ + 1 < NI:
            stB(s_ + 1)
        if 0 <= s_ < NI:
            stC(s_)
        flush(1)
    flush()
    return [b_out]


def alloc_banks(P):
    return [(P.ps(f"bank{i}", [128, 512], F32), P.buf(f"bank{i}")) for i in range(8)]


def build_conv(nc, P, NTC, d, banks, ident, b_id):
    N = NTC
    NTT = N // 512
    b_out = P.buf("conv_out")
    dww = P.sb("dww_sb", [128, 4, 31], F32); b_dww = P.buf("dww")
    prm = P.sb("cprm_sb", [128, 3, 4], F32); b_prm = P.buf("cprm")
    P.dma("sp", dww[:], d["dww"], writes=[b_dww])
    P.dma("sp", prm[:, 0, :], d["dwb"], writes=[b_prm])
    P.dma("sp", prm[:, 1, :], d["lng"], writes=[b_prm])
    P.dma("sp", prm[:, 2, :], d["lnb"], writes=[b_prm])
    onesF = P.sb("onesF", [128, 128], F32); b_ones = P.buf("onesF")
    P.op("pool", lambda: nc.gpsimd.memset(onesF[:], 1.0 / 512.0), writes=[b_ones])
    ain = [P.sb(f"ain{i}", [128, 2, N + 30], F32) for i in range(2)]; b_ain = [P.buf(f"ain{i}") for i in range(2)]
    abf = [P.sb(f"abf{i}", [128, N + 30], BF16) for i in range(2)]; b_abf = [P.buf(f"abf{i}") for i in range(2)]
    diag = [P.sb(f"diag{i}", [128, 31, 128], BF16) for i in range(2)]; b_diag = [P.buf(f"diag{i}") for i in range(2)]
    y = [P.sb(f"cy{c}", [128, N], F32) for c in range(4)]; b_y = [P.buf(f"cy{c}") for c in range(4)]
    ysq = P.sb("cysq", [128, 512], F32); b_ysq = P.buf("cysq")
    for c in range(4):
        ai, b_ai = ain[c % 2], b_ain[c % 2]
        ab, b_ab = abf[c % 2], b_abf[c % 2]
        dg, b_dg = diag[c % 2], b_diag[c % 2]
        P.dma("sp", ai[:], d["aT"][:, c * 128:(c + 1) * 128, :].rearrange("k p n -> p k n"), writes=[b_ai])
        P.op("act", lambda ai=ai: nc.scalar.activation(out=ai[:, 1, :], in_=ai[:, 1, :], func=AF.Sigmoid), reads=[b_ai], writes=[b_ai])
        P.op("dve", lambda ai=ai, ab=ab: nc.vector.tensor_tensor(out=ab[:], in0=ai[:, 0, :], in1=ai[:, 1, :], op=ALU.mult),
             reads=[b_ai], writes=[b_ab])
        for k in range(31):
            P.op("pool", lambda k=k, c=c, dg=dg: nc.gpsimd.tensor_scalar(out=dg[:, k, :], in0=ident[:], scalar1=dww[:, c, k:k + 1], scalar2=None,
                                                                          op0=ALU.mult), reads=[b_id, b_dww], writes=[b_dg])
        for tt in range(NTT):
            ps, b_ps = banks[tt % 2]
            for k in range(31):
                P.op("pe", lambda k=k, tt=tt, ps=ps, dg=dg, ab=ab: nc.tensor.matmul(ps[:], lhsT=dg[:, k, :], rhs=ab[:, tt * 512 + k: tt * 512 + k + 512],
                                                                                      start=(k == 0), stop=(k == 30)),
                     reads=[b_dg, b_ab], writes=[b_ps], skip_self=True)
            P.op("act", lambda c=c, tt=tt, ps=ps: nc.scalar.activation(out=y[c][:, tt * 512:(tt + 1) * 512], in_=ps[:], func=AF.Identity,
                                                                        bias=prm[:, 0, c:c + 1]), reads=[b_ps, b_prm], writes=[b_y[c]])
    mean = P.sb("cmean", [128, 512], F32); b_mean = P.buf("cmean")
    rstd = P.sb("crstd", [128, 512], F32); b_rstd = P.buf("crstd")
    yn = [P.sb(f"cyn{i}", [128, 512], F32) for i in range(2)]; b_yn = [P.buf(f"cyn{i}") for i in range(2)]
    co = [P.sb(f"cco{i}", [128, 512], BF16) for i in range(2)]; b_co = [P.buf(f"cco{i}") for i in range(2)]
    it = 0
    for tt in range(NTT):
        sl = slice(tt * 512, (tt + 1) * 512)
        pm, b_pm = banks[2]
        pq, b_pq = banks[3]
        for c in range(4):
            P.op("pe", lambda c=c, sl=sl: nc.tensor.matmul(pm[:], lhsT=onesF[:], rhs=y[c][:, sl], start=(c == 0), stop=(c == 3)),
                 reads=[b_ones, b_y[c]], writes=[b_pm], skip_self=True)
        for c in range(4):
            P.op("act", lambda c=c, sl=sl: nc.scalar.activation(out=ysq[:], in_=y[c][:, sl], func=AF.Square), reads=[b_y[c]], writes=[b_ysq])
            P.op("pe", lambda c=c: nc.tensor.matmul(pq[:], lhsT=onesF[:], rhs=ysq[:], start=(c == 0), stop=(c == 3)),
                 reads=[b_ones, b_ysq], writes=[b_pq], skip_self=True)
        P.op("dve", lambda: nc.vector.tensor_copy(out=mean[:], in_=pm[:]), reads=[b_pm], writes=[b_mean])
        P.op("dve", lambda: nc.vector.tensor_tensor(out=rstd[:], in0=mean[:], in1=mean[:], op=ALU.mult), reads=[b_mean], writes=[b_rstd])
        P.op("dve", lambda: nc.vector.tensor_tensor(out=rstd[:], in0=pq[:], in1=rstd[:], op=ALU.subtract), reads=[b_pq, b_rstd], writes=[b_rstd])
        P.op("act", lambda: nc.scalar.activation(out=rstd[:], in_=rstd[:], func=AF.Sqrt, bias=1e-5), reads=[b_rstd], writes=[b_rstd])
        P.op("dve", lambda: nc.vector.reciprocal(out=rstd[:], in_=rstd[:]), reads=[b_rstd], writes=[b_rstd])
        for c in range(4):
            i = it % 2
            it += 1
            P.op("dve", lambda c=c, sl=sl, i=i: nc.vector.tensor_tensor(out=yn[i][:], in0=y[c][:, sl], in1=mean[:], op=ALU.subtract),
                 reads=[b_y[c], b_mean], writes=[b_yn[i]])
            P.op("dve", lambda i=i: nc.vector.tensor_tensor(out=yn[i][:], in0=yn[i][:], in1=rstd[:], op=ALU.mult),
                 reads=[b_yn[i], b_rstd], writes=[b_yn[i]])
            P.op("act", lambda c=c, i=i: nc.scalar.activation(out=co[i][:], in_=yn[i][:], func=AF.Silu, scale=prm[:, 1, c:c + 1],
                                                               bias=prm[:, 2, c:c + 1]), reads=[b_yn[i], b_prm], writes=[b_co[i]])
            P.dma("sp", d["coutT"][c * 128:(c + 1) * 128, sl], co[i][:], reads=[b_co[i]], writes=[b_out])
    return [b_out]

bf = ml_dtypes.bfloat16

def nsa_consts(T):
    t = np.arange(T)
    j = np.arange(128)
    vis = (j[None, :] * 64 <= t[:, None])
    cur = t // 64
    forced = (j[None, :] == 0) | (j[None, :] == cur[:, None]) | (j[None, :] == cur[:, None] - 1)
    M1 = (vis & ~forced).astype(np.float32)
    A1 = np.where(vis, np.where(forced, 1e4, 0.0), -1.0).astype(np.float32)
    NS = T // 64
    M1[:, NS:] = 0.0; A1[:, NS:] = -1.0
    n = np.arange(512)
    poolm = ((n[:, None] >= 4 * j[None, :] - 1) & (n[:, None] <= 4 * j[None, :] + 3)).astype(np.float32).astype(bf)
    kaug_tok = np.stack([t // 64, t % 64, np.ones(T), np.ones(T)]).astype(np.float32).astype(bf)
    kaugc = np.stack([n // 4, 16 * (n % 4) + 15.5, np.ones(512), np.ones(512)]).astype(np.float32).astype(bf)[:, :T // 16]
    return dict(M1=M1, A1=A1, poolm=poolm[:T // 16], kaug_tok=kaug_tok, kaugc=kaugc)

def q_aug(T, h):
    t = np.arange(T)
    c = (2.0 ** (-(h + 1))) * 8.0
    return np.stack([np.full(T, 64 * c), np.full(T, c), -64 * c * (t // 64), -c * (t % 64)]).astype(np.float32).astype(bf)

def nsa_inputs(T, g, qT, kcT, vcT, ksT, kwT, vs, vw, graw, w, consts, horder=(0, 1, 2, 3)):
    d = {}
    QA = np.zeros((4, 68, T), dtype=bf)
    for r in range(4):
        h = 4 * g + horder[r]
        QA[r, :64] = qT[h * 64:(h + 1) * 64]
        QA[r, 64:] = q_aug(T, h)
    d["QA"] = QA
    for nm, src in (("KSA", ksT), ("KWA", kwT)):
        a = np.zeros((68, T), dtype=bf)
        a[:64] = src[g * 64:(g + 1) * 64]
        a[64:] = consts["kaug_tok"]
        d[nm] = a
    for nm, src in (("VSA", vs), ("VWA", vw)):
        a = np.ones((T, 65), dtype=bf)
        a[:, :64] = src[:, g * 64:(g + 1) * 64]
        d[nm] = a
    for nm, src in (("c2k", kcT), ("c2v", vcT)):
        a = np.zeros((128, T), dtype=bf)
        a[:64] = src[g * 64:(g + 1) * 64]
        a[64:, :T - 1] = src[g * 64:(g + 1) * 64, 1:]
        d[nm] = a
    for kv in ("k", "v"):
        w1 = np.asarray(w["w1_" + kv], dtype=np.float32)
        d["w1" + kv] = np.ascontiguousarray(w1.reshape(16, 2, 64, 128).transpose(1, 2, 0, 3).reshape(128, 16, 128))
        pe = np.asarray(w["pe_" + kv], dtype=np.float32)
        d["pe2" + kv] = np.ascontiguousarray(pe.reshape(16, 2, 64).transpose(1, 2, 0).reshape(128, 16))
        d["w2" + kv] = np.ascontiguousarray(np.asarray(w["w2_" + kv], dtype=np.float32))
    d["kaugc"] = consts["kaugc"]
    d["poolm"] = consts["poolm"]
    d["M1"] = consts["M1"]; d["A1"] = consts["A1"]
    h0 = 4 * g + horder[0]
    d["graw"] = np.ascontiguousarray(graw[:, h0 * 3:(h0 + 2) * 3])
    return d


T_SEQ = 8192
NTOK = 2048
NCORE = 8


def _launch(nc, in_maps):
    res = run_bass_kernel_spmd(nc, in_maps, core_ids=list(range(NCORE)))
    return res.results


def _mk(nc, d, name, shape, dt, out=False):
    d[name] = nc.dram_tensor(name, list(shape), dt, kind="ExternalOutput" if out else "ExternalInput").ap()
    return d[name]


def _build_dense(kind):
    nc = bass.Bass("TRN2", target_bir_lowering=False)
    d = {}
    NT = NTOK
    _mk(nc, d, "x", [NT, 1024], F32)
    if kind in ("B", "C"):
        _mk(nc, d, "oT", [1024, NT], BF16)
        _mk(nc, d, "wo", [1024, 1024], F32)
    nffn = {"A": 1, "B": 2, "C": 1}[kind]
    for i in range(nffn):
        _mk(nc, d, f"fg{i}", [1024], F32)
        _mk(nc, d, f"fwi{i}", [1024, 5632], F32)
        _mk(nc, d, f"fwo{i}", [2816, 1024], F32)
    if kind == "A":
        _mk(nc, d, "pg", [1024], F32); _mk(nc, d, "pw", [1024, 2328], F32)
        _mk(nc, d, "xo", [NT, 1024], F32, True)
        _mk(nc, d, "aT", [1024, NT], F32, True); _mk(nc, d, "qT", [512, NT], BF16, True)
        for n in ("kcT", "vcT", "ksT", "kwT"):
            _mk(nc, d, n, [128, NT], BF16, True)
        _mk(nc, d, "vs", [NT, 128], BF16, True); _mk(nc, d, "vw", [NT, 128], BF16, True)
        _mk(nc, d, "gg", [NT, 24], F32, True)
    elif kind == "B":
        _mk(nc, d, "pg", [1024], F32); _mk(nc, d, "pw", [1024, 3072], F32)
        _mk(nc, d, "xo", [NT, 1024], F32, True)
        _mk(nc, d, "cT", [1536, NT], F32, True); _mk(nc, d, "qT", [512, NT], BF16, True)
        _mk(nc, d, "kT", [512, NT], BF16, True); _mk(nc, d, "v", [NT, 512], BF16, True)
    else:
        _mk(nc, d, "gfin", [1024], F32)
        _mk(nc, d, "out", [NT, 1024], F32, True)
    with ExitStack() as es:
        P = Prog(nc, es)
        dn = Dense(nc, P, NT)
        outs = []
        dn.load_x(d["x"])
        if kind in ("B", "C"):
            dn.outproj(d["oT"], d["wo"])
        for i in range(nffn):
            dn.ffn(d[f"fg{i}"], d[f"fwi{i}"], d[f"fwo{i}"])
        if kind == "A":
            names = [(0, 1024, "F", "aT"), (1024, 1536, "F", "qT"), (1536, 1664, "F", "kcT"), (1664, 1792, "F", "vcT"),
                     (1792, 1920, "F", "ksT"), (1920, 2048, "T", "vs"), (2048, 2176, "F", "kwT"), (2176, 2304, "T", "vw"),
                     (2304, 2328, "T", "gg")]
        elif kind == "B":
            names = [(0, 1536, "F", "cT"), (1536, 2048, "F", "qT"), (2048, 2560, "F", "kT"), (2560, 3072, "T", "v")]
        if kind in ("A", "B"):
            bx = P.buf("xo_out")
            dn.store_x(d["xo"], bx)
            outs.append(bx)
            specs = []
            for (c0, c1, lay, n) in names:
                b = P.buf("o_" + n)
                outs.append(b)
                specs.append((c0, c1, lay, d[n], b))
            dn.proj(d["pg"], d["pw"], specs)
        else:
            dn.alloc_final()
            bo = P.buf("out_out")
            dn.final(d["gfin"], d["out"], bo)
            outs.append(bo)
        P.finish("sp", outs)
        P.emit()
    return nc


def _build_conv0():
    nc = bass.Bass("TRN2", target_bir_lowering=False)
    d = {}
    _mk(nc, d, "aT", [2, 512, NTOK + 30], F32); _mk(nc, d, "dww", [128, 4, 31], F32)
    for n in ("dwb", "lng", "lnb"):
        _mk(nc, d, n, [128, 4], F32)
    _mk(nc, d, "coutT", [512, NTOK], BF16, True)
    with ExitStack() as es:
        P = Prog(nc, es)
        banks = alloc_banks(P)
        ident, b_id = make_ident(nc, P, "identc")
        outs = build_conv(nc, P, NTOK, d, banks, ident, b_id)
        P.finish("sp", outs)
        P.emit()
    return nc


def _build_nsa():
    nc = bass.Bass("TRN2", target_bir_lowering=False)
    d = {}
    T = T_SEQ
    _mk(nc, d, "QA", [4, 68, T], BF16); _mk(nc, d, "KSA", [68, T], BF16); _mk(nc, d, "KWA", [68, T], BF16)
    _mk(nc, d, "VSA", [T, 65], BF16); _mk(nc, d, "VWA", [T, 65], BF16)
    _mk(nc, d, "c2k", [128, T], BF16); _mk(nc, d, "c2v", [128, T], BF16)
    for kv in "kv":
        _mk(nc, d, "w1" + kv, [128, 16, 128], F32); _mk(nc, d, "pe2" + kv, [128, 16], F32); _mk(nc, d, "w2" + kv, [128, 64], F32)
    _mk(nc, d, "kaugc", [4, T // 16], BF16); _mk(nc, d, "poolm", [T // 16, 128], BF16)
    _mk(nc, d, "M1", [T, 128], F32); _mk(nc, d, "A1", [T, 128], F32); _mk(nc, d, "graw", [T, 6], F32)
    _mk(nc, d, "o_nsa", [T, 128], BF16, True)
    with ExitStack() as es:
        P = Prog(nc, es)
        outs = build_nsa(nc, P, T, list(range(T // 512)), d, alloc_banks(P), NOWN=2)
        P.finish("sp", outs)
        P.emit()
    return nc


def _build_m1():
    nc = bass.Bass("TRN2", target_bir_lowering=False)
    d = {}
    T = T_SEQ
    _mk(nc, d, "qT", [2, 64, T], BF16); _mk(nc, d, "kT", [2, 64, T], BF16); _mk(nc, d, "v", [T, 2, 64], BF16)
    _mk(nc, d, "convin", [3, 512, NTOK + 2], F32); _mk(nc, d, "scw", [128, 12], F32)
    _mk(nc, d, "o_sbT", [2, 64, T], BF16, True); _mk(nc, d, "coutT", [512, NTOK], BF16, True)
    with ExitStack() as es:
        P = Prog(nc, es)
        outs = build_mixer1(nc, P, T, NTOK, d)
        P.finish("sp", outs)
        P.emit()
    return nc


def _cat_tok(res, name, axis):
    return [np.concatenate([np.asarray(res[b * 4 + j][name]) for j in range(4)], axis=axis) for b in range(2)]


def kernel(x, ffn1_norm, ffn1_w_in, ffn1_w_out, mix_norm, ffn2_norm, ffn2_w_in, ffn2_w_out,
           ab_w_in, conv_dw_w, conv_dw_b, conv_ln_g, conv_ln_b,
           nsa_pe_k, nsa_w1_k, nsa_w2_k, nsa_pe_v, nsa_w1_v, nsa_w2_v, ab_w_out,
           cd_w_in, sc_conv_w, cd_w_out, final_norm):
    f32 = lambda a: np.ascontiguousarray(np.asarray(a, dtype=np.float32))
    x = f32(x)
    T = T_SEQ
    xs = [np.ascontiguousarray(x[c // 4, (c % 4) * NTOK:(c % 4 + 1) * NTOK]) for c in range(NCORE)]
    common = {"fg0": f32(ffn1_norm[0]), "fwi0": f32(ffn1_w_in[0]), "fwo0": f32(ffn1_w_out[0]), "pg": f32(mix_norm[0]), "pw": f32(ab_w_in[0])}
    rA = _launch(_build_dense("A"), [dict(common, x=xs[c]) for c in range(NCORE)])
    aT = _cat_tok(rA, "aT", 1); qT = _cat_tok(rA, "qT", 1)
    kcT = _cat_tok(rA, "kcT", 1); vcT = _cat_tok(rA, "vcT", 1); ksT = _cat_tok(rA, "ksT", 1); kwT = _cat_tok(rA, "kwT", 1)
    vs = _cat_tok(rA, "vs", 0); vw = _cat_tok(rA, "vw", 0); gg = _cat_tok(rA, "gg", 0)
    lay4 = lambda v: np.ascontiguousarray(f32(v).reshape(4, 128).T)
    cc = {"dww": np.ascontiguousarray(f32(conv_dw_w[0]).reshape(31, 4, 128).transpose(2, 1, 0)),
          "dwb": lay4(conv_dw_b[0]), "lng": lay4(conv_ln_g[0]), "lnb": lay4(conv_ln_b[0])}
    maps = []
    for c in range(NCORE):
        b, j = c // 4, c % 4
        a = np.zeros((2, 512, NTOK + 30), dtype=np.float32)
        lo = j * NTOK - 30
        src = aT[b].reshape(2, 512, T)
        if lo < 0:
            a[:, :, 30:] = src[:, :, 0:NTOK]
        else:
            a[:] = src[:, :, lo:lo + NTOK + 30]
        maps.append(dict(cc, aT=a))
    rC0 = _launch(_build_conv0(), maps)
    consts = nsa_consts(T)
    w = dict(pe_k=nsa_pe_k[0], w1_k=nsa_w1_k[0], w2_k=nsa_w2_k[0], pe_v=nsa_pe_v[0], w1_v=nsa_w1_v[0], w2_v=nsa_w2_v[0])
    maps = []
    for c in range(NCORE):
        b, g, hh = c // 4, (c % 4) // 2, c % 2
        horder = [2 * hh, 2 * hh + 1, 2 * (1 - hh), 2 * (1 - hh) + 1]
        dd = nsa_inputs(T, g, qT[b], kcT[b], vcT[b], ksT[b], kwT[b], vs[b], vw[b], gg[b], w, consts, horder)
        maps.append(dd)
    rN = _launch(_build_nsa(), maps)
    oT = []
    for c in range(NCORE):
        b, j = c // 4, c % 4
        o = np.zeros((1024, NTOK), dtype=bf)
        o[0:512] = np.asarray(rC0[c]["coutT"])
        for g in range(2):
            for hh in range(2):
                src = np.asarray(rN[b * 4 + g * 2 + hh]["o_nsa"])[j * NTOK:(j + 1) * NTOK]
                r0 = 512 + (4 * g + 2 * hh) * 64
                o[r0:r0 + 128] = src.T
        oT.append(o)
    common = {"wo": f32(ab_w_out[0]), "fg0": f32(ffn2_norm[0]), "fwi0": f32(ffn2_w_in[0]), "fwo0": f32(ffn2_w_out[0]),
              "fg1": f32(ffn1_norm[1]), "fwi1": f32(ffn1_w_in[1]), "fwo1": f32(ffn1_w_out[1]), "pg": f32(mix_norm[1]), "pw": f32(cd_w_in[0])}
    rB = _launch(_build_dense("B"), [dict(common, x=np.asarray(rA[c]["xo"]), oT=oT[c]) for c in range(NCORE)])
    cT = _cat_tok(rB, "cT", 1); q1 = _cat_tok(rB, "qT", 1); k1 = _cat_tok(rB, "kT", 1); v1 = _cat_tok(rB, "v", 0)
    scw = np.ascontiguousarray(f32(sc_conv_w[0]).reshape(3, 4, 128).transpose(2, 1, 0).reshape(128, 12))
    maps = []
    for c in range(NCORE):
        b, j = c // 4, c % 4
        ci = np.zeros((3, 512, NTOK + 2), dtype=np.float32)
        src = cT[b].reshape(3, 512, T)
        lo = j * NTOK - 2
        if lo < 0:
            ci[:, :, 2:] = src[:, :, 0:NTOK]
        else:
            ci[:] = src[:, :, lo:lo + NTOK + 2]
        hp = j
        maps.append({"qT": np.ascontiguousarray(q1[b][hp * 128:(hp + 1) * 128].reshape(2, 64, T)),
                     "kT": np.ascontiguousarray(k1[b][hp * 128:(hp + 1) * 128].reshape(2, 64, T)),
                     "v": np.ascontiguousarray(v1[b][:, hp * 128:(hp + 1) * 128].reshape(T, 2, 64)),
                     "convin": ci, "scw": scw})
    rM1 = _launch(_build_m1(), maps)
    oT = []
    for c in range(NCORE):
        b, j = c // 4, c % 4
        o = np.zeros((1024, NTOK), dtype=bf)
        o[0:512] = np.asarray(rM1[c]["coutT"])
        for hp in range(4):
            src = np.asarray(rM1[b * 4 + hp]["o_sbT"]).reshape(128, T)[:, j * NTOK:(j + 1) * NTOK]
            o[512 + hp * 128:512 + (hp + 1) * 128] = src
        oT.append(o)
    common = {"wo": f32(cd_w_out[0]), "fg0": f32(ffn2_norm[1]), "fwi0": f32(ffn2_w_in[1]), "fwo0": f32(ffn2_w_out[1]), "gfin": f32(final_norm)}
    rC = _launch(_build_dense("C"), [dict(common, x=np.asarray(rB[c]["xo"]), oT=oT[c]) for c in range(NCORE)])
    out = np.zeros((2, T, 1024), dtype=np.float32)
    for c in range(NCORE):
        out[c // 4, (c % 4) * NTOK:(c % 4 + 1) * NTOK] = np.asarray(rC[c]["out"])
    return out
```

```python
import numpy as np
import math
from contextlib import ExitStack
import concourse.bass as bass
import concourse.mybir as mybir
from concourse.bass_utils import run_bass_kernel_spmd
import ml_dtypes

F32 = mybir.dt.float32
BF16 = mybir.dt.bfloat16
AF = mybir.ActivationFunctionType
ALU = mybir.AluOpType
AX = mybir.AxisListType

SEM_EPOCH = 30000


class Buf:
    __slots__ = ("name", "w", "r", "dsem", "dcnt")

    def __init__(self, name):
        self.name = name
        self.w = []
        self.r = []
        self.dsem = None
        self.dcnt = 0


class Prog:
    def __init__(self, nc, es):
        self.nc = nc
        self.es = es
        self.eng = {"pe": nc.tensor, "act": nc.scalar, "dve": nc.vector, "pool": nc.gpsimd, "sp": nc.sync}
        self.sem = {}
        self.cnt = {}
        self.waited = {k: {} for k in self.eng}
        self.nsem = 0
        for k in self.eng:
            self._new_eng_sem(k)
        self.n_inst = 0
        self.n_wait = 0
        self.q = {k: [] for k in self.eng}

    def _new_sem(self, name):
        self.nsem += 1
        return self.es.enter_context(self.nc.semaphore(f"{name}_{self.nsem}"))

    def _new_eng_sem(self, k):
        self.sem[k] = self._new_sem("e" + k)
        self.cnt[k] = 0

    def buf(self, name):
        return Buf(name)

    def sb(self, name, shape, dtype):
        t = self.es.enter_context(self.nc.sbuf_tensor(name, list(shape), dtype))
        return t

    def ps(self, name, shape, dtype):
        t = self.es.enter_context(self.nc.psum_tensor(name, list(shape), dtype))
        return t

    def _wait(self, e, conds, skip_self=False):
        eng = self.eng[e]
        wd = self.waited[e]
        best = {}
        for (s, v, owner) in conds:
            if skip_self and owner == e:
                continue
            key = id(s)
            if wd.get(key, 0) >= v:
                continue
            if key not in best or best[key][1] < v:
                best[key] = (s, v)
        for key, (s, v) in best.items():
            self.q[e].append(("w", s, v))
            wd[key] = v
            self.n_wait += 1

    def op(self, e, fn, reads=(), writes=(), skip_self=False):
        conds = []
        for b in reads:
            conds += b.w
        for b in writes:
            conds += b.w
            conds += b.r
        self._wait(e, conds, skip_self=skip_self)
        if self.cnt[e] >= SEM_EPOCH:
            self._new_eng_sem(e)
        self.cnt[e] += 1
        self.q[e].append(("i", fn, self.sem[e], 1))
        c = (self.sem[e], self.cnt[e], e)
        for b in reads:
            b.r = [x for x in b.r if x[0] is not c[0]] + [c]
        for b in writes:
            b.w = [c]
            b.r = []
        self.n_inst += 1

    def dma(self, e, out, in_, reads=(), writes=(), **kw):
        conds = []
        for b in reads:
            conds += b.w
        for b in writes:
            conds += b.w
            conds += b.r
        self._wait(e, conds)
        tgt = writes[0] if writes else reads[0]
        if tgt.dsem is None:
            tgt.dsem = self._new_sem("d" + tgt.name)
        tgt.dcnt += 1
        eng = self.eng[e]
        self.q[e].append(("i", (lambda: eng.dma_start(out=out, in_=in_, **kw)), tgt.dsem, 16))
        c = (tgt.dsem, 16 * tgt.dcnt, "dma")
        for b in reads:
            b.r = [x for x in b.r if x[0] is not c[0]] + [c]
        for b in writes:
            b.w = [x for x in b.w if x[0] is not c[0]] + [c]
            b.r = []
        self.n_inst += 1

    def dma_fn(self, e, fn, reads=(), writes=()):
        conds = []
        for b in reads:
            conds += b.w
        for b in writes:
            conds += b.w
            conds += b.r
        self._wait(e, conds)
        tgt = writes[0] if writes else reads[0]
        if tgt.dsem is None:
            tgt.dsem = self._new_sem("d" + tgt.name)
        tgt.dcnt += 1
        self.q[e].append(("i", fn, tgt.dsem, 16))
        c = (tgt.dsem, 16 * tgt.dcnt, "dma")
        for b in reads:
            b.r = [x for x in b.r if x[0] is not c[0]] + [c]
        for b in writes:
            b.w = [x for x in b.w if x[0] is not c[0]] + [c]
            b.r = []
        self.n_inst += 1

    def cc(self, kind, in_ap, out_ap, groups, reads=(), writes=()):
        nc = self.nc
        fn = lambda: nc.gpsimd.collective_compute(kind, mybir.AluOpType.bypass, replica_groups=groups, ins=[in_ap], outs=[out_ap])
        self.dma_fn("pool", fn, reads=reads, writes=writes)

    def finish(self, e, bufs):
        conds = []
        for b in bufs:
            conds += b.w
        self._wait(e, conds)

    def emit(self):
        nc = self.nc
        with nc.Block() as block:
            def run(e):
                eng = self.eng[e]
                for it in self.q[e]:
                    if it[0] == "w":
                        eng.wait_ge(it[1], it[2])
                    else:
                        it[1]().then_inc(it[2], it[3])

            @block.tensor
            def _(x):
                run("pe")

            @block.scalar
            def _(x):
                run("act")

            @block.vector
            def _(x):
                run("dve")

            @block.gpsimd
            def _(x):
                run("pool")

            @block.sync
            def _(x):
                run("sp")


D = 1024
DFF = 2816
NFC = DFF // 128


def make_ident(nc, P, name="ident"):
    ident = P.sb(name, [128, 128], BF16)
    b = P.buf(name)
    P.op("pool", lambda: nc.gpsimd.memset(ident[:], 0.0), writes=[b])
    P.op("pool", lambda: nc.gpsimd.affine_select(out=ident[:], in_=ident[:], pattern=[[-1, 128]],
                                                   compare_op=ALU.not_equal, fill=1.0, base=0,
                                                   channel_multiplier=1), reads=[b], writes=[b])
    return ident, b


class Dense:
    def __init__(self, nc, P, NT):
        self.nc, self.P, self.NT = nc, P, NT
        self.NTILE = NT // 128
        self.NST = NT // 512
        nt = self.NTILE
        self.x = P.sb("x_res", [128, nt, D], F32)
        self.b_x = [P.buf(f"x{t}") for t in range(nt)]
        self.xnT = P.sb("xnT", [128, 8, NT], BF16)
        self.b_xnT = [P.buf(f"xnT{t}") for t in range(nt)]
        self.ident, self.b_id = make_ident(nc, P)
        self.sq = P.sb("sq", [128, D], F32); self.b_sq = P.buf("sq")
        self.ss = P.sb("ss", [128, nt], F32); self.b_ss = P.buf("ss")
        self.rstd = P.sb("rstd", [128, nt], F32); self.b_rstd = P.buf("rstd")
        self.xs = [P.sb(f"xs{i}", [128, D], BF16) for i in range(2)]
        self.b_xs = [P.buf(f"xs{i}") for i in range(2)]
        self.gt = P.sb("gt", [128, 8], F32); self.b_gt = P.buf("gt")
        self.NWB = 6
        self.wb = [P.sb(f"wb{i}", [128, 8 * 512], BF16) for i in range(self.NWB)]
        self.b_wb = [P.buf(f"wb{i}") for i in range(self.NWB)]
        self.wi = 0
        self.tp = P.ps("tp", [128, 8, 128], BF16); self.b_tp = P.buf("tp")
        self.pg = [P.ps(f"pg{i}", [128, 512], F32) for i in range(2)]; self.b_pg = [P.buf(f"pg{i}") for i in range(2)]
        self.pu = [P.ps(f"pu{i}", [128, 512], F32) for i in range(2)]; self.b_pu = [P.buf(f"pu{i}") for i in range(2)]
        self.py = [P.ps(f"py{i}", [128, 512], F32) for i in range(2)]; self.b_py = [P.buf(f"py{i}") for i in range(2)]
        self.ipg = 0
        self.ipy = 0
        self.sg = [P.sb(f"sg{i}", [128, 512], F32) for i in range(2)]; self.b_sg = [P.buf(f"sg{i}") for i in range(2)]
        self.act = [P.sb(f"actT{i}", [128, 4, 512], BF16) for i in range(2)]
        self.b_act = [P.buf(f"actT{i}") for i in range(2)]
        self.iact = 0
        self.stg = [P.sb(f"stg{i}", [128, 512], F32) for i in range(3)]
        self.b_stg = [P.buf(f"stg{i}") for i in range(3)]
        self.istg = 0
        self.gfull = None

    def next_wb(self):
        i = self.wi % self.NWB
        self.wi += 1
        return self.wb[i], self.b_wb[i]

    def load_w(self, src_ap, rc, cols):
        wb, b = self.next_wb()
        view = wb[:, 0:rc * cols].rearrange("p (c n) -> p c n", c=rc)
        self.P.dma("pool", view, src_ap.rearrange("(c p) n -> p c n", p=128), writes=[b])
        return view, b

    def load_x(self, x_dram):
        for t in range(self.NTILE):
            self.P.dma("sp", self.x[:, t, :], x_dram[t * 128:(t + 1) * 128, :], writes=[self.b_x[t]])

    def store_x(self, out_dram, b_out):
        for t in range(self.NTILE):
            self.P.dma("sp", out_dram[t * 128:(t + 1) * 128, :], self.x[:, t, :], reads=[self.b_x[t]], writes=[b_out])

    def stats(self):
        nc, P = self.nc, self.P
        for t in range(self.NTILE):
            P.op("act", lambda t=t: nc.scalar.activation(out=self.sq[:], in_=self.x[:, t, :], func=AF.Square,
                                                           accum_out=self.ss[:, t:t + 1]),
                 reads=[self.b_x[t]], writes=[self.b_sq, self.b_ss])
        P.op("act", lambda: nc.scalar.activation(out=self.rstd[:], in_=self.ss[:], func=AF.Sqrt, scale=1.0 / D, bias=1e-6),
             reads=[self.b_ss], writes=[self.b_rstd])
        P.op("dve", lambda: nc.vector.reciprocal(out=self.rstd[:], in_=self.rstd[:]), reads=[self.b_rstd], writes=[self.b_rstd])

    def norm_T(self, g_dram):
        nc, P = self.nc, self.P
        P.dma("sp", self.gt[:], g_dram.rearrange("(c p) -> p c", p=128), writes=[self.b_gt], allow_slow_non_contiguous=True)
        self.stats()
        for t in range(self.NTILE):
            xs, b_xs = self.xs[t % 2], self.b_xs[t % 2]
            P.op("dve", lambda t=t, xs=xs: nc.vector.tensor_scalar(out=xs[:], in0=self.x[:, t, :], scalar1=self.rstd[:, t:t + 1],
                                                                    scalar2=None, op0=ALU.mult),
                 reads=[self.b_x[t], self.b_rstd], writes=[b_xs])
            for c in range(8):
                P.op("pe", lambda c=c, xs=xs: nc.tensor.transpose(out=self.tp[:, c, :], in_=xs[:, c * 128:(c + 1) * 128],
                                                                   identity=self.ident[:]),
                     reads=[b_xs, self.b_id], writes=[self.b_tp], skip_self=True)
            P.op("dve", lambda t=t: nc.vector.tensor_tensor(out=self.xnT[:, :, t * 128:(t + 1) * 128], in0=self.tp[:],
                                                              in1=self.gt[:].unsqueeze(2).to_broadcast([128, 8, 128]), op=ALU.mult),
                 reads=[self.b_tp, self.b_gt], writes=[self.b_xnT[t]])

    def ffn(self, g_dram, w_in, w_out):
        nc, P = self.nc, self.P
        self.norm_T(g_dram)
        groups = [(s, min(4, NFC - s)) for s in range(0, NFC, 4)]
        for (fc0, nfc) in groups:
            ncol = nfc * 128
            wg, b_wg = self.load_w(w_in[:, fc0 * 128: fc0 * 128 + ncol], 8, ncol)
            wu, b_wu = self.load_w(w_in[:, DFF + fc0 * 128: DFF + fc0 * 128 + ncol], 8, ncol)
            wo, b_wo = self.load_w(w_out[fc0 * 128: fc0 * 128 + ncol, :], nfc, D)
            for st in range(self.NST):
                tiles = list(range(st * 4, st * 4 + 4))
                xb = [self.b_xnT[t] for t in tiles]
                act, b_act = self.act[self.iact % 2], self.b_act[self.iact % 2]
                self.iact += 1
                for j in range(nfc):
                    i = self.ipg % 2
                    self.ipg += 1
                    pg, b_pg, pu, b_pu = self.pg[i], self.b_pg[i], self.pu[i], self.b_pu[i]
                    sg, b_sg = self.sg[i], self.b_sg[i]
                    for k in range(8):
                        P.op("pe", lambda k=k, j=j, pg=pg, wg=wg, st=st: nc.tensor.matmul(
                            pg[:], lhsT=wg[:, k, j * 128:(j + 1) * 128], rhs=self.xnT[:, k, st * 512:(st + 1) * 512],
                            start=(k == 0), stop=(k == 7)), reads=xb + [b_wg], writes=[b_pg], skip_self=True)
                    for k in range(8):
                        P.op("pe", lambda k=k, j=j, pu=pu, wu=wu, st=st: nc.tensor.matmul(
                            pu[:], lhsT=wu[:, k, j * 128:(j + 1) * 128], rhs=self.xnT[:, k, st * 512:(st + 1) * 512],
                            start=(k == 0), stop=(k == 7)), reads=xb + [b_wu], writes=[b_pu], skip_self=True)
                    P.op("act", lambda pg=pg, sg=sg: nc.scalar.activation(out=sg[:], in_=pg[:], func=AF.Silu),
                         reads=[b_pg], writes=[b_sg])
                    P.op("dve", lambda j=j, pu=pu, sg=sg, act=act: nc.vector.tensor_tensor(out=act[:, j, :], in0=pu[:], in1=sg[:], op=ALU.mult),
                         reads=[b_pu, b_sg], writes=[b_act])
                for sub in range(4):
                    t = st * 4 + sub
                    for dh in range(2):
                        i = self.ipy % 2
                        self.ipy += 1
                        py, b_py = self.py[i], self.b_py[i]
                        for j in range(nfc):
                            P.op("pe", lambda j=j, py=py, act=act, wo=wo, sub=sub, dh=dh, nfc=nfc: nc.tensor.matmul(
                                py[:], lhsT=act[:, j, sub * 128:(sub + 1) * 128], rhs=wo[:, j, dh * 512:(dh + 1) * 512],
                                start=(j == 0), stop=(j == nfc - 1)), reads=[b_act, b_wo], writes=[b_py], skip_self=True)
                        P.op("dve", lambda t=t, dh=dh, py=py: nc.vector.scalar_tensor_tensor(
                            out=self.x[:, t, dh * 512:(dh + 1) * 512], in0=py[:], scalar=0.5, in1=self.x[:, t, dh * 512:(dh + 1) * 512],
                            op0=ALU.mult, op1=ALU.add), reads=[b_py, self.b_x[t]], writes=[self.b_x[t]])

    def outproj(self, oT_dram, w_dram):
        nc, P = self.nc, self.P
        for t in range(self.NTILE):
            P.dma("sp", self.xnT[:, :, t * 128:(t + 1) * 128],
                  oT_dram[:, t * 128:(t + 1) * 128].rearrange("(c p) n -> p c n", p=128), writes=[self.b_xnT[t]])
        for dh in range(2):
            w, b_w = self.load_w(w_dram[:, dh * 512:(dh + 1) * 512], 8, 512)
            for t in range(self.NTILE):
                i = self.ipy % 2
                self.ipy += 1
                py, b_py = self.py[i], self.b_py[i]
                for k in range(8):
                    P.op("pe", lambda k=k, py=py, w=w, t=t: nc.tensor.matmul(
                        py[:], lhsT=self.xnT[:, k, t * 128:(t + 1) * 128], rhs=w[:, k, :], start=(k == 0), stop=(k == 7)),
                        reads=[self.b_xnT[t], b_w], writes=[b_py], skip_self=True)
                P.op("dve", lambda t=t, dh=dh, py=py: nc.vector.tensor_tensor(
                    out=self.x[:, t, dh * 512:(dh + 1) * 512], in0=py[:], in1=self.x[:, t, dh * 512:(dh + 1) * 512], op=ALU.add),
                    reads=[b_py, self.b_x[t]], writes=[self.b_x[t]])

    def proj(self, g_dram, w_dram, outs):
        nc, P = self.nc, self.P
        self.norm_T(g_dram)
        for (c0, c1, layout, o_ap, b_o) in outs:
            for cs in range(c0, c1, 512):
                ce = min(cs + 512, c1)
                ncol = ce - cs
                w, b_w = self.load_w(w_dram[:, cs:ce], 8, ncol)
                if layout == "F":
                    assert ncol % 128 == 0
                    for j in range(ncol // 128):
                        for st in range(self.NST):
                            i = self.ipy % 2
                            self.ipy += 1
                            py, b_py = self.py[i], self.b_py[i]
                            xb = [self.b_xnT[t] for t in range(st * 4, st * 4 + 4)]
                            for k in range(8):
                                P.op("pe", lambda k=k, j=j, py=py, w=w, st=st: nc.tensor.matmul(
                                    py[:], lhsT=w[:, k, j * 128:(j + 1) * 128], rhs=self.xnT[:, k, st * 512:(st + 1) * 512],
                                    start=(k == 0), stop=(k == 7)), reads=xb + [b_w], writes=[b_py], skip_self=True)
                            si = self.istg % 3
                            self.istg += 1
                            stg, b_stg = self.stg[si], self.b_stg[si]
                            if o_ap.dtype == BF16:
                                sv = stg[:].bitcast(BF16)[:, 0:512]
                            else:
                                sv = stg[:]
                            eng = "act" if (self.istg % 2) else "dve"
                            if eng == "act":
                                P.op("act", lambda sv=sv, py=py: nc.scalar.copy(out=sv, in_=py[:]), reads=[b_py], writes=[b_stg])
                            else:
                                P.op("dve", lambda sv=sv, py=py: nc.vector.tensor_copy(out=sv, in_=py[:]), reads=[b_py], writes=[b_stg])
                            r0 = cs - c0 + j * 128
                            P.dma("sp", o_ap[r0:r0 + 128, st * 512:(st + 1) * 512], sv, reads=[b_stg], writes=[b_o])
                else:
                    for t in range(self.NTILE):
                        i = self.ipy % 2
                        self.ipy += 1
                        py, b_py = self.py[i], self.b_py[i]
                        for k in range(8):
                            P.op("pe", lambda k=k, py=py, w=w, t=t, ncol=ncol: nc.tensor.matmul(
                                py[:, 0:ncol], lhsT=self.xnT[:, k, t * 128:(t + 1) * 128], rhs=w[:, k, :],
                                start=(k == 0), stop=(k == 7)), reads=[self.b_xnT[t], b_w], writes=[b_py], skip_self=True)
                        si = self.istg % 3
                        self.istg += 1
                        stg, b_stg = self.stg[si], self.b_stg[si]
                        if o_ap.dtype == BF16:
                            sv = stg[:].bitcast(BF16)[:, 0:ncol]
                        else:
                            sv = stg[:, 0:ncol]
                        P.op("dve", lambda sv=sv, py=py, ncol=ncol: nc.vector.tensor_copy(out=sv, in_=py[:, 0:ncol]), reads=[b_py], writes=[b_stg])
                        P.dma("sp", o_ap[t * 128:(t + 1) * 128, cs - c0:ce - c0], sv, reads=[b_stg], writes=[b_o])

    def final(self, g_dram, out_dram, b_out):
        nc, P = self.nc, self.P
        gfull = P.sb("gfull", [128, D], F32)
        b_g = P.buf("gfull")
        P.dma("sp", gfull[:], g_dram.partition_broadcast(128), writes=[b_g])
        self.stats()
        for t in range(self.NTILE):
            si = t % 2
            o = self.fin[si]
            b_o = self.b_fin[si]
            P.op("dve", lambda t=t, o=o: nc.vector.scalar_tensor_tensor(out=o[:], in0=self.x[:, t, :], scalar=self.rstd[:, t:t + 1],
                                                                        in1=gfull[:], op0=ALU.mult, op1=ALU.mult),
                 reads=[self.b_x[t], self.b_rstd, b_g], writes=[b_o])
            P.dma("sp", out_dram[t * 128:(t + 1) * 128, :], o[:], reads=[b_o], writes=[b_out])

    def alloc_final(self):
        P = self.P
        self.fin = [self.sq, self.sq]
        self.b_fin = [self.b_sq, self.b_sq]


def tri_consts(nc, P):
    triu = P.sb("triu", [128, 128], BF16); b_u = P.buf("triu")
    tril = P.sb("tril", [128, 128], BF16); b_l = P.buf("tril")
    P.op("pool", lambda: nc.gpsimd.memset(triu[:], 1.0), writes=[b_u])
    P.op("pool", lambda: nc.gpsimd.affine_select(out=triu[:], in_=triu[:], pattern=[[-1, 128]], compare_op=ALU.is_ge,
                                                   fill=0.0, base=0, channel_multiplier=1), reads=[b_u], writes=[b_u])
    P.op("pool", lambda: nc.gpsimd.memset(tril[:], 0.0), writes=[b_l])
    P.op("pool", lambda: nc.gpsimd.affine_select(out=tril[:], in_=tril[:], pattern=[[-1, 128]], compare_op=ALU.is_ge,
                                                   fill=1.0, base=0, channel_multiplier=1), reads=[b_l], writes=[b_l])
    return triu, b_u, tril, b_l


def causal_masks(nc, P, strict=True, dtype=BF16, name="cm"):
    m = P.sb(name, [128, 4, 512], dtype); b = P.buf(name)
    P.op("pool", lambda: nc.gpsimd.memset(m[:], 1.0), writes=[b])
    for o in range(4):
        P.op("pool", lambda o=o: nc.gpsimd.affine_select(out=m[:, o, :], in_=m[:, o, :], pattern=[[1, 512]],
                                                          compare_op=(ALU.is_gt if strict else ALU.is_ge), fill=0.0,
                                                          base=-128 * o, channel_multiplier=-1), reads=[b], writes=[b])
    return m, b


def build_mixer1(nc, P, T, NTC, d):
    scale = 64 ** -0.5
    NQB = T // 512
    NKT = T // 128
    qT = P.sb("qT_sb", [64, 2, T], BF16); b_q = P.buf("qT")
    kT = P.sb("kT_sb", [64, 2, T], BF16); b_k = P.buf("kT")
    v = P.sb("v_sb", [128, NKT, 2, 64], BF16); b_v = P.buf("v")
    for h in range(2):
        P.dma("sp", qT[:, h, :], d["qT"][h], writes=[b_q])
        P.dma("sp", kT[:, h, :], d["kT"][h], writes=[b_k])
    P.dma("sp", v[:], d["v"].rearrange("(n p) h e -> p n h e", p=128), writes=[b_v])
    triu, b_u, tril, b_l = tri_consts(nc, P)
    cm, b_cm = causal_masks(nc, P, strict=True)
    b_out = P.buf("o_sb_out")
    b_cout = P.buf("cout_out")

    N = NTC
    wT = P.sb("scw_sb", [128, 4, 3], F32); b_w = P.buf("scw")
    P.dma("sp", wT[:], d["scw"].rearrange("p (c k) -> p c k", c=4), writes=[b_w])
    cin = [P.sb(f"cin{i}", [128, 3, N + 2], F32) for i in range(2)]
    b_cin = [P.buf(f"cin{i}") for i in range(2)]
    vv = P.sb("cvv", [128, N + 2], F32); b_vv = P.buf("cvv")
    yy = P.sb("cyy", [128, N], F32); b_yy = P.buf("cyy")
    yo = [P.sb(f"cyo{i}", [128, N], BF16) for i in range(2)]
    b_yo = [P.buf(f"cyo{i}") for i in range(2)]
    for c in range(4):
        ci, b_ci = cin[c % 2], b_cin[c % 2]
        P.dma("sp", ci[:], d["convin"][:, c * 128:(c + 1) * 128, :].rearrange("k p n -> p k n"), writes=[b_ci])
        P.op("pool", lambda ci=ci: nc.gpsimd.tensor_tensor(out=vv[:], in0=ci[:, 1, :], in1=ci[:, 2, :], op=ALU.mult),
             reads=[b_ci], writes=[b_vv])
        P.op("dve", lambda c=c: nc.vector.tensor_scalar(out=yy[:], in0=vv[:, 0:N], scalar1=wT[:, c, 0:1], scalar2=None, op0=ALU.mult),
             reads=[b_vv, b_w], writes=[b_yy])
        for k in (1, 2):
            P.op("dve", lambda c=c, k=k: nc.vector.scalar_tensor_tensor(out=yy[:], in0=vv[:, k:N + k], scalar=wT[:, c, k:k + 1],
                                                                        in1=yy[:], op0=ALU.mult, op1=ALU.add),
                 reads=[b_vv, b_w, b_yy], writes=[b_yy])
        o, b_o = yo[c % 2], b_yo[c % 2]
        P.op("dve", lambda ci=ci, o=o: nc.vector.tensor_tensor(out=o[:], in0=yy[:], in1=ci[:, 0, 2:N + 2], op=ALU.mult),
             reads=[b_yy, b_ci], writes=[b_o])
        P.dma("sp", d["coutT"][c * 128:(c + 1) * 128, :], o[:], reads=[b_o], writes=[b_cout])

    S_ps = [P.ps(f"S{i}", [128, 2, 512], F32) for i in range(2)]
    b_S = [P.buf(f"S{i}") for i in range(2)]
    D_ps = [P.ps(f"D{h}", [128, 512], F32) for h in range(2)]
    b_D = [P.buf(f"D{h}") for h in range(2)]
    O_ps = [P.ps(f"O{h}", [64, 512], F32) for h in range(2)]
    b_O = [P.buf(f"O{h}") for h in range(2)]
    NE, NF, NA = 3, 4, 4
    e_sb = [P.sb(f"e{i}", [128, 2, 512], F32) for i in range(NE)]; b_e = [P.buf(f"e{i}") for i in range(NE)]
    sp_sb = [P.sb(f"sp{i}", [128, 2, 512], BF16) for i in range(NE)]; b_sp = [P.buf(f"sp{i}") for i in range(NE)]
    f_sb = [P.sb(f"f{i}", [128, 512], F32) for i in range(NF)]; b_f = [P.buf(f"f{i}") for i in range(NF)]
    a_sb = [P.sb(f"a{i}", [128, 512], BF16) for i in range(NA)]; b_a = [P.buf(f"a{i}") for i in range(NA)]
    oo = [P.sb(f"oo{h}", [64, 512], BF16) for h in range(2)]
    b_oo = [P.buf(f"oo{h}") for h in range(2)]
    zz = P.sb("zz", [128, 512], BF16); b_zz = P.buf("zz")
    P.op("pool", lambda: nc.gpsimd.memset(zz[:], 0.0), writes=[b_zz])
    items = []
    for qb in range(NQB):
        kmax = 4 * qb + 3
        for kb in range(kmax, -1, -1):
            for h in range(2):
                items.append(dict(qb=qb, kb=kb, h=h, diag=(kb >= 4 * qb), o=kb - 4 * qb, first=(kb == kmax), last=(kb == 0)))
    NI = len(items)
    NP = NI // 2

    def stA1(p):
        it = items[2 * p]; kb, qb = it["kb"], it["qb"]
        Sp, bS = S_ps[p % 2], b_S[p % 2]
        e, be = e_sb[p % NE], b_e[p % NE]
        for h in range(2):
            P.op("pe", lambda h=h: nc.tensor.matmul(Sp[:, h, :], lhsT=kT[:, h, kb * 128:(kb + 1) * 128], rhs=qT[:, h, qb * 512:(qb + 1) * 512],
                                                    start=True, stop=True), reads=[b_k, b_q], writes=[bS], skip_self=True)
        P.op("act", lambda: nc.scalar.activation(out=e[:], in_=Sp[:], func=AF.Exp, scale=scale), reads=[bS], writes=[be])

    def stA2(p):
        it = items[2 * p]; o = it["o"]
        e, be = e_sb[p % NE], b_e[p % NE]
        sp, bsp = sp_sb[p % NE], b_sp[p % NE]
        P.op("act", lambda: nc.scalar.activation(out=sp[:], in_=e[:], func=AF.Ln, bias=1.0), reads=[be], writes=[bsp])
        if it["diag"]:
            mb = cm[:, o, :].unsqueeze(1).to_broadcast([128, 2, 512])
            P.op("pool", lambda: nc.gpsimd.tensor_tensor(out=sp[:], in0=sp[:], in1=mb, op=ALU.mult), reads=[bsp, b_cm], writes=[bsp])
            P.op("pool", lambda: nc.gpsimd.tensor_tensor(out=e[:], in0=e[:], in1=mb, op=ALU.mult), reads=[be, b_cm], writes=[be])

    def stB1(i):
        it = items[i]; h = it["h"]
        sp, bsp = sp_sb[(i // 2) % NE], b_sp[(i // 2) % NE]
        P.op("pe", lambda: nc.tensor.matmul(D_ps[h][:], lhsT=triu[:], rhs=sp[:, h, :], start=it["first"], stop=True, skip_group_check=True),
             reads=[bsp, b_u], writes=[b_D[h]], skip_self=True)

    def stB2(i):
        it = items[i]; h = it["h"]
        f, bf_ = f_sb[i % NF], b_f[i % NF]
        P.op("act", lambda: nc.scalar.activation(out=f[:], in_=D_ps[h][:], func=AF.Exp, scale=-1.0), reads=[b_D[h]], writes=[bf_])

    def stC(i):
        it = items[i]; h = it["h"]
        sp, bsp = sp_sb[(i // 2) % NE], b_sp[(i // 2) % NE]
        e, be = e_sb[(i // 2) % NE], b_e[(i // 2) % NE]
        f, bf_ = f_sb[i % NF], b_f[i % NF]
        a, ba = a_sb[i % NA], b_a[i % NA]
        if not it["last"]:
            P.op("pe", lambda: nc.tensor.matmul(D_ps[h][:], lhsT=tril[:], rhs=sp[:, h, :], start=False, stop=True, skip_group_check=True),
                 reads=[bsp, b_l], writes=[b_D[h]], skip_self=True)
        P.op("dve", lambda: nc.vector.tensor_tensor(out=a[:], in0=e[:, h, :], in1=f[:], op=ALU.mult), reads=[be, bf_], writes=[ba])

    def stD(i):
        it = items[i]; h, kb, qb = it["h"], it["kb"], it["qb"]
        a, ba = a_sb[i % NA], b_a[i % NA]
        if it["first"]:
            P.op("pe", lambda: nc.tensor.matmul(O_ps[h][:], lhsT=zz[:, 0:64], rhs=zz[:], start=True, stop=True),
                 reads=[b_zz], writes=[b_O[h]], skip_self=True)
        P.op("pe", lambda: nc.tensor.matmul(O_ps[h][:], lhsT=v[:, kb, h, :], rhs=a[:], start=False, stop=True, skip_group_check=True),
             reads=[ba, b_v], writes=[b_O[h]], skip_self=True)
        if it["last"]:
            P.op("dve", lambda: nc.vector.tensor_copy(out=oo[h][:], in_=O_ps[h][:]), reads=[b_O[h]], writes=[b_oo[h]])
            P.dma("sp", d["o_sbT"][h, :, qb * 512:(qb + 1) * 512], oo[h][:], reads=[b_oo[h]], writes=[b_out])

    for s_ in range(-4, NI + 1):
        if s_ % 2 == 0 and 0 <= (s_ + 4) // 2 < NP:
            stA1((s_ + 4) // 2)
        if 0 <= s_ + 1 < NI:
            stB1(s_ + 1)
            stB2(s_ + 1)
        if s_ % 2 == 1 and 0 <= (s_ + 3) // 2 < NP:
            stA2((s_ + 3) // 2)
        if 0 <= s_ < NI:
            stC(s_)
        if 0 <= s_ - 1 < NI:
            stD(s_ - 1)
    return [b_out, b_cout]


def build_nsa(nc, P, T, qbs, d, banks, NOWN=4):
    scale = 64 ** -0.5
    NCP = T // 16
    NC = NCP - 1
    NCT = NCP // 128
    NKT = T // 128
    ident, b_id = make_ident(nc, P, "ident0")
    b_out = P.buf("nsa_out")

    QAq = [P.sb(f"QAq{i}", [68, 4, 512], BF16) for i in range(2)]; b_QAq = [P.buf(f"QAq{i}") for i in range(2)]
    cur = {}
    KSA = P.sb("KSA_sb", [68, T], BF16); b_KSA = P.buf("KSA")
    KWA = P.sb("KWA_sb", [68, T], BF16); b_KWA = P.buf("KWA")
    P.dma("sp", KSA[:], d["KSA"], writes=[b_KSA])
    P.dma("sp", KWA[:], d["KWA"], writes=[b_KWA])
    VSA = P.sb("VSA_sb", [128, NKT, 65], BF16); b_VSA = P.buf("VSA")
    VWA = P.sb("VWA_sb", [128, NKT, 65], BF16); b_VWA = P.buf("VWA")
    P.dma("sp", VSA[:], d["VSA"].rearrange("(n p) e -> p n e", p=128), writes=[b_VSA])
    P.dma("sp", VWA[:], d["VWA"].rearrange("(n p) e -> p n e", p=128), writes=[b_VWA])
    Wsel = P.sb("Wsel", [128, T], BF16); b_Wsel = P.buf("Wsel")
    P.op("pool", lambda: nc.gpsimd.memset(Wsel[:], 1.0), writes=[b_Wsel])
    P.op("pool", lambda: nc.gpsimd.affine_select(out=Wsel[:], in_=Wsel[:], pattern=[[1, T]], compare_op=ALU.is_ge, fill=0.0,
                                                   base=0, channel_multiplier=-64), reads=[b_Wsel], writes=[b_Wsel])
    P.op("pool", lambda: nc.gpsimd.affine_select(out=Wsel[:], in_=Wsel[:], pattern=[[-1, T]], compare_op=ALU.is_ge, fill=0.0,
                                                   base=63, channel_multiplier=64), reads=[b_Wsel], writes=[b_Wsel])

    S_ps = [banks[i][0] for i in range(2)]; b_S = [banks[i][1] for i in range(2)]
    MK, b_MK = banks[2]
    A_ps = [banks[3 + i][0] for i in range(4)]; b_A = [banks[3 + i][1] for i in range(4)]
    X, b_X = banks[7]
    zz = P.sb("nzz", [128, 512], BF16); b_zz = P.buf("nzz")
    P.op("pool", lambda: nc.gpsimd.memset(zz[:], 0.0), writes=[b_zz])

    def zero_bank(i):
        P.op("pe", lambda i=i: nc.tensor.matmul(A_ps[i][:], lhsT=zz[:, 0:128], rhs=zz[:], start=True, stop=True),
             reads=[b_zz], writes=[b_A[i]], skip_self=True)

    KcA = P.sb("KcA", [68, NCP], BF16); b_KcA = P.buf("KcA")
    VcX = P.sb("VcX", [128, NCT, 193], BF16); b_VcX = P.buf("VcX")
    P.dma("sp", KcA[64:68, :], d["kaugc"], writes=[b_KcA])
    P.dma("sp", VcX[:, :, 65:193], d["poolm"].rearrange("(n p) j -> p n j", p=128), writes=[b_VcX])
    P.op("pool", lambda: nc.gpsimd.memset(VcX[:, :, 64:65], 1.0), reads=[], writes=[b_VcX])
    w1 = P.sb("w1_sb", [128, 16, 128], BF16); b_w1 = P.buf("w1")
    pe2 = P.sb("pe2_sb", [128, 16], BF16); b_pe2 = P.buf("pe2")
    w2 = P.sb("w2_sb", [128, 64], BF16); b_w2 = P.buf("w2")
    c2 = P.sb("c2_sb", [128, T], BF16); b_c2 = P.buf("c2")
    pb = P.sb("pb_sb", [128, 1], F32); b_pb = P.buf("pb")
    xh = P.sb("xh_sb", [128, NCP], F32); b_xh = P.buf("xh")
    uh = P.sb("uh_sb", [128, NCP], F32); b_uh = P.buf("uh")
    hT = P.sb("hT_sb", [128, NCP], BF16); b_hT = P.buf("hT")
    P.op("pool", lambda: nc.gpsimd.memset(hT[:], 0.0), writes=[b_hT])
    for kv in ("k", "v"):
        P.dma("pool", w1[:], d["w1" + kv], writes=[b_w1])
        P.dma("pool", pe2[:], d["pe2" + kv], writes=[b_pe2])
        P.dma("pool", w2[:], d["w2" + kv], writes=[b_w2])
        P.dma("sp", c2[:], d["c2" + kv], writes=[b_c2])
        c2v = c2[:].rearrange("p (n s) -> p n s", s=16)
        for c in range(16):
            if 2 * c < 16:
                rhs = c2v[:, 0:NC, 2 * c]
            else:
                rhs = c2v[:, 1:NC + 1, 2 * c - 16]
            P.op("pe", lambda c=c, rhs=rhs: nc.tensor.matmul(X[:, 0:NC], lhsT=w1[:, c, :], rhs=rhs, start=(c == 0), stop=(c == 15)),
                 reads=[b_w1, b_c2], writes=[b_X], skip_self=True)
        P.op("act", lambda: nc.scalar.copy(out=xh[:, 0:NC], in_=X[:, 0:NC]), reads=[b_X], writes=[b_xh])
        for c in range(16):
            P.op("pe", lambda c=c: nc.tensor.matmul(X[:, 0:1], lhsT=w1[:, c, :], rhs=pe2[:, c:c + 1], start=(c == 0), stop=(c == 15)),
                 reads=[b_w1, b_pe2], writes=[b_X], skip_self=True)
        P.op("dve", lambda: nc.vector.tensor_copy(out=pb[:], in_=X[:, 0:1]), reads=[b_X], writes=[b_pb])
        P.op("dve", lambda: nc.vector.tensor_scalar(out=xh[:, 0:NC], in0=xh[:, 0:NC], scalar1=pb[:, 0:1], scalar2=None, op0=ALU.add),
             reads=[b_xh, b_pb], writes=[b_xh])
        P.op("dve", lambda: nc.vector.tensor_tensor(out=uh[:, 0:NC], in0=xh[:, 0:NC], in1=xh[:, 0:NC], op=ALU.mult),
             reads=[b_xh], writes=[b_uh])
        P.op("dve", lambda: nc.vector.tensor_scalar(out=uh[:, 0:NC], in0=uh[:, 0:NC], scalar1=0.044715, scalar2=1.0, op0=ALU.mult, op1=ALU.add),
             reads=[b_uh], writes=[b_uh])
        P.op("dve", lambda: nc.vector.tensor_tensor(out=uh[:, 0:NC], in0=uh[:, 0:NC], in1=xh[:, 0:NC], op=ALU.mult),
             reads=[b_uh, b_xh], writes=[b_uh])
        P.op("act", lambda: nc.scalar.activation(out=uh[:, 0:NC], in_=uh[:, 0:NC], func=AF.Sigmoid, scale=2.0 * math.sqrt(2.0 / math.pi)),
             reads=[b_uh], writes=[b_uh])
        P.op("dve", lambda: nc.vector.tensor_tensor(out=hT[:, 0:NC], in0=uh[:, 0:NC], in1=xh[:, 0:NC], op=ALU.mult),
             reads=[b_uh, b_xh], writes=[b_hT])
        if kv == "k":
            P.op("pe", lambda: nc.tensor.matmul(X[0:64, 0:NCP], lhsT=w2[:], rhs=hT[:], start=True, stop=True),
                 reads=[b_w2, b_hT], writes=[b_X], skip_self=True)
            P.op("dve", lambda: nc.vector.tensor_copy(out=KcA[0:64, :], in_=X[0:64, 0:NCP]), reads=[b_X], writes=[b_KcA])
        else:
            for n in range(NCT):
                P.op("pe", lambda n=n: nc.tensor.matmul(X[:, n * 64:(n + 1) * 64], lhsT=hT[:, n * 128:(n + 1) * 128], rhs=w2[:],
                                                        start=True, stop=True), reads=[b_w2, b_hT], writes=[b_X], skip_self=True)
            P.op("dve", lambda: nc.vector.tensor_copy(out=VcX[:, :, 0:64], in_=X[:, 0:NCT * 64].rearrange("p (n e) -> p n e", e=64)),
                 reads=[b_X], writes=[b_VcX])

    NE = 5
    e_sb = [P.sb(f"ne{i}", [128, 512], F32) for i in range(NE)]; b_e = [P.buf(f"ne{i}") for i in range(NE)]
    p_sb = [P.sb(f"np{i}", [128, 512], BF16) for i in range(NE)]; b_p = [P.buf(f"np{i}") for i in range(NE)]
    NMK = 3
    mk_sb = [P.sb(f"nmk{i}", [128, 512], BF16) for i in range(NMK)]; b_mk = [P.buf(f"nmk{i}") for i in range(NMK)]
    accC = [P.sb(f"accC{r}", [128, 4, 193], F32) for r in range(4)]; b_accC = [P.buf(f"accC{r}") for r in range(4)]
    accS = [P.sb(f"accS{r}", [128, 4, 65], F32) for r in range(NOWN)]; b_accS = [P.buf(f"accS{r}") for r in range(NOWN)]
    accW = [P.sb(f"accW{r}", [128, 4, 65], F32) for r in range(NOWN)]; b_accW = [P.buf(f"accW{r}") for r in range(NOWN)]
    rden = P.sb("rden", [128, 3, 4, 4], F32)
    b_rdenC = [P.buf(f"rdenC{r}") for r in range(4)]
    b_rdenS = [P.buf(f"rdenS{r}") for r in range(4)]
    b_rdenW = [P.buf(f"rdenW{r}") for r in range(4)]
    imp = P.sb("imp", [128, 4, 128], F32); b_imp = P.buf("imp")
    M1 = P.sb("M1_sb", [128, 4, 128], F32); b_M1 = P.buf("M1")
    A1 = P.sb("A1_sb", [128, 4, 128], F32); b_A1 = P.buf("A1")
    score = [P.sb(f"score{i}", [128, 128], F32) for i in range(2)]; b_score = [P.buf(f"score{i}") for i in range(2)]
    work = [P.sb(f"work{i}", [128, 128], F32) for i in range(2)]; b_work = [P.buf(f"work{i}") for i in range(2)]
    m8 = [P.sb(f"m8{i}", [128, 16], F32) for i in range(2)]; b_m8 = [P.buf(f"m8{i}") for i in range(2)]
    sel = P.sb("sel", [128, 4, 128], BF16); b_sel = P.buf("sel")
    selT = P.sb("selT", [128, 512], BF16); b_selT = P.buf("selT")
    graw2 = [P.sb(f"graw_sb{i}", [128, 4, 3 * NOWN], F32) for i in range(2)]; b_graw2 = [P.buf(f"graw{i}") for i in range(2)]
    gsig2 = [P.sb(f"gsig{i}", [128, 4, 3 * NOWN], F32) for i in range(2)]; b_gsig2 = [P.buf(f"gsig{i}") for i in range(2)]
    wgt = P.sb("wgt", [128, 3, 4, 4], F32); b_wgt = P.buf("wgt")
    oacc = P.sb("oacc", [128, 4, 64 * NOWN], F32); b_oacc = P.buf("oacc")
    obf = P.sb("obf", [128, 4, 64 * NOWN], BF16); b_obf = P.buf("obf")

    negm = P.sb("negm", [128, 4, 512], BF16); b_negm = P.buf("negm")
    P.op("pool", lambda: nc.gpsimd.memset(negm[:], 0.0), writes=[b_negm])
    for o_ in range(4):
        P.op("pool", lambda o_=o_: nc.gpsimd.affine_select(out=negm[:, o_, :], in_=negm[:, o_, :], pattern=[[1, 512]], compare_op=ALU.is_ge,
                                                            fill=-30000.0, base=-128 * o_, channel_multiplier=-1), reads=[b_negm], writes=[b_negm])

    items = []
    for qb in qbs:
        nct = min(NCT, (32 * (qb + 1) + 127) // 128)
        for r in range(4):
            for kt in range(nct):
                items.append(dict(kind="C", qb=qb, kt=kt, r=r, first=(kt == 0), last=(kt == nct - 1), qbstart=(r == 0 and kt == 0)))
        kw0 = max(0, 4 * qb - 4)
        for kt in range(kw0, 4 * qb + 4):
            for r in range(NOWN):
                items.append(dict(kind="W", qb=qb, kt=kt, r=r, first=(kt == kw0), last=(kt == 4 * qb + 3), qbstart=False))
        for kt in range(0, 4 * qb + 4):
            for r in range(NOWN):
                items.append(dict(kind="S", qb=qb, kt=kt, r=r, first=(kt == 0), last=(kt == 4 * qb + 3), qbstart=False))
    NI = len(items)

    def banks_of(it):
        r = it["r"]
        if it["kind"] == "C":
            return [2 * (r % 2), 2 * (r % 2) + 1]
        if it["kind"] == "W":
            return [r]
        return [2 + r] if NOWN == 2 else [r]

    def load_qb(qb_):
        q0_ = qb_ * 512
        qi_ = qb_ % 2
        for rr in range(4):
            P.dma("sp", QAq[qi_][:, rr, :], d["QA"][rr][:, q0_:q0_ + 512], writes=[b_QAq[qi_]])
        P.dma("sp", M1[:], d["M1"][q0_:q0_ + 512, :].rearrange("(c p) j -> p c j", p=128), writes=[b_M1])
        P.dma("sp", A1[:], d["A1"][q0_:q0_ + 512, :].rearrange("(c p) j -> p c j", p=128), writes=[b_A1])
        graw, b_graw, gsig, b_gsig = graw2[qi_], b_graw2[qi_], gsig2[qi_], b_gsig2[qi_]
        P.dma("sp", graw[:], d["graw"][q0_:q0_ + 512, :].rearrange("(c p) j -> p c j", p=128), writes=[b_graw])
        P.op("act", lambda: nc.scalar.activation(out=gsig[:], in_=graw[:], func=AF.Sigmoid), reads=[b_graw], writes=[b_gsig])

    def stA(i):
        it = items[i]; kind, qb, kt, r = it["kind"], it["qb"], it["kt"], it["r"]
        q0 = qb * 512
        qi = qb % 2
        if it["qbstart"] and qb == qbs[0]:
            load_qb(qb)
        diag = kt >= 4 * qb
        if kind == "S" and r == 0 and kt == 0:
            selection_pe()
            nxt = qbs.index(qb) + 1
            if nxt < len(qbs):
                load_qb(qbs[nxt])
        if kind == "S" and r == 0:
            MKc, bMKc = (MK, b_MK) if kt % 2 == 0 else (X, b_X)
            P.op("pe", lambda: nc.tensor.matmul(MKc[:], lhsT=Wsel[:, kt * 128:(kt + 1) * 128], rhs=selT[:], start=True, stop=True),
                 reads=[b_Wsel, b_selT], writes=[bMKc], skip_self=True)
        KA, bKA = {"C": (KcA, b_KcA), "W": (KWA, b_KWA), "S": (KSA, b_KSA)}[kind]
        clamp = True if kind == "C" else diag
        si = i % 2
        ei = i % NE
        QAc, bQAc = QAq[qi], b_QAq[qi]
        addmask = diag and kind in ("W", "S")
        P.op("pe", lambda: nc.tensor.matmul(S_ps[si][:], lhsT=KA[:, kt * 128:(kt + 1) * 128], rhs=QAc[:, r, :], start=True, stop=not addmask),
             reads=[bKA, bQAc], writes=[b_S[si]], skip_self=True)
        if addmask:
            o_ = kt - 4 * qb
            P.op("pe", lambda: nc.tensor.matmul(S_ps[si][:], lhsT=ident[:], rhs=negm[:, o_, :], start=False, stop=True),
                 reads=[b_id, b_negm], writes=[b_S[si]], skip_self=True)
            if kind == "W":
                P.op("act", lambda: nc.scalar.activation(out=p_sb[ei][:], in_=S_ps[si][:], func=AF.Exp, scale=scale), reads=[b_S[si]], writes=[b_p[ei]])
            else:
                P.op("act", lambda: nc.scalar.activation(out=e_sb[ei][:], in_=S_ps[si][:], func=AF.Exp, scale=scale), reads=[b_S[si]], writes=[b_e[ei]])
        elif clamp:
            P.op("dve", lambda: nc.vector.tensor_scalar(out=e_sb[ei][:], in0=S_ps[si][:], scalar1=40.0 / scale, scalar2=None, op0=ALU.min),
                 reads=[b_S[si]], writes=[b_e[ei]])
            P.op("act", lambda: nc.scalar.activation(out=e_sb[ei][:], in_=e_sb[ei][:], func=AF.Exp, scale=scale), reads=[b_e[ei]], writes=[b_e[ei]])
        else:
            P.op("act", lambda: nc.scalar.activation(out=e_sb[ei][:], in_=S_ps[si][:], func=AF.Exp, scale=scale), reads=[b_S[si]], writes=[b_e[ei]])

    def stB(i):
        it = items[i]; kind, qb, kt, r = it["kind"], it["qb"], it["kt"], it["r"]
        q0 = qb * 512
        ei = i % NE
        diag = kt >= 4 * qb
        e, be, p, bp = e_sb[ei], b_e[ei], p_sb[ei], b_p[ei]
        if kind == "C":
            bs = -(2048 * kt + 31 - q0)
            P.op("pool", lambda: nc.gpsimd.affine_select(out=p[:], in_=e[:], pattern=[[1, 512]], compare_op=ALU.is_ge, fill=0.0,
                                                          base=bs, channel_multiplier=-16), reads=[be], writes=[bp])
        elif kind == "W":
            if diag:
                pass
            else:
                bs = 511 - (q0 - 128 * kt)
                P.op("pool", lambda: nc.gpsimd.affine_select(out=p[:], in_=e[:], pattern=[[-1, 512]], compare_op=ALU.is_ge, fill=0.0,
                                                              base=bs, channel_multiplier=1), reads=[be], writes=[bp])
        else:
            mi = kt % NMK
            MKc, bMKc = (MK, b_MK) if kt % 2 == 0 else (X, b_X)
            P.op("dve", lambda: nc.vector.tensor_tensor(out=p[:], in0=e[:], in1=MKc[:], op=ALU.mult), reads=[be, bMKc], writes=[bp])

    import collections as _col
    pending = _col.deque()

    def flush(n=None):
        while pending and (n is None or n > 0):
            pending.popleft()()
            if n is not None:
                n -= 1

    def selection():
        for c in range(4):
            pending.append(lambda c=c: sel_chunk(c))

    def sel_chunk(c):
        if True:
            k = c % 2
            sc, bsc, wk, bwk, mm, bmm = score[k], b_score[k], work[k], b_work[k], m8[k], b_m8[k]
            P.op("dve", lambda c=c, sc=sc: nc.vector.tensor_tensor(out=sc[:], in0=imp[:, c, :], in1=M1[:, c, :], op=ALU.mult),
                 reads=[b_imp, b_M1], writes=[bsc])
            P.op("dve", lambda c=c, sc=sc: nc.vector.tensor_tensor(out=sc[:], in0=sc[:], in1=A1[:, c, :], op=ALU.add),
                 reads=[bsc, b_A1], writes=[bsc])
            P.op("dve", lambda sc=sc, mm=mm: nc.vector.max(out=mm[:, 0:8], in_=sc[:]), reads=[bsc], writes=[bmm])
            P.op("dve", lambda sc=sc, mm=mm, wk=wk: nc.vector.match_replace(out=wk[:], in_to_replace=mm[:, 0:8], in_values=sc[:], imm_value=-1e9),
                 reads=[bsc, bmm], writes=[bwk])
            P.op("dve", lambda mm=mm, wk=wk: nc.vector.max(out=mm[:, 8:16], in_=wk[:]), reads=[bwk], writes=[bmm])
            P.op("dve", lambda mm=mm: nc.vector.tensor_scalar(out=mm[:, 15:16], in0=mm[:, 15:16], scalar1=0.0, scalar2=None, op0=ALU.max),
                 reads=[bmm], writes=[bmm])
            P.op("dve", lambda c=c, sc=sc, mm=mm: nc.vector.tensor_scalar(out=sel[:, c, :], in0=sc[:], scalar1=mm[:, 15:16], scalar2=None, op0=ALU.is_ge),
                 reads=[bsc, bmm], writes=[b_sel])

    def selection_pe():
        flush()
        for c in range(4):
            P.op("pe", lambda c=c: nc.tensor.matmul(X[:, c * 128:(c + 1) * 128], lhsT=sel[:, c, :], rhs=ident[:], start=True, stop=True),
                 reads=[b_sel, b_id], writes=[b_X], skip_self=True)
        P.op("act", lambda: nc.scalar.copy(out=selT[:], in_=X[:]), reads=[b_X], writes=[b_selT])

    def combine(qb):
        pending.append(lambda: combine_w(qb))
        for r in range(NOWN):
            pending.append(lambda r=r: combine_r(qb, r))
        pending.append(lambda: combine_out(qb))

    def combine_w(qb):
        gsig, b_gsig = gsig2[qb % 2], b_gsig2[qb % 2]
        for br, brd in ((0, b_rdenC), (1, b_rdenS), (2, b_rdenW)):
            P.op("dve", lambda br=br: nc.vector.tensor_tensor(
                out=wgt[:, br, 0:NOWN, :], in0=rden[:, br, 0:NOWN, :],
                in1=gsig[:].rearrange("p c (r b) -> p r c b", b=3)[:, :, :, br], op=ALU.mult),
                reads=brd[0:NOWN] + [b_gsig], writes=[b_wgt])

    def combine_r(qb, r):
        if True:
            for c in range(4):
                P.op("dve", lambda r=r, c=c: nc.vector.tensor_scalar(out=oacc[:, c, r * 64:(r + 1) * 64], in0=accC[r][:, c, 0:64],
                                                                      scalar1=wgt[:, 0, r, c:c + 1], scalar2=None, op0=ALU.mult),
                     reads=[b_accC[r], b_wgt], writes=[b_oacc])
                P.op("dve", lambda r=r, c=c: nc.vector.scalar_tensor_tensor(out=oacc[:, c, r * 64:(r + 1) * 64], in0=accS[r][:, c, 0:64],
                                                                            scalar=wgt[:, 1, r, c:c + 1], in1=oacc[:, c, r * 64:(r + 1) * 64],
                                                                            op0=ALU.mult, op1=ALU.add),
                     reads=[b_accS[r], b_wgt, b_oacc], writes=[b_oacc])
                P.op("dve", lambda r=r, c=c: nc.vector.scalar_tensor_tensor(out=obf[:, c, r * 64:(r + 1) * 64], in0=accW[r][:, c, 0:64],
                                                                            scalar=wgt[:, 2, r, c:c + 1], in1=oacc[:, c, r * 64:(r + 1) * 64],
                                                                            op0=ALU.mult, op1=ALU.add),
                     reads=[b_accW[r], b_wgt, b_oacc], writes=[b_obf])

    def combine_out(qb):
        q0 = qb * 512
        P.dma("sp", d["o_nsa"][q0:q0 + 512, :].rearrange("(c p) e -> p c e", p=128), obf[:], reads=[b_obf], writes=[b_out])

    def stC(i):
        it = items[i]; kind, qb, kt, r = it["kind"], it["qb"], it["kt"], it["r"]
        ei = i % NE
        p, bp = p_sb[ei], b_p[ei]
        bks = banks_of(it)
        diag = kt >= 4 * qb
        if it["first"]:
            for bk in bks:
                zero_bank(bk)
        if kind == "C":
            for c in range(4):
                bk, off = bks[c // 2], (c % 2) * 193
                P.op("pe", lambda c=c, bk=bk, off=off: nc.tensor.matmul(A_ps[bk][:, off:off + 193], lhsT=p[:, c * 128:(c + 1) * 128], rhs=VcX[:, kt, 0:193],
                                                                        start=False, stop=True, skip_group_check=True),
                     reads=[bp, b_VcX], writes=[b_A[bk]], skip_self=True)
        else:
            VA, bVA = (VWA, b_VWA) if kind == "W" else (VSA, b_VSA)
            bk = bks[0]
            for c in (range(kt - 4 * qb, 4) if diag else range(4)):
                P.op("pe", lambda c=c: nc.tensor.matmul(A_ps[bk][:, c * 65:(c + 1) * 65], lhsT=p[:, c * 128:(c + 1) * 128], rhs=VA[:, kt, 0:65],
                                                        start=False, stop=True, skip_group_check=True),
                     reads=[bp, bVA], writes=[b_A[bk]], skip_self=True)
        if not it["last"]:
            return
        if kind == "C":
            flush()
            P.op("act", lambda: nc.scalar.copy(out=accC[r][:, 0:2, :].rearrange("p c e -> p (c e)"), in_=A_ps[bks[0]][:, 0:386]),
                 reads=[b_A[bks[0]]], writes=[b_accC[r]])
            P.op("dve", lambda: nc.vector.tensor_copy(out=accC[r][:, 2:4, :].rearrange("p c e -> p (c e)"), in_=A_ps[bks[1]][:, 0:386]),
                 reads=[b_A[bks[1]]], writes=[b_accC[r]])
            P.op("dve", lambda: nc.vector.tensor_scalar(out=rden[:, 0, r, :], in0=accC[r][:, :, 64], scalar1=1e-30, scalar2=None, op0=ALU.max),
                 reads=[b_accC[r]], writes=[b_rdenC[r]])
            P.op("dve", lambda: nc.vector.reciprocal(out=rden[:, 0, r, :], in_=rden[:, 0, r, :]), reads=[b_rdenC[r]], writes=[b_rdenC[r]])
            for c in range(4):
                if r == 0:
                    P.op("dve", lambda c=c: nc.vector.tensor_scalar(out=imp[:, c, :], in0=accC[r][:, c, 65:193], scalar1=rden[:, 0, r, c:c + 1],
                                                                     scalar2=None, op0=ALU.mult), reads=[b_accC[r], b_rdenC[r]], writes=[b_imp])
                else:
                    P.op("dve", lambda c=c: nc.vector.scalar_tensor_tensor(out=imp[:, c, :], in0=accC[r][:, c, 65:193], scalar=rden[:, 0, r, c:c + 1],
                                                                           in1=imp[:, c, :], op0=ALU.mult, op1=ALU.add),
                         reads=[b_accC[r], b_rdenC[r], b_imp], writes=[b_imp])
            if r == 3:
                selection()
        else:
            acc, bacc, brd, bri = (accW, b_accW, b_rdenW, 2) if kind == "W" else (accS, b_accS, b_rdenS, 1)
            bk = bks[0]
            if r % 2 == 0:
                P.op("act", lambda: nc.scalar.copy(out=acc[r][:].rearrange("p c e -> p (c e)"), in_=A_ps[bk][:, 0:260]), reads=[b_A[bk]], writes=[bacc[r]])
            else:
                P.op("dve", lambda: nc.vector.tensor_copy(out=acc[r][:].rearrange("p c e -> p (c e)"), in_=A_ps[bk][:, 0:260]), reads=[b_A[bk]], writes=[bacc[r]])
            P.op("dve", lambda: nc.vector.reciprocal(out=rden[:, bri, r, :], in_=acc[r][:, :, 64]), reads=[bacc[r]], writes=[brd[r]])
            if kind == "S" and r == NOWN - 1:
                combine(qb)

    for s_ in range(-2, NI):
        if 0 <= s_ + 2 < NI:
            stA(s_ + 2)
        if 0 <= s_ + 1 < NI:
            stB(s_ + 1)
        if 0 <= s_ < NI:
            stC(s_)
        flush(1)
    flush()
    return [b_out]


def alloc_banks(P):
    return [(P.ps(f"bank{i}", [128, 512], F32), P.buf(f"bank{i}")) for i in range(8)]


def build_conv(nc, P, NTC, d, banks, ident, b_id):
    N = NTC
    NTT = N // 512
    b_out = P.buf("conv_out")
    dww = P.sb("dww_sb", [128, 4, 31], F32); b_dww = P.buf("dww")
    prm = P.sb("cprm_sb", [128, 3, 4], F32); b_prm = P.buf("cprm")
    P.dma("sp", dww[:], d["dww"], writes=[b_dww])
    P.dma("sp", prm[:, 0, :], d["dwb"], writes=[b_prm])
    P.dma("sp", prm[:, 1, :], d["lng"], writes=[b_prm])
    P.dma("sp", prm[:, 2, :], d["lnb"], writes=[b_prm])
    onesF = P.sb("onesF", [128, 128], F32); b_ones = P.buf("onesF")
    P.op("pool", lambda: nc.gpsimd.memset(onesF[:], 1.0 / 512.0), writes=[b_ones])
    ain = [P.sb(f"ain{i}", [128, 2, N + 30], F32) for i in range(2)]; b_ain = [P.buf(f"ain{i}") for i in range(2)]
    abf = [P.sb(f"abf{i}", [128, N + 30], BF16) for i in range(2)]; b_abf = [P.buf(f"abf{i}") for i in range(2)]
    diag = [P.sb(f"diag{i}", [128, 31, 128], BF16) for i in range(2)]; b_diag = [P.buf(f"diag{i}") for i in range(2)]
    y = [P.sb(f"cy{c}", [128, N], F32) for c in range(4)]; b_y = [P.buf(f"cy{c}") for c in range(4)]
    ysq = P.sb("cysq", [128, 512], F32); b_ysq = P.buf("cysq")
    for c in range(4):
        ai, b_ai = ain[c % 2], b_ain[c % 2]
        ab, b_ab = abf[c % 2], b_abf[c % 2]
        dg, b_dg = diag[c % 2], b_diag[c % 2]
        P.dma("sp", ai[:], d["aT"][:, c * 128:(c + 1) * 128, :].rearrange("k p n -> p k n"), writes=[b_ai])
        P.op("act", lambda ai=ai: nc.scalar.activation(out=ai[:, 1, :], in_=ai[:, 1, :], func=AF.Sigmoid), reads=[b_ai], writes=[b_ai])
        P.op("dve", lambda ai=ai, ab=ab: nc.vector.tensor_tensor(out=ab[:], in0=ai[:, 0, :], in1=ai[:, 1, :], op=ALU.mult),
             reads=[b_ai], writes=[b_ab])
        for k in range(31):
            P.op("pool", lambda k=k, c=c, dg=dg: nc.gpsimd.tensor_scalar(out=dg[:, k, :], in0=ident[:], scalar1=dww[:, c, k:k + 1], scalar2=None,
                                                                          op0=ALU.mult), reads=[b_id, b_dww], writes=[b_dg])
        for tt in range(NTT):
            ps, b_ps = banks[tt % 2]
            for k in range(31):
                P.op("pe", lambda k=k, tt=tt, ps=ps, dg=dg, ab=ab: nc.tensor.matmul(ps[:], lhsT=dg[:, k, :], rhs=ab[:, tt * 512 + k: tt * 512 + k + 512],
                                                                                      start=(k == 0), stop=(k == 30)),
                     reads=[b_dg, b_ab], writes=[b_ps], skip_self=True)
            P.op("act", lambda c=c, tt=tt, ps=ps: nc.scalar.activation(out=y[c][:, tt * 512:(tt + 1) * 512], in_=ps[:], func=AF.Identity,
                                                                        bias=prm[:, 0, c:c + 1]), reads=[b_ps, b_prm], writes=[b_y[c]])
    mean = P.sb("cmean", [128, 512], F32); b_mean = P.buf("cmean")
    rstd = P.sb("crstd", [128, 512], F32); b_rstd = P.buf("crstd")
    yn = [P.sb(f"cyn{i}", [128, 512], F32) for i in range(2)]; b_yn = [P.buf(f"cyn{i}") for i in range(2)]
    co = [P.sb(f"cco{i}", [128, 512], BF16) for i in range(2)]; b_co = [P.buf(f"cco{i}") for i in range(2)]
    it = 0
    for tt in range(NTT):
        sl = slice(tt * 512, (tt + 1) * 512)
        pm, b_pm = banks[2]
        pq, b_pq = banks[3]
        for c in range(4):
            P.op("pe", lambda c=c, sl=sl: nc.tensor.matmul(pm[:], lhsT=onesF[:], rhs=y[c][:, sl], start=(c == 0), stop=(c == 3)),
                 reads=[b_ones, b_y[c]], writes=[b_pm], skip_self=True)
        for c in range(4):
            P.op("act", lambda c=c, sl=sl: nc.scalar.activation(out=ysq[:], in_=y[c][:, sl], func=AF.Square), reads=[b_y[c]], writes=[b_ysq])
            P.op("pe", lambda c=c: nc.tensor.matmul(pq[:], lhsT=onesF[:], rhs=ysq[:], start=(c == 0), stop=(c == 3)),
                 reads=[b_ones, b_ysq], writes=[b_pq], skip_self=True)
        P.op("dve", lambda: nc.vector.tensor_copy(out=mean[:], in_=pm[:]), reads=[b_pm], writes=[b_mean])
        P.op("dve", lambda: nc.vector.tensor_tensor(out=rstd[:], in0=mean[:], in1=mean[:], op=ALU.mult), reads=[b_mean], writes=[b_rstd])
        P.op("dve", lambda: nc.vector.tensor_tensor(out=rstd[:], in0=pq[:], in1=rstd[:], op=ALU.subtract), reads=[b_pq, b_rstd], writes=[b_rstd])
        P.op("act", lambda: nc.scalar.activation(out=rstd[:], in_=rstd[:], func=AF.Sqrt, bias=1e-5), reads=[b_rstd], writes=[b_rstd])
        P.op("dve", lambda: nc.vector.reciprocal(out=rstd[:], in_=rstd[:]), reads=[b_rstd], writes=[b_rstd])
        for c in range(4):
            i = it % 2
            it += 1
            P.op("dve", lambda c=c, sl=sl, i=i: nc.vector.tensor_tensor(out=yn[i][:], in0=y[c][:, sl], in1=mean[:], op=ALU.subtract),
                 reads=[b_y[c], b_mean], writes=[b_yn[i]])
            P.op("dve", lambda i=i: nc.vector.tensor_tensor(out=yn[i][:], in0=yn[i][:], in1=rstd[:], op=ALU.mult),
                 reads=[b_yn[i], b_rstd], writes=[b_yn[i]])
            P.op("act", lambda c=c, i=i: nc.scalar.activation(out=co[i][:], in_=yn[i][:], func=AF.Silu, scale=prm[:, 1, c:c + 1],
                                                               bias=prm[:, 2, c:c + 1]), reads=[b_yn[i], b_prm], writes=[b_co[i]])
            P.dma("sp", d["coutT"][c * 128:(c + 1) * 128, sl], co[i][:], reads=[b_co[i]], writes=[b_out])
    return [b_out]

bf = ml_dtypes.bfloat16

def nsa_consts(T):
    t = np.arange(T)
    j = np.arange(128)
    vis = (j[None, :] * 64 <= t[:, None])
    cur = t // 64
    forced = (j[None, :] == 0) | (j[None, :] == cur[:, None]) | (j[None, :] == cur[:, None] - 1)
    M1 = (vis & ~forced).astype(np.float32)
    A1 = np.where(vis, np.where(forced, 1e4, 0.0), -1.0).astype(np.float32)
    NS = T // 64
    M1[:, NS:] = 0.0; A1[:, NS:] = -1.0
    n = np.arange(512)
    poolm = ((n[:, None] >= 4 * j[None, :] - 1) & (n[:, None] <= 4 * j[None, :] + 3)).astype(np.float32).astype(bf)
    kaug_tok = np.stack([t // 64, t % 64, np.ones(T), np.ones(T)]).astype(np.float32).astype(bf)
    kaugc = np.stack([n // 4, 16 * (n % 4) + 15.5, np.ones(512), np.ones(512)]).astype(np.float32).astype(bf)[:, :T // 16]
    return dict(M1=M1, A1=A1, poolm=poolm[:T // 16], kaug_tok=kaug_tok, kaugc=kaugc)

def q_aug(T, h):
    t = np.arange(T)
    c = (2.0 ** (-(h + 1))) * 8.0
    return np.stack([np.full(T, 64 * c), np.full(T, c), -64 * c * (t // 64), -c * (t % 64)]).astype(np.float32).astype(bf)

def nsa_inputs(T, g, qT, kcT, vcT, ksT, kwT, vs, vw, graw, w, consts, horder=(0, 1, 2, 3)):
    d = {}
    QA = np.zeros((4, 68, T), dtype=bf)
    for r in range(4):
        h = 4 * g + horder[r]
        QA[r, :64] = qT[h * 64:(h + 1) * 64]
        QA[r, 64:] = q_aug(T, h)
    d["QA"] = QA
    for nm, src in (("KSA", ksT), ("KWA", kwT)):
        a = np.zeros((68, T), dtype=bf)
        a[:64] = src[g * 64:(g + 1) * 64]
        a[64:] = consts["kaug_tok"]
        d[nm] = a
    for nm, src in (("VSA", vs), ("VWA", vw)):
        a = np.ones((T, 65), dtype=bf)
        a[:, :64] = src[:, g * 64:(g + 1) * 64]
        d[nm] = a
    for nm, src in (("c2k", kcT), ("c2v", vcT)):
        a = np.zeros((128, T), dtype=bf)
        a[:64] = src[g * 64:(g + 1) * 64]
        a[64:, :T - 1] = src[g * 64:(g + 1) * 64, 1:]
        d[nm] = a
    for kv in ("k", "v"):
        w1 = np.asarray(w["w1_" + kv], dtype=np.float32)
        d["w1" + kv] = np.ascontiguousarray(w1.reshape(16, 2, 64, 128).transpose(1, 2, 0, 3).reshape(128, 16, 128))
        pe = np.asarray(w["pe_" + kv], dtype=np.float32)
        d["pe2" + kv] = np.ascontiguousarray(pe.reshape(16, 2, 64).transpose(1, 2, 0).reshape(128, 16))
        d["w2" + kv] = np.ascontiguousarray(np.asarray(w["w2_" + kv], dtype=np.float32))
    d["kaugc"] = consts["kaugc"]
    d["poolm"] = consts["poolm"]
    d["M1"] = consts["M1"]; d["A1"] = consts["A1"]
    h0 = 4 * g + horder[0]
    d["graw"] = np.ascontiguousarray(graw[:, h0 * 3:(h0 + 2) * 3])
    return d


T_SEQ = 8192
NTOK = 2048
NCORE = 8


def _launch(nc, in_maps):
    res = run_bass_kernel_spmd(nc, in_maps, core_ids=list(range(NCORE)))
    return res.results


def _mk(nc, d, name, shape, dt, out=False):
    d[name] = nc.dram_tensor(name, list(shape), dt, kind="ExternalOutput" if out else "ExternalInput").ap()
    return d[name]


def _build_dense(kind):
    nc = bass.Bass("TRN2", target_bir_lowering=False)
    d = {}
    NT = NTOK
    _mk(nc, d, "x", [NT, 1024], F32)
    if kind in ("B", "C"):
        _mk(nc, d, "oT", [1024, NT], BF16)
        _mk(nc, d, "wo", [1024, 1024], F32)
    nffn = {"A": 1, "B": 2, "C": 1}[kind]
    for i in range(nffn):
        _mk(nc, d, f"fg{i}", [1024], F32)
        _mk(nc, d, f"fwi{i}", [1024, 5632], F32)
        _mk(nc, d, f"fwo{i}", [2816, 1024], F32)
    if kind == "A":
        _mk(nc, d, "pg", [1024], F32); _mk(nc, d, "pw", [1024, 2328], F32)
        _mk(nc, d, "xo", [NT, 1024], F32, True)
        _mk(nc, d, "aT", [1024, NT], F32, True); _mk(nc, d, "qT", [512, NT], BF16, True)
        for n in ("kcT", "vcT", "ksT", "kwT"):
            _mk(nc, d, n, [128, NT], BF16, True)
        _mk(nc, d, "vs", [NT, 128], BF16, True); _mk(nc, d, "vw", [NT, 128], BF16, True)
        _mk(nc, d, "gg", [NT, 24], F32, True)
    elif kind == "B":
        _mk(nc, d, "pg", [1024], F32); _mk(nc, d, "pw", [1024, 3072], F32)
        _mk(nc, d, "xo", [NT, 1024], F32, True)
        _mk(nc, d, "cT", [1536, NT], F32, True); _mk(nc, d, "qT", [512, NT], BF16, True)
        _mk(nc, d, "kT", [512, NT], BF16, True); _mk(nc, d, "v", [NT, 512], BF16, True)
    else:
        _mk(nc, d, "gfin", [1024], F32)
        _mk(nc, d, "out", [NT, 1024], F32, True)
    with ExitStack() as es:
        P = Prog(nc, es)
        dn = Dense(nc, P, NT)
        outs = []
        dn.load_x(d["x"])
        if kind in ("B", "C"):
            dn.outproj(d["oT"], d["wo"])
        for i in range(nffn):
            dn.ffn(d[f"fg{i}"], d[f"fwi{i}"], d[f"fwo{i}"])
        if kind == "A":
            names = [(0, 1024, "F", "aT"), (1024, 1536, "F", "qT"), (1536, 1664, "F", "kcT"), (1664, 1792, "F", "vcT"),
                     (1792, 1920, "F", "ksT"), (1920, 2048, "T", "vs"), (2048, 2176, "F", "kwT"), (2176, 2304, "T", "vw"),
                     (2304, 2328, "T", "gg")]
        elif kind == "B":
            names = [(0, 1536, "F", "cT"), (1536, 2048, "F", "qT"), (2048, 2560, "F", "kT"), (2560, 3072, "T", "v")]
        if kind in ("A", "B"):
            bx = P.buf("xo_out")
            dn.store_x(d["xo"], bx)
            outs.append(bx)
            specs = []
            for (c0, c1, lay, n) in names:
                b = P.buf("o_" + n)
                outs.append(b)
                specs.append((c0, c1, lay, d[n], b))
            dn.proj(d["pg"], d["pw"], specs)
        else:
            dn.alloc_final()
            bo = P.buf("out_out")
            dn.final(d["gfin"], d["out"], bo)
            outs.append(bo)
        P.finish("sp", outs)
        P.emit()
    return nc


def _build_conv0():
    nc = bass.Bass("TRN2", target_bir_lowering=False)
    d = {}
    _mk(nc, d, "aT", [2, 512, NTOK + 30], F32); _mk(nc, d, "dww", [128, 4, 31], F32)
    for n in ("dwb", "lng", "lnb"):
        _mk(nc, d, n, [128, 4], F32)
    _mk(nc, d, "coutT", [512, NTOK], BF16, True)
    with ExitStack() as es:
        P = Prog(nc, es)
        banks = alloc_banks(P)
        ident, b_id = make_ident(nc, P, "identc")
        outs = build_conv(nc, P, NTOK, d, banks, ident, b_id)
        P.finish("sp", outs)
        P.emit()
    return nc


def _build_nsa():
    nc = bass.Bass("TRN2", target_bir_lowering=False)
    d = {}
    T = T_SEQ
    _mk(nc, d, "QA", [4, 68, T], BF16); _mk(nc, d, "KSA", [68, T], BF16); _mk(nc, d, "KWA", [68, T], BF16)
    _mk(nc, d, "VSA", [T, 65], BF16); _mk(nc, d, "VWA", [T, 65], BF16)
    _mk(nc, d, "c2k", [128, T], BF16); _mk(nc, d, "c2v", [128, T], BF16)
    for kv in "kv":
        _mk(nc, d, "w1" + kv, [128, 16, 128], F32); _mk(nc, d, "pe2" + kv, [128, 16], F32); _mk(nc, d, "w2" + kv, [128, 64], F32)
    _mk(nc, d, "kaugc", [4, T // 16], BF16); _mk(nc, d, "poolm", [T // 16, 128], BF16)
    _mk(nc, d, "M1", [T, 128], F32); _mk(nc, d, "A1", [T, 128], F32); _mk(nc, d, "graw", [T, 6], F32)
    _mk(nc, d, "o_nsa", [T, 128], BF16, True)
    with ExitStack() as es:
        P = Prog(nc, es)
        outs = build_nsa(nc, P, T, list(range(T // 512)), d, alloc_banks(P), NOWN=2)
        P.finish("sp", outs)
        P.emit()
    return nc


def _build_m1():
    nc = bass.Bass("TRN2", target_bir_lowering=False)
    d = {}
    T = T_SEQ
    _mk(nc, d, "qT", [2, 64, T], BF16); _mk(nc, d, "kT", [2, 64, T], BF16); _mk(nc, d, "v", [T, 2, 64], BF16)
    _mk(nc, d, "convin", [3, 512, NTOK + 2], F32); _mk(nc, d, "scw", [128, 12], F32)
    _mk(nc, d, "o_sbT", [2, 64, T], BF16, True); _mk(nc, d, "coutT", [512, NTOK], BF16, True)
    with ExitStack() as es:
        P = Prog(nc, es)
        outs = build_mixer1(nc, P, T, NTOK, d)
        P.finish("sp", outs)
        P.emit()
    return nc


def _cat_tok(res, name, axis):
    return [np.concatenate([np.asarray(res[b * 4 + j][name]) for j in range(4)], axis=axis) for b in range(2)]


def kernel(x, ffn1_norm, ffn1_w_in, ffn1_w_out, mix_norm, ffn2_norm, ffn2_w_in, ffn2_w_out,
           ab_w_in, conv_dw_w, conv_dw_b, conv_ln_g, conv_ln_b,
           nsa_pe_k, nsa_w1_k, nsa_w2_k, nsa_pe_v, nsa_w1_v, nsa_w2_v, ab_w_out,
           cd_w_in, sc_conv_w, cd_w_out, final_norm):
    f32 = lambda a: np.ascontiguousarray(np.asarray(a, dtype=np.float32))
    x = f32(x)
    T = T_SEQ
    xs = [np.ascontiguousarray(x[c // 4, (c % 4) * NTOK:(c % 4 + 1) * NTOK]) for c in range(NCORE)]
    common = {"fg0": f32(ffn1_norm[0]), "fwi0": f32(ffn1_w_in[0]), "fwo0": f32(ffn1_w_out[0]), "pg": f32(mix_norm[0]), "pw": f32(ab_w_in[0])}
    rA = _launch(_build_dense("A"), [dict(common, x=xs[c]) for c in range(NCORE)])
    aT = _cat_tok(rA, "aT", 1); qT = _cat_tok(rA, "qT", 1)
    kcT = _cat_tok(rA, "kcT", 1); vcT = _cat_tok(rA, "vcT", 1); ksT = _cat_tok(rA, "ksT", 1); kwT = _cat_tok(rA, "kwT", 1)
    vs = _cat_tok(rA, "vs", 0); vw = _cat_tok(rA, "vw", 0); gg = _cat_tok(rA, "gg", 0)
    lay4 = lambda v: np.ascontiguousarray(f32(v).reshape(4, 128).T)
    cc = {"dww": np.ascontiguousarray(f32(conv_dw_w[0]).reshape(31, 4, 128).transpose(2, 1, 0)),
          "dwb": lay4(conv_dw_b[0]), "lng": lay4(conv_ln_g[0]), "lnb": lay4(conv_ln_b[0])}
    maps = []
    for c in range(NCORE):
        b, j = c // 4, c % 4
        a = np.zeros((2, 512, NTOK + 30), dtype=np.float32)
        lo = j * NTOK - 30
        src = aT[b].reshape(2, 512, T)
        if lo < 0:
            a[:, :, 30:] = src[:, :, 0:NTOK]
        else:
            a[:] = src[:, :, lo:lo + NTOK + 30]
        maps.append(dict(cc, aT=a))
    rC0 = _launch(_build_conv0(), maps)
    consts = nsa_consts(T)
    w = dict(pe_k=nsa_pe_k[0], w1_k=nsa_w1_k[0], w2_k=nsa_w2_k[0], pe_v=nsa_pe_v[0], w1_v=nsa_w1_v[0], w2_v=nsa_w2_v[0])
    maps = []
    for c in range(NCORE):
        b, g, hh = c // 4, (c % 4) // 2, c % 2
        horder = [2 * hh, 2 * hh + 1, 2 * (1 - hh), 2 * (1 - hh) + 1]
        dd = nsa_inputs(T, g, qT[b], kcT[b], vcT[b], ksT[b], kwT[b], vs[b], vw[b], gg[b], w, consts, horder)
        maps.append(dd)
    rN = _launch(_build_nsa(), maps)
    oT = []
    for c in range(NCORE):
        b, j = c // 4, c % 4
        o = np.zeros((1024, NTOK), dtype=bf)
        o[0:512] = np.asarray(rC0[c]["coutT"])
        for g in range(2):
            for hh in range(2):
                src = np.asarray(rN[b * 4 + g * 2 + hh]["o_nsa"])[j * NTOK:(j + 1) * NTOK]
                r0 = 512 + (4 * g + 2 * hh) * 64
                o[r0:r0 + 128] = src.T
        oT.append(o)
    common = {"wo": f32(ab_w_out[0]), "fg0": f32(ffn2_norm[0]), "fwi0": f32(ffn2_w_in[0]), "fwo0": f32(ffn2_w_out[0]),
              "fg1": f32(ffn1_norm[1]), "fwi1": f32(ffn1_w_in[1]), "fwo1": f32(ffn1_w_out[1]), "pg": f32(mix_norm[1]), "pw": f32(cd_w_in[0])}
    rB = _launch(_build_dense("B"), [dict(common, x=np.asarray(rA[c]["xo"]), oT=oT[c]) for c in range(NCORE)])
    cT = _cat_tok(rB, "cT", 1); q1 = _cat_tok(rB, "qT", 1); k1 = _cat_tok(rB, "kT", 1); v1 = _cat_tok(rB, "v", 0)
    scw = np.ascontiguousarray(f32(sc_conv_w[0]).reshape(3, 4, 128).transpose(2, 1, 0).reshape(128, 12))
    maps = []
    for c in range(NCORE):
        b, j = c // 4, c % 4
        ci = np.zeros((3, 512, NTOK + 2), dtype=np.float32)
        src = cT[b].reshape(3, 512, T)
        lo = j * NTOK - 2
        if lo < 0:
            ci[:, :, 2:] = src[:, :, 0:NTOK]
        else:
            ci[:] = src[:, :, lo:lo + NTOK + 2]
        hp = j
        maps.append({"qT": np.ascontiguousarray(q1[b][hp * 128:(hp + 1) * 128].reshape(2, 64, T)),
                     "kT": np.ascontiguousarray(k1[b][hp * 128:(hp + 1) * 128].reshape(2, 64, T)),
                     "v": np.ascontiguousarray(v1[b][:, hp * 128:(hp + 1) * 128].reshape(T, 2, 64)),
                     "convin": ci, "scw": scw})
    rM1 = _launch(_build_m1(), maps)
    oT = []
    for c in range(NCORE):
        b, j = c // 4, c % 4
        o = np.zeros((1024, NTOK), dtype=bf)
        o[0:512] = np.asarray(rM1[c]["coutT"])
        for hp in range(4):
            src = np.asarray(rM1[b * 4 + hp]["o_sbT"]).reshape(128, T)[:, j * NTOK:(j + 1) * NTOK]
            o[512 + hp * 128:512 + (hp + 1) * 128] = src
        oT.append(o)
    common = {"wo": f32(cd_w_out[0]), "fg0": f32(ffn2_norm[1]), "fwi0": f32(ffn2_w_in[1]), "fwo0": f32(ffn2_w_out[1]), "gfin": f32(final_norm)}
    rC = _launch(_build_dense("C"), [dict(common, x=np.asarray(rB[c]["xo"]), oT=oT[c]) for c in range(NCORE)])
    out = np.zeros((2, T, 1024), dtype=np.float32)
    for c in range(NCORE):
        out[c // 4, (c % 4) * NTOK:(c % 4 + 1) * NTOK] = np.asarray(rC[c]["out"])
    return out
```

```python
import numpy as np
import math
from contextlib import ExitStack
import concourse.bass as bass
import concourse.mybir as mybir
from concourse.bass_utils import run_bass_kernel_spmd
import ml_dtypes

F32 = mybir.dt.float32
BF16 = mybir.dt.bfloat16
AF = mybir.ActivationFunctionType
ALU = mybir.AluOpType
AX = mybir.AxisListType

SEM_EPOCH = 30000


class Buf:
    __slots__ = ("name", "w", "r", "dsem", "dcnt")

    def __init__(self, name):
        self.name = name
        self.w = []
        self.r = []
        self.dsem = None
        self.dcnt = 0


class Prog:
    def __init__(self, nc, es):
        self.nc = nc
        self.es = es
        self.eng = {"pe": nc.tensor, "act": nc.scalar, "dve": nc.vector, "pool": nc.gpsimd, "sp": nc.sync}
        self.sem = {}
        self.cnt = {}
        self.waited = {k: {} for k in self.eng}
        self.nsem = 0
        for k in self.eng:
            self._new_eng_sem(k)
        self.n_inst = 0
        self.n_wait = 0
        self.q = {k: [] for k in self.eng}

    def _new_sem(self, name):
        self.nsem += 1
        return self.es.enter_context(self.nc.semaphore(f"{name}_{self.nsem}"))

    def _new_eng_sem(self, k):
        self.sem[k] = self._new_sem("e" + k)
        self.cnt[k] = 0

    def buf(self, name):
        return Buf(name)

    def sb(self, name, shape, dtype):
        t = self.es.enter_context(self.nc.sbuf_tensor(name, list(shape), dtype))
        return t

    def ps(self, name, shape, dtype):
        t = self.es.enter_context(self.nc.psum_tensor(name, list(shape), dtype))
        return t

    def _wait(self, e, conds, skip_self=False):
        eng = self.eng[e]
        wd = self.waited[e]
        best = {}
        for (s, v, owner) in conds:
            if skip_self and owner == e:
                continue
            key = id(s)
            if wd.get(key, 0) >= v:
                continue
            if key not in best or best[key][1] < v:
                best[key] = (s, v)
        for key, (s, v) in best.items():
            self.q[e].append(("w", s, v))
            wd[key] = v
            self.n_wait += 1

    def op(self, e, fn, reads=(), writes=(), skip_self=False):
        conds = []
        for b in reads:
            conds += b.w
        for b in writes:
            conds += b.w
            conds += b.r
        self._wait(e, conds, skip_self=skip_self)
        if self.cnt[e] >= SEM_EPOCH:
            self._new_eng_sem(e)
        self.cnt[e] += 1
        self.q[e].append(("i", fn, self.sem[e], 1))
        c = (self.sem[e], self.cnt[e], e)
        for b in reads:
            b.r = [x for x in b.r if x[0] is not c[0]] + [c]
        for b in writes:
            b.w = [c]
            b.r = []
        self.n_inst += 1

    def dma(self, e, out, in_, reads=(), writes=(), **kw):
        conds = []
        for b in reads:
            conds += b.w
        for b in writes:
            conds += b.w
            conds += b.r
        self._wait(e, conds)
        tgt = writes[0] if writes else reads[0]
        if tgt.dsem is None:
            tgt.dsem = self._new_sem("d" + tgt.name)
        tgt.dcnt += 1
        eng = self.eng[e]
        self.q[e].append(("i", (lambda: eng.dma_start(out=out, in_=in_, **kw)), tgt.dsem, 16))
        c = (tgt.dsem, 16 * tgt.dcnt, "dma")
        for b in reads:
            b.r = [x for x in b.r if x[0] is not c[0]] + [c]
        for b in writes:
            b.w = [x for x in b.w if x[0] is not c[0]] + [c]
            b.r = []
        self.n_inst += 1

    def dma_fn(self, e, fn, reads=(), writes=()):
        conds = []
        for b in reads:
            conds += b.w
        for b in writes:
            conds += b.w
            conds += b.r
        self._wait(e, conds)
        tgt = writes[0] if writes else reads[0]
        if tgt.dsem is None:
            tgt.dsem = self._new_sem("d" + tgt.name)
        tgt.dcnt += 1
        self.q[e].append(("i", fn, tgt.dsem, 16))
        c = (tgt.dsem, 16 * tgt.dcnt, "dma")
        for b in reads:
            b.r = [x for x in b.r if x[0] is not c[0]] + [c]
        for b in writes:
            b.w = [x for x in b.w if x[0] is not c[0]] + [c]
            b.r = []
        self.n_inst += 1

    def cc(self, kind, in_ap, out_ap, groups, reads=(), writes=()):
        nc = self.nc
        fn = lambda: nc.gpsimd.collective_compute(kind, mybir.AluOpType.bypass, replica_groups=groups, ins=[in_ap], outs=[out_ap])
        self.dma_fn("pool", fn, reads=reads, writes=writes)

    def finish(self, e, bufs):
        conds = []
        for b in bufs:
            conds += b.w
        self._wait(e, conds)

    def emit(self):
        nc = self.nc
        with nc.Block() as block:
            def run(e):
                eng = self.eng[e]
                for it in self.q[e]:
                    if it[0] == "w":
                        eng.wait_ge(it[1], it[2])
                    else:
                        it[1]().then_inc(it[2], it[3])

            @block.tensor
            def _(x):
                run("pe")

            @block.scalar
            def _(x):
                run("act")

            @block.vector
            def _(x):
                run("dve")

            @block.gpsimd
            def _(x):
                run("pool")

            @block.sync
            def _(x):
                run("sp")


D = 1024
DFF = 2816
NFC = DFF // 128


def make_ident(nc, P, name="ident"):
    ident = P.sb(name, [128, 128], BF16)
    b = P.buf(name)
    P.op("pool", lambda: nc.gpsimd.memset(ident[:], 0.0), writes=[b])
    P.op("pool", lambda: nc.gpsimd.affine_select(out=ident[:], in_=ident[:], pattern=[[-1, 128]],
                                                   compare_op=ALU.not_equal, fill=1.0, base=0,
                                                   channel_multiplier=1), reads=[b], writes=[b])
    return ident, b


class Dense:
    def __init__(self, nc, P, NT):
        self.nc, self.P, self.NT = nc, P, NT
        self.NTILE = NT // 128
        self.NST = NT // 512
        nt = self.NTILE
        self.x = P.sb("x_res", [128, nt, D], F32)
        self.b_x = [P.buf(f"x{t}") for t in range(nt)]
        self.xnT = P.sb("xnT", [128, 8, NT], BF16)
        self.b_xnT = [P.buf(f"xnT{t}") for t in range(nt)]
        self.ident, self.b_id = make_ident(nc, P)
        self.sq = P.sb("sq", [128, D], F32); self.b_sq = P.buf("sq")
        self.ss = P.sb("ss", [128, nt], F32); self.b_ss = [P.buf(f"ss{g}") for g in range(nt // 4)]
        self.rstd = P.sb("rstd", [128, nt], F32); self.b_rstd = [P.buf(f"rstd{g}") for g in range(nt // 4)]
        self.sq2 = P.sb("sq2", [128, D], F32); self.b_sq2 = P.buf("sq2")
        self.xs = [P.sb(f"xs{i}", [128, D], BF16) for i in range(2)]
        self.b_xs = [P.buf(f"xs{i}") for i in range(2)]
        self.gt = P.sb("gt", [128, 8], F32); self.b_gt = P.buf("gt")
        self.NWB = 6
        self.wb = [P.sb(f"wb{i}", [128, 8 * 512], BF16) for i in range(self.NWB)]
        self.b_wb = [P.buf(f"wb{i}") for i in range(self.NWB)]
        self.wi = 0
        self.tp = [P.ps(f"tp{i}", [128, 8, 128], BF16) for i in range(2)]; self.b_tp = [P.buf(f"tp{i}") for i in range(2)]
        self.pg = [P.ps(f"pg{i}", [128, 512], F32) for i in range(2)]; self.b_pg = [P.buf(f"pg{i}") for i in range(2)]
        self.pu = [P.ps(f"pu{i}", [128, 512], F32) for i in range(2)]; self.b_pu = [P.buf(f"pu{i}") for i in range(2)]
        self.py = [P.ps(f"py{i}", [128, 512], F32) for i in range(2)]; self.b_py = [P.buf(f"py{i}") for i in range(2)]
        self.ipg = 0
        self.ipy = 0
        self.sg = [P.sb(f"sg{i}", [128, 512], F32) for i in range(2)]; self.b_sg = [P.buf(f"sg{i}") for i in range(2)]
        self.act = [P.sb(f"actT{i}", [128, 4, 512], BF16) for i in range(2)]
        self.b_act = [P.buf(f"actT{i}") for i in range(2)]
        self.iact = 0
        self.stg = [P.sb(f"stg{i}", [128, 512], F32) for i in range(3)]
        self.b_stg = [P.buf(f"stg{i}") for i in range(3)]
        self.istg = 0
        self.gfull = None

    def next_wb(self):
        i = self.wi % self.NWB
        self.wi += 1
        return self.wb[i], self.b_wb[i]

    def load_w(self, src_ap, rc, cols):
        wb, b = self.next_wb()
        view = wb[:, 0:rc * cols].rearrange("p (c n) -> p c n", c=rc)
        self.P.dma("pool", view, src_ap.rearrange("(c p) n -> p c n", p=128), writes=[b])
        return view, b

    def load_x(self, x_dram):
        for t in range(self.NTILE):
            self.P.dma("sp", self.x[:, t, :], x_dram[t * 128:(t + 1) * 128, :], writes=[self.b_x[t]])

    def store_x(self, out_dram, b_out):
        for t in range(self.NTILE):
            self.P.dma("sp", out_dram[t * 128:(t + 1) * 128, :], self.x[:, t, :], reads=[self.b_x[t]], writes=[b_out])

    def stats_group(self, g):
        nc, P = self.nc, self.P
        for t in range(4 * g, 4 * g + 4):
            sq, b_sq = (self.sq, self.b_sq) if t % 2 == 0 else (self.sq2, self.b_sq2)
            P.op("act", lambda t=t, sq=sq: nc.scalar.activation(out=sq[:], in_=self.x[:, t, :], func=AF.Square,
                                                                 accum_out=self.ss[:, t:t + 1]),
                 reads=[self.b_x[t]], writes=[b_sq, self.b_ss[g]])
        P.op("act", lambda g=g: nc.scalar.activation(out=self.rstd[:, 4 * g:4 * g + 4], in_=self.ss[:, 4 * g:4 * g + 4], func=AF.Sqrt,
                                                     scale=1.0 / D, bias=1e-6), reads=[self.b_ss[g]], writes=[self.b_rstd[g]])
        P.op("dve", lambda g=g: nc.vector.reciprocal(out=self.rstd[:, 4 * g:4 * g + 4], in_=self.rstd[:, 4 * g:4 * g + 4]),
             reads=[self.b_rstd[g]], writes=[self.b_rstd[g]])

    def stats(self):
        for g in range(self.NTILE // 4):
            self.stats_group(g)

    def norm_T(self, g_dram):
        nc, P = self.nc, self.P
        P.dma("sp", self.gt[:], g_dram.rearrange("(c p) -> p c", p=128), writes=[self.b_gt], allow_slow_non_contiguous=True)
        self.stats_group(0)
        for t in range(self.NTILE):
            xs, b_xs = self.xs[t % 2], self.b_xs[t % 2]
            tp, b_tp = self.tp[t % 2], self.b_tp[t % 2]
            g = t // 4
            if t % 4 == 0 and g + 1 < self.NTILE // 4:
                self.stats_group(g + 1)
            if t % 2 == 0:
                P.op("act", lambda t=t, xs=xs: nc.scalar.activation(out=xs[:], in_=self.x[:, t, :], func=AF.Identity, scale=self.rstd[:, t:t + 1]),
                     reads=[self.b_x[t], self.b_rstd[g]], writes=[b_xs])
            else:
                P.op("dve", lambda t=t, xs=xs: nc.vector.tensor_scalar(out=xs[:], in0=self.x[:, t, :], scalar1=self.rstd[:, t:t + 1],
                                                                        scalar2=None, op0=ALU.mult),
                     reads=[self.b_x[t], self.b_rstd[g]], writes=[b_xs])
            for c in range(8):
                P.op("pe", lambda c=c, xs=xs, tp=tp: nc.tensor.transpose(out=tp[:, c, :], in_=xs[:, c * 128:(c + 1) * 128],
                                                                          identity=self.ident[:]),
                     reads=[b_xs, self.b_id], writes=[b_tp], skip_self=True)
            P.op("dve", lambda t=t, tp=tp: nc.vector.tensor_tensor(out=self.xnT[:, :, t * 128:(t + 1) * 128], in0=tp[:],
                                                                    in1=self.gt[:].unsqueeze(2).to_broadcast([128, 8, 128]), op=ALU.mult),
                 reads=[b_tp, self.b_gt], writes=[self.b_xnT[t]])

    def ffn(self, g_dram, w_in, w_out):
        nc, P = self.nc, self.P
        self.norm_T(g_dram)
        groups = [(s, min(4, NFC - s)) for s in range(0, NFC, 4)]
        for (fc0, nfc) in groups:
            ncol = nfc * 128
            wg, b_wg = self.load_w(w_in[:, fc0 * 128: fc0 * 128 + ncol], 8, ncol)
            wu, b_wu = self.load_w(w_in[:, DFF + fc0 * 128: DFF + fc0 * 128 + ncol], 8, ncol)
            wo, b_wo = self.load_w(w_out[fc0 * 128: fc0 * 128 + ncol, :], nfc, D)
            for st in range(self.NST):
                tiles = list(range(st * 4, st * 4 + 4))
                xb = [self.b_xnT[t] for t in tiles]
                act, b_act = self.act[self.iact % 2], self.b_act[self.iact % 2]
                self.iact += 1
                for j in range(nfc):
                    i = self.ipg % 2
                    self.ipg += 1
                    pg, b_pg, pu, b_pu = self.pg[i], self.b_pg[i], self.pu[i], self.b_pu[i]
                    sg, b_sg = self.sg[i], self.b_sg[i]
                    for k in range(8):
                        P.op("pe", lambda k=k, j=j, pg=pg, wg=wg, st=st: nc.tensor.matmul(
                            pg[:], lhsT=wg[:, k, j * 128:(j + 1) * 128], rhs=self.xnT[:, k, st * 512:(st + 1) * 512],
                            start=(k == 0), stop=(k == 7)), reads=xb + [b_wg], writes=[b_pg], skip_self=True)
                    for k in range(8):
                        P.op("pe", lambda k=k, j=j, pu=pu, wu=wu, st=st: nc.tensor.matmul(
                            pu[:], lhsT=wu[:, k, j * 128:(j + 1) * 128], rhs=self.xnT[:, k, st * 512:(st + 1) * 512],
                            start=(k == 0), stop=(k == 7)), reads=xb + [b_wu], writes=[b_pu], skip_self=True)
                    P.op("act", lambda pg=pg, sg=sg: nc.scalar.activation(out=sg[:], in_=pg[:], func=AF.Silu),
                         reads=[b_pg], writes=[b_sg])
                    P.op("dve", lambda j=j, pu=pu, sg=sg, act=act: nc.vector.tensor_tensor(out=act[:, j, :], in0=pu[:], in1=sg[:], op=ALU.mult),
                         reads=[b_pu, b_sg], writes=[b_act])
                for sub in range(4):
                    t = st * 4 + sub
                    for dh in range(2):
                        i = self.ipy % 2
                        self.ipy += 1
                        py, b_py = self.py[i], self.b_py[i]
                        for j in range(nfc):
                            P.op("pe", lambda j=j, py=py, act=act, wo=wo, sub=sub, dh=dh, nfc=nfc: nc.tensor.matmul(
                                py[:], lhsT=act[:, j, sub * 128:(sub + 1) * 128], rhs=wo[:, j, dh * 512:(dh + 1) * 512],
                                start=(j == 0), stop=(j == nfc - 1)), reads=[b_act, b_wo], writes=[b_py], skip_self=True)
                        P.op("dve", lambda t=t, dh=dh, py=py: nc.vector.scalar_tensor_tensor(
                            out=self.x[:, t, dh * 512:(dh + 1) * 512], in0=py[:], scalar=0.5, in1=self.x[:, t, dh * 512:(dh + 1) * 512],
                            op0=ALU.mult, op1=ALU.add), reads=[b_py, self.b_x[t]], writes=[self.b_x[t]])

    def outproj(self, oT_dram, w_dram):
        nc, P = self.nc, self.P
        for t in range(self.NTILE):
            P.dma("sp", self.xnT[:, :, t * 128:(t + 1) * 128],
                  oT_dram[:, t * 128:(t + 1) * 128].rearrange("(c p) n -> p c n", p=128), writes=[self.b_xnT[t]])
        for dh in range(2):
            w, b_w = self.load_w(w_dram[:, dh * 512:(dh + 1) * 512], 8, 512)
            for t in range(self.NTILE):
                i = self.ipy % 2
                self.ipy += 1
                py, b_py = self.py[i], self.b_py[i]
                for k in range(8):
                    P.op("pe", lambda k=k, py=py, w=w, t=t: nc.tensor.matmul(
                        py[:], lhsT=self.xnT[:, k, t * 128:(t + 1) * 128], rhs=w[:, k, :], start=(k == 0), stop=(k == 7)),
                        reads=[self.b_xnT[t], b_w], writes=[b_py], skip_self=True)
                P.op("dve", lambda t=t, dh=dh, py=py: nc.vector.tensor_tensor(
                    out=self.x[:, t, dh * 512:(dh + 1) * 512], in0=py[:], in1=self.x[:, t, dh * 512:(dh + 1) * 512], op=ALU.add),
                    reads=[b_py, self.b_x[t]], writes=[self.b_x[t]])

    def proj(self, g_dram, w_dram, outs):
        nc, P = self.nc, self.P
        self.norm_T(g_dram)
        for (c0, c1, layout, o_ap, b_o) in outs:
            for cs in range(c0, c1, 512):
                ce = min(cs + 512, c1)
                ncol = ce - cs
                w, b_w = self.load_w(w_dram[:, cs:ce], 8, ncol)
                if layout == "F":
                    assert ncol % 128 == 0
                    for j in range(ncol // 128):
                        for st in range(self.NST):
                            i = self.ipy % 2
                            self.ipy += 1
                            py, b_py = self.py[i], self.b_py[i]
                            xb = [self.b_xnT[t] for t in range(st * 4, st * 4 + 4)]
                            for k in range(8):
                                P.op("pe", lambda k=k, j=j, py=py, w=w, st=st: nc.tensor.matmul(
                                    py[:], lhsT=w[:, k, j * 128:(j + 1) * 128], rhs=self.xnT[:, k, st * 512:(st + 1) * 512],
                                    start=(k == 0), stop=(k == 7)), reads=xb + [b_w], writes=[b_py], skip_self=True)
                            si = self.istg % 3
                            self.istg += 1
                            stg, b_stg = self.stg[si], self.b_stg[si]
                            if o_ap.dtype == BF16:
                                sv = stg[:].bitcast(BF16)[:, 0:512]
                            else:
                                sv = stg[:]
                            eng = "act" if (self.istg % 2) else "dve"
                            if eng == "act":
                                P.op("act", lambda sv=sv, py=py: nc.scalar.copy(out=sv, in_=py[:]), reads=[b_py], writes=[b_stg])
                            else:
                                P.op("dve", lambda sv=sv, py=py: nc.vector.tensor_copy(out=sv, in_=py[:]), reads=[b_py], writes=[b_stg])
                            r0 = cs - c0 + j * 128
                            P.dma("sp", o_ap[r0:r0 + 128, st * 512:(st + 1) * 512], sv, reads=[b_stg], writes=[b_o])
                else:
                    for t in range(self.NTILE):
                        i = self.ipy % 2
                        self.ipy += 1
                        py, b_py = self.py[i], self.b_py[i]
                        for k in range(8):
                            P.op("pe", lambda k=k, py=py, w=w, t=t, ncol=ncol: nc.tensor.matmul(
                                py[:, 0:ncol], lhsT=self.xnT[:, k, t * 128:(t + 1) * 128], rhs=w[:, k, :],
                                start=(k == 0), stop=(k == 7)), reads=[self.b_xnT[t], b_w], writes=[b_py], skip_self=True)
                        si = self.istg % 3
                        self.istg += 1
                        stg, b_stg = self.stg[si], self.b_stg[si]
                        if o_ap.dtype == BF16:
                            sv = stg[:].bitcast(BF16)[:, 0:ncol]
                        else:
                            sv = stg[:, 0:ncol]
                        P.op("dve", lambda sv=sv, py=py, ncol=ncol: nc.vector.tensor_copy(out=sv, in_=py[:, 0:ncol]), reads=[b_py], writes=[b_stg])
                        P.dma("sp", o_ap[t * 128:(t + 1) * 128, cs - c0:ce - c0], sv, reads=[b_stg], writes=[b_o])

    def final(self, g_dram, out_dram, b_out):
        nc, P = self.nc, self.P
        gfull = P.sb("gfull", [128, D], F32)
        b_g = P.buf("gfull")
        P.dma("sp", gfull[:], g_dram.partition_broadcast(128), writes=[b_g])
        self.stats()
        for t in range(self.NTILE):
            si = t % 2
            o = self.fin[si]
            b_o = self.b_fin[si]
            P.op("dve", lambda t=t, o=o: nc.vector.scalar_tensor_tensor(out=o[:], in0=self.x[:, t, :], scalar=self.rstd[:, t:t + 1],
                                                                        in1=gfull[:], op0=ALU.mult, op1=ALU.mult),
                 reads=[self.b_x[t], self.b_rstd[t // 4], b_g], writes=[b_o])
            P.dma("sp", out_dram[t * 128:(t + 1) * 128, :], o[:], reads=[b_o], writes=[b_out])

    def alloc_final(self):
        P = self.P
        self.fin = [self.sq, self.sq]
        self.b_fin = [self.b_sq, self.b_sq]


def tri_consts(nc, P):
    triu = P.sb("triu", [128, 128], BF16); b_u = P.buf("triu")
    tril = P.sb("tril", [128, 128], BF16); b_l = P.buf("tril")
    P.op("pool", lambda: nc.gpsimd.memset(triu[:], 1.0), writes=[b_u])
    P.op("pool", lambda: nc.gpsimd.affine_select(out=triu[:], in_=triu[:], pattern=[[-1, 128]], compare_op=ALU.is_ge,
                                                   fill=0.0, base=0, channel_multiplier=1), reads=[b_u], writes=[b_u])
    P.op("pool", lambda: nc.gpsimd.memset(tril[:], 0.0), writes=[b_l])
    P.op("pool", lambda: nc.gpsimd.affine_select(out=tril[:], in_=tril[:], pattern=[[-1, 128]], compare_op=ALU.is_ge,
                                                   fill=1.0, base=0, channel_multiplier=1), reads=[b_l], writes=[b_l])
    return triu, b_u, tril, b_l


def causal_masks(nc, P, strict=True, dtype=BF16, name="cm"):
    m = P.sb(name, [128, 4, 512], dtype); b = P.buf(name)
    P.op("pool", lambda: nc.gpsimd.memset(m[:], 1.0), writes=[b])
    for o in range(4):
        P.op("pool", lambda o=o: nc.gpsimd.affine_select(out=m[:, o, :], in_=m[:, o, :], pattern=[[1, 512]],
                                                          compare_op=(ALU.is_gt if strict else ALU.is_ge), fill=0.0,
                                                          base=-128 * o, channel_multiplier=-1), reads=[b], writes=[b])
    return m, b


def build_mixer1(nc, P, T, NTC, d):
    scale = 64 ** -0.5
    NQB = T // 512
    NKT = T // 128
    qT = P.sb("qT_sb", [64, 2, T], BF16); b_q = P.buf("qT")
    kT = P.sb("kT_sb", [64, 2, T], BF16); b_k = P.buf("kT")
    v = P.sb("v_sb", [128, NKT, 2, 64], BF16); b_v = P.buf("v")
    for h in range(2):
        P.dma("sp", qT[:, h, :], d["qT"][h], writes=[b_q])
        P.dma("sp", kT[:, h, :], d["kT"][h], writes=[b_k])
    P.dma("sp", v[:], d["v"].rearrange("(n p) h e -> p n h e", p=128), writes=[b_v])
    triu, b_u, tril, b_l = tri_consts(nc, P)
    cm, b_cm = causal_masks(nc, P, strict=True)
    b_out = P.buf("o_sb_out")
    b_cout = P.buf("cout_out")

    N = NTC
    wT = P.sb("scw_sb", [128, 4, 3], F32); b_w = P.buf("scw")
    P.dma("sp", wT[:], d["scw"].rearrange("p (c k) -> p c k", c=4), writes=[b_w])
    cin = [P.sb(f"cin{i}", [128, 3, N + 2], F32) for i in range(2)]
    b_cin = [P.buf(f"cin{i}") for i in range(2)]
    vv = P.sb("cvv", [128, N + 2], F32); b_vv = P.buf("cvv")
    yy = P.sb("cyy", [128, N], F32); b_yy = P.buf("cyy")
    yo = [P.sb(f"cyo{i}", [128, N], BF16) for i in range(2)]
    b_yo = [P.buf(f"cyo{i}") for i in range(2)]
    for c in range(4):
        ci, b_ci = cin[c % 2], b_cin[c % 2]
        P.dma("sp", ci[:], d["convin"][:, c * 128:(c + 1) * 128, :].rearrange("k p n -> p k n"), writes=[b_ci])
        P.op("pool", lambda ci=ci: nc.gpsimd.tensor_tensor(out=vv[:], in0=ci[:, 1, :], in1=ci[:, 2, :], op=ALU.mult),
             reads=[b_ci], writes=[b_vv])
        P.op("dve", lambda c=c: nc.vector.tensor_scalar(out=yy[:], in0=vv[:, 0:N], scalar1=wT[:, c, 0:1], scalar2=None, op0=ALU.mult),
             reads=[b_vv, b_w], writes=[b_yy])
        for k in (1, 2):
            P.op("dve", lambda c=c, k=k: nc.vector.scalar_tensor_tensor(out=yy[:], in0=vv[:, k:N + k], scalar=wT[:, c, k:k + 1],
                                                                        in1=yy[:], op0=ALU.mult, op1=ALU.add),
                 reads=[b_vv, b_w, b_yy], writes=[b_yy])
        o, b_o = yo[c % 2], b_yo[c % 2]
        P.op("dve", lambda ci=ci, o=o: nc.vector.tensor_tensor(out=o[:], in0=yy[:], in1=ci[:, 0, 2:N + 2], op=ALU.mult),
             reads=[b_yy, b_ci], writes=[b_o])
        P.dma("sp", d["coutT"][c * 128:(c + 1) * 128, :], o[:], reads=[b_o], writes=[b_cout])

    S_ps = [P.ps(f"S{i}", [128, 2, 512], F32) for i in range(2)]
    b_S = [P.buf(f"S{i}") for i in range(2)]
    D_ps = [P.ps(f"D{h}", [128, 512], F32) for h in range(2)]
    b_D = [P.buf(f"D{h}") for h in range(2)]
    O_ps = [P.ps(f"O{h}", [64, 512], F32) for h in range(2)]
    b_O = [P.buf(f"O{h}") for h in range(2)]
    NE, NF, NA = 3, 4, 4
    e_sb = [P.sb(f"e{i}", [128, 2, 512], F32) for i in range(NE)]; b_e = [P.buf(f"e{i}") for i in range(NE)]
    sp_sb = [P.sb(f"sp{i}", [128, 2, 512], BF16) for i in range(NE)]; b_sp = [P.buf(f"sp{i}") for i in range(NE)]
    f_sb = [P.sb(f"f{i}", [128, 512], F32) for i in range(NF)]; b_f = [P.buf(f"f{i}") for i in range(NF)]
    a_sb = [P.sb(f"a{i}", [128, 512], BF16) for i in range(NA)]; b_a = [P.buf(f"a{i}") for i in range(NA)]
    oo = [P.sb(f"oo{h}", [64, 512], BF16) for h in range(2)]
    b_oo = [P.buf(f"oo{h}") for h in range(2)]
    zz = P.sb("zz", [128, 512], BF16); b_zz = P.buf("zz")
    P.op("pool", lambda: nc.gpsimd.memset(zz[:], 0.0), writes=[b_zz])
    items = []
    for qb in range(NQB):
        kmax = 4 * qb + 3
        for kb in range(kmax, -1, -1):
            for h in range(2):
                items.append(dict(qb=qb, kb=kb, h=h, diag=(kb >= 4 * qb), o=kb - 4 * qb, first=(kb == kmax), last=(kb == 0)))
    NI = len(items)
    NP = NI // 2

    def stA1(p):
        it = items[2 * p]; kb, qb = it["kb"], it["qb"]
        Sp, bS = S_ps[p % 2], b_S[p % 2]
        e, be = e_sb[p % NE], b_e[p % NE]
        for h in range(2):
            P.op("pe", lambda h=h: nc.tensor.matmul(Sp[:, h, :], lhsT=kT[:, h, kb * 128:(kb + 1) * 128], rhs=qT[:, h, qb * 512:(qb + 1) * 512],
                                                    start=True, stop=True), reads=[b_k, b_q], writes=[bS], skip_self=True)
        P.op("act", lambda: nc.scalar.activation(out=e[:], in_=Sp[:], func=AF.Exp, scale=scale), reads=[bS], writes=[be])

    def stA2(p):
        it = items[2 * p]; o = it["o"]
        e, be = e_sb[p % NE], b_e[p % NE]
        sp, bsp = sp_sb[p % NE], b_sp[p % NE]
        P.op("act", lambda: nc.scalar.activation(out=sp[:], in_=e[:], func=AF.Ln, bias=1.0), reads=[be], writes=[bsp])
        if it["diag"]:
            mb = cm[:, o, :].unsqueeze(1).to_broadcast([128, 2, 512])
            P.op("pool", lambda: nc.gpsimd.tensor_tensor(out=sp[:], in0=sp[:], in1=mb, op=ALU.mult), reads=[bsp, b_cm], writes=[bsp])
            P.op("pool", lambda: nc.gpsimd.tensor_tensor(out=e[:], in0=e[:], in1=mb, op=ALU.mult), reads=[be, b_cm], writes=[be])

    def stB1(i):
        it = items[i]; h = it["h"]
        sp, bsp = sp_sb[(i // 2) % NE], b_sp[(i // 2) % NE]
        P.op("pe", lambda: nc.tensor.matmul(D_ps[h][:], lhsT=triu[:], rhs=sp[:, h, :], start=it["first"], stop=True, skip_group_check=True),
             reads=[bsp, b_u], writes=[b_D[h]], skip_self=True)

    def stB2(i):
        it = items[i]; h = it["h"]
        f, bf_ = f_sb[i % NF], b_f[i % NF]
        P.op("act", lambda: nc.scalar.activation(out=f[:], in_=D_ps[h][:], func=AF.Exp, scale=-1.0), reads=[b_D[h]], writes=[bf_])

    def stC(i):
        it = items[i]; h = it["h"]
        sp, bsp = sp_sb[(i // 2) % NE], b_sp[(i // 2) % NE]
        e, be = e_sb[(i // 2) % NE], b_e[(i // 2) % NE]
        f, bf_ = f_sb[i % NF], b_f[i % NF]
        a, ba = a_sb[i % NA], b_a[i % NA]
        if not it["last"]:
            P.op("pe", lambda: nc.tensor.matmul(D_ps[h][:], lhsT=tril[:], rhs=sp[:, h, :], start=False, stop=True, skip_group_check=True),
                 reads=[bsp, b_l], writes=[b_D[h]], skip_self=True)
        P.op("dve", lambda: nc.vector.tensor_tensor(out=a[:], in0=e[:, h, :], in1=f[:], op=ALU.mult), reads=[be, bf_], writes=[ba])

    def stD(i):
        it = items[i]; h, kb, qb = it["h"], it["kb"], it["qb"]
        a, ba = a_sb[i % NA], b_a[i % NA]
        if it["first"]:
            P.op("pe", lambda: nc.tensor.matmul(O_ps[h][:], lhsT=zz[:, 0:64], rhs=zz[:], start=True, stop=True),
                 reads=[b_zz], writes=[b_O[h]], skip_self=True)
        P.op("pe", lambda: nc.tensor.matmul(O_ps[h][:], lhsT=v[:, kb, h, :], rhs=a[:], start=False, stop=True, skip_group_check=True),
             reads=[ba, b_v], writes=[b_O[h]], skip_self=True)
        if it["last"]:
            P.op("dve", lambda: nc.vector.tensor_copy(out=oo[h][:], in_=O_ps[h][:]), reads=[b_O[h]], writes=[b_oo[h]])
            P.dma("sp", d["o_sbT"][h, :, qb * 512:(qb + 1) * 512], oo[h][:], reads=[b_oo[h]], writes=[b_out])

    for s_ in range(-4, NI + 1):
        if s_ % 2 == 0 and 0 <= (s_ + 4) // 2 < NP:
            stA1((s_ + 4) // 2)
        if 0 <= s_ + 1 < NI:
            stB1(s_ + 1)
            stB2(s_ + 1)
        if s_ % 2 == 1 and 0 <= (s_ + 3) // 2 < NP:
            stA2((s_ + 3) // 2)
        if 0 <= s_ < NI:
            stC(s_)
        if 0 <= s_ - 1 < NI:
            stD(s_ - 1)
    return [b_out, b_cout]


def build_nsa(nc, P, T, qbs, d, banks, NOWN=4):
    scale = 64 ** -0.5
    NCP = T // 16
    NC = NCP - 1
    NCT = NCP // 128
    NKT = T // 128
    ident, b_id = make_ident(nc, P, "ident0")
    b_out = P.buf("nsa_out")

    QAq = [P.sb(f"QAq{i}", [68, 4, 512], BF16) for i in range(2)]; b_QAq = [P.buf(f"QAq{i}") for i in range(2)]
    cur = {}
    KSA = P.sb("KSA_sb", [68, T], BF16); b_KSA = P.buf("KSA")
    KWA = P.sb("KWA_sb", [68, T], BF16); b_KWA = P.buf("KWA")
    P.dma("sp", KSA[:], d["KSA"], writes=[b_KSA])
    P.dma("sp", KWA[:], d["KWA"], writes=[b_KWA])
    VSA = P.sb("VSA_sb", [128, NKT, 65], BF16); b_VSA = P.buf("VSA")
    VWA = P.sb("VWA_sb", [128, NKT, 65], BF16); b_VWA = P.buf("VWA")
    P.dma("sp", VSA[:], d["VSA"].rearrange("(n p) e -> p n e", p=128), writes=[b_VSA])
    P.dma("sp", VWA[:], d["VWA"].rearrange("(n p) e -> p n e", p=128), writes=[b_VWA])
    Wsel = P.sb("Wsel", [128, T], BF16); b_Wsel = P.buf("Wsel")
    P.op("pool", lambda: nc.gpsimd.memset(Wsel[:], 1.0), writes=[b_Wsel])
    P.op("pool", lambda: nc.gpsimd.affine_select(out=Wsel[:], in_=Wsel[:], pattern=[[1, T]], compare_op=ALU.is_ge, fill=0.0,
                                                   base=0, channel_multiplier=-64), reads=[b_Wsel], writes=[b_Wsel])
    P.op("pool", lambda: nc.gpsimd.affine_select(out=Wsel[:], in_=Wsel[:], pattern=[[-1, T]], compare_op=ALU.is_ge, fill=0.0,
                                                   base=63, channel_multiplier=64), reads=[b_Wsel], writes=[b_Wsel])

    S_ps = [banks[i][0] for i in range(2)]; b_S = [banks[i][1] for i in range(2)]
    MK, b_MK = banks[2]
    A_ps = [banks[3 + i][0] for i in range(4)]; b_A = [banks[3 + i][1] for i in range(4)]
    X, b_X = banks[7]
    zz = P.sb("nzz", [128, 512], BF16); b_zz = P.buf("nzz")
    P.op("pool", lambda: nc.gpsimd.memset(zz[:], 0.0), writes=[b_zz])

    def zero_bank(i):
        P.op("pe", lambda i=i: nc.tensor.matmul(A_ps[i][:], lhsT=zz[:, 0:128], rhs=zz[:], start=True, stop=True),
             reads=[b_zz], writes=[b_A[i]], skip_self=True)

    KcA = P.sb("KcA", [68, NCP], BF16); b_KcA = P.buf("KcA")
    VcX = P.sb("VcX", [128, NCT, 193], BF16); b_VcX = P.buf("VcX")
    P.dma("sp", KcA[64:68, :], d["kaugc"], writes=[b_KcA])
    P.dma("sp", VcX[:, :, 65:193], d["poolm"].rearrange("(n p) j -> p n j", p=128), writes=[b_VcX])
    P.op("pool", lambda: nc.gpsimd.memset(VcX[:, :, 64:65], 1.0), reads=[], writes=[b_VcX])
    w1 = P.sb("w1_sb", [128, 16, 128], BF16); b_w1 = P.buf("w1")
    pe2 = P.sb("pe2_sb", [128, 16], BF16); b_pe2 = P.buf("pe2")
    w2 = P.sb("w2_sb", [128, 64], BF16); b_w2 = P.buf("w2")
    c2 = P.sb("c2_sb", [128, T], BF16); b_c2 = P.buf("c2")
    pb = P.sb("pb_sb", [128, 1], F32); b_pb = P.buf("pb")
    xh = P.sb("xh_sb", [128, NCP], F32); b_xh = P.buf("xh")
    uh = P.sb("uh_sb", [128, NCP], F32); b_uh = P.buf("uh")
    hT = P.sb("hT_sb", [128, NCP], BF16); b_hT = P.buf("hT")
    P.op("pool", lambda: nc.gpsimd.memset(hT[:], 0.0), writes=[b_hT])
    for kv in ("k", "v"):
        P.dma("pool", w1[:], d["w1" + kv], writes=[b_w1])
        P.dma("pool", pe2[:], d["pe2" + kv], writes=[b_pe2])
        P.dma("pool", w2[:], d["w2" + kv], writes=[b_w2])
        P.dma("sp", c2[:], d["c2" + kv], writes=[b_c2])
        c2v = c2[:].rearrange("p (n s) -> p n s", s=16)
        for c in range(16):
            if 2 * c < 16:
                rhs = c2v[:, 0:NC, 2 * c]
            else:
                rhs = c2v[:, 1:NC + 1, 2 * c - 16]
            P.op("pe", lambda c=c, rhs=rhs: nc.tensor.matmul(X[:, 0:NC], lhsT=w1[:, c, :], rhs=rhs, start=(c == 0), stop=(c == 15)),
                 reads=[b_w1, b_c2], writes=[b_X], skip_self=True)
        P.op("act", lambda: nc.scalar.copy(out=xh[:, 0:NC], in_=X[:, 0:NC]), reads=[b_X], writes=[b_xh])
        for c in range(16):
            P.op("pe", lambda c=c: nc.tensor.matmul(X[:, 0:1], lhsT=w1[:, c, :], rhs=pe2[:, c:c + 1], start=(c == 0), stop=(c == 15)),
                 reads=[b_w1, b_pe2], writes=[b_X], skip_self=True)
        P.op("dve", lambda: nc.vector.tensor_copy(out=pb[:], in_=X[:, 0:1]), reads=[b_X], writes=[b_pb])
        P.op("dve", lambda: nc.vector.tensor_scalar(out=xh[:, 0:NC], in0=xh[:, 0:NC], scalar1=pb[:, 0:1], scalar2=None, op0=ALU.add),
             reads=[b_xh, b_pb], writes=[b_xh])
        P.op("dve", lambda: nc.vector.tensor_tensor(out=uh[:, 0:NC], in0=xh[:, 0:NC], in1=xh[:, 0:NC], op=ALU.mult),
             reads=[b_xh], writes=[b_uh])
        P.op("dve", lambda: nc.vector.tensor_scalar(out=uh[:, 0:NC], in0=uh[:, 0:NC], scalar1=0.044715, scalar2=1.0, op0=ALU.mult, op1=ALU.add),
             reads=[b_uh], writes=[b_uh])
        P.op("dve", lambda: nc.vector.tensor_tensor(out=uh[:, 0:NC], in0=uh[:, 0:NC], in1=xh[:, 0:NC], op=ALU.mult),
             reads=[b_uh, b_xh], writes=[b_uh])
        P.op("act", lambda: nc.scalar.activation(out=uh[:, 0:NC], in_=uh[:, 0:NC], func=AF.Sigmoid, scale=2.0 * math.sqrt(2.0 / math.pi)),
             reads=[b_uh], writes=[b_uh])
        P.op("dve", lambda: nc.vector.tensor_tensor(out=hT[:, 0:NC], in0=uh[:, 0:NC], in1=xh[:, 0:NC], op=ALU.mult),
             reads=[b_uh, b_xh], writes=[b_hT])
        if kv == "k":
            P.op("pe", lambda: nc.tensor.matmul(X[0:64, 0:NCP], lhsT=w2[:], rhs=hT[:], start=True, stop=True),
                 reads=[b_w2, b_hT], writes=[b_X], skip_self=True)
            P.op("dve", lambda: nc.vector.tensor_copy(out=KcA[0:64, :], in_=X[0:64, 0:NCP]), reads=[b_X], writes=[b_KcA])
        else:
            for n in range(NCT):
                P.op("pe", lambda n=n: nc.tensor.matmul(X[:, n * 64:(n + 1) * 64], lhsT=hT[:, n * 128:(n + 1) * 128], rhs=w2[:],
                                                        start=True, stop=True), reads=[b_w2, b_hT], writes=[b_X], skip_self=True)
            P.op("dve", lambda: nc.vector.tensor_copy(out=VcX[:, :, 0:64], in_=X[:, 0:NCT * 64].rearrange("p (n e) -> p n e", e=64)),
                 reads=[b_X], writes=[b_VcX])

    NE = 5
    e_sb = [P.sb(f"ne{i}", [128, 512], F32) for i in range(NE)]; b_e = [P.buf(f"ne{i}") for i in range(NE)]
    p_sb = [P.sb(f"np{i}", [128, 512], BF16) for i in range(NE)]; b_p = [P.buf(f"np{i}") for i in range(NE)]
    NMK = 3
    mk_sb = [P.sb(f"nmk{i}", [128, 512], BF16) for i in range(NMK)]; b_mk = [P.buf(f"nmk{i}") for i in range(NMK)]
    accC = [P.sb(f"accC{r}", [128, 4, 193], F32) for r in range(4)]; b_accC = [P.buf(f"accC{r}") for r in range(4)]
    accS = [P.sb(f"accS{r}", [128, 4, 65], F32) for r in range(NOWN)]; b_accS = [P.buf(f"accS{r}") for r in range(NOWN)]
    accW = [P.sb(f"accW{r}", [128, 4, 65], F32) for r in range(NOWN)]; b_accW = [P.buf(f"accW{r}") for r in range(NOWN)]
    rden = P.sb("rden", [128, 3, 4, 4], F32)
    b_rdenC = [P.buf(f"rdenC{r}") for r in range(4)]
    b_rdenS = [P.buf(f"rdenS{r}") for r in range(4)]
    b_rdenW = [P.buf(f"rdenW{r}") for r in range(4)]
    imp = P.sb("imp", [128, 4, 128], F32); b_imp = P.buf("imp")
    M1 = P.sb("M1_sb", [128, 4, 128], F32); b_M1 = P.buf("M1")
    A1 = P.sb("A1_sb", [128, 4, 128], F32); b_A1 = P.buf("A1")
    score = [P.sb(f"score{i}", [128, 128], F32) for i in range(2)]; b_score = [P.buf(f"score{i}") for i in range(2)]
    work = [P.sb(f"work{i}", [128, 128], F32) for i in range(2)]; b_work = [P.buf(f"work{i}") for i in range(2)]
    m8 = [P.sb(f"m8{i}", [128, 16], F32) for i in range(2)]; b_m8 = [P.buf(f"m8{i}") for i in range(2)]
    sel = P.sb("sel", [128, 4, 128], BF16); b_sel = P.buf("sel")
    selT = P.sb("selT", [128, 512], BF16); b_selT = P.buf("selT")
    graw2 = [P.sb(f"graw_sb{i}", [128, 4, 3 * NOWN], F32) for i in range(2)]; b_graw2 = [P.buf(f"graw{i}") for i in range(2)]
    gsig2 = [P.sb(f"gsig{i}", [128, 4, 3 * NOWN], F32) for i in range(2)]; b_gsig2 = [P.buf(f"gsig{i}") for i in range(2)]
    wgt = P.sb("wgt", [128, 3, 4, 4], F32); b_wgt = P.buf("wgt")
    oacc = P.sb("oacc", [128, 4, 64 * NOWN], F32); b_oacc = P.buf("oacc")
    obf = P.sb("obf", [128, 4, 64 * NOWN], BF16); b_obf = P.buf("obf")

    negm = P.sb("negm", [128, 4, 512], BF16); b_negm = P.buf("negm")
    P.op("pool", lambda: nc.gpsimd.memset(negm[:], 0.0), writes=[b_negm])
    for o_ in range(4):
        P.op("pool", lambda o_=o_: nc.gpsimd.affine_select(out=negm[:, o_, :], in_=negm[:, o_, :], pattern=[[1, 512]], compare_op=ALU.is_ge,
                                                            fill=-30000.0, base=-128 * o_, channel_multiplier=-1), reads=[b_negm], writes=[b_negm])

    items = []
    for qb in qbs:
        nct = min(NCT, (32 * (qb + 1) + 127) // 128)
        for r in range(4):
            for kt in range(nct):
                items.append(dict(kind="C", qb=qb, kt=kt, r=r, first=(kt == 0), last=(kt == nct - 1), qbstart=(r == 0 and kt == 0)))
        kw0 = max(0, 4 * qb - 4)
        for kt in range(kw0, 4 * qb + 4):
            for r in range(NOWN):
                items.append(dict(kind="W", qb=qb, kt=kt, r=r, first=(kt == kw0), last=(kt == 4 * qb + 3), qbstart=False))
        for kt in range(0, 4 * qb + 4):
            for r in range(NOWN):
                items.append(dict(kind="S", qb=qb, kt=kt, r=r, first=(kt == 0), last=(kt == 4 * qb + 3), qbstart=False))
    NI = len(items)

    def banks_of(it):
        r = it["r"]
        if it["kind"] == "C":
            return [2 * (r % 2), 2 * (r % 2) + 1]
        if it["kind"] == "W":
            return [r]
        return [2 + r] if NOWN == 2 else [r]

    def load_qb(qb_):
        q0_ = qb_ * 512
        qi_ = qb_ % 2
        for rr in range(4):
            P.dma("sp", QAq[qi_][:, rr, :], d["QA"][rr][:, q0_:q0_ + 512], writes=[b_QAq[qi_]])
        P.dma("sp", M1[:], d["M1"][q0_:q0_ + 512, :].rearrange("(c p) j -> p c j", p=128), writes=[b_M1])
        P.dma("sp", A1[:], d["A1"][q0_:q0_ + 512, :].rearrange("(c p) j -> p c j", p=128), writes=[b_A1])
        graw, b_graw, gsig, b_gsig = graw2[qi_], b_graw2[qi_], gsig2[qi_], b_gsig2[qi_]
        P.dma("sp", graw[:], d["graw"][q0_:q0_ + 512, :].rearrange("(c p) j -> p c j", p=128), writes=[b_graw])
        P.op("act", lambda: nc.scalar.activation(out=gsig[:], in_=graw[:], func=AF.Sigmoid), reads=[b_graw], writes=[b_gsig])

    def stA(i):
        it = items[i]; kind, qb, kt, r = it["kind"], it["qb"], it["kt"], it["r"]
        q0 = qb * 512
        qi = qb % 2
        if it["qbstart"] and qb == qbs[0]:
            load_qb(qb)
        diag = kt >= 4 * qb
        if kind == "S" and r == 0 and kt == 0:
            selection_pe()
            nxt = qbs.index(qb) + 1
            if nxt < len(qbs):
                load_qb(qbs[nxt])
        if kind == "S" and r == 0:
            MKc, bMKc = (MK, b_MK) if kt % 2 == 0 else (X, b_X)
            P.op("pe", lambda: nc.tensor.matmul(MKc[:], lhsT=Wsel[:, kt * 128:(kt + 1) * 128], rhs=selT[:], start=True, stop=True),
                 reads=[b_Wsel, b_selT], writes=[bMKc], skip_self=True)
        KA, bKA = {"C": (KcA, b_KcA), "W": (KWA, b_KWA), "S": (KSA, b_KSA)}[kind]
        clamp = True if kind == "C" else diag
        si = i % 2
        ei = i % NE
        QAc, bQAc = QAq[qi], b_QAq[qi]
        addmask = diag and kind in ("W", "S")
        P.op("pe", lambda: nc.tensor.matmul(S_ps[si][:], lhsT=KA[:, kt * 128:(kt + 1) * 128], rhs=QAc[:, r, :], start=True, stop=not addmask),
             reads=[bKA, bQAc], writes=[b_S[si]], skip_self=True)
        if addmask:
            o_ = kt - 4 * qb
            P.op("pe", lambda: nc.tensor.matmul(S_ps[si][:], lhsT=ident[:], rhs=negm[:, o_, :], start=False, stop=True),
                 reads=[b_id, b_negm], writes=[b_S[si]], skip_self=True)
            if kind == "W":
                P.op("act", lambda: nc.scalar.activation(out=p_sb[ei][:], in_=S_ps[si][:], func=AF.Exp, scale=scale), reads=[b_S[si]], writes=[b_p[ei]])
            else:
                P.op("act", lambda: nc.scalar.activation(out=e_sb[ei][:], in_=S_ps[si][:], func=AF.Exp, scale=scale), reads=[b_S[si]], writes=[b_e[ei]])
        elif clamp:
            P.op("dve", lambda: nc.vector.tensor_scalar(out=e_sb[ei][:], in0=S_ps[si][:], scalar1=40.0 / scale, scalar2=None, op0=ALU.min),
                 reads=[b_S[si]], writes=[b_e[ei]])
            P.op("act", lambda: nc.scalar.activation(out=e_sb[ei][:], in_=e_sb[ei][:], func=AF.Exp, scale=scale), reads=[b_e[ei]], writes=[b_e[ei]])
        else:
            P.op("act", lambda: nc.scalar.activation(out=e_sb[ei][:], in_=S_ps[si][:], func=AF.Exp, scale=scale), reads=[b_S[si]], writes=[b_e[ei]])

    def stB(i):
        it = items[i]; kind, qb, kt, r = it["kind"], it["qb"], it["kt"], it["r"]
        q0 = qb * 512
        ei = i % NE
        diag = kt >= 4 * qb
        e, be, p, bp = e_sb[ei], b_e[ei], p_sb[ei], b_p[ei]
        if kind == "C":
            bs = -(2048 * kt + 31 - q0)
            P.op("pool", lambda: nc.gpsimd.affine_select(out=p[:], in_=e[:], pattern=[[1, 512]], compare_op=ALU.is_ge, fill=0.0,
                                                          base=bs, channel_multiplier=-16), reads=[be], writes=[bp])
        elif kind == "W":
            if diag:
                pass
            else:
                bs = 511 - (q0 - 128 * kt)
                P.op("pool", lambda: nc.gpsimd.affine_select(out=p[:], in_=e[:], pattern=[[-1, 512]], compare_op=ALU.is_ge, fill=0.0,
                                                              base=bs, channel_multiplier=1), reads=[be], writes=[bp])
        else:
            mi = kt % NMK
            MKc, bMKc = (MK, b_MK) if kt % 2 == 0 else (X, b_X)
            P.op("dve", lambda: nc.vector.tensor_tensor(out=p[:], in0=e[:], in1=MKc[:], op=ALU.mult), reads=[be, bMKc], writes=[bp])

    import collections as _col
    pending = _col.deque()

    def flush(n=None):
        while pending and (n is None or n > 0):
            pending.popleft()()
            if n is not None:
                n -= 1

    def selection():
        for c in range(4):
            pending.append(lambda c=c: sel_chunk(c))

    def sel_chunk(c):
        if True:
            k = c % 2
            sc, bsc, wk, bwk, mm, bmm = score[k], b_score[k], work[k], b_work[k], m8[k], b_m8[k]
            P.op("dve", lambda c=c, sc=sc: nc.vector.tensor_tensor(out=sc[:], in0=imp[:, c, :], in1=M1[:, c, :], op=ALU.mult),
                 reads=[b_imp, b_M1], writes=[bsc])
            P.op("dve", lambda c=c, sc=sc: nc.vector.tensor_tensor(out=sc[:], in0=sc[:], in1=A1[:, c, :], op=ALU.add),
                 reads=[bsc, b_A1], writes=[bsc])
            P.op("dve", lambda sc=sc, mm=mm: nc.vector.max(out=mm[:, 0:8], in_=sc[:]), reads=[bsc], writes=[bmm])
            P.op("dve", lambda sc=sc, mm=mm, wk=wk: nc.vector.match_replace(out=wk[:], in_to_replace=mm[:, 0:8], in_values=sc[:], imm_value=-1e9),
                 reads=[bsc, bmm], writes=[bwk])
            P.op("dve", lambda mm=mm, wk=wk: nc.vector.max(out=mm[:, 8:16], in_=wk[:]), reads=[bwk], writes=[bmm])
            P.op("dve", lambda mm=mm: nc.vector.tensor_scalar(out=mm[:, 15:16], in0=mm[:, 15:16], scalar1=0.0, scalar2=None, op0=ALU.max),
                 reads=[bmm], writes=[bmm])
            P.op("dve", lambda c=c, sc=sc, mm=mm: nc.vector.tensor_scalar(out=sel[:, c, :], in0=sc[:], scalar1=mm[:, 15:16], scalar2=None, op0=ALU.is_ge),
                 reads=[bsc, bmm], writes=[b_sel])

    def selection_pe():
        flush()
        for c in range(4):
            P.op("pe", lambda c=c: nc.tensor.matmul(X[:, c * 128:(c + 1) * 128], lhsT=sel[:, c, :], rhs=ident[:], start=True, stop=True),
                 reads=[b_sel, b_id], writes=[b_X], skip_self=True)
        P.op("act", lambda: nc.scalar.copy(out=selT[:], in_=X[:]), reads=[b_X], writes=[b_selT])

    def combine(qb):
        pending.append(lambda: combine_w(qb))
        for r in range(NOWN):
            pending.append(lambda r=r: combine_r(qb, r))
        pending.append(lambda: combine_out(qb))

    def combine_w(qb):
        gsig, b_gsig = gsig2[qb % 2], b_gsig2[qb % 2]
        for br, brd in ((0, b_rdenC), (1, b_rdenS), (2, b_rdenW)):
            P.op("dve", lambda br=br: nc.vector.tensor_tensor(
                out=wgt[:, br, 0:NOWN, :], in0=rden[:, br, 0:NOWN, :],
                in1=gsig[:].rearrange("p c (r b) -> p r c b", b=3)[:, :, :, br], op=ALU.mult),
                reads=brd[0:NOWN] + [b_gsig], writes=[b_wgt])

    def combine_r(qb, r):
        if True:
            for c in range(4):
                P.op("dve", lambda r=r, c=c: nc.vector.tensor_scalar(out=oacc[:, c, r * 64:(r + 1) * 64], in0=accC[r][:, c, 0:64],
                                                                      scalar1=wgt[:, 0, r, c:c + 1], scalar2=None, op0=ALU.mult),
                     reads=[b_accC[r], b_wgt], writes=[b_oacc])
                P.op("dve", lambda r=r, c=c: nc.vector.scalar_tensor_tensor(out=oacc[:, c, r * 64:(r + 1) * 64], in0=accS[r][:, c, 0:64],
                                                                            scalar=wgt[:, 1, r, c:c + 1], in1=oacc[:, c, r * 64:(r + 1) * 64],
                                                                            op0=ALU.mult, op1=ALU.add),
                     reads=[b_accS[r], b_wgt, b_oacc], writes=[b_oacc])
                P.op("dve", lambda r=r, c=c: nc.vector.scalar_tensor_tensor(out=obf[:, c, r * 64:(r + 1) * 64], in0=accW[r][:, c, 0:64],
                                                                            scalar=wgt[:, 2, r, c:c + 1], in1=oacc[:, c, r * 64:(r + 1) * 64],
                                                                            op0=ALU.mult, op1=ALU.add),
                     reads=[b_accW[r], b_wgt, b_oacc], writes=[b_obf])

    def combine_out(qb):
        q0 = qb * 512
        P.dma("sp", d["o_nsa"][q0:q0 + 512, :].rearrange("(c p) e -> p c e", p=128), obf[:], reads=[b_obf], writes=[b_out])

    def stC(i):
        it = items[i]; kind, qb, kt, r = it["kind"], it["qb"], it["kt"], it["r"]
        ei = i % NE
        p, bp = p_sb[ei], b_p[ei]
        bks = banks_of(it)
        diag = kt >= 4 * qb
        if it["first"]:
            for bk in bks:
                zero_bank(bk)
        if kind == "C":
            for c in range(4):
                bk, off = bks[c // 2], (c % 2) * 193
                P.op("pe", lambda c=c, bk=bk, off=off: nc.tensor.matmul(A_ps[bk][:, off:off + 193], lhsT=p[:, c * 128:(c + 1) * 128], rhs=VcX[:, kt, 0:193],
                                                                        start=False, stop=True, skip_group_check=True),
                     reads=[bp, b_VcX], writes=[b_A[bk]], skip_self=True)
        else:
            VA, bVA = (VWA, b_VWA) if kind == "W" else (VSA, b_VSA)
            bk = bks[0]
            for c in (range(kt - 4 * qb, 4) if diag else range(4)):
                P.op("pe", lambda c=c: nc.tensor.matmul(A_ps[bk][:, c * 65:(c + 1) * 65], lhsT=p[:, c * 128:(c + 1) * 128], rhs=VA[:, kt, 0:65],
                                                        start=False, stop=True, skip_group_check=True),
                     reads=[bp, bVA], writes=[b_A[bk]], skip_self=True)
        if not it["last"]:
            return
        if kind == "C":
            flush()
            P.op("act", lambda: nc.scalar.copy(out=accC[r][:, 0:2, :].rearrange("p c e -> p (c e)"), in_=A_ps[bks[0]][:, 0:386]),
                 reads=[b_A[bks[0]]], writes=[b_accC[r]])
            P.op("dve", lambda: nc.vector.tensor_copy(out=accC[r][:, 2:4, :].rearrange("p c e -> p (c e)"), in_=A_ps[bks[1]][:, 0:386]),
                 reads=[b_A[bks[1]]], writes=[b_accC[r]])
            P.op("dve", lambda: nc.vector.tensor_scalar(out=rden[:, 0, r, :], in0=accC[r][:, :, 64], scalar1=1e-30, scalar2=None, op0=ALU.max),
                 reads=[b_accC[r]], writes=[b_rdenC[r]])
            P.op("dve", lambda: nc.vector.reciprocal(out=rden[:, 0, r, :], in_=rden[:, 0, r, :]), reads=[b_rdenC[r]], writes=[b_rdenC[r]])
            for c in range(4):
                if r == 0:
                    P.op("dve", lambda c=c: nc.vector.tensor_scalar(out=imp[:, c, :], in0=accC[r][:, c, 65:193], scalar1=rden[:, 0, r, c:c + 1],
                                                                     scalar2=None, op0=ALU.mult), reads=[b_accC[r], b_rdenC[r]], writes=[b_imp])
                else:
                    P.op("dve", lambda c=c: nc.vector.scalar_tensor_tensor(out=imp[:, c, :], in0=accC[r][:, c, 65:193], scalar=rden[:, 0, r, c:c + 1],
                                                                           in1=imp[:, c, :], op0=ALU.mult, op1=ALU.add),
                         reads=[b_accC[r], b_rdenC[r], b_imp], writes=[b_imp])
            if r == 3:
                selection()
        else:
            acc, bacc, brd, bri = (accW, b_accW, b_rdenW, 2) if kind == "W" else (accS, b_accS, b_rdenS, 1)
            bk = bks[0]
            if r % 2 == 0:
                P.op("act", lambda: nc.scalar.copy(out=acc[r][:].rearrange("p c e -> p (c e)"), in_=A_ps[bk][:, 0:260]), reads=[b_A[bk]], writes=[bacc[r]])
            else:
                P.op("dve", lambda: nc.vector.tensor_copy(out=acc[r][:].rearrange("p c e -> p (c e)"), in_=A_ps[bk][:, 0:260]), reads=[b_A[bk]], writes=[bacc[r]])
            P.op("dve", lambda: nc.vector.reciprocal(out=rden[:, bri, r, :], in_=acc[r][:, :, 64]), reads=[bacc[r]], writes=[brd[r]])
            if kind == "S" and r == NOWN - 1:
                combine(qb)

    for s_ in range(-2, NI):
        if 0 <= s_ + 2 < NI:
            stA(s_ + 2)
        if 0 <= s_ + 1 < NI:
            stB(s_ + 1)
        if 0 <= s_ < NI:
            stC(s_)
        flush(1)
    flush()
    return [b_out]


def alloc_banks(P):
    return [(P.ps(f"bank{i}", [128, 512], F32), P.buf(f"bank{i}")) for i in range(8)]


def build_conv(nc, P, NTC, d, banks, ident, b_id):
    N = NTC
    NTT = N // 512
    b_out = P.buf("conv_out")
    dww = P.sb("dww_sb", [128, 4, 31], F32); b_dww = P.buf("dww")
    prm = P.sb("cprm_sb", [128, 3, 4], F32); b_prm = P.buf("cprm")
    P.dma("sp", dww[:], d["dww"], writes=[b_dww])
    P.dma("sp", prm[:, 0, :], d["dwb"], writes=[b_prm])
    P.dma("sp", prm[:, 1, :], d["lng"], writes=[b_prm])
    P.dma("sp", prm[:, 2, :], d["lnb"], writes=[b_prm])
    onesF = P.sb("onesF", [128, 128], F32); b_ones = P.buf("onesF")
    P.op("pool", lambda: nc.gpsimd.memset(onesF[:], 1.0 / 512.0), writes=[b_ones])
    ain = [P.sb(f"ain{i}", [128, 2, N + 30], F32) for i in range(2)]; b_ain = [P.buf(f"ain{i}") for i in range(2)]
    abf = [P.sb(f"abf{i}", [128, N + 30], BF16) for i in range(2)]; b_abf = [P.buf(f"abf{i}") for i in range(2)]
    diag = [P.sb(f"diag{i}", [128, 31, 128], BF16) for i in range(2)]; b_diag = [P.buf(f"diag{i}") for i in range(2)]
    y = [P.sb(f"cy{c}", [128, N], F32) for c in range(4)]; b_y = [P.buf(f"cy{c}") for c in range(4)]
    ysq = P.sb("cysq", [128, 512], F32); b_ysq = P.buf("cysq")
    for c in range(4):
        ai, b_ai = ain[c % 2], b_ain[c % 2]
        ab, b_ab = abf[c % 2], b_abf[c % 2]
        dg, b_dg = diag[c % 2], b_diag[c % 2]
        P.dma("sp", ai[:], d["aT"][:, c * 128:(c + 1) * 128, :].rearrange("k p n -> p k n"), writes=[b_ai])
        P.op("act", lambda ai=ai: nc.scalar.activation(out=ai[:, 1, :], in_=ai[:, 1, :], func=AF.Sigmoid), reads=[b_ai], writes=[b_ai])
        P.op("dve", lambda ai=ai, ab=ab: nc.vector.tensor_tensor(out=ab[:], in0=ai[:, 0, :], in1=ai[:, 1, :], op=ALU.mult),
             reads=[b_ai], writes=[b_ab])
        for k in range(31):
            P.op("dve", lambda k=k, c=c, dg=dg: nc.vector.tensor_scalar(out=dg[:, k, :], in0=ident[:], scalar1=dww[:, c, k:k + 1], scalar2=None,
                                                                         op0=ALU.mult), reads=[b_id, b_dww], writes=[b_dg])
        for tt in range(NTT):
            ps, b_ps = banks[tt % 2]
            for k in range(31):
                P.op("pe", lambda k=k, tt=tt, ps=ps, dg=dg, ab=ab: nc.tensor.matmul(ps[:], lhsT=dg[:, k, :], rhs=ab[:, tt * 512 + k: tt * 512 + k + 512],
                                                                                      start=(k == 0), stop=(k == 30)),
                     reads=[b_dg, b_ab], writes=[b_ps], skip_self=True)
            P.op("act", lambda c=c, tt=tt, ps=ps: nc.scalar.activation(out=y[c][:, tt * 512:(tt + 1) * 512], in_=ps[:], func=AF.Identity,
                                                                        bias=prm[:, 0, c:c + 1]), reads=[b_ps, b_prm], writes=[b_y[c]])
    mean = P.sb("cmean", [128, 512], F32); b_mean = P.buf("cmean")
    rstd = P.sb("crstd", [128, 512], F32); b_rstd = P.buf("crstd")
    yn = [P.sb(f"cyn{i}", [128, 512], F32) for i in range(2)]; b_yn = [P.buf(f"cyn{i}") for i in range(2)]
    co = [P.sb(f"cco{i}", [128, 512], BF16) for i in range(2)]; b_co = [P.buf(f"cco{i}") for i in range(2)]
    it = 0
    for tt in range(NTT):
        sl = slice(tt * 512, (tt + 1) * 512)
        pm, b_pm = banks[2]
        pq, b_pq = banks[3]
        for c in range(4):
            P.op("pe", lambda c=c, sl=sl: nc.tensor.matmul(pm[:], lhsT=onesF[:], rhs=y[c][:, sl], start=(c == 0), stop=(c == 3)),
                 reads=[b_ones, b_y[c]], writes=[b_pm], skip_self=True)
        for c in range(4):
            P.op("act", lambda c=c, sl=sl: nc.scalar.activation(out=ysq[:], in_=y[c][:, sl], func=AF.Square), reads=[b_y[c]], writes=[b_ysq])
            P.op("pe", lambda c=c: nc.tensor.matmul(pq[:], lhsT=onesF[:], rhs=ysq[:], start=(c == 0), stop=(c == 3)),
                 reads=[b_ones, b_ysq], writes=[b_pq], skip_self=True)
        P.op("dve", lambda: nc.vector.tensor_copy(out=mean[:], in_=pm[:]), reads=[b_pm], writes=[b_mean])
        P.op("dve", lambda: nc.vector.tensor_tensor(out=rstd[:], in0=mean[:], in1=mean[:], op=ALU.mult), reads=[b_mean], writes=[b_rstd])
        P.op("dve", lambda: nc.vector.tensor_tensor(out=rstd[:], in0=pq[:], in1=rstd[:], op=ALU.subtract), reads=[b_pq, b_rstd], writes=[b_rstd])
        P.op("act", lambda: nc.scalar.activation(out=rstd[:], in_=rstd[:], func=AF.Sqrt, bias=1e-5), reads=[b_rstd], writes=[b_rstd])
        P.op("dve", lambda: nc.vector.reciprocal(out=rstd[:], in_=rstd[:]), reads=[b_rstd], writes=[b_rstd])
        for c in range(4):
            i = it % 2
            it += 1
            P.op("dve", lambda c=c, sl=sl, i=i: nc.vector.tensor_tensor(out=yn[i][:], in0=y[c][:, sl], in1=mean[:], op=ALU.subtract),
                 reads=[b_y[c], b_mean], writes=[b_yn[i]])
            P.op("dve", lambda i=i: nc.vector.tensor_tensor(out=yn[i][:], in0=yn[i][:], in1=rstd[:], op=ALU.mult),
                 reads=[b_yn[i], b_rstd], writes=[b_yn[i]])
            P.op("act", lambda c=c, i=i: nc.scalar.activation(out=co[i][:], in_=yn[i][:], func=AF.Silu, scale=prm[:, 1, c:c + 1],
                                                               bias=prm[:, 2, c:c + 1]), reads=[b_yn[i], b_prm], writes=[b_co[i]])
            P.dma("sp", d["coutT"][c * 128:(c + 1) * 128, sl], co[i][:], reads=[b_co[i]], writes=[b_out])
    return [b_out]

bf = ml_dtypes.bfloat16

def nsa_consts(T):
    t = np.arange(T)
    j = np.arange(128)
    vis = (j[None, :] * 64 <= t[:, None])
    cur = t // 64
    forced = (j[None, :] == 0) | (j[None, :] == cur[:, None]) | (j[None, :] == cur[:, None] - 1)
    M1 = (vis & ~forced).astype(np.float32)
    A1 = np.where(vis, np.where(forced, 1e4, 0.0), -1.0).astype(np.float32)
    NS = T // 64
    M1[:, NS:] = 0.0; A1[:, NS:] = -1.0
    n = np.arange(512)
    poolm = ((n[:, None] >= 4 * j[None, :] - 1) & (n[:, None] <= 4 * j[None, :] + 3)).astype(np.float32).astype(bf)
    kaug_tok = np.stack([t // 64, t % 64, np.ones(T), np.ones(T)]).astype(np.float32).astype(bf)
    kaugc = np.stack([n // 4, 16 * (n % 4) + 15.5, np.ones(512), np.ones(512)]).astype(np.float32).astype(bf)[:, :T // 16]
    return dict(M1=M1, A1=A1, poolm=poolm[:T // 16], kaug_tok=kaug_tok, kaugc=kaugc)

def q_aug(T, h):
    t = np.arange(T)
    c = (2.0 ** (-(h + 1))) * 8.0
    return np.stack([np.full(T, 64 * c), np.full(T, c), -64 * c * (t // 64), -c * (t % 64)]).astype(np.float32).astype(bf)

def nsa_inputs(T, g, qT, kcT, vcT, ksT, kwT, vs, vw, graw, w, consts, horder=(0, 1, 2, 3)):
    d = {}
    QA = np.zeros((4, 68, T), dtype=bf)
    for r in range(4):
        h = 4 * g + horder[r]
        QA[r, :64] = qT[h * 64:(h + 1) * 64]
        QA[r, 64:] = q_aug(T, h)
    d["QA"] = QA
    for nm, src in (("KSA", ksT), ("KWA", kwT)):
        a = np.zeros((68, T), dtype=bf)
        a[:64] = src[g * 64:(g + 1) * 64]
        a[64:] = consts["kaug_tok"]
        d[nm] = a
    for nm, src in (("VSA", vs), ("VWA", vw)):
        a = np.ones((T, 65), dtype=bf)
        a[:, :64] = src[:, g * 64:(g + 1) * 64]
        d[nm] = a
    for nm, src in (("c2k", kcT), ("c2v", vcT)):
        a = np.zeros((128, T), dtype=bf)
        a[:64] = src[g * 64:(g + 1) * 64]
        a[64:, :T - 1] = src[g * 64:(g + 1) * 64, 1:]
        d[nm] = a
    for kv in ("k", "v"):
        w1 = np.asarray(w["w1_" + kv], dtype=np.float32)
        d["w1" + kv] = np.ascontiguousarray(w1.reshape(16, 2, 64, 128).transpose(1, 2, 0, 3).reshape(128, 16, 128))
        pe = np.asarray(w["pe_" + kv], dtype=np.float32)
        d["pe2" + kv] = np.ascontiguousarray(pe.reshape(16, 2, 64).transpose(1, 2, 0).reshape(128, 16))
        d["w2" + kv] = np.ascontiguousarray(np.asarray(w["w2_" + kv], dtype=np.float32))
    d["kaugc"] = consts["kaugc"]
    d["poolm"] = consts["poolm"]
    d["M1"] = consts["M1"]; d["A1"] = consts["A1"]
    h0 = 4 * g + horder[0]
    d["graw"] = np.ascontiguousarray(graw[:, h0 * 3:(h0 + 2) * 3])
    return d


T_SEQ = 8192
NTOK = 2048
NCORE = 8


def _launch(nc, in_maps):
    res = run_bass_kernel_spmd(nc, in_maps, core_ids=list(range(NCORE)))
    return res.results


def _mk(nc, d, name, shape, dt, out=False):
    d[name] = nc.dram_tensor(name, list(shape), dt, kind="ExternalOutput" if out else "ExternalInput").ap()
    return d[name]


def _build_dense(kind):
    nc = bass.Bass("TRN2", target_bir_lowering=False)
    d = {}
    NT = NTOK
    _mk(nc, d, "x", [NT, 1024], F32)
    if kind in ("B", "C"):
        _mk(nc, d, "oT", [1024, NT], BF16)
        _mk(nc, d, "wo", [1024, 1024], F32)
    nffn = {"A": 1, "B": 2, "C": 1}[kind]
    for i in range(nffn):
        _mk(nc, d, f"fg{i}", [1024], F32)
        _mk(nc, d, f"fwi{i}", [1024, 5632], F32)
        _mk(nc, d, f"fwo{i}", [2816, 1024], F32)
    if kind == "A":
        _mk(nc, d, "pg", [1024], F32); _mk(nc, d, "pw", [1024, 2328], F32)
        _mk(nc, d, "xo", [NT, 1024], F32, True)
        _mk(nc, d, "aT", [1024, NT], F32, True); _mk(nc, d, "qT", [512, NT], BF16, True)
        for n in ("kcT", "vcT", "ksT", "kwT"):
            _mk(nc, d, n, [128, NT], BF16, True)
        _mk(nc, d, "vs", [NT, 128], BF16, True); _mk(nc, d, "vw", [NT, 128], BF16, True)
        _mk(nc, d, "gg", [NT, 24], F32, True)
    elif kind == "B":
        _mk(nc, d, "pg", [1024], F32); _mk(nc, d, "pw", [1024, 3072], F32)
        _mk(nc, d, "xo", [NT, 1024], F32, True)
        _mk(nc, d, "cT", [1536, NT], F32, True); _mk(nc, d, "qT", [512, NT], BF16, True)
        _mk(nc, d, "kT", [512, NT], BF16, True); _mk(nc, d, "v", [NT, 512], BF16, True)
    else:
        _mk(nc, d, "gfin", [1024], F32)
        _mk(nc, d, "out", [NT, 1024], F32, True)
    with ExitStack() as es:
        P = Prog(nc, es)
        dn = Dense(nc, P, NT)
        outs = []
        dn.load_x(d["x"])
        if kind in ("B", "C"):
            dn.outproj(d["oT"], d["wo"])
        for i in range(nffn):
            dn.ffn(d[f"fg{i}"], d[f"fwi{i}"], d[f"fwo{i}"])
        if kind == "A":
            names = [(0, 1024, "F", "aT"), (1024, 1536, "F", "qT"), (1536, 1664, "F", "kcT"), (1664, 1792, "F", "vcT"),
                     (1792, 1920, "F", "ksT"), (1920, 2048, "T", "vs"), (2048, 2176, "F", "kwT"), (2176, 2304, "T", "vw"),
                     (2304, 2328, "T", "gg")]
        elif kind == "B":
            names = [(0, 1536, "F", "cT"), (1536, 2048, "F", "qT"), (2048, 2560, "F", "kT"), (2560, 3072, "T", "v")]
        if kind in ("A", "B"):
            bx = P.buf("xo_out")
            dn.store_x(d["xo"], bx)
            outs.append(bx)
            specs = []
            for (c0, c1, lay, n) in names:
                b = P.buf("o_" + n)
                outs.append(b)
                specs.append((c0, c1, lay, d[n], b))
            dn.proj(d["pg"], d["pw"], specs)
        else:
            dn.alloc_final()
            bo = P.buf("out_out")
            dn.final(d["gfin"], d["out"], bo)
            outs.append(bo)
        P.finish("sp", outs)
        P.emit()
    return nc


def _build_conv0():
    nc = bass.Bass("TRN2", target_bir_lowering=False)
    d = {}
    _mk(nc, d, "aT", [2, 512, NTOK + 30], F32); _mk(nc, d, "dww", [128, 4, 31], F32)
    for n in ("dwb", "lng", "lnb"):
        _mk(nc, d, n, [128, 4], F32)
    _mk(nc, d, "coutT", [512, NTOK], BF16, True)
    with ExitStack() as es:
        P = Prog(nc, es)
        banks = alloc_banks(P)
        ident, b_id = make_ident(nc, P, "identc")
        outs = build_conv(nc, P, NTOK, d, banks, ident, b_id)
        P.finish("sp", outs)
        P.emit()
    return nc


def _build_nsa():
    nc = bass.Bass("TRN2", target_bir_lowering=False)
    d = {}
    T = T_SEQ
    _mk(nc, d, "QA", [4, 68, T], BF16); _mk(nc, d, "KSA", [68, T], BF16); _mk(nc, d, "KWA", [68, T], BF16)
    _mk(nc, d, "VSA", [T, 65], BF16); _mk(nc, d, "VWA", [T, 65], BF16)
    _mk(nc, d, "c2k", [128, T], BF16); _mk(nc, d, "c2v", [128, T], BF16)
    for kv in "kv":
        _mk(nc, d, "w1" + kv, [128, 16, 128], F32); _mk(nc, d, "pe2" + kv, [128, 16], F32); _mk(nc, d, "w2" + kv, [128, 64], F32)
    _mk(nc, d, "kaugc", [4, T // 16], BF16); _mk(nc, d, "poolm", [T // 16, 128], BF16)
    _mk(nc, d, "M1", [T, 128], F32); _mk(nc, d, "A1", [T, 128], F32); _mk(nc, d, "graw", [T, 6], F32)
    _mk(nc, d, "o_nsa", [T, 128], BF16, True)
    with ExitStack() as es:
        P = Prog(nc, es)
        outs = build_nsa(nc, P, T, list(range(T // 512)), d, alloc_banks(P), NOWN=2)
        P.finish("sp", outs)
        P.emit()
    return nc


def _build_m1():
    nc = bass.Bass("TRN2", target_bir_lowering=False)
    d = {}
    T = T_SEQ
    _mk(nc, d, "qT", [2, 64, T], BF16); _mk(nc, d, "kT", [2, 64, T], BF16); _mk(nc, d, "v", [T, 2, 64], BF16)
    _mk(nc, d, "convin", [3, 512, NTOK + 2], F32); _mk(nc, d, "scw", [128, 12], F32)
    _mk(nc, d, "o_sbT", [2, 64, T], BF16, True); _mk(nc, d, "coutT", [512, NTOK], BF16, True)
    with ExitStack() as es:
        P = Prog(nc, es)
        outs = build_mixer1(nc, P, T, NTOK, d)
        P.finish("sp", outs)
        P.emit()
    return nc


def _cat_tok(res, name, axis):
    return [np.concatenate([np.asarray(res[b * 4 + j][name]) for j in range(4)], axis=axis) for b in range(2)]


def kernel(x, ffn1_norm, ffn1_w_in, ffn1_w_out, mix_norm, ffn2_norm, ffn2_w_in, ffn2_w_out,
           ab_w_in, conv_dw_w, conv_dw_b, conv_ln_g, conv_ln_b,
           nsa_pe_k, nsa_w1_k, nsa_w2_k, nsa_pe_v, nsa_w1_v, nsa_w2_v, ab_w_out,
           cd_w_in, sc_conv_w, cd_w_out, final_norm):
    f32 = lambda a: np.ascontiguousarray(np.asarray(a, dtype=np.float32))
    x = f32(x)
    T = T_SEQ
    xs = [np.ascontiguousarray(x[c // 4, (c % 4) * NTOK:(c % 4 + 1) * NTOK]) for c in range(NCORE)]
    common = {"fg0": f32(ffn1_norm[0]), "fwi0": f32(ffn1_w_in[0]), "fwo0": f32(ffn1_w_out[0]), "pg": f32(mix_norm[0]), "pw": f32(ab_w_in[0])}
    rA = _launch(_build_dense("A"), [dict(common, x=xs[c]) for c in range(NCORE)])
    aT = _cat_tok(rA, "aT", 1); qT = _cat_tok(rA, "qT", 1)
    kcT = _cat_tok(rA, "kcT", 1); vcT = _cat_tok(rA, "vcT", 1); ksT = _cat_tok(rA, "ksT", 1); kwT = _cat_tok(rA, "kwT", 1)
    vs = _cat_tok(rA, "vs", 0); vw = _cat_tok(rA, "vw", 0); gg = _cat_tok(rA, "gg", 0)
    lay4 = lambda v: np.ascontiguousarray(f32(v).reshape(4, 128).T)
    cc = {"dww": np.ascontiguousarray(f32(conv_dw_w[0]).reshape(31, 4, 128).transpose(2, 1, 0)),
          "dwb": lay4(conv_dw_b[0]), "lng": lay4(conv_ln_g[0]), "lnb": lay4(conv_ln_b[0])}
    maps = []
    for c in range(NCORE):
        b, j = c // 4, c % 4
        a = np.zeros((2, 512, NTOK + 30), dtype=np.float32)
        lo = j * NTOK - 30
        src = aT[b].reshape(2, 512, T)
        if lo < 0:
            a[:, :, 30:] = src[:, :, 0:NTOK]
        else:
            a[:] = src[:, :, lo:lo + NTOK + 30]
        maps.append(dict(cc, aT=a))
    rC0 = _launch(_build_conv0(), maps)
    consts = nsa_consts(T)
    w = dict(pe_k=nsa_pe_k[0], w1_k=nsa_w1_k[0], w2_k=nsa_w2_k[0], pe_v=nsa_pe_v[0], w1_v=nsa_w1_v[0], w2_v=nsa_w2_v[0])
    maps = []
    for c in range(NCORE):
        b, g, hh = c // 4, (c % 4) // 2, c % 2
        horder = [2 * hh, 2 * hh + 1, 2 * (1 - hh), 2 * (1 - hh) + 1]
        dd = nsa_inputs(T, g, qT[b], kcT[b], vcT[b], ksT[b], kwT[b], vs[b], vw[b], gg[b], w, consts, horder)
        maps.append(dd)
    rN = _launch(_build_nsa(), maps)
    oT = []
    for c in range(NCORE):
        b, j = c // 4, c % 4
        o = np.zeros((1024, NTOK), dtype=bf)
        o[0:512] = np.asarray(rC0[c]["coutT"])
        for g in range(2):
            for hh in range(2):
                src = np.asarray(rN[b * 4 + g * 2 + hh]["o_nsa"])[j * NTOK:(j + 1) * NTOK]
                r0 = 512 + (4 * g + 2 * hh) * 64
                o[r0:r0 + 128] = src.T
        oT.append(o)
    common = {"wo": f32(ab_w_out[0]), "fg0": f32(ffn2_norm[0]), "fwi0": f32(ffn2_w_in[0]), "fwo0": f32(ffn2_w_out[0]),
              "fg1": f32(ffn1_norm[1]), "fwi1": f32(ffn1_w_in[1]), "fwo1": f32(ffn1_w_out[1]), "pg": f32(mix_norm[1]), "pw": f32(cd_w_in[0])}
    rB = _launch(_build_dense("B"), [dict(common, x=np.asarray(rA[c]["xo"]), oT=oT[c]) for c in range(NCORE)])
    cT = _cat_tok(rB, "cT", 1); q1 = _cat_tok(rB, "qT", 1); k1 = _cat_tok(rB, "kT", 1); v1 = _cat_tok(rB, "v", 0)
    scw = np.ascontiguousarray(f32(sc_conv_w[0]).reshape(3, 4, 128).transpose(2, 1, 0).reshape(128, 12))
    maps = []
    for c in range(NCORE):
        b, j = c // 4, c % 4
        ci = np.zeros((3, 512, NTOK + 2), dtype=np.float32)
        src = cT[b].reshape(3, 512, T)
        lo = j * NTOK - 2
        if lo < 0:
            ci[:, :, 2:] = src[:, :, 0:NTOK]
        else:
            ci[:] = src[:, :, lo:lo + NTOK + 2]
        hp = j
        maps.append({"qT": np.ascontiguousarray(q1[b][hp * 128:(hp + 1) * 128].reshape(2, 64, T)),
                     "kT": np.ascontiguousarray(k1[b][hp * 128:(hp + 1) * 128].reshape(2, 64, T)),
                     "v": np.ascontiguousarray(v1[b][:, hp * 128:(hp + 1) * 128].reshape(T, 2, 64)),
                     "convin": ci, "scw": scw})
    rM1 = _launch(_build_m1(), maps)
    oT = []
    for c in range(NCORE):
        b, j = c // 4, c % 4
        o = np.zeros((1024, NTOK), dtype=bf)
        o[0:512] = np.asarray(rM1[c]["coutT"])
        for hp in range(4):
            src = np.asarray(rM1[b * 4 + hp]["o_sbT"]).reshape(128, T)[:, j * NTOK:(j + 1) * NTOK]
            o[512 + hp * 128:512 + (hp + 1) * 128] = src
        oT.append(o)
    common = {"wo": f32(cd_w_out[0]), "fg0": f32(ffn2_norm[1]), "fwi0": f32(ffn2_w_in[1]), "fwo0": f32(ffn2_w_out[1]), "gfin": f32(final_norm)}
    rC = _launch(_build_dense("C"), [dict(common, x=np.asarray(rB[c]["xo"]), oT=oT[c]) for c in range(NCORE)])
    out = np.zeros((2, T, 1024), dtype=np.float32)
    for c in range(NCORE):
        out[c // 4, (c % 4) * NTOK:(c % 4 + 1) * NTOK] = np.asarray(rC[c]["out"])
    return out
```

```python
import numpy as np
import math
from contextlib import ExitStack
import concourse.bass as bass
import concourse.mybir as mybir
from concourse.bass_utils import run_bass_kernel_spmd
import ml_dtypes

F32 = mybir.dt.float32
BF16 = mybir.dt.bfloat16
AF = mybir.ActivationFunctionType
ALU = mybir.AluOpType
AX = mybir.AxisListType

SEM_EPOCH = 30000


class Buf:
    __slots__ = ("name", "w", "r", "dsem", "dcnt")

    def __init__(self, name):
        self.name = name
        self.w = []
        self.r = []
        self.dsem = None
        self.dcnt = 0


class Prog:
    def __init__(self, nc, es):
        self.nc = nc
        self.es = es
        self.eng = {"pe": nc.tensor, "act": nc.scalar, "dve": nc.vector, "pool": nc.gpsimd, "sp": nc.sync}
        self.sem = {}
        self.cnt = {}
        self.waited = {k: {} for k in self.eng}
        self.nsem = 0
        for k in self.eng:
            self._new_eng_sem(k)
        self.n_inst = 0
        self.n_wait = 0
        self.q = {k: [] for k in self.eng}

    def _new_sem(self, name):
        self.nsem += 1
        return self.es.enter_context(self.nc.semaphore(f"{name}_{self.nsem}"))

    def _new_eng_sem(self, k):
        self.sem[k] = self._new_sem("e" + k)
        self.cnt[k] = 0

    def buf(self, name):
        return Buf(name)

    def sb(self, name, shape, dtype):
        t = self.es.enter_context(self.nc.sbuf_tensor(name, list(shape), dtype))
        return t

    def ps(self, name, shape, dtype):
        t = self.es.enter_context(self.nc.psum_tensor(name, list(shape), dtype))
        return t

    def _wait(self, e, conds, skip_self=False):
        eng = self.eng[e]
        wd = self.waited[e]
        best = {}
        for (s, v, owner) in conds:
            if skip_self and owner == e:
                continue
            key = id(s)
            if wd.get(key, 0) >= v:
                continue
            if key not in best or best[key][1] < v:
                best[key] = (s, v)
        for key, (s, v) in best.items():
            self.q[e].append(("w", s, v))
            wd[key] = v
            self.n_wait += 1

    def op(self, e, fn, reads=(), writes=(), skip_self=False):
        conds = []
        for b in reads:
            conds += b.w
        for b in writes:
            conds += b.w
            conds += b.r
        self._wait(e, conds, skip_self=skip_self)
        if self.cnt[e] >= SEM_EPOCH:
            self._new_eng_sem(e)
        self.cnt[e] += 1
        self.q[e].append(("i", fn, self.sem[e], 1))
        c = (self.sem[e], self.cnt[e], e)
        for b in reads:
            b.r = [x for x in b.r if x[0] is not c[0]] + [c]
        for b in writes:
            b.w = [c]
            b.r = []
        self.n_inst += 1

    def dma(self, e, out, in_, reads=(), writes=(), **kw):
        conds = []
        for b in reads:
            conds += b.w
        for b in writes:
            conds += b.w
            conds += b.r
        self._wait(e, conds)
        tgt = writes[0] if writes else reads[0]
        if tgt.dsem is None:
            tgt.dsem = self._new_sem("d" + tgt.name)
        tgt.dcnt += 1
        eng = self.eng[e]
        self.q[e].append(("i", (lambda: eng.dma_start(out=out, in_=in_, **kw)), tgt.dsem, 16))
        c = (tgt.dsem, 16 * tgt.dcnt, "dma")
        for b in reads:
            b.r = [x for x in b.r if x[0] is not c[0]] + [c]
        for b in writes:
            b.w = [x for x in b.w if x[0] is not c[0]] + [c]
            b.r = []
        self.n_inst += 1

    def dma_fn(self, e, fn, reads=(), writes=()):
        conds = []
        for b in reads:
            conds += b.w
        for b in writes:
            conds += b.w
            conds += b.r
        self._wait(e, conds)
        tgt = writes[0] if writes else reads[0]
        if tgt.dsem is None:
            tgt.dsem = self._new_sem("d" + tgt.name)
        tgt.dcnt += 1
        self.q[e].append(("i", fn, tgt.dsem, 16))
        c = (tgt.dsem, 16 * tgt.dcnt, "dma")
        for b in reads:
            b.r = [x for x in b.r if x[0] is not c[0]] + [c]
        for b in writes:
            b.w = [x for x in b.w if x[0] is not c[0]] + [c]
            b.r = []
        self.n_inst += 1

    def cc(self, kind, in_ap, out_ap, groups, reads=(), writes=()):
        nc = self.nc
        fn = lambda: nc.gpsimd.collective_compute(kind, mybir.AluOpType.bypass, replica_groups=groups, ins=[in_ap], outs=[out_ap])
        self.dma_fn("pool", fn, reads=reads, writes=writes)

    def finish(self, e, bufs):
        conds = []
        for b in bufs:
            conds += b.w
        self._wait(e, conds)

    def emit(self):
        nc = self.nc
        with nc.Block() as block:
            def run(e):
                eng = self.eng[e]
                for it in self.q[e]:
                    if it[0] == "w":
                        eng.wait_ge(it[1], it[2])
                    else:
                        it[1]().then_inc(it[2], it[3])

            @block.tensor
            def _(x):
                run("pe")

            @block.scalar
            def _(x):
                run("act")

            @block.vector
            def _(x):
                run("dve")

            @block.gpsimd
            def _(x):
                run("pool")

            @block.sync
            def _(x):
                run("sp")


D = 1024
DFF = 2816
NFC = DFF // 128


def make_ident(nc, P, name="ident"):
    ident = P.sb(name, [128, 128], BF16)
    b = P.buf(name)
    P.op("pool", lambda: nc.gpsimd.memset(ident[:], 0.0), writes=[b])
    P.op("pool", lambda: nc.gpsimd.affine_select(out=ident[:], in_=ident[:], pattern=[[-1, 128]],
                                                   compare_op=ALU.not_equal, fill=1.0, base=0,
                                                   channel_multiplier=1), reads=[b], writes=[b])
    return ident, b


class Dense:
    def __init__(self, nc, P, NT):
        self.nc, self.P, self.NT = nc, P, NT
        self.NTILE = NT // 128
        self.NST = NT // 512
        nt = self.NTILE
        self.x = P.sb("x_res", [128, nt, D], F32)
        self.b_x = [P.buf(f"x{t}") for t in range(nt)]
        self.xnT = P.sb("xnT", [128, 8, NT], BF16)
        self.b_xnT = [P.buf(f"xnT{t}") for t in range(nt)]
        self.ident, self.b_id = make_ident(nc, P)
        self.sq = P.sb("sq", [128, D], F32); self.b_sq = P.buf("sq")
        self.ss = P.sb("ss", [128, nt], F32); self.b_ss = [P.buf(f"ss{g}") for g in range(nt // 4)]
        self.rstd = P.sb("rstd", [128, nt], F32); self.b_rstd = [P.buf(f"rstd{g}") for g in range(nt // 4)]
        self.sq2 = P.sb("sq2", [128, D], F32); self.b_sq2 = P.buf("sq2")
        self.xs = [P.sb(f"xs{i}", [128, D], BF16) for i in range(2)]
        self.b_xs = [P.buf(f"xs{i}") for i in range(2)]
        self.gt = P.sb("gt", [128, 8], F32); self.b_gt = P.buf("gt")
        self.NWB = 6
        self.wb = [P.sb(f"wb{i}", [128, 8 * 512], BF16) for i in range(self.NWB)]
        self.b_wb = [P.buf(f"wb{i}") for i in range(self.NWB)]
        self.wi = 0
        self.tp = [P.ps(f"tp{i}", [128, 8, 128], BF16) for i in range(2)]; self.b_tp = [P.buf(f"tp{i}") for i in range(2)]
        self.pg = [P.ps(f"pg{i}", [128, 512], F32) for i in range(2)]; self.b_pg = [P.buf(f"pg{i}") for i in range(2)]
        self.pu = [P.ps(f"pu{i}", [128, 512], F32) for i in range(2)]; self.b_pu = [P.buf(f"pu{i}") for i in range(2)]
        self.py = [P.ps(f"py{i}", [128, 512], F32) for i in range(2)]; self.b_py = [P.buf(f"py{i}") for i in range(2)]
        self.ipg = 0
        self.ipy = 0
        self.sg = [P.sb(f"sg{i}", [128, 512], F32) for i in range(2)]; self.b_sg = [P.buf(f"sg{i}") for i in range(2)]
        self.act = [P.sb(f"actT{i}", [128, 4, 512], BF16) for i in range(2)]
        self.b_act = [P.buf(f"actT{i}") for i in range(2)]
        self.iact = 0
        self.stg = [P.sb(f"stg{i}", [128, 512], F32) for i in range(3)]
        self.b_stg = [P.buf(f"stg{i}") for i in range(3)]
        self.istg = 0
        self.gfull = None

    def next_wb(self):
        i = self.wi % self.NWB
        self.wi += 1
        return self.wb[i], self.b_wb[i]

    def load_w(self, src_ap, rc, cols):
        wb, b = self.next_wb()
        view = wb[:, 0:rc * cols].rearrange("p (c n) -> p c n", c=rc)
        self.P.dma("pool", view, src_ap.rearrange("(c p) n -> p c n", p=128), writes=[b])
        return view, b

    def load_x(self, x_dram):
        for t in range(self.NTILE):
            self.P.dma("sp", self.x[:, t, :], x_dram[t * 128:(t + 1) * 128, :], writes=[self.b_x[t]])

    def store_x(self, out_dram, b_out):
        for t in range(self.NTILE):
            self.P.dma("sp", out_dram[t * 128:(t + 1) * 128, :], self.x[:, t, :], reads=[self.b_x[t]], writes=[b_out])

    def stats_group(self, g):
        nc, P = self.nc, self.P
        for t in range(4 * g, 4 * g + 4):
            sq, b_sq = (self.sq, self.b_sq) if t % 2 == 0 else (self.sq2, self.b_sq2)
            P.op("act", lambda t=t, sq=sq: nc.scalar.activation(out=sq[:], in_=self.x[:, t, :], func=AF.Square,
                                                                 accum_out=self.ss[:, t:t + 1]),
                 reads=[self.b_x[t]], writes=[b_sq, self.b_ss[g]])
        P.op("act", lambda g=g: nc.scalar.activation(out=self.rstd[:, 4 * g:4 * g + 4], in_=self.ss[:, 4 * g:4 * g + 4], func=AF.Sqrt,
                                                     scale=1.0 / D, bias=1e-6), reads=[self.b_ss[g]], writes=[self.b_rstd[g]])
        P.op("dve", lambda g=g: nc.vector.reciprocal(out=self.rstd[:, 4 * g:4 * g + 4], in_=self.rstd[:, 4 * g:4 * g + 4]),
             reads=[self.b_rstd[g]], writes=[self.b_rstd[g]])

    def stats(self):
        for g in range(self.NTILE // 4):
            self.stats_group(g)

    def norm_T(self, g_dram):
        nc, P = self.nc, self.P
        P.dma("sp", self.gt[:], g_dram.rearrange("(c p) -> p c", p=128), writes=[self.b_gt], allow_slow_non_contiguous=True)
        self.stats_group(0)
        for t in range(self.NTILE):
            xs, b_xs = self.xs[t % 2], self.b_xs[t % 2]
            tp, b_tp = self.tp[t % 2], self.b_tp[t % 2]
            g = t // 4
            if t % 4 == 0 and g + 1 < self.NTILE // 4:
                self.stats_group(g + 1)
            if t % 2 == 0:
                P.op("act", lambda t=t, xs=xs: nc.scalar.activation(out=xs[:], in_=self.x[:, t, :], func=AF.Identity, scale=self.rstd[:, t:t + 1]),
                     reads=[self.b_x[t], self.b_rstd[g]], writes=[b_xs])
            else:
                P.op("dve", lambda t=t, xs=xs: nc.vector.tensor_scalar(out=xs[:], in0=self.x[:, t, :], scalar1=self.rstd[:, t:t + 1],
                                                                        scalar2=None, op0=ALU.mult),
                     reads=[self.b_x[t], self.b_rstd[g]], writes=[b_xs])
            for c in range(8):
                P.op("pe", lambda c=c, xs=xs, tp=tp: nc.tensor.transpose(out=tp[:, c, :], in_=xs[:, c * 128:(c + 1) * 128],
                                                                          identity=self.ident[:]),
                     reads=[b_xs, self.b_id], writes=[b_tp], skip_self=True)
            P.op("dve", lambda t=t, tp=tp: nc.vector.tensor_tensor(out=self.xnT[:, :, t * 128:(t + 1) * 128], in0=tp[:],
                                                                    in1=self.gt[:].unsqueeze(2).to_broadcast([128, 8, 128]), op=ALU.mult),
                 reads=[b_tp, self.b_gt], writes=[self.b_xnT[t]])

    def ffn(self, g_dram, w_in, w_out):
        nc, P = self.nc, self.P
        self.norm_T(g_dram)
        groups = [(s, min(4, NFC - s)) for s in range(0, NFC, 4)]
        for (fc0, nfc) in groups:
            ncol = nfc * 128
            wg, b_wg = self.load_w(w_in[:, fc0 * 128: fc0 * 128 + ncol], 8, ncol)
            wu, b_wu = self.load_w(w_in[:, DFF + fc0 * 128: DFF + fc0 * 128 + ncol], 8, ncol)
            wo, b_wo = self.load_w(w_out[fc0 * 128: fc0 * 128 + ncol, :], nfc, D)
            for st in range(self.NST):
                tiles = list(range(st * 4, st * 4 + 4))
                xb = [self.b_xnT[t] for t in tiles]
                act, b_act = self.act[self.iact % 2], self.b_act[self.iact % 2]
                self.iact += 1
                for j in range(nfc):
                    i = self.ipg % 2
                    self.ipg += 1
                    pg, b_pg, pu, b_pu = self.pg[i], self.b_pg[i], self.pu[i], self.b_pu[i]
                    sg, b_sg = self.sg[i], self.b_sg[i]
                    for k in range(8):
                        P.op("pe", lambda k=k, j=j, pg=pg, wg=wg, st=st: nc.tensor.matmul(
                            pg[:], lhsT=wg[:, k, j * 128:(j + 1) * 128], rhs=self.xnT[:, k, st * 512:(st + 1) * 512],
                            start=(k == 0), stop=(k == 7)), reads=xb + [b_wg], writes=[b_pg], skip_self=True)
                    for k in range(8):
                        P.op("pe", lambda k=k, j=j, pu=pu, wu=wu, st=st: nc.tensor.matmul(
                            pu[:], lhsT=wu[:, k, j * 128:(j + 1) * 128], rhs=self.xnT[:, k, st * 512:(st + 1) * 512],
                            start=(k == 0), stop=(k == 7)), reads=xb + [b_wu], writes=[b_pu], skip_self=True)
                    P.op("act", lambda pg=pg, sg=sg: nc.scalar.activation(out=sg[:], in_=pg[:], func=AF.Silu),
                         reads=[b_pg], writes=[b_sg])
                    P.op("dve", lambda j=j, pu=pu, sg=sg, act=act: nc.vector.tensor_tensor(out=act[:, j, :], in0=pu[:], in1=sg[:], op=ALU.mult),
                         reads=[b_pu, b_sg], writes=[b_act])
                for sub in range(4):
                    t = st * 4 + sub
                    for dh in range(2):
                        i = self.ipy % 2
                        self.ipy += 1
                        py, b_py = self.py[i], self.b_py[i]
                        for j in range(nfc):
                            P.op("pe", lambda j=j, py=py, act=act, wo=wo, sub=sub, dh=dh, nfc=nfc: nc.tensor.matmul(
                                py[:], lhsT=act[:, j, sub * 128:(sub + 1) * 128], rhs=wo[:, j, dh * 512:(dh + 1) * 512],
                                start=(j == 0), stop=(j == nfc - 1)), reads=[b_act, b_wo], writes=[b_py], skip_self=True)
                        P.op("dve", lambda t=t, dh=dh, py=py: nc.vector.scalar_tensor_tensor(
                            out=self.x[:, t, dh * 512:(dh + 1) * 512], in0=py[:], scalar=0.5, in1=self.x[:, t, dh * 512:(dh + 1) * 512],
                            op0=ALU.mult, op1=ALU.add), reads=[b_py, self.b_x[t]], writes=[self.b_x[t]])

    def outproj(self, oT_dram, w_dram):
        nc, P = self.nc, self.P
        for t in range(self.NTILE):
            P.dma("sp", self.xnT[:, :, t * 128:(t + 1) * 128],
                  oT_dram[:, t * 128:(t + 1) * 128].rearrange("(c p) n -> p c n", p=128), writes=[self.b_xnT[t]])
        for dh in range(2):
            w, b_w = self.load_w(w_dram[:, dh * 512:(dh + 1) * 512], 8, 512)
            for t in range(self.NTILE):
                i = self.ipy % 2
                self.ipy += 1
                py, b_py = self.py[i], self.b_py[i]
                for k in range(8):
                    P.op("pe", lambda k=k, py=py, w=w, t=t: nc.tensor.matmul(
                        py[:], lhsT=self.xnT[:, k, t * 128:(t + 1) * 128], rhs=w[:, k, :], start=(k == 0), stop=(k == 7)),
                        reads=[self.b_xnT[t], b_w], writes=[b_py], skip_self=True)
                P.op("dve", lambda t=t, dh=dh, py=py: nc.vector.tensor_tensor(
                    out=self.x[:, t, dh * 512:(dh + 1) * 512], in0=py[:], in1=self.x[:, t, dh * 512:(dh + 1) * 512], op=ALU.add),
                    reads=[b_py, self.b_x[t]], writes=[self.b_x[t]])

    def proj(self, g_dram, w_dram, outs):
        nc, P = self.nc, self.P
        self.norm_T(g_dram)
        for (c0, c1, layout, o_ap, b_o) in outs:
            for cs in range(c0, c1, 512):
                ce = min(cs + 512, c1)
                ncol = ce - cs
                w, b_w = self.load_w(w_dram[:, cs:ce], 8, ncol)
                if layout == "F":
                    assert ncol % 128 == 0
                    for j in range(ncol // 128):
                        for st in range(self.NST):
                            i = self.ipy % 2
                            self.ipy += 1
                            py, b_py = self.py[i], self.b_py[i]
                            xb = [self.b_xnT[t] for t in range(st * 4, st * 4 + 4)]
                            for k in range(8):
                                P.op("pe", lambda k=k, j=j, py=py, w=w, st=st: nc.tensor.matmul(
                                    py[:], lhsT=w[:, k, j * 128:(j + 1) * 128], rhs=self.xnT[:, k, st * 512:(st + 1) * 512],
                                    start=(k == 0), stop=(k == 7)), reads=xb + [b_w], writes=[b_py], skip_self=True)
                            si = self.istg % 3
                            self.istg += 1
                            stg, b_stg = self.stg[si], self.b_stg[si]
                            if o_ap.dtype == BF16:
                                sv = stg[:].bitcast(BF16)[:, 0:512]
                            else:
                                sv = stg[:]
                            eng = "act" if (self.istg % 2) else "dve"
                            if eng == "act":
                                P.op("act", lambda sv=sv, py=py: nc.scalar.copy(out=sv, in_=py[:]), reads=[b_py], writes=[b_stg])
                            else:
                                P.op("dve", lambda sv=sv, py=py: nc.vector.tensor_copy(out=sv, in_=py[:]), reads=[b_py], writes=[b_stg])
                            r0 = cs - c0 + j * 128
                            P.dma("sp", o_ap[r0:r0 + 128, st * 512:(st + 1) * 512], sv, reads=[b_stg], writes=[b_o])
                else:
                    for t in range(self.NTILE):
                        i = self.ipy % 2
                        self.ipy += 1
                        py, b_py = self.py[i], self.b_py[i]
                        for k in range(8):
                            P.op("pe", lambda k=k, py=py, w=w, t=t, ncol=ncol: nc.tensor.matmul(
                                py[:, 0:ncol], lhsT=self.xnT[:, k, t * 128:(t + 1) * 128], rhs=w[:, k, :],
                                start=(k == 0), stop=(k == 7)), reads=[self.b_xnT[t], b_w], writes=[b_py], skip_self=True)
                        si = self.istg % 3
                        self.istg += 1
                        stg, b_stg = self.stg[si], self.b_stg[si]
                        if o_ap.dtype == BF16:
                            sv = stg[:].bitcast(BF16)[:, 0:ncol]
                        else:
                            sv = stg[:, 0:ncol]
                        P.op("dve", lambda sv=sv, py=py, ncol=ncol: nc.vector.tensor_copy(out=sv, in_=py[:, 0:ncol]), reads=[b_py], writes=[b_stg])
                        P.dma("sp", o_ap[t * 128:(t + 1) * 128, cs - c0:ce - c0], sv, reads=[b_stg], writes=[b_o])

    def final(self, g_dram, out_dram, b_out):
        nc, P = self.nc, self.P
        gfull = P.sb("gfull", [128, D], F32)
        b_g = P.buf("gfull")
        P.dma("sp", gfull[:], g_dram.partition_broadcast(128), writes=[b_g])
        self.stats()
        for t in range(self.NTILE):
            si = t % 2
            o = self.fin[si]
            b_o = self.b_fin[si]
            P.op("dve", lambda t=t, o=o: nc.vector.scalar_tensor_tensor(out=o[:], in0=self.x[:, t, :], scalar=self.rstd[:, t:t + 1],
                                                                        in1=gfull[:], op0=ALU.mult, op1=ALU.mult),
                 reads=[self.b_x[t], self.b_rstd[t // 4], b_g], writes=[b_o])
            P.dma("sp", out_dram[t * 128:(t + 1) * 128, :], o[:], reads=[b_o], writes=[b_out])

    def alloc_final(self):
        P = self.P
        self.fin = [self.sq, self.sq]
        self.b_fin = [self.b_sq, self.b_sq]


def tri_consts(nc, P):
    triu = P.sb("triu", [128, 128], BF16); b_u = P.buf("triu")
    tril = P.sb("tril", [128, 128], BF16); b_l = P.buf("tril")
    P.op("pool", lambda: nc.gpsimd.memset(triu[:], 1.0), writes=[b_u])
    P.op("pool", lambda: nc.gpsimd.affine_select(out=triu[:], in_=triu[:], pattern=[[-1, 128]], compare_op=ALU.is_ge,
                                                   fill=0.0, base=0, channel_multiplier=1), reads=[b_u], writes=[b_u])
    P.op("pool", lambda: nc.gpsimd.memset(tril[:], 0.0), writes=[b_l])
    P.op("pool", lambda: nc.gpsimd.affine_select(out=tril[:], in_=tril[:], pattern=[[-1, 128]], compare_op=ALU.is_ge,
                                                   fill=1.0, base=0, channel_multiplier=1), reads=[b_l], writes=[b_l])
    return triu, b_u, tril, b_l


def causal_masks(nc, P, strict=True, dtype=BF16, name="cm"):
    m = P.sb(name, [128, 4, 512], dtype); b = P.buf(name)
    P.op("pool", lambda: nc.gpsimd.memset(m[:], 1.0), writes=[b])
    for o in range(4):
        P.op("pool", lambda o=o: nc.gpsimd.affine_select(out=m[:, o, :], in_=m[:, o, :], pattern=[[1, 512]],
                                                          compare_op=(ALU.is_gt if strict else ALU.is_ge), fill=0.0,
                                                          base=-128 * o, channel_multiplier=-1), reads=[b], writes=[b])
    return m, b


def build_mixer1(nc, P, T, NTC, d):
    scale = 64 ** -0.5
    NQB = T // 512
    NKT = T // 128
    qT = P.sb("qT_sb", [64, 2, T], BF16); b_q = P.buf("qT")
    kT = P.sb("kT_sb", [64, 2, T], BF16); b_k = P.buf("kT")
    v = P.sb("v_sb", [128, NKT, 2, 64], BF16); b_v = P.buf("v")
    for h in range(2):
        P.dma("sp", qT[:, h, :], d["qT"][h], writes=[b_q])
        P.dma("sp", kT[:, h, :], d["kT"][h], writes=[b_k])
    P.dma("sp", v[:], d["v"].rearrange("(n p) h e -> p n h e", p=128), writes=[b_v])
    triu, b_u, tril, b_l = tri_consts(nc, P)
    cm, b_cm = causal_masks(nc, P, strict=True)
    b_out = P.buf("o_sb_out")
    b_cout = P.buf("cout_out")

    N = NTC
    wT = P.sb("scw_sb", [128, 4, 3], F32); b_w = P.buf("scw")
    P.dma("sp", wT[:], d["scw"].rearrange("p (c k) -> p c k", c=4), writes=[b_w])
    cin = [P.sb(f"cin{i}", [128, 3, N + 2], F32) for i in range(2)]
    b_cin = [P.buf(f"cin{i}") for i in range(2)]
    vv = P.sb("cvv", [128, N + 2], F32); b_vv = P.buf("cvv")
    yy = P.sb("cyy", [128, N], F32); b_yy = P.buf("cyy")
    yo = [P.sb(f"cyo{i}", [128, N], BF16) for i in range(2)]
    b_yo = [P.buf(f"cyo{i}") for i in range(2)]
    for c in range(4):
        ci, b_ci = cin[c % 2], b_cin[c % 2]
        P.dma("sp", ci[:], d["convin"][:, c * 128:(c + 1) * 128, :].rearrange("k p n -> p k n"), writes=[b_ci])
        P.op("pool", lambda ci=ci: nc.gpsimd.tensor_tensor(out=vv[:], in0=ci[:, 1, :], in1=ci[:, 2, :], op=ALU.mult),
             reads=[b_ci], writes=[b_vv])
        P.op("dve", lambda c=c: nc.vector.tensor_scalar(out=yy[:], in0=vv[:, 0:N], scalar1=wT[:, c, 0:1], scalar2=None, op0=ALU.mult),
             reads=[b_vv, b_w], writes=[b_yy])
        for k in (1, 2):
            P.op("dve", lambda c=c, k=k: nc.vector.scalar_tensor_tensor(out=yy[:], in0=vv[:, k:N + k], scalar=wT[:, c, k:k + 1],
                                                                        in1=yy[:], op0=ALU.mult, op1=ALU.add),
                 reads=[b_vv, b_w, b_yy], writes=[b_yy])
        o, b_o = yo[c % 2], b_yo[c % 2]
        P.op("dve", lambda ci=ci, o=o: nc.vector.tensor_tensor(out=o[:], in0=yy[:], in1=ci[:, 0, 2:N + 2], op=ALU.mult),
             reads=[b_yy, b_ci], writes=[b_o])
        P.dma("sp", d["coutT"][c * 128:(c + 1) * 128, :], o[:], reads=[b_o], writes=[b_cout])

    S_ps = [P.ps(f"S{i}", [128, 2, 512], F32) for i in range(2)]
    b_S = [P.buf(f"S{i}") for i in range(2)]
    D_ps = [P.ps(f"D{h}", [128, 512], F32) for h in range(2)]
    b_D = [P.buf(f"D{h}") for h in range(2)]
    O_ps = [P.ps(f"O{h}", [64, 512], F32) for h in range(2)]
    b_O = [P.buf(f"O{h}") for h in range(2)]
    NE, NF, NA = 3, 4, 4
    e_sb = [P.sb(f"e{i}", [128, 2, 512], F32) for i in range(NE)]; b_e = [P.buf(f"e{i}") for i in range(NE)]
    sp_sb = [P.sb(f"sp{i}", [128, 2, 512], BF16) for i in range(NE)]; b_sp = [P.buf(f"sp{i}") for i in range(NE)]
    f_sb = [P.sb(f"f{i}", [128, 512], F32) for i in range(NF)]; b_f = [P.buf(f"f{i}") for i in range(NF)]
    a_sb = [P.sb(f"a{i}", [128, 512], BF16) for i in range(NA)]; b_a = [P.buf(f"a{i}") for i in range(NA)]
    oo = [P.sb(f"oo{h}", [64, 512], BF16) for h in range(2)]
    b_oo = [P.buf(f"oo{h}") for h in range(2)]
    zz = P.sb("zz", [128, 512], BF16); b_zz = P.buf("zz")
    P.op("pool", lambda: nc.gpsimd.memset(zz[:], 0.0), writes=[b_zz])
    items = []
    for qb in range(NQB):
        kmax = 4 * qb + 3
        for kb in range(kmax, -1, -1):
            for h in range(2):
                items.append(dict(qb=qb, kb=kb, h=h, diag=(kb >= 4 * qb), o=kb - 4 * qb, first=(kb == kmax), last=(kb == 0)))
    NI = len(items)
    NP = NI // 2

    def stA1(p):
        it = items[2 * p]; kb, qb = it["kb"], it["qb"]
        Sp, bS = S_ps[p % 2], b_S[p % 2]
        e, be = e_sb[p % NE], b_e[p % NE]
        for h in range(2):
            P.op("pe", lambda h=h: nc.tensor.matmul(Sp[:, h, :], lhsT=kT[:, h, kb * 128:(kb + 1) * 128], rhs=qT[:, h, qb * 512:(qb + 1) * 512],
                                                    start=True, stop=True), reads=[b_k, b_q], writes=[bS], skip_self=True)
        P.op("act", lambda: nc.scalar.activation(out=e[:], in_=Sp[:], func=AF.Exp, scale=scale), reads=[bS], writes=[be])

    def stA2(p):
        it = items[2 * p]; o = it["o"]
        e, be = e_sb[p % NE], b_e[p % NE]
        sp, bsp = sp_sb[p % NE], b_sp[p % NE]
        P.op("act", lambda: nc.scalar.activation(out=sp[:], in_=e[:], func=AF.Ln, bias=1.0), reads=[be], writes=[bsp])
        if it["diag"]:
            mb = cm[:, o, :].unsqueeze(1).to_broadcast([128, 2, 512])
            P.op("pool", lambda: nc.gpsimd.tensor_tensor(out=sp[:], in0=sp[:], in1=mb, op=ALU.mult), reads=[bsp, b_cm], writes=[bsp])
            P.op("pool", lambda: nc.gpsimd.tensor_tensor(out=e[:], in0=e[:], in1=mb, op=ALU.mult), reads=[be, b_cm], writes=[be])

    def stB1(i):
        it = items[i]; h = it["h"]
        sp, bsp = sp_sb[(i // 2) % NE], b_sp[(i // 2) % NE]
        P.op("pe", lambda: nc.tensor.matmul(D_ps[h][:], lhsT=triu[:], rhs=sp[:, h, :], start=it["first"], stop=True, skip_group_check=True),
             reads=[bsp, b_u], writes=[b_D[h]], skip_self=True)

    def stB2(i):
        it = items[i]; h = it["h"]
        f, bf_ = f_sb[i % NF], b_f[i % NF]
        P.op("act", lambda: nc.scalar.activation(out=f[:], in_=D_ps[h][:], func=AF.Exp, scale=-1.0), reads=[b_D[h]], writes=[bf_])

    def stC(i):
        it = items[i]; h = it["h"]
        sp, bsp = sp_sb[(i // 2) % NE], b_sp[(i // 2) % NE]
        e, be = e_sb[(i // 2) % NE], b_e[(i // 2) % NE]
        f, bf_ = f_sb[i % NF], b_f[i % NF]
        a, ba = a_sb[i % NA], b_a[i % NA]
        if not it["last"]:
            P.op("pe", lambda: nc.tensor.matmul(D_ps[h][:], lhsT=tril[:], rhs=sp[:, h, :], start=False, stop=True, skip_group_check=True),
                 reads=[bsp, b_l], writes=[b_D[h]], skip_self=True)
        P.op("dve", lambda: nc.vector.tensor_tensor(out=a[:], in0=e[:, h, :], in1=f[:], op=ALU.mult), reads=[be, bf_], writes=[ba])

    def stD(i):
        it = items[i]; h, kb, qb = it["h"], it["kb"], it["qb"]
        a, ba = a_sb[i % NA], b_a[i % NA]
        if it["first"]:
            P.op("pe", lambda: nc.tensor.matmul(O_ps[h][:], lhsT=zz[:, 0:64], rhs=zz[:], start=True, stop=True),
                 reads=[b_zz], writes=[b_O[h]], skip_self=True)
        P.op("pe", lambda: nc.tensor.matmul(O_ps[h][:], lhsT=v[:, kb, h, :], rhs=a[:], start=False, stop=True, skip_group_check=True),
             reads=[ba, b_v], writes=[b_O[h]], skip_self=True)
        if it["last"]:
            P.op("dve", lambda: nc.vector.tensor_copy(out=oo[h][:], in_=O_ps[h][:]), reads=[b_O[h]], writes=[b_oo[h]])
            P.dma("sp", d["o_sbT"][h, :, qb * 512:(qb + 1) * 512], oo[h][:], reads=[b_oo[h]], writes=[b_out])

    for s_ in range(-4, NI + 1):
        if s_ % 2 == 0 and 0 <= (s_ + 4) // 2 < NP:
            stA1((s_ + 4) // 2)
        if 0 <= s_ + 1 < NI:
            stB1(s_ + 1)
            stB2(s_ + 1)
        if s_ % 2 == 1 and 0 <= (s_ + 3) // 2 < NP:
            stA2((s_ + 3) // 2)
        if 0 <= s_ < NI:
            stC(s_)
        if 0 <= s_ - 1 < NI:
            stD(s_ - 1)
    return [b_out, b_cout]


def build_nsa(nc, P, T, qbs, d, banks, NOWN=4):
    scale = 64 ** -0.5
    NCP = T // 16
    NC = NCP - 1
    NCT = NCP // 128
    NKT = T // 128
    ident, b_id = make_ident(nc, P, "ident0")
    b_out = P.buf("nsa_out")

    QAq = [P.sb(f"QAq{i}", [68, 4, 512], BF16) for i in range(2)]; b_QAq = [P.buf(f"QAq{i}") for i in range(2)]
    cur = {}
    S_ps = [banks[i][0] for i in range(2)]; b_S = [banks[i][1] for i in range(2)]
    MK, b_MK = banks[2]
    A_ps = [banks[3 + i][0] for i in range(4)]; b_A = [banks[3 + i][1] for i in range(4)]
    X, b_X = banks[7]
    zz = P.sb("nzz", [128, 512], BF16); b_zz = P.buf("nzz")
    P.op("pool", lambda: nc.gpsimd.memset(zz[:], 0.0), writes=[b_zz])

    def zero_bank(i):
        P.op("pe", lambda i=i: nc.tensor.matmul(A_ps[i][:], lhsT=zz[:, 0:128], rhs=zz[:], start=True, stop=True),
             reads=[b_zz], writes=[b_A[i]], skip_self=True)

    KcA = P.sb("KcA", [68, NCP], BF16); b_KcA = P.buf("KcA")
    VcX = P.sb("VcX", [128, NCT, 193], BF16); b_VcX = P.buf("VcX")
    P.dma("sp", KcA[64:68, :], d["kaugc"], writes=[b_KcA])
    P.dma("sp", VcX[:, :, 65:193], d["poolm"].rearrange("(n p) j -> p n j", p=128), writes=[b_VcX])
    P.op("pool", lambda: nc.gpsimd.memset(VcX[:, :, 64:65], 1.0), reads=[], writes=[b_VcX])
    w1 = P.sb("w1_sb", [128, 16, 128], BF16); b_w1 = P.buf("w1")
    pe2 = P.sb("pe2_sb", [128, 16], BF16); b_pe2 = P.buf("pe2")
    w2 = P.sb("w2_sb", [128, 64], BF16); b_w2 = P.buf("w2")
    c2 = P.sb("c2_sb", [128, T], BF16); b_c2 = P.buf("c2")
    pb = P.sb("pb_sb", [128, 1], F32); b_pb = P.buf("pb")
    xh = P.sb("xh_sb", [128, NCP], F32); b_xh = P.buf("xh")
    uh = P.sb("uh_sb", [128, NCP], F32); b_uh = P.buf("uh")
    hT = P.sb("hT_sb", [128, NCP], BF16); b_hT = P.buf("hT")
    P.op("pool", lambda: nc.gpsimd.memset(hT[:], 0.0), writes=[b_hT])
    for kv in ("k", "v"):
        P.dma("pool", w1[:], d["w1" + kv], writes=[b_w1])
        P.dma("pool", pe2[:], d["pe2" + kv], writes=[b_pe2])
        P.dma("pool", w2[:], d["w2" + kv], writes=[b_w2])
        P.dma("sp", c2[:], d["c2" + kv], writes=[b_c2])
        c2v = c2[:].rearrange("p (n s) -> p n s", s=16)
        for c in range(16):
            if 2 * c < 16:
                rhs = c2v[:, 0:NC, 2 * c]
            else:
                rhs = c2v[:, 1:NC + 1, 2 * c - 16]
            P.op("pe", lambda c=c, rhs=rhs: nc.tensor.matmul(X[:, 0:NC], lhsT=w1[:, c, :], rhs=rhs, start=(c == 0), stop=(c == 15)),
                 reads=[b_w1, b_c2], writes=[b_X], skip_self=True)
        P.op("act", lambda: nc.scalar.copy(out=xh[:, 0:NC], in_=X[:, 0:NC]), reads=[b_X], writes=[b_xh])
        for c in range(16):
            P.op("pe", lambda c=c: nc.tensor.matmul(X[:, 0:1], lhsT=w1[:, c, :], rhs=pe2[:, c:c + 1], start=(c == 0), stop=(c == 15)),
                 reads=[b_w1, b_pe2], writes=[b_X], skip_self=True)
        P.op("dve", lambda: nc.vector.tensor_copy(out=pb[:], in_=X[:, 0:1]), reads=[b_X], writes=[b_pb])
        P.op("dve", lambda: nc.vector.tensor_scalar(out=xh[:, 0:NC], in0=xh[:, 0:NC], scalar1=pb[:, 0:1], scalar2=None, op0=ALU.add),
             reads=[b_xh, b_pb], writes=[b_xh])
        P.op("dve", lambda: nc.vector.tensor_tensor(out=uh[:, 0:NC], in0=xh[:, 0:NC], in1=xh[:, 0:NC], op=ALU.mult),
             reads=[b_xh], writes=[b_uh])
        P.op("dve", lambda: nc.vector.tensor_scalar(out=uh[:, 0:NC], in0=uh[:, 0:NC], scalar1=0.044715, scalar2=1.0, op0=ALU.mult, op1=ALU.add),
             reads=[b_uh], writes=[b_uh])
        P.op("dve", lambda: nc.vector.tensor_tensor(out=uh[:, 0:NC], in0=uh[:, 0:NC], in1=xh[:, 0:NC], op=ALU.mult),
             reads=[b_uh, b_xh], writes=[b_uh])
        P.op("act", lambda: nc.scalar.activation(out=uh[:, 0:NC], in_=uh[:, 0:NC], func=AF.Sigmoid, scale=2.0 * math.sqrt(2.0 / math.pi)),
             reads=[b_uh], writes=[b_uh])
        P.op("dve", lambda: nc.vector.tensor_tensor(out=hT[:, 0:NC], in0=uh[:, 0:NC], in1=xh[:, 0:NC], op=ALU.mult),
             reads=[b_uh, b_xh], writes=[b_hT])
        if kv == "k":
            P.op("pe", lambda: nc.tensor.matmul(X[0:64, 0:NCP], lhsT=w2[:], rhs=hT[:], start=True, stop=True),
                 reads=[b_w2, b_hT], writes=[b_X], skip_self=True)
            P.op("dve", lambda: nc.vector.tensor_copy(out=KcA[0:64, :], in_=X[0:64, 0:NCP]), reads=[b_X], writes=[b_KcA])
        else:
            for n in range(NCT):
                P.op("pe", lambda n=n: nc.tensor.matmul(X[:, n * 64:(n + 1) * 64], lhsT=hT[:, n * 128:(n + 1) * 128], rhs=w2[:],
                                                        start=True, stop=True), reads=[b_w2, b_hT], writes=[b_X], skip_self=True)
            P.op("dve", lambda: nc.vector.tensor_copy(out=VcX[:, :, 0:64], in_=X[:, 0:NCT * 64].rearrange("p (n e) -> p n e", e=64)),
                 reads=[b_X], writes=[b_VcX])

    KSA = P.sb("KSA_sb", [68, T], BF16); b_KSA = P.buf("KSA")
    KWA = P.sb("KWA_sb", [68, T], BF16); b_KWA = P.buf("KWA")
    P.dma("sp", KSA[:], d["KSA"], writes=[b_KSA])
    P.dma("sp", KWA[:], d["KWA"], writes=[b_KWA])
    VSA = P.sb("VSA_sb", [128, NKT, 65], BF16); b_VSA = P.buf("VSA")
    VWA = P.sb("VWA_sb", [128, NKT, 65], BF16); b_VWA = P.buf("VWA")
    P.dma("sp", VSA[:], d["VSA"], writes=[b_VSA])
    P.dma("sp", VWA[:], d["VWA"], writes=[b_VWA])
    Wsel = P.sb("Wsel", [128, T], BF16); b_Wsel = P.buf("Wsel")
    P.op("pool", lambda: nc.gpsimd.memset(Wsel[:], 1.0), writes=[b_Wsel])
    P.op("pool", lambda: nc.gpsimd.affine_select(out=Wsel[:], in_=Wsel[:], pattern=[[1, T]], compare_op=ALU.is_ge, fill=0.0,
                                                   base=0, channel_multiplier=-64), reads=[b_Wsel], writes=[b_Wsel])
    P.op("pool", lambda: nc.gpsimd.affine_select(out=Wsel[:], in_=Wsel[:], pattern=[[-1, T]], compare_op=ALU.is_ge, fill=0.0,
                                                   base=63, channel_multiplier=64), reads=[b_Wsel], writes=[b_Wsel])

    NE = 5
    e_sb = [P.sb(f"ne{i}", [128, 512], F32) for i in range(NE)]; b_e = [P.buf(f"ne{i}") for i in range(NE)]
    p_sb = [P.sb(f"np{i}", [128, 512], BF16) for i in range(NE)]; b_p = [P.buf(f"np{i}") for i in range(NE)]
    NMK = 3
    mk_sb = [P.sb(f"nmk{i}", [128, 512], BF16) for i in range(NMK)]; b_mk = [P.buf(f"nmk{i}") for i in range(NMK)]
    accC = [P.sb(f"accC{r}", [128, 4, 193], F32) for r in range(4)]; b_accC = [P.buf(f"accC{r}") for r in range(4)]
    accS = [P.sb(f"accS{r}", [128, 4, 65], F32) for r in range(NOWN)]; b_accS = [P.buf(f"accS{r}") for r in range(NOWN)]
    accW = [P.sb(f"accW{r}", [128, 4, 65], F32) for r in range(NOWN)]; b_accW = [P.buf(f"accW{r}") for r in range(NOWN)]
    rden = P.sb("rden", [128, 3, 4, 4], F32)
    b_rdenC = [P.buf(f"rdenC{r}") for r in range(4)]
    b_rdenS = [P.buf(f"rdenS{r}") for r in range(4)]
    b_rdenW = [P.buf(f"rdenW{r}") for r in range(4)]
    imp = P.sb("imp", [128, 4, 128], F32); b_imp = P.buf("imp")
    M1 = P.sb("M1_sb", [128, 4, 128], F32); b_M1 = P.buf("M1")
    A1 = P.sb("A1_sb", [128, 4, 128], F32); b_A1 = P.buf("A1")
    score = [P.sb(f"score{i}", [128, 128], F32) for i in range(2)]; b_score = [P.buf(f"score{i}") for i in range(2)]
    work = [P.sb(f"work{i}", [128, 128], F32) for i in range(2)]; b_work = [P.buf(f"work{i}") for i in range(2)]
    m8 = [P.sb(f"m8{i}", [128, 16], F32) for i in range(2)]; b_m8 = [P.buf(f"m8{i}") for i in range(2)]
    sel = P.sb("sel", [128, 4, 128], BF16); b_sel = P.buf("sel")
    selT = P.sb("selT", [128, 512], BF16); b_selT = P.buf("selT")
    graw2 = [P.sb(f"graw_sb{i}", [128, 4, 3 * NOWN], F32) for i in range(2)]; b_graw2 = [P.buf(f"graw{i}") for i in range(2)]
    gsig2 = [P.sb(f"gsig{i}", [128, 4, 3 * NOWN], F32) for i in range(2)]; b_gsig2 = [P.buf(f"gsig{i}") for i in range(2)]
    wgt = P.sb("wgt", [128, 3, 4, 4], F32); b_wgt = P.buf("wgt")
    oacc = P.sb("oacc", [128, 4, 64 * NOWN], F32); b_oacc = P.buf("oacc")
    obf = P.sb("obf", [128, 4, 64 * NOWN], BF16); b_obf = P.buf("obf")

    negm = P.sb("negm", [128, 4, 512], BF16); b_negm = P.buf("negm")
    P.op("pool", lambda: nc.gpsimd.memset(negm[:], 0.0), writes=[b_negm])
    for o_ in range(4):
        P.op("pool", lambda o_=o_: nc.gpsimd.affine_select(out=negm[:, o_, :], in_=negm[:, o_, :], pattern=[[1, 512]], compare_op=ALU.is_ge,
                                                            fill=-30000.0, base=-128 * o_, channel_multiplier=-1), reads=[b_negm], writes=[b_negm])

    items = []
    for qb in qbs:
        nct = min(NCT, (32 * (qb + 1) + 127) // 128)
        for r in range(4):
            for kt in range(nct):
                items.append(dict(kind="C", qb=qb, kt=kt, r=r, first=(kt == 0), last=(kt == nct - 1), qbstart=(r == 0 and kt == 0)))
        kw0 = max(0, 4 * qb - 4)
        for kt in range(kw0, 4 * qb + 4):
            for r in range(NOWN):
                items.append(dict(kind="W", qb=qb, kt=kt, r=r, first=(kt == kw0), last=(kt == 4 * qb + 3), qbstart=False))
        for kt in range(0, 4 * qb + 4):
            for r in range(NOWN):
                items.append(dict(kind="S", qb=qb, kt=kt, r=r, first=(kt == 0), last=(kt == 4 * qb + 3), qbstart=False))
    NI = len(items)

    def banks_of(it):
        r = it["r"]
        if it["kind"] == "C":
            return [2 * (r % 2), 2 * (r % 2) + 1]
        if it["kind"] == "W":
            return [r]
        return [2 + r] if NOWN == 2 else [r]

    def load_qb(qb_):
        q0_ = qb_ * 512
        qi_ = qb_ % 2
        for rr in range(4):
            P.dma("sp", QAq[qi_][:, rr, :], d["QA"][rr][:, q0_:q0_ + 512], writes=[b_QAq[qi_]])
        P.dma("sp", M1[:], d["M1"][q0_:q0_ + 512, :].rearrange("(c p) j -> p c j", p=128), writes=[b_M1])
        P.dma("sp", A1[:], d["A1"][q0_:q0_ + 512, :].rearrange("(c p) j -> p c j", p=128), writes=[b_A1])
        graw, b_graw, gsig, b_gsig = graw2[qi_], b_graw2[qi_], gsig2[qi_], b_gsig2[qi_]
        P.dma("sp", graw[:], d["graw"][q0_:q0_ + 512, :].rearrange("(c p) j -> p c j", p=128), writes=[b_graw])
        P.op("act", lambda: nc.scalar.activation(out=gsig[:], in_=graw[:], func=AF.Sigmoid), reads=[b_graw], writes=[b_gsig])

    def stA(i):
        it = items[i]; kind, qb, kt, r = it["kind"], it["qb"], it["kt"], it["r"]
        q0 = qb * 512
        qi = qb % 2
        if it["qbstart"] and qb == qbs[0]:
            load_qb(qb)
        diag = kt >= 4 * qb
        if kind == "S" and r == 0 and kt == 0:
            selection_pe()
            nxt = qbs.index(qb) + 1
            if nxt < len(qbs):
                load_qb(qbs[nxt])
        if kind == "S" and r == 0:
            MKc, bMKc = (MK, b_MK) if kt % 2 == 0 else (X, b_X)
            P.op("pe", lambda: nc.tensor.matmul(MKc[:], lhsT=Wsel[:, kt * 128:(kt + 1) * 128], rhs=selT[:], start=True, stop=True),
                 reads=[b_Wsel, b_selT], writes=[bMKc], skip_self=True)
        KA, bKA = {"C": (KcA, b_KcA), "W": (KWA, b_KWA), "S": (KSA, b_KSA)}[kind]
        clamp = True if kind == "C" else diag
        si = i % 2
        ei = i % NE
        QAc, bQAc = QAq[qi], b_QAq[qi]
        addmask = diag and kind in ("W", "S")
        P.op("pe", lambda: nc.tensor.matmul(S_ps[si][:], lhsT=KA[:, kt * 128:(kt + 1) * 128], rhs=QAc[:, r, :], start=True, stop=not addmask),
             reads=[bKA, bQAc], writes=[b_S[si]], skip_self=True)
        if addmask:
            o_ = kt - 4 * qb
            P.op("pe", lambda: nc.tensor.matmul(S_ps[si][:], lhsT=ident[:], rhs=negm[:, o_, :], start=False, stop=True),
                 reads=[b_id, b_negm], writes=[b_S[si]], skip_self=True)
            if kind == "W":
                P.op("act", lambda: nc.scalar.activation(out=p_sb[ei][:], in_=S_ps[si][:], func=AF.Exp, scale=scale), reads=[b_S[si]], writes=[b_p[ei]])
            else:
                P.op("act", lambda: nc.scalar.activation(out=e_sb[ei][:], in_=S_ps[si][:], func=AF.Exp, scale=scale), reads=[b_S[si]], writes=[b_e[ei]])
        elif clamp:
            P.op("dve", lambda: nc.vector.tensor_scalar(out=e_sb[ei][:], in0=S_ps[si][:], scalar1=40.0 / scale, scalar2=None, op0=ALU.min),
                 reads=[b_S[si]], writes=[b_e[ei]])
            P.op("act", lambda: nc.scalar.activation(out=e_sb[ei][:], in_=e_sb[ei][:], func=AF.Exp, scale=scale), reads=[b_e[ei]], writes=[b_e[ei]])
        else:
            P.op("act", lambda: nc.scalar.activation(out=e_sb[ei][:], in_=S_ps[si][:], func=AF.Exp, scale=scale), reads=[b_S[si]], writes=[b_e[ei]])

    def stB(i):
        it = items[i]; kind, qb, kt, r = it["kind"], it["qb"], it["kt"], it["r"]
        q0 = qb * 512
        ei = i % NE
        diag = kt >= 4 * qb
        e, be, p, bp = e_sb[ei], b_e[ei], p_sb[ei], b_p[ei]
        if kind == "C":
            bs = -(2048 * kt + 31 - q0)
            P.op("pool", lambda: nc.gpsimd.affine_select(out=p[:], in_=e[:], pattern=[[1, 512]], compare_op=ALU.is_ge, fill=0.0,
                                                          base=bs, channel_multiplier=-16), reads=[be], writes=[bp])
        elif kind == "W":
            if diag:
                pass
            else:
                bs = 511 - (q0 - 128 * kt)
                P.op("pool", lambda: nc.gpsimd.affine_select(out=p[:], in_=e[:], pattern=[[-1, 512]], compare_op=ALU.is_ge, fill=0.0,
                                                              base=bs, channel_multiplier=1), reads=[be], writes=[bp])
        else:
            mi = kt % NMK
            MKc, bMKc = (MK, b_MK) if kt % 2 == 0 else (X, b_X)
            P.op("dve", lambda: nc.vector.tensor_tensor(out=p[:], in0=e[:], in1=MKc[:], op=ALU.mult), reads=[be, bMKc], writes=[bp])

    import collections as _col
    pending = _col.deque()

    def flush(n=None):
        while pending and (n is None or n > 0):
            pending.popleft()()
            if n is not None:
                n -= 1

    def selection():
        for c in range(4):
            pending.append(lambda c=c: sel_chunk(c))

    def sel_chunk(c):
        if True:
            k = c % 2
            sc, bsc, wk, bwk, mm, bmm = score[k], b_score[k], work[k], b_work[k], m8[k], b_m8[k]
            P.op("dve", lambda c=c, sc=sc: nc.vector.tensor_tensor(out=sc[:], in0=imp[:, c, :], in1=M1[:, c, :], op=ALU.mult),
                 reads=[b_imp, b_M1], writes=[bsc])
            P.op("dve", lambda c=c, sc=sc: nc.vector.tensor_tensor(out=sc[:], in0=sc[:], in1=A1[:, c, :], op=ALU.add),
                 reads=[bsc, b_A1], writes=[bsc])
            P.op("dve", lambda sc=sc, mm=mm: nc.vector.max(out=mm[:, 0:8], in_=sc[:]), reads=[bsc], writes=[bmm])
            P.op("dve", lambda sc=sc, mm=mm, wk=wk: nc.vector.match_replace(out=wk[:], in_to_replace=mm[:, 0:8], in_values=sc[:], imm_value=-1e9),
                 reads=[bsc, bmm], writes=[bwk])
            P.op("dve", lambda mm=mm, wk=wk: nc.vector.max(out=mm[:, 8:16], in_=wk[:]), reads=[bwk], writes=[bmm])
            P.op("dve", lambda mm=mm: nc.vector.tensor_scalar(out=mm[:, 15:16], in0=mm[:, 15:16], scalar1=0.0, scalar2=None, op0=ALU.max),
                 reads=[bmm], writes=[bmm])
            P.op("dve", lambda c=c, sc=sc, mm=mm: nc.vector.tensor_scalar(out=sel[:, c, :], in0=sc[:], scalar1=mm[:, 15:16], scalar2=None, op0=ALU.is_ge),
                 reads=[bsc, bmm], writes=[b_sel])

    def selection_pe():
        flush()
        for c in range(4):
            P.op("pe", lambda c=c: nc.tensor.matmul(X[:, c * 128:(c + 1) * 128], lhsT=sel[:, c, :], rhs=ident[:], start=True, stop=True),
                 reads=[b_sel, b_id], writes=[b_X], skip_self=True)
        P.op("act", lambda: nc.scalar.copy(out=selT[:], in_=X[:]), reads=[b_X], writes=[b_selT])

    def combine(qb):
        pending.append(lambda: combine_w(qb))
        for r in range(NOWN):
            pending.append(lambda r=r: combine_r(qb, r))
        pending.append(lambda: combine_out(qb))

    def combine_w(qb):
        gsig, b_gsig = gsig2[qb % 2], b_gsig2[qb % 2]
        for br, brd in ((0, b_rdenC), (1, b_rdenS), (2, b_rdenW)):
            P.op("dve", lambda br=br: nc.vector.tensor_tensor(
                out=wgt[:, br, 0:NOWN, :], in0=rden[:, br, 0:NOWN, :],
                in1=gsig[:].rearrange("p c (r b) -> p r c b", b=3)[:, :, :, br], op=ALU.mult),
                reads=brd[0:NOWN] + [b_gsig], writes=[b_wgt])

    def combine_r(qb, r):
        if True:
            for c in range(4):
                P.op("dve", lambda r=r, c=c: nc.vector.tensor_scalar(out=oacc[:, c, r * 64:(r + 1) * 64], in0=accC[r][:, c, 0:64],
                                                                      scalar1=wgt[:, 0, r, c:c + 1], scalar2=None, op0=ALU.mult),
                     reads=[b_accC[r], b_wgt], writes=[b_oacc])
                P.op("dve", lambda r=r, c=c: nc.vector.scalar_tensor_tensor(out=oacc[:, c, r * 64:(r + 1) * 64], in0=accS[r][:, c, 0:64],
                                                                            scalar=wgt[:, 1, r, c:c + 1], in1=oacc[:, c, r * 64:(r + 1) * 64],
                                                                            op0=ALU.mult, op1=ALU.add),
                     reads=[b_accS[r], b_wgt, b_oacc], writes=[b_oacc])
                P.op("dve", lambda r=r, c=c: nc.vector.scalar_tensor_tensor(out=obf[:, c, r * 64:(r + 1) * 64], in0=accW[r][:, c, 0:64],
                                                                            scalar=wgt[:, 2, r, c:c + 1], in1=oacc[:, c, r * 64:(r + 1) * 64],
                                                                            op0=ALU.mult, op1=ALU.add),
                     reads=[b_accW[r], b_wgt, b_oacc], writes=[b_obf])

    def combine_out(qb):
        q0 = qb * 512
        P.dma("sp", d["o_nsa"][q0:q0 + 512, :].rearrange("(c p) e -> p c e", p=128), obf[:], reads=[b_obf], writes=[b_out])

    def stC(i):
        it = items[i]; kind, qb, kt, r = it["kind"], it["qb"], it["kt"], it["r"]
        ei = i % NE
        p, bp = p_sb[ei], b_p[ei]
        bks = banks_of(it)
        diag = kt >= 4 * qb
        if it["first"]:
            for bk in bks:
                zero_bank(bk)
        if kind == "C":
            for c in range(4):
                bk, off = bks[c // 2], (c % 2) * 193
                P.op("pe", lambda c=c, bk=bk, off=off: nc.tensor.matmul(A_ps[bk][:, off:off + 193], lhsT=p[:, c * 128:(c + 1) * 128], rhs=VcX[:, kt, 0:193],
                                                                        start=False, stop=True, skip_group_check=True),
                     reads=[bp, b_VcX], writes=[b_A[bk]], skip_self=True)
        else:
            VA, bVA = (VWA, b_VWA) if kind == "W" else (VSA, b_VSA)
            bk = bks[0]
            for c in (range(kt - 4 * qb, 4) if diag else range(4)):
                P.op("pe", lambda c=c: nc.tensor.matmul(A_ps[bk][:, c * 65:(c + 1) * 65], lhsT=p[:, c * 128:(c + 1) * 128], rhs=VA[:, kt, 0:65],
                                                        start=False, stop=True, skip_group_check=True),
                     reads=[bp, bVA], writes=[b_A[bk]], skip_self=True)
        if not it["last"]:
            return
        if kind == "C":
            flush()
            P.op("act", lambda: nc.scalar.copy(out=accC[r][:, 0:2, :].rearrange("p c e -> p (c e)"), in_=A_ps[bks[0]][:, 0:386]),
                 reads=[b_A[bks[0]]], writes=[b_accC[r]])
            P.op("dve", lambda: nc.vector.tensor_copy(out=accC[r][:, 2:4, :].rearrange("p c e -> p (c e)"), in_=A_ps[bks[1]][:, 0:386]),
                 reads=[b_A[bks[1]]], writes=[b_accC[r]])
            P.op("dve", lambda: nc.vector.tensor_scalar(out=rden[:, 0, r, :], in0=accC[r][:, :, 64], scalar1=1e-30, scalar2=None, op0=ALU.max),
                 reads=[b_accC[r]], writes=[b_rdenC[r]])
            P.op("dve", lambda: nc.vector.reciprocal(out=rden[:, 0, r, :], in_=rden[:, 0, r, :]), reads=[b_rdenC[r]], writes=[b_rdenC[r]])
            for c in range(4):
                if r == 0:
                    P.op("dve", lambda c=c: nc.vector.tensor_scalar(out=imp[:, c, :], in0=accC[r][:, c, 65:193], scalar1=rden[:, 0, r, c:c + 1],
                                                                     scalar2=None, op0=ALU.mult), reads=[b_accC[r], b_rdenC[r]], writes=[b_imp])
                else:
                    P.op("dve", lambda c=c: nc.vector.scalar_tensor_tensor(out=imp[:, c, :], in0=accC[r][:, c, 65:193], scalar=rden[:, 0, r, c:c + 1],
                                                                           in1=imp[:, c, :], op0=ALU.mult, op1=ALU.add),
                         reads=[b_accC[r], b_rdenC[r], b_imp], writes=[b_imp])
            if r == 3:
                selection()
        else:
            acc, bacc, brd, bri = (accW, b_accW, b_rdenW, 2) if kind == "W" else (accS, b_accS, b_rdenS, 1)
            bk = bks[0]
            if r % 2 == 0:
                P.op("act", lambda: nc.scalar.copy(out=acc[r][:].rearrange("p c e -> p (c e)"), in_=A_ps[bk][:, 0:260]), reads=[b_A[bk]], writes=[bacc[r]])
            else:
                P.op("dve", lambda: nc.vector.tensor_copy(out=acc[r][:].rearrange("p c e -> p (c e)"), in_=A_ps[bk][:, 0:260]), reads=[b_A[bk]], writes=[bacc[r]])
            P.op("dve", lambda: nc.vector.reciprocal(out=rden[:, bri, r, :], in_=acc[r][:, :, 64]), reads=[bacc[r]], writes=[brd[r]])
            if kind == "S" and r == NOWN - 1:
                combine(qb)

    for s_ in range(-2, NI):
        if 0 <= s_ + 2 < NI:
            stA(s_ + 2)
        if 0 <= s_ + 1 < NI:
            stB(s_ + 1)
        if 0 <= s_ < NI:
            stC(s_)
        flush(1)
    flush()
    return [b_out]


def alloc_banks(P):
    return [(P.ps(f"bank{i}", [128, 512], F32), P.buf(f"bank{i}")) for i in range(8)]


def build_conv(nc, P, NTC, d, banks, ident, b_id):
    N = NTC
    NTT = N // 512
    b_out = P.buf("conv_out")
    dww = P.sb("dww_sb", [128, 4, 31], F32); b_dww = P.buf("dww")
    prm = P.sb("cprm_sb", [128, 3, 4], F32); b_prm = P.buf("cprm")
    P.dma("sp", dww[:], d["dww"], writes=[b_dww])
    P.dma("sp", prm[:, 0, :], d["dwb"], writes=[b_prm])
    P.dma("sp", prm[:, 1, :], d["lng"], writes=[b_prm])
    P.dma("sp", prm[:, 2, :], d["lnb"], writes=[b_prm])
    onesF = P.sb("onesF", [128, 128], F32); b_ones = P.buf("onesF")
    P.op("pool", lambda: nc.gpsimd.memset(onesF[:], 1.0 / 512.0), writes=[b_ones])
    ain = [P.sb(f"ain{i}", [128, 2, N + 30], F32) for i in range(2)]; b_ain = [P.buf(f"ain{i}") for i in range(2)]
    abf = [P.sb(f"abf{i}", [128, N + 30], BF16) for i in range(2)]; b_abf = [P.buf(f"abf{i}") for i in range(2)]
    diag = [P.sb(f"diag{i}", [128, 31, 128], BF16) for i in range(2)]; b_diag = [P.buf(f"diag{i}") for i in range(2)]
    y = [P.sb(f"cy{c}", [128, N], F32) for c in range(4)]; b_y = [P.buf(f"cy{c}") for c in range(4)]
    ysq = P.sb("cysq", [128, 512], F32); b_ysq = P.buf("cysq")
    for c in range(4):
        ai, b_ai = ain[c % 2], b_ain[c % 2]
        ab, b_ab = abf[c % 2], b_abf[c % 2]
        dg, b_dg = diag[c % 2], b_diag[c % 2]
        P.dma("sp", ai[:], d["aT"][:, c * 128:(c + 1) * 128, :].rearrange("k p n -> p k n"), writes=[b_ai])
        P.op("act", lambda ai=ai: nc.scalar.activation(out=ai[:, 1, :], in_=ai[:, 1, :], func=AF.Sigmoid), reads=[b_ai], writes=[b_ai])
        P.op("dve", lambda ai=ai, ab=ab: nc.vector.tensor_tensor(out=ab[:], in0=ai[:, 0, :], in1=ai[:, 1, :], op=ALU.mult),
             reads=[b_ai], writes=[b_ab])
        for k in range(31):
            P.op("dve", lambda k=k, c=c, dg=dg: nc.vector.tensor_scalar(out=dg[:, k, :], in0=ident[:], scalar1=dww[:, c, k:k + 1], scalar2=None,
                                                                         op0=ALU.mult), reads=[b_id, b_dww], writes=[b_dg])
        for tt in range(NTT):
            ps, b_ps = banks[tt % 2]
            for k in range(31):
                P.op("pe", lambda k=k, tt=tt, ps=ps, dg=dg, ab=ab: nc.tensor.matmul(ps[:], lhsT=dg[:, k, :], rhs=ab[:, tt * 512 + k: tt * 512 + k + 512],
                                                                                      start=(k == 0), stop=(k == 30)),
                     reads=[b_dg, b_ab], writes=[b_ps], skip_self=True)
            P.op("act", lambda c=c, tt=tt, ps=ps: nc.scalar.activation(out=y[c][:, tt * 512:(tt + 1) * 512], in_=ps[:], func=AF.Identity,
                                                                        bias=prm[:, 0, c:c + 1]), reads=[b_ps, b_prm], writes=[b_y[c]])
    mean = P.sb("cmean", [128, 512], F32); b_mean = P.buf("cmean")
    rstd = P.sb("crstd", [128, 512], F32); b_rstd = P.buf("crstd")
    yn = [P.sb(f"cyn{i}", [128, 512], F32) for i in range(2)]; b_yn = [P.buf(f"cyn{i}") for i in range(2)]
    co = [P.sb(f"cco{i}", [128, 512], BF16) for i in range(2)]; b_co = [P.buf(f"cco{i}") for i in range(2)]
    it = 0
    for tt in range(NTT):
        sl = slice(tt * 512, (tt + 1) * 512)
        pm, b_pm = banks[2]
        pq, b_pq = banks[3]
        for c in range(4):
            P.op("pe", lambda c=c, sl=sl: nc.tensor.matmul(pm[:], lhsT=onesF[:], rhs=y[c][:, sl], start=(c == 0), stop=(c == 3)),
                 reads=[b_ones, b_y[c]], writes=[b_pm], skip_self=True)
        for c in range(4):
            P.op("act", lambda c=c, sl=sl: nc.scalar.activation(out=ysq[:], in_=y[c][:, sl], func=AF.Square), reads=[b_y[c]], writes=[b_ysq])
            P.op("pe", lambda c=c: nc.tensor.matmul(pq[:], lhsT=onesF[:], rhs=ysq[:], start=(c == 0), stop=(c == 3)),
                 reads=[b_ones, b_ysq], writes=[b_pq], skip_self=True)
        P.op("dve", lambda: nc.vector.tensor_copy(out=mean[:], in_=pm[:]), reads=[b_pm], writes=[b_mean])
        P.op("dve", lambda: nc.vector.tensor_tensor(out=rstd[:], in0=mean[:], in1=mean[:], op=ALU.mult), reads=[b_mean], writes=[b_rstd])
        P.op("dve", lambda: nc.vector.tensor_tensor(out=rstd[:], in0=pq[:], in1=rstd[:], op=ALU.subtract), reads=[b_pq, b_rstd], writes=[b_rstd])
        P.op("act", lambda: nc.scalar.activation(out=rstd[:], in_=rstd[:], func=AF.Sqrt, bias=1e-5), reads=[b_rstd], writes=[b_rstd])
        P.op("dve", lambda: nc.vector.reciprocal(out=rstd[:], in_=rstd[:]), reads=[b_rstd], writes=[b_rstd])
        for c in range(4):
            i = it % 2
            it += 1
            P.op("dve", lambda c=c, sl=sl, i=i: nc.vector.tensor_tensor(out=yn[i][:], in0=y[c][:, sl], in1=mean[:], op=ALU.subtract),
                 reads=[b_y[c], b_mean], writes=[b_yn[i]])
            P.op("dve", lambda i=i: nc.vector.tensor_tensor(out=yn[i][:], in0=yn[i][:], in1=rstd[:], op=ALU.mult),
                 reads=[b_yn[i], b_rstd], writes=[b_yn[i]])
            P.op("act", lambda c=c, i=i: nc.scalar.activation(out=co[i][:], in_=yn[i][:], func=AF.Silu, scale=prm[:, 1, c:c + 1],
                                                               bias=prm[:, 2, c:c + 1]), reads=[b_yn[i], b_prm], writes=[b_co[i]])
            P.dma("sp", d["coutT"][c * 128:(c + 1) * 128, sl], co[i][:], reads=[b_co[i]], writes=[b_out])
    return [b_out]

bf = ml_dtypes.bfloat16

def nsa_consts(T):
    t = np.arange(T)
    j = np.arange(128)
    vis = (j[None, :] * 64 <= t[:, None])
    cur = t // 64
    forced = (j[None, :] == 0) | (j[None, :] == cur[:, None]) | (j[None, :] == cur[:, None] - 1)
    M1 = (vis & ~forced).astype(np.float32)
    A1 = np.where(vis, np.where(forced, 1e4, 0.0), -1.0).astype(np.float32)
    NS = T // 64
    M1[:, NS:] = 0.0; A1[:, NS:] = -1.0
    n = np.arange(512)
    poolm = ((n[:, None] >= 4 * j[None, :] - 1) & (n[:, None] <= 4 * j[None, :] + 3)).astype(np.float32).astype(bf)
    kaug_tok = np.stack([t // 64, t % 64, np.ones(T), np.ones(T)]).astype(np.float32).astype(bf)
    kaugc = np.stack([n // 4, 16 * (n % 4) + 15.5, np.ones(512), np.ones(512)]).astype(np.float32).astype(bf)[:, :T // 16]
    return dict(M1=M1, A1=A1, poolm=poolm[:T // 16], kaug_tok=kaug_tok, kaugc=kaugc)

def q_aug(T, h):
    t = np.arange(T)
    c = (2.0 ** (-(h + 1))) * 8.0
    return np.stack([np.full(T, 64 * c), np.full(T, c), -64 * c * (t // 64), -c * (t % 64)]).astype(np.float32).astype(bf)

def nsa_inputs(T, g, qT, kcT, vcT, ksT, kwT, vs, vw, graw, w, consts, horder=(0, 1, 2, 3)):
    d = {}
    QA = np.zeros((4, 68, T), dtype=bf)
    for r in range(4):
        h = 4 * g + horder[r]
        QA[r, :64] = qT[h * 64:(h + 1) * 64]
        QA[r, 64:] = q_aug(T, h)
    d["QA"] = QA
    for nm, src in (("KSA", ksT), ("KWA", kwT)):
        a = np.zeros((68, T), dtype=bf)
        a[:64] = src[g * 64:(g + 1) * 64]
        a[64:] = consts["kaug_tok"]
        d[nm] = a
    for nm, src in (("VSA", vs), ("VWA", vw)):
        a = np.ones((T, 65), dtype=bf)
        a[:, :64] = src[:, g * 64:(g + 1) * 64]
        d[nm] = np.ascontiguousarray(a.reshape(T // 128, 128, 65).transpose(1, 0, 2))
    for nm, src in (("c2k", kcT), ("c2v", vcT)):
        a = np.zeros((128, T), dtype=bf)
        a[:64] = src[g * 64:(g + 1) * 64]
        a[64:, :T - 1] = src[g * 64:(g + 1) * 64, 1:]
        d[nm] = a
    for kv in ("k", "v"):
        w1 = np.asarray(w["w1_" + kv], dtype=np.float32)
        d["w1" + kv] = np.ascontiguousarray(w1.reshape(16, 2, 64, 128).transpose(1, 2, 0, 3).reshape(128, 16, 128))
        pe = np.asarray(w["pe_" + kv], dtype=np.float32)
        d["pe2" + kv] = np.ascontiguousarray(pe.reshape(16, 2, 64).transpose(1, 2, 0).reshape(128, 16))
        d["w2" + kv] = np.ascontiguousarray(np.asarray(w["w2_" + kv], dtype=np.float32))
    d["kaugc"] = consts["kaugc"]
    d["poolm"] = consts["poolm"]
    d["M1"] = consts["M1"]; d["A1"] = consts["A1"]
    h0 = 4 * g + horder[0]
    d["graw"] = np.ascontiguousarray(graw[:, h0 * 3:(h0 + 2) * 3])
    return d


T_SEQ = 8192
NTOK = 2048
NCORE = 8


def _launch(nc, in_maps):
    res = run_bass_kernel_spmd(nc, in_maps, core_ids=list(range(NCORE)))
    return res.results


def _mk(nc, d, name, shape, dt, out=False):
    d[name] = nc.dram_tensor(name, list(shape), dt, kind="ExternalOutput" if out else "ExternalInput").ap()
    return d[name]


def _build_dense(kind):
    nc = bass.Bass("TRN2", target_bir_lowering=False)
    d = {}
    NT = NTOK
    _mk(nc, d, "x", [NT, 1024], F32)
    if kind in ("B", "C"):
        _mk(nc, d, "oT", [1024, NT], BF16)
        _mk(nc, d, "wo", [1024, 1024], F32)
    nffn = {"A": 1, "B": 2, "C": 1}[kind]
    for i in range(nffn):
        _mk(nc, d, f"fg{i}", [1024], F32)
        _mk(nc, d, f"fwi{i}", [1024, 5632], F32)
        _mk(nc, d, f"fwo{i}", [2816, 1024], F32)
    if kind == "A":
        _mk(nc, d, "pg", [1024], F32); _mk(nc, d, "pw", [1024, 2328], F32)
        _mk(nc, d, "xo", [NT, 1024], F32, True)
        _mk(nc, d, "aT", [1024, NT], F32, True); _mk(nc, d, "qT", [512, NT], BF16, True)
        for n in ("kcT", "vcT", "ksT", "kwT"):
            _mk(nc, d, n, [128, NT], BF16, True)
        _mk(nc, d, "vs", [NT, 128], BF16, True); _mk(nc, d, "vw", [NT, 128], BF16, True)
        _mk(nc, d, "gg", [NT, 24], F32, True)
    elif kind == "B":
        _mk(nc, d, "pg", [1024], F32); _mk(nc, d, "pw", [1024, 3072], F32)
        _mk(nc, d, "xo", [NT, 1024], F32, True)
        _mk(nc, d, "cT", [1536, NT], F32, True); _mk(nc, d, "qT", [512, NT], BF16, True)
        _mk(nc, d, "kT", [512, NT], BF16, True); _mk(nc, d, "v", [NT, 512], BF16, True)
    else:
        _mk(nc, d, "gfin", [1024], F32)
        _mk(nc, d, "out", [NT, 1024], F32, True)
    with ExitStack() as es:
        P = Prog(nc, es)
        dn = Dense(nc, P, NT)
        outs = []
        dn.load_x(d["x"])
        if kind in ("B", "C"):
            dn.outproj(d["oT"], d["wo"])
        for i in range(nffn):
            dn.ffn(d[f"fg{i}"], d[f"fwi{i}"], d[f"fwo{i}"])
        if kind == "A":
            names = [(0, 1024, "F", "aT"), (1024, 1536, "F", "qT"), (1536, 1664, "F", "kcT"), (1664, 1792, "F", "vcT"),
                     (1792, 1920, "F", "ksT"), (1920, 2048, "T", "vs"), (2048, 2176, "F", "kwT"), (2176, 2304, "T", "vw"),
                     (2304, 2328, "T", "gg")]
        elif kind == "B":
            names = [(0, 1536, "F", "cT"), (1536, 2048, "F", "qT"), (2048, 2560, "F", "kT"), (2560, 3072, "T", "v")]
        if kind in ("A", "B"):
            bx = P.buf("xo_out")
            dn.store_x(d["xo"], bx)
            outs.append(bx)
            specs = []
            for (c0, c1, lay, n) in names:
                b = P.buf("o_" + n)
                outs.append(b)
                specs.append((c0, c1, lay, d[n], b))
            dn.proj(d["pg"], d["pw"], specs)
        else:
            dn.alloc_final()
            bo = P.buf("out_out")
            dn.final(d["gfin"], d["out"], bo)
            outs.append(bo)
        P.finish("sp", outs)
        P.emit()
    return nc


def _build_conv0():
    nc = bass.Bass("TRN2", target_bir_lowering=False)
    d = {}
    _mk(nc, d, "aT", [2, 512, NTOK + 30], F32); _mk(nc, d, "dww", [128, 4, 31], F32)
    for n in ("dwb", "lng", "lnb"):
        _mk(nc, d, n, [128, 4], F32)
    _mk(nc, d, "coutT", [512, NTOK], BF16, True)
    with ExitStack() as es:
        P = Prog(nc, es)
        banks = alloc_banks(P)
        ident, b_id = make_ident(nc, P, "identc")
        outs = build_conv(nc, P, NTOK, d, banks, ident, b_id)
        P.finish("sp", outs)
        P.emit()
    return nc


def _build_nsa():
    nc = bass.Bass("TRN2", target_bir_lowering=False)
    d = {}
    T = T_SEQ
    _mk(nc, d, "QA", [4, 68, T], BF16); _mk(nc, d, "KSA", [68, T], BF16); _mk(nc, d, "KWA", [68, T], BF16)
    _mk(nc, d, "VSA", [128, T // 128, 65], BF16); _mk(nc, d, "VWA", [128, T // 128, 65], BF16)
    _mk(nc, d, "c2k", [128, T], BF16); _mk(nc, d, "c2v", [128, T], BF16)
    for kv in "kv":
        _mk(nc, d, "w1" + kv, [128, 16, 128], F32); _mk(nc, d, "pe2" + kv, [128, 16], F32); _mk(nc, d, "w2" + kv, [128, 64], F32)
    _mk(nc, d, "kaugc", [4, T // 16], BF16); _mk(nc, d, "poolm", [T // 16, 128], BF16)
    _mk(nc, d, "M1", [T, 128], F32); _mk(nc, d, "A1", [T, 128], F32); _mk(nc, d, "graw", [T, 6], F32)
    _mk(nc, d, "o_nsa", [T, 128], BF16, True)
    with ExitStack() as es:
        P = Prog(nc, es)
        outs = build_nsa(nc, P, T, list(range(T // 512)), d, alloc_banks(P), NOWN=2)
        P.finish("sp", outs)
        P.emit()
    return nc


def _build_m1():
    nc = bass.Bass("TRN2", target_bir_lowering=False)
    d = {}
    T = T_SEQ
    _mk(nc, d, "qT", [2, 64, T], BF16); _mk(nc, d, "kT", [2, 64, T], BF16); _mk(nc, d, "v", [T, 2, 64], BF16)
    _mk(nc, d, "convin", [3, 512, NTOK + 2], F32); _mk(nc, d, "scw", [128, 12], F32)
    _mk(nc, d, "o_sbT", [2, 64, T], BF16, True); _mk(nc, d, "coutT", [512, NTOK], BF16, True)
    with ExitStack() as es:
        P = Prog(nc, es)
        outs = build_mixer1(nc, P, T, NTOK, d)
        P.finish("sp", outs)
        P.emit()
    return nc


def _cat_tok(res, name, axis):
    return [np.concatenate([np.asarray(res[b * 4 + j][name]) for j in range(4)], axis=axis) for b in range(2)]


def kernel(x, ffn1_norm, ffn1_w_in, ffn1_w_out, mix_norm, ffn2_norm, ffn2_w_in, ffn2_w_out,
           ab_w_in, conv_dw_w, conv_dw_b, conv_ln_g, conv_ln_b,
           nsa_pe_k, nsa_w1_k, nsa_w2_k, nsa_pe_v, nsa_w1_v, nsa_w2_v, ab_w_out,
           cd_w_in, sc_conv_w, cd_w_out, final_norm):
    f32 = lambda a: np.ascontiguousarray(np.asarray(a, dtype=np.float32))
    x = f32(x)
    T = T_SEQ
    xs = [np.ascontiguousarray(x[c // 4, (c % 4) * NTOK:(c % 4 + 1) * NTOK]) for c in range(NCORE)]
    common = {"fg0": f32(ffn1_norm[0]), "fwi0": f32(ffn1_w_in[0]), "fwo0": f32(ffn1_w_out[0]), "pg": f32(mix_norm[0]), "pw": f32(ab_w_in[0])}
    rA = _launch(_build_dense("A"), [dict(common, x=xs[c]) for c in range(NCORE)])
    aT = _cat_tok(rA, "aT", 1); qT = _cat_tok(rA, "qT", 1)
    kcT = _cat_tok(rA, "kcT", 1); vcT = _cat_tok(rA, "vcT", 1); ksT = _cat_tok(rA, "ksT", 1); kwT = _cat_tok(rA, "kwT", 1)
    vs = _cat_tok(rA, "vs", 0); vw = _cat_tok(rA, "vw", 0); gg = _cat_tok(rA, "gg", 0)
    lay4 = lambda v: np.ascontiguousarray(f32(v).reshape(4, 128).T)
    cc = {"dww": np.ascontiguousarray(f32(conv_dw_w[0]).reshape(31, 4, 128).transpose(2, 1, 0)),
          "dwb": lay4(conv_dw_b[0]), "lng": lay4(conv_ln_g[0]), "lnb": lay4(conv_ln_b[0])}
    maps = []
    for c in range(NCORE):
        b, j = c // 4, c % 4
        a = np.zeros((2, 512, NTOK + 30), dtype=np.float32)
        lo = j * NTOK - 30
        src = aT[b].reshape(2, 512, T)
        if lo < 0:
            a[:, :, 30:] = src[:, :, 0:NTOK]
        else:
            a[:] = src[:, :, lo:lo + NTOK + 30]
        maps.append(dict(cc, aT=a))
    rC0 = _launch(_build_conv0(), maps)
    consts = nsa_consts(T)
    w = dict(pe_k=nsa_pe_k[0], w1_k=nsa_w1_k[0], w2_k=nsa_w2_k[0], pe_v=nsa_pe_v[0], w1_v=nsa_w1_v[0], w2_v=nsa_w2_v[0])
    maps = []
    for c in range(NCORE):
        b, g, hh = c // 4, (c % 4) // 2, c % 2
        horder = [2 * hh, 2 * hh + 1, 2 * (1 - hh), 2 * (1 - hh) + 1]
        dd = nsa_inputs(T, g, qT[b], kcT[b], vcT[b], ksT[b], kwT[b], vs[b], vw[b], gg[b], w, consts, horder)
        maps.append(dd)
    rN = _launch(_build_nsa(), maps)
    oT = []
    for c in range(NCORE):
        b, j = c // 4, c % 4
        o = np.zeros((1024, NTOK), dtype=bf)
        o[0:512] = np.asarray(rC0[c]["coutT"])
        for g in range(2):
            for hh in range(2):
                src = np.asarray(rN[b * 4 + g * 2 + hh]["o_nsa"])[j * NTOK:(j + 1) * NTOK]
                r0 = 512 + (4 * g + 2 * hh) * 64
                o[r0:r0 + 128] = src.T
        oT.append(o)
    common = {"wo": f32(ab_w_out[0]), "fg0": f32(ffn2_norm[0]), "fwi0": f32(ffn2_w_in[0]), "fwo0": f32(ffn2_w_out[0]),
              "fg1": f32(ffn1_norm[1]), "fwi1": f32(ffn1_w_in[1]), "fwo1": f32(ffn1_w_out[1]), "pg": f32(mix_norm[1]), "pw": f32(cd_w_in[0])}
    rB = _launch(_build_dense("B"), [dict(common, x=np.asarray(rA[c]["xo"]), oT=oT[c]) for c in range(NCORE)])
    cT = _cat_tok(rB, "cT", 1); q1 = _cat_tok(rB, "qT", 1); k1 = _cat_tok(rB, "kT", 1); v1 = _cat_tok(rB, "v", 0)
    scw = np.ascontiguousarray(f32(sc_conv_w[0]).reshape(3, 4, 128).transpose(2, 1, 0).reshape(128, 12))
    maps = []
    for c in range(NCORE):
        b, j = c // 4, c % 4
        ci = np.zeros((3, 512, NTOK + 2), dtype=np.float32)
        src = cT[b].reshape(3, 512, T)
        lo = j * NTOK - 2
        if lo < 0:
            ci[:, :, 2:] = src[:, :, 0:NTOK]
        else:
            ci[:] = src[:, :, lo:lo + NTOK + 2]
        hp = j
        maps.append({"qT": np.ascontiguousarray(q1[b][hp * 128:(hp + 1) * 128].reshape(2, 64, T)),
                     "kT": np.ascontiguousarray(k1[b][hp * 128:(hp + 1) * 128].reshape(2, 64, T)),
                     "v": np.ascontiguousarray(v1[b][:, hp * 128:(hp + 1) * 128].reshape(T, 2, 64)),
                     "convin": ci, "scw": scw})
    rM1 = _launch(_build_m1(), maps)
    oT = []
    for c in range(NCORE):
        b, j = c // 4, c % 4
        o = np.zeros((1024, NTOK), dtype=bf)
        o[0:512] = np.asarray(rM1[c]["coutT"])
        for hp in range(4):
            src = np.asarray(rM1[b * 4 + hp]["o_sbT"]).reshape(128, T)[:, j * NTOK:(j + 1) * NTOK]
            o[512 + hp * 128:512 + (hp + 1) * 128] = src
        oT.append(o)
    common = {"wo": f32(cd_w_out[0]), "fg0": f32(ffn2_norm[1]), "fwi0": f32(ffn2_w_in[1]), "fwo0": f32(ffn2_w_out[1]), "gfin": f32(final_norm)}
    rC = _launch(_build_dense("C"), [dict(common, x=np.asarray(rB[c]["xo"]), oT=oT[c]) for c in range(NCORE)])
    out = np.zeros((2, T, 1024), dtype=np.float32)
    for c in range(NCORE):
        out[c // 4, (c % 4) * NTOK:(c % 4 + 1) * NTOK] = np.asarray(rC[c]["out"])
    return out
```

```python
import numpy as np
import math
from contextlib import ExitStack
import concourse.bass as bass
import concourse.mybir as mybir
from concourse.bass_utils import run_bass_kernel_spmd
import ml_dtypes

F32 = mybir.dt.float32
BF16 = mybir.dt.bfloat16
AF = mybir.ActivationFunctionType
ALU = mybir.AluOpType
AX = mybir.AxisListType

SEM_EPOCH = 30000


class Buf:
    __slots__ = ("name", "w", "r", "dsem", "dcnt")

    def __init__(self, name):
        self.name = name
        self.w = []
        self.r = []
        self.dsem = None
        self.dcnt = 0


class Prog:
    def __init__(self, nc, es):
        self.nc = nc
        self.es = es
        self.eng = {"pe": nc.tensor, "act": nc.scalar, "dve": nc.vector, "pool": nc.gpsimd, "sp": nc.sync}
        self.sem = {}
        self.cnt = {}
        self.waited = {k: {} for k in self.eng}
        self.nsem = 0
        for k in self.eng:
            self._new_eng_sem(k)
        self.n_inst = 0
        self.n_wait = 0
        self.q = {k: [] for k in self.eng}

    def _new_sem(self, name):
        self.nsem += 1
        return self.es.enter_context(self.nc.semaphore(f"{name}_{self.nsem}"))

    def _new_eng_sem(self, k):
        self.sem[k] = self._new_sem("e" + k)
        self.cnt[k] = 0

    def buf(self, name):
        return Buf(name)

    def sb(self, name, shape, dtype):
        t = self.es.enter_context(self.nc.sbuf_tensor(name, list(shape), dtype))
        return t

    def ps(self, name, shape, dtype):
        t = self.es.enter_context(self.nc.psum_tensor(name, list(shape), dtype))
        return t

    def _wait(self, e, conds, skip_self=False):
        eng = self.eng[e]
        wd = self.waited[e]
        best = {}
        for (s, v, owner) in conds:
            if skip_self and owner == e:
                continue
            key = id(s)
            if wd.get(key, 0) >= v:
                continue
            if key not in best or best[key][1] < v:
                best[key] = (s, v)
        for key, (s, v) in best.items():
            self.q[e].append(("w", s, v))
            wd[key] = v
            self.n_wait += 1

    def op(self, e, fn, reads=(), writes=(), skip_self=False):
        conds = []
        for b in reads:
            conds += b.w
        for b in writes:
            conds += b.w
            conds += b.r
        self._wait(e, conds, skip_self=skip_self)
        if self.cnt[e] >= SEM_EPOCH:
            self._new_eng_sem(e)
        self.cnt[e] += 1
        self.q[e].append(("i", fn, self.sem[e], 1))
        c = (self.sem[e], self.cnt[e], e)
        for b in reads:
            b.r = [x for x in b.r if x[0] is not c[0]] + [c]
        for b in writes:
            b.w = [c]
            b.r = []
        self.n_inst += 1

    def dma(self, e, out, in_, reads=(), writes=(), **kw):
        conds = []
        for b in reads:
            conds += b.w
        for b in writes:
            conds += b.w
            conds += b.r
        self._wait(e, conds)
        tgt = writes[0] if writes else reads[0]
        if tgt.dsem is None:
            tgt.dsem = self._new_sem("d" + tgt.name)
        tgt.dcnt += 1
        eng = self.eng[e]
        self.q[e].append(("i", (lambda: eng.dma_start(out=out, in_=in_, **kw)), tgt.dsem, 16))
        c = (tgt.dsem, 16 * tgt.dcnt, "dma")
        for b in reads:
            b.r = [x for x in b.r if x[0] is not c[0]] + [c]
        for b in writes:
            b.w = [x for x in b.w if x[0] is not c[0]] + [c]
            b.r = []
        self.n_inst += 1

    def dma_fn(self, e, fn, reads=(), writes=()):
        conds = []
        for b in reads:
            conds += b.w
        for b in writes:
            conds += b.w
            conds += b.r
        self._wait(e, conds)
        tgt = writes[0] if writes else reads[0]
        if tgt.dsem is None:
            tgt.dsem = self._new_sem("d" + tgt.name)
        tgt.dcnt += 1
        self.q[e].append(("i", fn, tgt.dsem, 16))
        c = (tgt.dsem, 16 * tgt.dcnt, "dma")
        for b in reads:
            b.r = [x for x in b.r if x[0] is not c[0]] + [c]
        for b in writes:
            b.w = [x for x in b.w if x[0] is not c[0]] + [c]
            b.r = []
        self.n_inst += 1

    def cc(self, kind, in_ap, out_ap, groups, reads=(), writes=()):
        nc = self.nc
        fn = lambda: nc.gpsimd.collective_compute(kind, mybir.AluOpType.bypass, replica_groups=groups, ins=[in_ap], outs=[out_ap])
        self.dma_fn("pool", fn, reads=reads, writes=writes)

    def finish(self, e, bufs):
        conds = []
        for b in bufs:
            conds += b.w
        self._wait(e, conds)

    def emit(self):
        nc = self.nc
        with nc.Block() as block:
            def run(e):
                eng = self.eng[e]
                for it in self.q[e]:
                    if it[0] == "w":
                        eng.wait_ge(it[1], it[2])
                    else:
                        it[1]().then_inc(it[2], it[3])

            @block.tensor
            def _(x):
                run("pe")

            @block.scalar
            def _(x):
                run("act")

            @block.vector
            def _(x):
                run("dve")

            @block.gpsimd
            def _(x):
                run("pool")

            @block.sync
            def _(x):
                run("sp")


D = 1024
DFF = 2816
NFC = DFF // 128


def make_ident(nc, P, name="ident"):
    ident = P.sb(name, [128, 128], BF16)
    b = P.buf(name)
    P.op("pool", lambda: nc.gpsimd.memset(ident[:], 0.0), writes=[b])
    P.op("pool", lambda: nc.gpsimd.affine_select(out=ident[:], in_=ident[:], pattern=[[-1, 128]],
                                                   compare_op=ALU.not_equal, fill=1.0, base=0,
                                                   channel_multiplier=1), reads=[b], writes=[b])
    return ident, b


class Dense:
    def __init__(self, nc, P, NT):
        self.nc, self.P, self.NT = nc, P, NT
        self.NTILE = NT // 128
        self.NST = NT // 512
        nt = self.NTILE
        self.x = P.sb("x_res", [128, nt, D], F32)
        self.b_x = [P.buf(f"x{t}") for t in range(nt)]
        self.xnT = P.sb("xnT", [128, 8, NT], BF16)
        self.b_xnT = [P.buf(f"xnT{t}") for t in range(nt)]
        self.ident, self.b_id = make_ident(nc, P)
        self.sq = P.sb("sq", [128, D], F32); self.b_sq = P.buf("sq")
        self.ss = P.sb("ss", [128, nt], F32); self.b_ss = [P.buf(f"ss{g}") for g in range(nt // 4)]
        self.rstd = P.sb("rstd", [128, nt], F32); self.b_rstd = [P.buf(f"rstd{g}") for g in range(nt // 4)]
        self.sq2 = P.sb("sq2", [128, D], F32); self.b_sq2 = P.buf("sq2")
        self.xs = [P.sb(f"xs{i}", [128, D], BF16) for i in range(2)]
        self.b_xs = [P.buf(f"xs{i}") for i in range(2)]
        self.gt = P.sb("gt", [128, 8], F32); self.b_gt = P.buf("gt")
        self.NWB = 6
        self.wb = [P.sb(f"wb{i}", [128, 8 * 512], BF16) for i in range(self.NWB)]
        self.b_wb = [P.buf(f"wb{i}") for i in range(self.NWB)]
        self.wi = 0
        self.tp = [P.ps(f"tp{i}", [128, 8, 128], BF16) for i in range(2)]; self.b_tp = [P.buf(f"tp{i}") for i in range(2)]
        self.pg = [P.ps(f"pg{i}", [128, 512], F32) for i in range(2)]; self.b_pg = [P.buf(f"pg{i}") for i in range(2)]
        self.pu = [P.ps(f"pu{i}", [128, 512], F32) for i in range(2)]; self.b_pu = [P.buf(f"pu{i}") for i in range(2)]
        self.py = [P.ps(f"py{i}", [128, 512], F32) for i in range(2)]; self.b_py = [P.buf(f"py{i}") for i in range(2)]
        self.ipg = 0
        self.ipy = 0
        self.sg = [P.sb(f"sg{i}", [128, 512], F32) for i in range(2)]; self.b_sg = [P.buf(f"sg{i}") for i in range(2)]
        self.act = [P.sb(f"actT{i}", [128, 4, 512], BF16) for i in range(2)]
        self.b_act = [P.buf(f"actT{i}") for i in range(2)]
        self.iact = 0
        self.stg = [P.sb(f"stg{i}", [128, 512], F32) for i in range(3)]
        self.b_stg = [P.buf(f"stg{i}") for i in range(3)]
        self.istg = 0
        self.gfull = None

    def next_wb(self):
        i = self.wi % self.NWB
        self.wi += 1
        return self.wb[i], self.b_wb[i]

    def load_w(self, src_ap, rc, cols):
        wb, b = self.next_wb()
        view = wb[:, 0:rc * cols].rearrange("p (c n) -> p c n", c=rc)
        self.P.dma("pool", view, src_ap.rearrange("(c p) n -> p c n", p=128), writes=[b])
        return view, b

    def load_x(self, x_dram):
        for t in range(self.NTILE):
            self.P.dma("sp", self.x[:, t, :], x_dram[t * 128:(t + 1) * 128, :], writes=[self.b_x[t]])

    def store_x(self, out_dram, b_out):
        for t in range(self.NTILE):
            self.P.dma("sp", out_dram[t * 128:(t + 1) * 128, :], self.x[:, t, :], reads=[self.b_x[t]], writes=[b_out])

    def stats_group(self, g):
        nc, P = self.nc, self.P
        for t in range(4 * g, 4 * g + 4):
            sq, b_sq = (self.sq, self.b_sq) if t % 2 == 0 else (self.sq2, self.b_sq2)
            P.op("act", lambda t=t, sq=sq: nc.scalar.activation(out=sq[:], in_=self.x[:, t, :], func=AF.Square,
                                                                 accum_out=self.ss[:, t:t + 1]),
                 reads=[self.b_x[t]], writes=[b_sq, self.b_ss[g]])
        P.op("act", lambda g=g: nc.scalar.activation(out=self.rstd[:, 4 * g:4 * g + 4], in_=self.ss[:, 4 * g:4 * g + 4], func=AF.Sqrt,
                                                     scale=1.0 / D, bias=1e-6), reads=[self.b_ss[g]], writes=[self.b_rstd[g]])
        P.op("dve", lambda g=g: nc.vector.reciprocal(out=self.rstd[:, 4 * g:4 * g + 4], in_=self.rstd[:, 4 * g:4 * g + 4]),
             reads=[self.b_rstd[g]], writes=[self.b_rstd[g]])

    def stats(self):
        for g in range(self.NTILE // 4):
            self.stats_group(g)

    def norm_T(self, g_dram):
        nc, P = self.nc, self.P
        P.dma("sp", self.gt[:], g_dram.rearrange("(c p) -> p c", p=128), writes=[self.b_gt], allow_slow_non_contiguous=True)
        self.stats_group(0)
        for t in range(self.NTILE):
            xs, b_xs = self.xs[t % 2], self.b_xs[t % 2]
            tp, b_tp = self.tp[t % 2], self.b_tp[t % 2]
            g = t // 4
            if t % 4 == 0 and g + 1 < self.NTILE // 4:
                self.stats_group(g + 1)
            if t % 2 == 0:
                P.op("act", lambda t=t, xs=xs: nc.scalar.activation(out=xs[:], in_=self.x[:, t, :], func=AF.Identity, scale=self.rstd[:, t:t + 1]),
                     reads=[self.b_x[t], self.b_rstd[g]], writes=[b_xs])
            else:
                P.op("dve", lambda t=t, xs=xs: nc.vector.tensor_scalar(out=xs[:], in0=self.x[:, t, :], scalar1=self.rstd[:, t:t + 1],
                                                                        scalar2=None, op0=ALU.mult),
                     reads=[self.b_x[t], self.b_rstd[g]], writes=[b_xs])
            for c in range(8):
                P.op("pe", lambda c=c, xs=xs, tp=tp: nc.tensor.transpose(out=tp[:, c, :], in_=xs[:, c * 128:(c + 1) * 128],
                                                                          identity=self.ident[:]),
                     reads=[b_xs, self.b_id], writes=[b_tp], skip_self=True)
            P.op("dve", lambda t=t, tp=tp: nc.vector.tensor_tensor(out=self.xnT[:, :, t * 128:(t + 1) * 128], in0=tp[:],
                                                                    in1=self.gt[:].unsqueeze(2).to_broadcast([128, 8, 128]), op=ALU.mult),
                 reads=[b_tp, self.b_gt], writes=[self.b_xnT[t]])

    def ffn(self, g_dram, w_in, w_out):
        nc, P = self.nc, self.P
        self.norm_T(g_dram)
        groups = [(s, min(4, NFC - s)) for s in range(0, NFC, 4)]
        for (fc0, nfc) in groups:
            ncol = nfc * 128
            wg, b_wg = self.load_w(w_in[:, fc0 * 128: fc0 * 128 + ncol], 8, ncol)
            wu, b_wu = self.load_w(w_in[:, DFF + fc0 * 128: DFF + fc0 * 128 + ncol], 8, ncol)
            wo, b_wo = self.load_w(w_out[fc0 * 128: fc0 * 128 + ncol, :], nfc, D)
            for st in range(self.NST):
                tiles = list(range(st * 4, st * 4 + 4))
                xb = [self.b_xnT[t] for t in tiles]
                act, b_act = self.act[self.iact % 2], self.b_act[self.iact % 2]
                self.iact += 1
                for j in range(nfc):
                    i = self.ipg % 2
                    self.ipg += 1
                    pg, b_pg, pu, b_pu = self.pg[i], self.b_pg[i], self.pu[i], self.b_pu[i]
                    sg, b_sg = self.sg[i], self.b_sg[i]
                    for k in range(8):
                        P.op("pe", lambda k=k, j=j, pg=pg, wg=wg, st=st: nc.tensor.matmul(
                            pg[:], lhsT=wg[:, k, j * 128:(j + 1) * 128], rhs=self.xnT[:, k, st * 512:(st + 1) * 512],
                            start=(k == 0), stop=(k == 7)), reads=xb + [b_wg], writes=[b_pg], skip_self=True)
                    for k in range(8):
                        P.op("pe", lambda k=k, j=j, pu=pu, wu=wu, st=st: nc.tensor.matmul(
                            pu[:], lhsT=wu[:, k, j * 128:(j + 1) * 128], rhs=self.xnT[:, k, st * 512:(st + 1) * 512],
                            start=(k == 0), stop=(k == 7)), reads=xb + [b_wu], writes=[b_pu], skip_self=True)
                    P.op("act", lambda pg=pg, sg=sg: nc.scalar.activation(out=sg[:], in_=pg[:], func=AF.Silu),
                         reads=[b_pg], writes=[b_sg])
                    P.op("dve", lambda j=j, pu=pu, sg=sg, act=act: nc.vector.tensor_tensor(out=act[:, j, :], in0=pu[:], in1=sg[:], op=ALU.mult),
                         reads=[b_pu, b_sg], writes=[b_act])
                for sub in range(4):
                    t = st * 4 + sub
                    for dh in range(2):
                        i = self.ipy % 2
                        self.ipy += 1
                        py, b_py = self.py[i], self.b_py[i]
                        for j in range(nfc):
                            P.op("pe", lambda j=j, py=py, act=act, wo=wo, sub=sub, dh=dh, nfc=nfc: nc.tensor.matmul(
                                py[:], lhsT=act[:, j, sub * 128:(sub + 1) * 128], rhs=wo[:, j, dh * 512:(dh + 1) * 512],
                                start=(j == 0), stop=(j == nfc - 1)), reads=[b_act, b_wo], writes=[b_py], skip_self=True)
                        P.op("dve", lambda t=t, dh=dh, py=py: nc.vector.scalar_tensor_tensor(
                            out=self.x[:, t, dh * 512:(dh + 1) * 512], in0=py[:], scalar=0.5, in1=self.x[:, t, dh * 512:(dh + 1) * 512],
                            op0=ALU.mult, op1=ALU.add), reads=[b_py, self.b_x[t]], writes=[self.b_x[t]])

    def outproj(self, oT_dram, w_dram):
        nc, P = self.nc, self.P
        for t in range(self.NTILE):
            P.dma("sp", self.xnT[:, :, t * 128:(t + 1) * 128],
                  oT_dram[:, t * 128:(t + 1) * 128].rearrange("(c p) n -> p c n", p=128), writes=[self.b_xnT[t]])
        for dh in range(2):
            w, b_w = self.load_w(w_dram[:, dh * 512:(dh + 1) * 512], 8, 512)
            for t in range(self.NTILE):
                i = self.ipy % 2
                self.ipy += 1
                py, b_py = self.py[i], self.b_py[i]
                for k in range(8):
                    P.op("pe", lambda k=k, py=py, w=w, t=t: nc.tensor.matmul(
                        py[:], lhsT=self.xnT[:, k, t * 128:(t + 1) * 128], rhs=w[:, k, :], start=(k == 0), stop=(k == 7)),
                        reads=[self.b_xnT[t], b_w], writes=[b_py], skip_self=True)
                P.op("dve", lambda t=t, dh=dh, py=py: nc.vector.tensor_tensor(
                    out=self.x[:, t, dh * 512:(dh + 1) * 512], in0=py[:], in1=self.x[:, t, dh * 512:(dh + 1) * 512], op=ALU.add),
                    reads=[b_py, self.b_x[t]], writes=[self.b_x[t]])

    def proj(self, g_dram, w_dram, outs):
        nc, P = self.nc, self.P
        self.norm_T(g_dram)
        for (c0, c1, layout, o_ap, b_o) in outs:
            for cs in range(c0, c1, 512):
                ce = min(cs + 512, c1)
                ncol = ce - cs
                w, b_w = self.load_w(w_dram[:, cs:ce], 8, ncol)
                if layout == "F":
                    assert ncol % 128 == 0
                    for j in range(ncol // 128):
                        for st in range(self.NST):
                            i = self.ipy % 2
                            self.ipy += 1
                            py, b_py = self.py[i], self.b_py[i]
                            xb = [self.b_xnT[t] for t in range(st * 4, st * 4 + 4)]
                            for k in range(8):
                                P.op("pe", lambda k=k, j=j, py=py, w=w, st=st: nc.tensor.matmul(
                                    py[:], lhsT=w[:, k, j * 128:(j + 1) * 128], rhs=self.xnT[:, k, st * 512:(st + 1) * 512],
                                    start=(k == 0), stop=(k == 7)), reads=xb + [b_w], writes=[b_py], skip_self=True)
                            si = self.istg % 3
                            self.istg += 1
                            stg, b_stg = self.stg[si], self.b_stg[si]
                            if o_ap.dtype == BF16:
                                sv = stg[:].bitcast(BF16)[:, 0:512]
                            else:
                                sv = stg[:]
                            eng = "act" if (self.istg % 2) else "dve"
                            if eng == "act":
                                P.op("act", lambda sv=sv, py=py: nc.scalar.copy(out=sv, in_=py[:]), reads=[b_py], writes=[b_stg])
                            else:
                                P.op("dve", lambda sv=sv, py=py: nc.vector.tensor_copy(out=sv, in_=py[:]), reads=[b_py], writes=[b_stg])
                            r0 = cs - c0 + j * 128
                            P.dma("sp", o_ap[r0:r0 + 128, st * 512:(st + 1) * 512], sv, reads=[b_stg], writes=[b_o])
                else:
                    for t in range(self.NTILE):
                        i = self.ipy % 2
                        self.ipy += 1
                        py, b_py = self.py[i], self.b_py[i]
                        for k in range(8):
                            P.op("pe", lambda k=k, py=py, w=w, t=t, ncol=ncol: nc.tensor.matmul(
                                py[:, 0:ncol], lhsT=self.xnT[:, k, t * 128:(t + 1) * 128], rhs=w[:, k, :],
                                start=(k == 0), stop=(k == 7)), reads=[self.b_xnT[t], b_w], writes=[b_py], skip_self=True)
                        si = self.istg % 3
                        self.istg += 1
                        stg, b_stg = self.stg[si], self.b_stg[si]
                        if o_ap.dtype == BF16:
                            sv = stg[:].bitcast(BF16)[:, 0:ncol]
                        else:
                            sv = stg[:, 0:ncol]
                        P.op("dve", lambda sv=sv, py=py, ncol=ncol: nc.vector.tensor_copy(out=sv, in_=py[:, 0:ncol]), reads=[b_py], writes=[b_stg])
                        P.dma("sp", o_ap[t * 128:(t + 1) * 128, cs - c0:ce - c0], sv, reads=[b_stg], writes=[b_o])

    def final(self, g_dram, out_dram, b_out):
        nc, P = self.nc, self.P
        gfull = P.sb("gfull", [128, D], F32)
        b_g = P.buf("gfull")
        P.dma("sp", gfull[:], g_dram.partition_broadcast(128), writes=[b_g])
        self.stats()
        for t in range(self.NTILE):
            si = t % 2
            o = self.fin[si]
            b_o = self.b_fin[si]
            P.op("dve", lambda t=t, o=o: nc.vector.scalar_tensor_tensor(out=o[:], in0=self.x[:, t, :], scalar=self.rstd[:, t:t + 1],
                                                                        in1=gfull[:], op0=ALU.mult, op1=ALU.mult),
                 reads=[self.b_x[t], self.b_rstd[t // 4], b_g], writes=[b_o])
            P.dma("sp", out_dram[t * 128:(t + 1) * 128, :], o[:], reads=[b_o], writes=[b_out])

    def alloc_final(self):
        P = self.P
        self.fin = [self.sq, self.sq]
        self.b_fin = [self.b_sq, self.b_sq]


def tri_consts(nc, P):
    triu = P.sb("triu", [128, 128], BF16); b_u = P.buf("triu")
    tril = P.sb("tril", [128, 128], BF16); b_l = P.buf("tril")
    P.op("pool", lambda: nc.gpsimd.memset(triu[:], 1.0), writes=[b_u])
    P.op("pool", lambda: nc.gpsimd.affine_select(out=triu[:], in_=triu[:], pattern=[[-1, 128]], compare_op=ALU.is_ge,
                                                   fill=0.0, base=0, channel_multiplier=1), reads=[b_u], writes=[b_u])
    P.op("pool", lambda: nc.gpsimd.memset(tril[:], 0.0), writes=[b_l])
    P.op("pool", lambda: nc.gpsimd.affine_select(out=tril[:], in_=tril[:], pattern=[[-1, 128]], compare_op=ALU.is_ge,
                                                   fill=1.0, base=0, channel_multiplier=1), reads=[b_l], writes=[b_l])
    return triu, b_u, tril, b_l


def causal_masks(nc, P, strict=True, dtype=BF16, name="cm"):
    m = P.sb(name, [128, 4, 512], dtype); b = P.buf(name)
    P.op("pool", lambda: nc.gpsimd.memset(m[:], 1.0), writes=[b])
    for o in range(4):
        P.op("pool", lambda o=o: nc.gpsimd.affine_select(out=m[:, o, :], in_=m[:, o, :], pattern=[[1, 512]],
                                                          compare_op=(ALU.is_gt if strict else ALU.is_ge), fill=0.0,
                                                          base=-128 * o, channel_multiplier=-1), reads=[b], writes=[b])
    return m, b


def build_mixer1(nc, P, T, NTC, d):
    scale = 64 ** -0.5
    NQB = T // 512
    NKT = T // 128
    qT = P.sb("qT_sb", [64, 2, T], BF16); b_q = P.buf("qT")
    kT = P.sb("kT_sb", [64, 2, T], BF16); b_k = P.buf("kT")
    v = P.sb("v_sb", [128, NKT, 2, 64], BF16); b_v = P.buf("v")
    for h in range(2):
        P.dma("sp", qT[:, h, :], d["qT"][h], writes=[b_q])
        P.dma("sp", kT[:, h, :], d["kT"][h], writes=[b_k])
    P.dma("sp", v[:], d["v"].rearrange("(n p) h e -> p n h e", p=128), writes=[b_v])
    triu, b_u, tril, b_l = tri_consts(nc, P)
    cm, b_cm = causal_masks(nc, P, strict=True)
    b_out = P.buf("o_sb_out")
    b_cout = P.buf("cout_out")

    N = NTC
    wT = P.sb("scw_sb", [128, 4, 3], F32); b_w = P.buf("scw")
    P.dma("sp", wT[:], d["scw"].rearrange("p (c k) -> p c k", c=4), writes=[b_w])
    cin = [P.sb(f"cin{i}", [128, 3, N + 2], F32) for i in range(2)]
    b_cin = [P.buf(f"cin{i}") for i in range(2)]
    vv = P.sb("cvv", [128, N + 2], F32); b_vv = P.buf("cvv")
    yy = P.sb("cyy", [128, N], F32); b_yy = P.buf("cyy")
    yo = [P.sb(f"cyo{i}", [128, N], BF16) for i in range(2)]
    b_yo = [P.buf(f"cyo{i}") for i in range(2)]
    for c in range(4):
        ci, b_ci = cin[c % 2], b_cin[c % 2]
        P.dma("sp", ci[:], d["convin"][:, c * 128:(c + 1) * 128, :].rearrange("k p n -> p k n"), writes=[b_ci])
        P.op("pool", lambda ci=ci: nc.gpsimd.tensor_tensor(out=vv[:], in0=ci[:, 1, :], in1=ci[:, 2, :], op=ALU.mult),
             reads=[b_ci], writes=[b_vv])
        P.op("dve", lambda c=c: nc.vector.tensor_scalar(out=yy[:], in0=vv[:, 0:N], scalar1=wT[:, c, 0:1], scalar2=None, op0=ALU.mult),
             reads=[b_vv, b_w], writes=[b_yy])
        for k in (1, 2):
            P.op("dve", lambda c=c, k=k: nc.vector.scalar_tensor_tensor(out=yy[:], in0=vv[:, k:N + k], scalar=wT[:, c, k:k + 1],
                                                                        in1=yy[:], op0=ALU.mult, op1=ALU.add),
                 reads=[b_vv, b_w, b_yy], writes=[b_yy])
        o, b_o = yo[c % 2], b_yo[c % 2]
        P.op("dve", lambda ci=ci, o=o: nc.vector.tensor_tensor(out=o[:], in0=yy[:], in1=ci[:, 0, 2:N + 2], op=ALU.mult),
             reads=[b_yy, b_ci], writes=[b_o])
        P.dma("sp", d["coutT"][c * 128:(c + 1) * 128, :], o[:], reads=[b_o], writes=[b_cout])

    S_ps = [P.ps(f"S{i}", [128, 2, 512], F32) for i in range(2)]
    b_S = [P.buf(f"S{i}") for i in range(2)]
    D_ps = [P.ps(f"D{h}", [128, 512], F32) for h in range(2)]
    b_D = [P.buf(f"D{h}") for h in range(2)]
    O_ps = [P.ps(f"O{h}", [64, 512], F32) for h in range(2)]
    b_O = [P.buf(f"O{h}") for h in range(2)]
    NE, NF, NA = 3, 4, 4
    e_sb = [P.sb(f"e{i}", [128, 2, 512], F32) for i in range(NE)]; b_e = [P.buf(f"e{i}") for i in range(NE)]
    sp_sb = [P.sb(f"sp{i}", [128, 2, 512], BF16) for i in range(NE)]; b_sp = [P.buf(f"sp{i}") for i in range(NE)]
    f_sb = [P.sb(f"f{i}", [128, 512], F32) for i in range(NF)]; b_f = [P.buf(f"f{i}") for i in range(NF)]
    a_sb = [P.sb(f"a{i}", [128, 512], BF16) for i in range(NA)]; b_a = [P.buf(f"a{i}") for i in range(NA)]
    oo = [P.sb(f"oo{h}", [64, 512], BF16) for h in range(2)]
    b_oo = [P.buf(f"oo{h}") for h in range(2)]
    zz = P.sb("zz", [128, 512], BF16); b_zz = P.buf("zz")
    P.op("pool", lambda: nc.gpsimd.memset(zz[:], 0.0), writes=[b_zz])
    items = []
    for qb in range(NQB):
        kmax = 4 * qb + 3
        for kb in range(kmax, -1, -1):
            for h in range(2):
                items.append(dict(qb=qb, kb=kb, h=h, diag=(kb >= 4 * qb), o=kb - 4 * qb, first=(kb == kmax), last=(kb == 0)))
    NI = len(items)
    NP = NI // 2

    def stA1(p):
        it = items[2 * p]; kb, qb = it["kb"], it["qb"]
        Sp, bS = S_ps[p % 2], b_S[p % 2]
        e, be = e_sb[p % NE], b_e[p % NE]
        for h in range(2):
            P.op("pe", lambda h=h: nc.tensor.matmul(Sp[:, h, :], lhsT=kT[:, h, kb * 128:(kb + 1) * 128], rhs=qT[:, h, qb * 512:(qb + 1) * 512],
                                                    start=True, stop=True), reads=[b_k, b_q], writes=[bS], skip_self=True)
        P.op("act", lambda: nc.scalar.activation(out=e[:], in_=Sp[:], func=AF.Exp, scale=scale), reads=[bS], writes=[be])

    def stA2(p):
        it = items[2 * p]; o = it["o"]
        e, be = e_sb[p % NE], b_e[p % NE]
        sp, bsp = sp_sb[p % NE], b_sp[p % NE]
        P.op("act", lambda: nc.scalar.activation(out=sp[:], in_=e[:], func=AF.Ln, bias=1.0), reads=[be], writes=[bsp])
        if it["diag"]:
            mb = cm[:, o, :].unsqueeze(1).to_broadcast([128, 2, 512])
            P.op("pool", lambda: nc.gpsimd.tensor_tensor(out=sp[:], in0=sp[:], in1=mb, op=ALU.mult), reads=[bsp, b_cm], writes=[bsp])
            P.op("pool", lambda: nc.gpsimd.tensor_tensor(out=e[:], in0=e[:], in1=mb, op=ALU.mult), reads=[be, b_cm], writes=[be])

    def stB1(i):
        it = items[i]; h = it["h"]
        sp, bsp = sp_sb[(i // 2) % NE], b_sp[(i // 2) % NE]
        P.op("pe", lambda: nc.tensor.matmul(D_ps[h][:], lhsT=triu[:], rhs=sp[:, h, :], start=it["first"], stop=True, skip_group_check=True),
             reads=[bsp, b_u], writes=[b_D[h]], skip_self=True)

    def stB2(i):
        it = items[i]; h = it["h"]
        f, bf_ = f_sb[i % NF], b_f[i % NF]
        P.op("act", lambda: nc.scalar.activation(out=f[:], in_=D_ps[h][:], func=AF.Exp, scale=-1.0), reads=[b_D[h]], writes=[bf_])

    def stC(i):
        it = items[i]; h = it["h"]
        sp, bsp = sp_sb[(i // 2) % NE], b_sp[(i // 2) % NE]
        e, be = e_sb[(i // 2) % NE], b_e[(i // 2) % NE]
        f, bf_ = f_sb[i % NF], b_f[i % NF]
        a, ba = a_sb[i % NA], b_a[i % NA]
        if not it["last"]:
            P.op("pe", lambda: nc.tensor.matmul(D_ps[h][:], lhsT=tril[:], rhs=sp[:, h, :], start=False, stop=True, skip_group_check=True),
                 reads=[bsp, b_l], writes=[b_D[h]], skip_self=True)
        P.op("dve", lambda: nc.vector.tensor_tensor(out=a[:], in0=e[:, h, :], in1=f[:], op=ALU.mult), reads=[be, bf_], writes=[ba])

    def stD(i):
        it = items[i]; h, kb, qb = it["h"], it["kb"], it["qb"]
        a, ba = a_sb[i % NA], b_a[i % NA]
        if it["first"]:
            P.op("pe", lambda: nc.tensor.matmul(O_ps[h][:], lhsT=zz[:, 0:64], rhs=zz[:], start=True, stop=True),
                 reads=[b_zz], writes=[b_O[h]], skip_self=True)
        P.op("pe", lambda: nc.tensor.matmul(O_ps[h][:], lhsT=v[:, kb, h, :], rhs=a[:], start=False, stop=True, skip_group_check=True),
             reads=[ba, b_v], writes=[b_O[h]], skip_self=True)
        if it["last"]:
            P.op("dve", lambda: nc.vector.tensor_copy(out=oo[h][:], in_=O_ps[h][:]), reads=[b_O[h]], writes=[b_oo[h]])
            P.dma("sp", d["o_sbT"][h, :, qb * 512:(qb + 1) * 512], oo[h][:], reads=[b_oo[h]], writes=[b_out])

    for s_ in range(-4, NI + 1):
        if s_ % 2 == 0 and 0 <= (s_ + 4) // 2 < NP:
            stA1((s_ + 4) // 2)
        if 0 <= s_ + 1 < NI:
            stB1(s_ + 1)
            stB2(s_ + 1)
        if s_ % 2 == 1 and 0 <= (s_ + 3) // 2 < NP:
            stA2((s_ + 3) // 2)
        if 0 <= s_ < NI:
            stC(s_)
        if 0 <= s_ - 1 < NI:
            stD(s_ - 1)
    return [b_out, b_cout]


def build_nsa(nc, P, T, qbs, d, banks, NOWN=4):
    scale = 64 ** -0.5
    NCP = T // 16
    NC = NCP - 1
    NCT = NCP // 128
    NKT = T // 128
    ident, b_id = make_ident(nc, P, "ident0")
    b_out = P.buf("nsa_out")

    QAq = [P.sb(f"QAq{i}", [68, 4, 512], BF16) for i in range(2)]; b_QAq = [P.buf(f"QAq{i}") for i in range(2)]
    cur = {}
    S_ps = [banks[i][0] for i in range(2)]; b_S = [banks[i][1] for i in range(2)]
    MK, b_MK = banks[2]
    A_ps = [banks[3 + i][0] for i in range(4)]; b_A = [banks[3 + i][1] for i in range(4)]
    X, b_X = banks[7]
    zz = P.sb("nzz", [128, 512], BF16); b_zz = P.buf("nzz")
    P.op("pool", lambda: nc.gpsimd.memset(zz[:], 0.0), writes=[b_zz])

    def zero_bank(i):
        P.op("pe", lambda i=i: nc.tensor.matmul(A_ps[i][:], lhsT=zz[:, 0:128], rhs=zz[:], start=True, stop=True),
             reads=[b_zz], writes=[b_A[i]], skip_self=True)

    KcA = P.sb("KcA", [68, NCP], BF16); b_KcA = P.buf("KcA")
    VcX = P.sb("VcX", [128, NCT, 193], BF16); b_VcX = P.buf("VcX")
    P.dma("sp", KcA[64:68, :], d["kaugc"], writes=[b_KcA])
    P.dma("sp", VcX[:, :, 65:193], d["poolm"].rearrange("(n p) j -> p n j", p=128), writes=[b_VcX])
    P.op("pool", lambda: nc.gpsimd.memset(VcX[:, :, 64:65], 1.0), reads=[], writes=[b_VcX])
    w1 = P.sb("w1_sb", [128, 16, 128], BF16); b_w1 = P.buf("w1")
    pe2 = P.sb("pe2_sb", [128, 16], BF16); b_pe2 = P.buf("pe2")
    w2 = P.sb("w2_sb", [128, 64], BF16); b_w2 = P.buf("w2")
    c2 = P.sb("c2_sb", [128, T], BF16); b_c2 = P.buf("c2")
    pb = P.sb("pb_sb", [128, 1], F32); b_pb = P.buf("pb")
    xh = P.sb("xh_sb", [128, NCP], F32); b_xh = P.buf("xh")
    uh = P.sb("uh_sb", [128, NCP], F32); b_uh = P.buf("uh")
    hT = P.sb("hT_sb", [128, NCP], BF16); b_hT = P.buf("hT")
    P.op("pool", lambda: nc.gpsimd.memset(hT[:], 0.0), writes=[b_hT])
    for kv in ("k", "v"):
        P.dma("pool", w1[:], d["w1" + kv], writes=[b_w1])
        P.dma("pool", pe2[:], d["pe2" + kv], writes=[b_pe2])
        P.dma("pool", w2[:], d["w2" + kv], writes=[b_w2])
        P.dma("sp", c2[:], d["c2" + kv], writes=[b_c2])
        c2v = c2[:].rearrange("p (n s) -> p n s", s=16)
        for c in range(16):
            if 2 * c < 16:
                rhs = c2v[:, 0:NC, 2 * c]
            else:
                rhs = c2v[:, 1:NC + 1, 2 * c - 16]
            P.op("pe", lambda c=c, rhs=rhs: nc.tensor.matmul(X[:, 0:NC], lhsT=w1[:, c, :], rhs=rhs, start=(c == 0), stop=(c == 15)),
                 reads=[b_w1, b_c2], writes=[b_X], skip_self=True)
        P.op("act", lambda: nc.scalar.copy(out=xh[:, 0:NC], in_=X[:, 0:NC]), reads=[b_X], writes=[b_xh])
        for c in range(16):
            P.op("pe", lambda c=c: nc.tensor.matmul(X[:, 0:1], lhsT=w1[:, c, :], rhs=pe2[:, c:c + 1], start=(c == 0), stop=(c == 15)),
                 reads=[b_w1, b_pe2], writes=[b_X], skip_self=True)
        P.op("dve", lambda: nc.vector.tensor_copy(out=pb[:], in_=X[:, 0:1]), reads=[b_X], writes=[b_pb])
        P.op("dve", lambda: nc.vector.tensor_scalar(out=xh[:, 0:NC], in0=xh[:, 0:NC], scalar1=pb[:, 0:1], scalar2=None, op0=ALU.add),
             reads=[b_xh, b_pb], writes=[b_xh])
        P.op("dve", lambda: nc.vector.tensor_tensor(out=uh[:, 0:NC], in0=xh[:, 0:NC], in1=xh[:, 0:NC], op=ALU.mult),
             reads=[b_xh], writes=[b_uh])
        P.op("dve", lambda: nc.vector.tensor_scalar(out=uh[:, 0:NC], in0=uh[:, 0:NC], scalar1=0.044715, scalar2=1.0, op0=ALU.mult, op1=ALU.add),
             reads=[b_uh], writes=[b_uh])
        P.op("dve", lambda: nc.vector.tensor_tensor(out=uh[:, 0:NC], in0=uh[:, 0:NC], in1=xh[:, 0:NC], op=ALU.mult),
             reads=[b_uh, b_xh], writes=[b_uh])
        P.op("act", lambda: nc.scalar.activation(out=uh[:, 0:NC], in_=uh[:, 0:NC], func=AF.Sigmoid, scale=2.0 * math.sqrt(2.0 / math.pi)),
             reads=[b_uh], writes=[b_uh])
        P.op("dve", lambda: nc.vector.tensor_tensor(out=hT[:, 0:NC], in0=uh[:, 0:NC], in1=xh[:, 0:NC], op=ALU.mult),
             reads=[b_uh, b_xh], writes=[b_hT])
        if kv == "k":
            P.op("pe", lambda: nc.tensor.matmul(X[0:64, 0:NCP], lhsT=w2[:], rhs=hT[:], start=True, stop=True),
                 reads=[b_w2, b_hT], writes=[b_X], skip_self=True)
            P.op("dve", lambda: nc.vector.tensor_copy(out=KcA[0:64, :], in_=X[0:64, 0:NCP]), reads=[b_X], writes=[b_KcA])
        else:
            for n in range(NCT):
                P.op("pe", lambda n=n: nc.tensor.matmul(X[:, n * 64:(n + 1) * 64], lhsT=hT[:, n * 128:(n + 1) * 128], rhs=w2[:],
                                                        start=True, stop=True), reads=[b_w2, b_hT], writes=[b_X], skip_self=True)
            P.op("dve", lambda: nc.vector.tensor_copy(out=VcX[:, :, 0:64], in_=X[:, 0:NCT * 64].rearrange("p (n e) -> p n e", e=64)),
                 reads=[b_X], writes=[b_VcX])

    KSA = P.sb("KSA_sb", [68, T], BF16); b_KSA = P.buf("KSA")
    KWA = P.sb("KWA_sb", [68, T], BF16); b_KWA = P.buf("KWA")
    P.dma("sp", KSA[:], d["KSA"], writes=[b_KSA])
    P.dma("sp", KWA[:], d["KWA"], writes=[b_KWA])
    VSA = P.sb("VSA_sb", [128, NKT, 65], BF16); b_VSA = P.buf("VSA")
    VWA = P.sb("VWA_sb", [128, NKT, 65], BF16); b_VWA = P.buf("VWA")
    P.dma("sp", VSA[:], d["VSA"], writes=[b_VSA])
    P.dma("sp", VWA[:], d["VWA"], writes=[b_VWA])
    Wsel = P.sb("Wsel", [128, T], BF16); b_Wsel = P.buf("Wsel")
    P.op("pool", lambda: nc.gpsimd.memset(Wsel[:], 1.0), writes=[b_Wsel])
    P.op("pool", lambda: nc.gpsimd.affine_select(out=Wsel[:], in_=Wsel[:], pattern=[[1, T]], compare_op=ALU.is_ge, fill=0.0,
                                                   base=0, channel_multiplier=-64), reads=[b_Wsel], writes=[b_Wsel])
    P.op("pool", lambda: nc.gpsimd.affine_select(out=Wsel[:], in_=Wsel[:], pattern=[[-1, T]], compare_op=ALU.is_ge, fill=0.0,
                                                   base=63, channel_multiplier=64), reads=[b_Wsel], writes=[b_Wsel])

    NE = 5
    e_sb = [P.sb(f"ne{i}", [128, 512], F32) for i in range(NE)]; b_e = [P.buf(f"ne{i}") for i in range(NE)]
    p_sb = [P.sb(f"np{i}", [128, 512], BF16) for i in range(NE)]; b_p = [P.buf(f"np{i}") for i in range(NE)]
    NMK = 3
    mk_sb = [P.sb(f"nmk{i}", [128, 512], BF16) for i in range(NMK)]; b_mk = [P.buf(f"nmk{i}") for i in range(NMK)]
    accC = [P.sb(f"accC{r}", [128, 4, 193], F32) for r in range(4)]; b_accC = [P.buf(f"accC{r}") for r in range(4)]
    accS = [P.sb(f"accS{r}", [128, 4, 65], F32) for r in range(NOWN)]; b_accS = [P.buf(f"accS{r}") for r in range(NOWN)]
    accW = [P.sb(f"accW{r}", [128, 4, 65], F32) for r in range(NOWN)]; b_accW = [P.buf(f"accW{r}") for r in range(NOWN)]
    rden = P.sb("rden", [128, 3, 4, 4], F32)
    b_rdenC = [P.buf(f"rdenC{r}") for r in range(4)]
    b_rdenS = [P.buf(f"rdenS{r}") for r in range(4)]
    b_rdenW = [P.buf(f"rdenW{r}") for r in range(4)]
    imp = P.sb("imp", [128, 4, 128], F32); b_imp = P.buf("imp")
    M1 = P.sb("M1_sb", [128, 4, 128], F32); b_M1 = P.buf("M1")
    A1 = P.sb("A1_sb", [128, 4, 128], F32); b_A1 = P.buf("A1")
    score = [P.sb(f"score{i}", [128, 128], F32) for i in range(2)]; b_score = [P.buf(f"score{i}") for i in range(2)]
    work = [P.sb(f"work{i}", [128, 128], F32) for i in range(2)]; b_work = [P.buf(f"work{i}") for i in range(2)]
    m8 = [P.sb(f"m8{i}", [128, 16], F32) for i in range(2)]; b_m8 = [P.buf(f"m8{i}") for i in range(2)]
    sel = P.sb("sel", [128, 4, 128], BF16); b_sel = P.buf("sel")
    selT = P.sb("selT", [128, 512], BF16); b_selT = P.buf("selT")
    graw2 = [P.sb(f"graw_sb{i}", [128, 4, 3 * NOWN], F32) for i in range(2)]; b_graw2 = [P.buf(f"graw{i}") for i in range(2)]
    gsig2 = [P.sb(f"gsig{i}", [128, 4, 3 * NOWN], F32) for i in range(2)]; b_gsig2 = [P.buf(f"gsig{i}") for i in range(2)]
    wgt = P.sb("wgt", [128, 3, 4, 4], F32); b_wgt = P.buf("wgt")
    oacc = P.sb("oacc", [128, 4, 64 * NOWN], F32); b_oacc = P.buf("oacc")
    obf = P.sb("obf", [128, 4, 64 * NOWN], BF16); b_obf = P.buf("obf")

    negm = P.sb("negm", [128, 4, 512], BF16); b_negm = P.buf("negm")
    P.op("pool", lambda: nc.gpsimd.memset(negm[:], 0.0), writes=[b_negm])
    for o_ in range(4):
        P.op("pool", lambda o_=o_: nc.gpsimd.affine_select(out=negm[:, o_, :], in_=negm[:, o_, :], pattern=[[1, 512]], compare_op=ALU.is_ge,
                                                            fill=-30000.0, base=-128 * o_, channel_multiplier=-1), reads=[b_negm], writes=[b_negm])

    negv = P.sb("negv", [128, 5, 512], BF16); b_negv = P.buf("negv")
    P.op("pool", lambda: nc.gpsimd.memset(negv[:], 0.0), writes=[b_negv])
    for m_ in range(5):
        P.op("pool", lambda m_=m_: nc.gpsimd.affine_select(out=negv[:, m_, :], in_=negv[:, m_, :], pattern=[[1, 512]], compare_op=ALU.is_ge,
                                                            fill=-30000.0, base=-(31 + 512 * (m_ - 4)), channel_multiplier=-16),
             reads=[b_negv], writes=[b_negv])

    items = []
    for qb in qbs:
        nct = min(NCT, (32 * (qb + 1) + 127) // 128)
        for r in range(4):
            for kt in range(nct):
                items.append(dict(kind="C", qb=qb, kt=kt, r=r, first=(kt == 0), last=(kt == nct - 1), qbstart=(r == 0 and kt == 0)))
        kw0 = max(0, 4 * qb - 4)
        for kt in range(kw0, 4 * qb + 4):
            for r in range(NOWN):
                items.append(dict(kind="W", qb=qb, kt=kt, r=r, first=(kt == kw0), last=(kt == 4 * qb + 3), qbstart=False))
        for kt in range(0, 4 * qb + 4):
            for r in range(NOWN):
                items.append(dict(kind="S", qb=qb, kt=kt, r=r, first=(kt == 0), last=(kt == 4 * qb + 3), qbstart=False))
    NI = len(items)

    def banks_of(it):
        r = it["r"]
        if it["kind"] == "C":
            return [2 * (r % 2), 2 * (r % 2) + 1]
        if it["kind"] == "W":
            return [r]
        return [2 + r] if NOWN == 2 else [r]

    def load_qb(qb_):
        q0_ = qb_ * 512
        qi_ = qb_ % 2
        for rr in range(4):
            P.dma("sp", QAq[qi_][:, rr, :], d["QA"][rr][:, q0_:q0_ + 512], writes=[b_QAq[qi_]])
        P.dma("sp", M1[:], d["M1"][q0_:q0_ + 512, :].rearrange("(c p) j -> p c j", p=128), writes=[b_M1])
        P.dma("sp", A1[:], d["A1"][q0_:q0_ + 512, :].rearrange("(c p) j -> p c j", p=128), writes=[b_A1])
        graw, b_graw, gsig, b_gsig = graw2[qi_], b_graw2[qi_], gsig2[qi_], b_gsig2[qi_]
        P.dma("sp", graw[:], d["graw"][q0_:q0_ + 512, :].rearrange("(c p) j -> p c j", p=128), writes=[b_graw])
        P.op("act", lambda: nc.scalar.activation(out=gsig[:], in_=graw[:], func=AF.Sigmoid), reads=[b_graw], writes=[b_gsig])

    def stA(i):
        it = items[i]; kind, qb, kt, r = it["kind"], it["qb"], it["kt"], it["r"]
        q0 = qb * 512
        qi = qb % 2
        if it["qbstart"] and qb == qbs[0]:
            load_qb(qb)
        diag = kt >= 4 * qb
        if kind == "S" and r == 0 and kt == 0:
            selection_pe()
            nxt = qbs.index(qb) + 1
            if nxt < len(qbs):
                load_qb(qbs[nxt])
        if kind == "S" and r == 0:
            MKc, bMKc = (MK, b_MK) if kt % 2 == 0 else (X, b_X)
            P.op("pe", lambda: nc.tensor.matmul(MKc[:], lhsT=Wsel[:, kt * 128:(kt + 1) * 128], rhs=selT[:], start=True, stop=True),
                 reads=[b_Wsel, b_selT], writes=[bMKc], skip_self=True)
        KA, bKA = {"C": (KcA, b_KcA), "W": (KWA, b_KWA), "S": (KSA, b_KSA)}[kind]
        clamp = True if kind == "C" else diag
        si = i % 2
        ei = i % NE
        QAc, bQAc = QAq[qi], b_QAq[qi]
        addmask = diag and kind in ("W", "S")
        cmask = None
        if kind == "C":
            delta = 2048 * kt + 31 - q0
            if delta > -2032:
                cmask = (delta - 31) // 512 + 4
                assert 0 <= cmask <= 4 and 31 + 512 * (cmask - 4) == delta, (delta, cmask)
        P.op("pe", lambda: nc.tensor.matmul(S_ps[si][:], lhsT=KA[:, kt * 128:(kt + 1) * 128], rhs=QAc[:, r, :], start=True,
                                            stop=not (addmask or cmask is not None)),
             reads=[bKA, bQAc], writes=[b_S[si]], skip_self=True)
        if kind == "C":
            if cmask is not None:
                P.op("pe", lambda: nc.tensor.matmul(S_ps[si][:], lhsT=ident[:], rhs=negv[:, cmask, :], start=False, stop=True),
                     reads=[b_id, b_negv], writes=[b_S[si]], skip_self=True)
            P.op("act", lambda: nc.scalar.activation(out=p_sb[ei][:], in_=S_ps[si][:], func=AF.Exp, scale=scale), reads=[b_S[si]], writes=[b_p[ei]])
        elif addmask:
            o_ = kt - 4 * qb
            P.op("pe", lambda: nc.tensor.matmul(S_ps[si][:], lhsT=ident[:], rhs=negm[:, o_, :], start=False, stop=True),
                 reads=[b_id, b_negm], writes=[b_S[si]], skip_self=True)
            if kind == "W":
                P.op("act", lambda: nc.scalar.activation(out=p_sb[ei][:], in_=S_ps[si][:], func=AF.Exp, scale=scale), reads=[b_S[si]], writes=[b_p[ei]])
            else:
                P.op("act", lambda: nc.scalar.activation(out=e_sb[ei][:], in_=S_ps[si][:], func=AF.Exp, scale=scale), reads=[b_S[si]], writes=[b_e[ei]])
        elif clamp:
            P.op("dve", lambda: nc.vector.tensor_scalar(out=e_sb[ei][:], in0=S_ps[si][:], scalar1=40.0 / scale, scalar2=None, op0=ALU.min),
                 reads=[b_S[si]], writes=[b_e[ei]])
            P.op("act", lambda: nc.scalar.activation(out=e_sb[ei][:], in_=e_sb[ei][:], func=AF.Exp, scale=scale), reads=[b_e[ei]], writes=[b_e[ei]])
        else:
            P.op("act", lambda: nc.scalar.activation(out=e_sb[ei][:], in_=S_ps[si][:], func=AF.Exp, scale=scale), reads=[b_S[si]], writes=[b_e[ei]])

    def stB(i):
        it = items[i]; kind, qb, kt, r = it["kind"], it["qb"], it["kt"], it["r"]
        q0 = qb * 512
        ei = i % NE
        diag = kt >= 4 * qb
        e, be, p, bp = e_sb[ei], b_e[ei], p_sb[ei], b_p[ei]
        if kind == "C":
            pass
        elif kind == "W":
            if diag:
                pass
            else:
                bs = 511 - (q0 - 128 * kt)
                P.op("pool", lambda: nc.gpsimd.affine_select(out=p[:], in_=e[:], pattern=[[-1, 512]], compare_op=ALU.is_ge, fill=0.0,
                                                              base=bs, channel_multiplier=1), reads=[be], writes=[bp])
        else:
            mi = kt % NMK
            MKc, bMKc = (MK, b_MK) if kt % 2 == 0 else (X, b_X)
            P.op("dve", lambda: nc.vector.tensor_tensor(out=p[:], in0=e[:], in1=MKc[:], op=ALU.mult), reads=[be, bMKc], writes=[bp])

    import collections as _col
    pending = _col.deque()

    def flush(n=None):
        while pending and (n is None or n > 0):
            pending.popleft()()
            if n is not None:
                n -= 1

    def selection():
        for c in range(4):
            pending.append(lambda c=c: sel_chunk(c))

    def sel_chunk(c):
        if True:
            k = c % 2
            sc, bsc, wk, bwk, mm, bmm = score[k], b_score[k], work[k], b_work[k], m8[k], b_m8[k]
            P.op("dve", lambda c=c, sc=sc: nc.vector.tensor_tensor(out=sc[:], in0=imp[:, c, :], in1=M1[:, c, :], op=ALU.mult),
                 reads=[b_imp, b_M1], writes=[bsc])
            P.op("dve", lambda c=c, sc=sc: nc.vector.tensor_tensor(out=sc[:], in0=sc[:], in1=A1[:, c, :], op=ALU.add),
                 reads=[bsc, b_A1], writes=[bsc])
            P.op("dve", lambda sc=sc, mm=mm: nc.vector.max(out=mm[:, 0:8], in_=sc[:]), reads=[bsc], writes=[bmm])
            P.op("dve", lambda sc=sc, mm=mm, wk=wk: nc.vector.match_replace(out=wk[:], in_to_replace=mm[:, 0:8], in_values=sc[:], imm_value=-1e9),
                 reads=[bsc, bmm], writes=[bwk])
            P.op("dve", lambda mm=mm, wk=wk: nc.vector.max(out=mm[:, 8:16], in_=wk[:]), reads=[bwk], writes=[bmm])
            P.op("dve", lambda mm=mm: nc.vector.tensor_scalar(out=mm[:, 15:16], in0=mm[:, 15:16], scalar1=0.0, scalar2=None, op0=ALU.max),
                 reads=[bmm], writes=[bmm])
            P.op("dve", lambda c=c, sc=sc, mm=mm: nc.vector.tensor_scalar(out=sel[:, c, :], in0=sc[:], scalar1=mm[:, 15:16], scalar2=None, op0=ALU.is_ge),
                 reads=[bsc, bmm], writes=[b_sel])

    def selection_pe():
        flush()
        for c in range(4):
            P.op("pe", lambda c=c: nc.tensor.matmul(X[:, c * 128:(c + 1) * 128], lhsT=sel[:, c, :], rhs=ident[:], start=True, stop=True),
                 reads=[b_sel, b_id], writes=[b_X], skip_self=True)
        P.op("act", lambda: nc.scalar.copy(out=selT[:], in_=X[:]), reads=[b_X], writes=[b_selT])

    def combine(qb):
        pending.append(lambda: combine_w(qb))
        for r in range(NOWN):
            pending.append(lambda r=r: combine_r(qb, r))
        pending.append(lambda: combine_out(qb))

    def combine_w(qb):
        gsig, b_gsig = gsig2[qb % 2], b_gsig2[qb % 2]
        for br, brd in ((0, b_rdenC), (1, b_rdenS), (2, b_rdenW)):
            P.op("dve", lambda br=br: nc.vector.tensor_tensor(
                out=wgt[:, br, 0:NOWN, :], in0=rden[:, br, 0:NOWN, :],
                in1=gsig[:].rearrange("p c (r b) -> p r c b", b=3)[:, :, :, br], op=ALU.mult),
                reads=brd[0:NOWN] + [b_gsig], writes=[b_wgt])

    def combine_r(qb, r):
        if True:
            for c in range(4):
                P.op("dve", lambda r=r, c=c: nc.vector.tensor_scalar(out=oacc[:, c, r * 64:(r + 1) * 64], in0=accC[r][:, c, 0:64],
                                                                      scalar1=wgt[:, 0, r, c:c + 1], scalar2=None, op0=ALU.mult),
                     reads=[b_accC[r], b_wgt], writes=[b_oacc])
                P.op("dve", lambda r=r, c=c: nc.vector.scalar_tensor_tensor(out=oacc[:, c, r * 64:(r + 1) * 64], in0=accS[r][:, c, 0:64],
                                                                            scalar=wgt[:, 1, r, c:c + 1], in1=oacc[:, c, r * 64:(r + 1) * 64],
                                                                            op0=ALU.mult, op1=ALU.add),
                     reads=[b_accS[r], b_wgt, b_oacc], writes=[b_oacc])
                P.op("dve", lambda r=r, c=c: nc.vector.scalar_tensor_tensor(out=obf[:, c, r * 64:(r + 1) * 64], in0=accW[r][:, c, 0:64],
                                                                            scalar=wgt[:, 2, r, c:c + 1], in1=oacc[:, c, r * 64:(r + 1) * 64],
                                                                            op0=ALU.mult, op1=ALU.add),
                     reads=[b_accW[r], b_wgt, b_oacc], writes=[b_obf])

    def combine_out(qb):
        q0 = qb * 512
        P.dma("sp", d["o_nsa"][q0:q0 + 512, :].rearrange("(c p) e -> p c e", p=128), obf[:], reads=[b_obf], writes=[b_out])

    def stC(i):
        it = items[i]; kind, qb, kt, r = it["kind"], it["qb"], it["kt"], it["r"]
        ei = i % NE
        p, bp = p_sb[ei], b_p[ei]
        bks = banks_of(it)
        diag = kt >= 4 * qb
        if it["first"]:
            for bk in bks:
                zero_bank(bk)
        if kind == "C":
            for c in range(4):
                bk, off = bks[c // 2], (c % 2) * 193
                P.op("pe", lambda c=c, bk=bk, off=off: nc.tensor.matmul(A_ps[bk][:, off:off + 193], lhsT=p[:, c * 128:(c + 1) * 128], rhs=VcX[:, kt, 0:193],
                                                                        start=False, stop=True, skip_group_check=True),
                     reads=[bp, b_VcX], writes=[b_A[bk]], skip_self=True)
        else:
            VA, bVA = (VWA, b_VWA) if kind == "W" else (VSA, b_VSA)
            bk = bks[0]
            for c in (range(kt - 4 * qb, 4) if diag else range(4)):
                P.op("pe", lambda c=c: nc.tensor.matmul(A_ps[bk][:, c * 65:(c + 1) * 65], lhsT=p[:, c * 128:(c + 1) * 128], rhs=VA[:, kt, 0:65],
                                                        start=False, stop=True, skip_group_check=True),
                     reads=[bp, bVA], writes=[b_A[bk]], skip_self=True)
        if not it["last"]:
            return
        if kind == "C":
            flush()
            P.op("act", lambda: nc.scalar.copy(out=accC[r][:, 0:2, :].rearrange("p c e -> p (c e)"), in_=A_ps[bks[0]][:, 0:386]),
                 reads=[b_A[bks[0]]], writes=[b_accC[r]])
            P.op("dve", lambda: nc.vector.tensor_copy(out=accC[r][:, 2:4, :].rearrange("p c e -> p (c e)"), in_=A_ps[bks[1]][:, 0:386]),
                 reads=[b_A[bks[1]]], writes=[b_accC[r]])
            P.op("dve", lambda: nc.vector.tensor_scalar(out=rden[:, 0, r, :], in0=accC[r][:, :, 64], scalar1=1e-30, scalar2=None, op0=ALU.max),
                 reads=[b_accC[r]], writes=[b_rdenC[r]])
            P.op("dve", lambda: nc.vector.reciprocal(out=rden[:, 0, r, :], in_=rden[:, 0, r, :]), reads=[b_rdenC[r]], writes=[b_rdenC[r]])
            for c in range(4):
                if r == 0:
                    P.op("dve", lambda c=c: nc.vector.tensor_scalar(out=imp[:, c, :], in0=accC[r][:, c, 65:193], scalar1=rden[:, 0, r, c:c + 1],
                                                                     scalar2=None, op0=ALU.mult), reads=[b_accC[r], b_rdenC[r]], writes=[b_imp])
                else:
                    P.op("dve", lambda c=c: nc.vector.scalar_tensor_tensor(out=imp[:, c, :], in0=accC[r][:, c, 65:193], scalar=rden[:, 0, r, c:c + 1],
                                                                           in1=imp[:, c, :], op0=ALU.mult, op1=ALU.add),
                         reads=[b_accC[r], b_rdenC[r], b_imp], writes=[b_imp])
            if r == 3:
                selection()
        else:
            acc, bacc, brd, bri = (accW, b_accW, b_rdenW, 2) if kind == "W" else (accS, b_accS, b_rdenS, 1)
            bk = bks[0]
            if r % 2 == 0:
                P.op("act", lambda: nc.scalar.copy(out=acc[r][:].rearrange("p c e -> p (c e)"), in_=A_ps[bk][:, 0:260]), reads=[b_A[bk]], writes=[bacc[r]])
            else:
                P.op("dve", lambda: nc.vector.tensor_copy(out=acc[r][:].rearrange("p c e -> p (c e)"), in_=A_ps[bk][:, 0:260]), reads=[b_A[bk]], writes=[bacc[r]])
            P.op("dve", lambda: nc.vector.reciprocal(out=rden[:, bri, r, :], in_=acc[r][:, :, 64]), reads=[bacc[r]], writes=[brd[r]])
            if kind == "S" and r == NOWN - 1:
                combine(qb)

    for s_ in range(-2, NI):
        if 0 <= s_ + 2 < NI:
            stA(s_ + 2)
        if 0 <= s_ + 1 < NI:
            stB(s_ + 1)
        if 0 <= s_ < NI:
            stC(s_)
        flush(1)
    flush()
    return [b_out]


def alloc_banks(P):
    return [(P.ps(f"bank{i}", [128, 512], F32), P.buf(f"bank{i}")) for i in range(8)]


def build_conv(nc, P, NTC, d, banks, ident, b_id):
    N = NTC
    NTT = N // 512
    b_out = P.buf("conv_out")
    dww = P.sb("dww_sb", [128, 4, 31], F32); b_dww = P.buf("dww")
    prm = P.sb("cprm_sb", [128, 3, 4], F32); b_prm = P.buf("cprm")
    P.dma("sp", dww[:], d["dww"], writes=[b_dww])
    P.dma("sp", prm[:, 0, :], d["dwb"], writes=[b_prm])
    P.dma("sp", prm[:, 1, :], d["lng"], writes=[b_prm])
    P.dma("sp", prm[:, 2, :], d["lnb"], writes=[b_prm])
    onesF = P.sb("onesF", [128, 128], F32); b_ones = P.buf("onesF")
    P.op("pool", lambda: nc.gpsimd.memset(onesF[:], 1.0 / 512.0), writes=[b_ones])
    ain = [P.sb(f"ain{i}", [128, 2, N + 30], F32) for i in range(2)]; b_ain = [P.buf(f"ain{i}") for i in range(2)]
    abf = [P.sb(f"abf{i}", [128, N + 30], BF16) for i in range(2)]; b_abf = [P.buf(f"abf{i}") for i in range(2)]
    diag = [P.sb(f"diag{i}", [128, 31, 128], BF16) for i in range(2)]; b_diag = [P.buf(f"diag{i}") for i in range(2)]
    y = [P.sb(f"cy{c}", [128, N], F32) for c in range(4)]; b_y = [P.buf(f"cy{c}") for c in range(4)]
    ysq = P.sb("cysq", [128, 512], F32); b_ysq = P.buf("cysq")
    for c in range(4):
        ai, b_ai = ain[c % 2], b_ain[c % 2]
        ab, b_ab = abf[c % 2], b_abf[c % 2]
        dg, b_dg = diag[c % 2], b_diag[c % 2]
        P.dma("sp", ai[:], d["aT"][:, c * 128:(c + 1) * 128, :].rearrange("k p n -> p k n"), writes=[b_ai])
        P.op("act", lambda ai=ai: nc.scalar.activation(out=ai[:, 1, :], in_=ai[:, 1, :], func=AF.Sigmoid), reads=[b_ai], writes=[b_ai])
        P.op("dve", lambda ai=ai, ab=ab: nc.vector.tensor_tensor(out=ab[:], in0=ai[:, 0, :], in1=ai[:, 1, :], op=ALU.mult),
             reads=[b_ai], writes=[b_ab])
        for k in range(31):
            P.op("dve", lambda k=k, c=c, dg=dg: nc.vector.tensor_scalar(out=dg[:, k, :], in0=ident[:], scalar1=dww[:, c, k:k + 1], scalar2=None,
                                                                         op0=ALU.mult), reads=[b_id, b_dww], writes=[b_dg])
        for tt in range(NTT):
            ps, b_ps = banks[tt % 2]
            for k in range(31):
                P.op("pe", lambda k=k, tt=tt, ps=ps, dg=dg, ab=ab: nc.tensor.matmul(ps[:], lhsT=dg[:, k, :], rhs=ab[:, tt * 512 + k: tt * 512 + k + 512],
                                                                                      start=(k == 0), stop=(k == 30)),
                     reads=[b_dg, b_ab], writes=[b_ps], skip_self=True)
            P.op("act", lambda c=c, tt=tt, ps=ps: nc.scalar.activation(out=y[c][:, tt * 512:(tt + 1) * 512], in_=ps[:], func=AF.Identity,
                                                                        bias=prm[:, 0, c:c + 1]), reads=[b_ps, b_prm], writes=[b_y[c]])
    mean = P.sb("cmean", [128, 512], F32); b_mean = P.buf("cmean")
    rstd = P.sb("crstd", [128, 512], F32); b_rstd = P.buf("crstd")
    yn = [P.sb(f"cyn{i}", [128, 512], F32) for i in range(2)]; b_yn = [P.buf(f"cyn{i}") for i in range(2)]
    co = [P.sb(f"cco{i}", [128, 512], BF16) for i in range(2)]; b_co = [P.buf(f"cco{i}") for i in range(2)]
    it = 0
    for tt in range(NTT):
        sl = slice(tt * 512, (tt + 1) * 512)
        pm, b_pm = banks[2]
        pq, b_pq = banks[3]
        for c in range(4):
            P.op("pe", lambda c=c, sl=sl: nc.tensor.matmul(pm[:], lhsT=onesF[:], rhs=y[c][:, sl], start=(c == 0), stop=(c == 3)),
                 reads=[b_ones, b_y[c]], writes=[b_pm], skip_self=True)
        for c in range(4):
            P.op("act", lambda c=c, sl=sl: nc.scalar.activation(out=ysq[:], in_=y[c][:, sl], func=AF.Square), reads=[b_y[c]], writes=[b_ysq])
            P.op("pe", lambda c=c: nc.tensor.matmul(pq[:], lhsT=onesF[:], rhs=ysq[:], start=(c == 0), stop=(c == 3)),
                 reads=[b_ones, b_ysq], writes=[b_pq], skip_self=True)
        P.op("dve", lambda: nc.vector.tensor_copy(out=mean[:], in_=pm[:]), reads=[b_pm], writes=[b_mean])
        P.op("dve", lambda: nc.vector.tensor_tensor(out=rstd[:], in0=mean[:], in1=mean[:], op=ALU.mult), reads=[b_mean], writes=[b_rstd])
        P.op("dve", lambda: nc.vector.tensor_tensor(out=rstd[:], in0=pq[:], in1=rstd[:], op=ALU.subtract), reads=[b_pq, b_rstd], writes=[b_rstd])
        P.op("act", lambda: nc.scalar.activation(out=rstd[:], in_=rstd[:], func=AF.Sqrt, bias=1e-5), reads=[b_rstd], writes=[b_rstd])
        P.op("dve", lambda: nc.vector.reciprocal(out=rstd[:], in_=rstd[:]), reads=[b_rstd], writes=[b_rstd])
        for c in range(4):
            i = it % 2
            it += 1
            P.op("dve", lambda c=c, sl=sl, i=i: nc.vector.tensor_tensor(out=yn[i][:], in0=y[c][:, sl], in1=mean[:], op=ALU.subtract),
                 reads=[b_y[c], b_mean], writes=[b_yn[i]])
            P.op("dve", lambda i=i: nc.vector.tensor_tensor(out=yn[i][:], in0=yn[i][:], in1=rstd[:], op=ALU.mult),
                 reads=[b_yn[i], b_rstd], writes=[b_yn[i]])
            P.op("act", lambda c=c, i=i: nc.scalar.activation(out=co[i][:], in_=yn[i][:], func=AF.Silu, scale=prm[:, 1, c:c + 1],
                                                               bias=prm[:, 2, c:c + 1]), reads=[b_yn[i], b_prm], writes=[b_co[i]])
            P.dma("sp", d["coutT"][c * 128:(c + 1) * 128, sl], co[i][:], reads=[b_co[i]], writes=[b_out])
    return [b_out]

bf = ml_dtypes.bfloat16

def nsa_consts(T):
    t = np.arange(T)
    j = np.arange(128)
    vis = (j[None, :] * 64 <= t[:, None])
    cur = t // 64
    forced = (j[None, :] == 0) | (j[None, :] == cur[:, None]) | (j[None, :] == cur[:, None] - 1)
    M1 = (vis & ~forced).astype(np.float32)
    A1 = np.where(vis, np.where(forced, 1e4, 0.0), -1.0).astype(np.float32)
    NS = T // 64
    M1[:, NS:] = 0.0; A1[:, NS:] = -1.0
    n = np.arange(512)
    poolm = ((n[:, None] >= 4 * j[None, :] - 1) & (n[:, None] <= 4 * j[None, :] + 3)).astype(np.float32).astype(bf)
    kaug_tok = np.stack([t // 64, t % 64, np.ones(T), np.ones(T)]).astype(np.float32).astype(bf)
    kaugc = np.stack([n // 4, 16 * (n % 4) + 15.5, np.ones(512), np.ones(512)]).astype(np.float32).astype(bf)[:, :T // 16]
    return dict(M1=M1, A1=A1, poolm=poolm[:T // 16], kaug_tok=kaug_tok, kaugc=kaugc)

def q_aug(T, h):
    t = np.arange(T)
    c = (2.0 ** (-(h + 1))) * 8.0
    return np.stack([np.full(T, 64 * c), np.full(T, c), -64 * c * (t // 64), -c * (t % 64)]).astype(np.float32).astype(bf)

def nsa_inputs(T, g, qT, kcT, vcT, ksT, kwT, vs, vw, graw, w, consts, horder=(0, 1, 2, 3)):
    d = {}
    QA = np.zeros((4, 68, T), dtype=bf)
    for r in range(4):
        h = 4 * g + horder[r]
        QA[r, :64] = qT[h * 64:(h + 1) * 64]
        QA[r, 64:] = q_aug(T, h)
    d["QA"] = QA
    for nm, src in (("KSA", ksT), ("KWA", kwT)):
        a = np.zeros((68, T), dtype=bf)
        a[:64] = src[g * 64:(g + 1) * 64]
        a[64:] = consts["kaug_tok"]
        d[nm] = a
    for nm, src in (("VSA", vs), ("VWA", vw)):
        a = np.ones((T, 65), dtype=bf)
        a[:, :64] = src[:, g * 64:(g + 1) * 64]
        d[nm] = np.ascontiguousarray(a.reshape(T // 128, 128, 65).transpose(1, 0, 2))
    for nm, src in (("c2k", kcT), ("c2v", vcT)):
        a = np.zeros((128, T), dtype=bf)
        a[:64] = src[g * 64:(g + 1) * 64]
        a[64:, :T - 1] = src[g * 64:(g + 1) * 64, 1:]
        d[nm] = a
    for kv in ("k", "v"):
        w1 = np.asarray(w["w1_" + kv], dtype=np.float32)
        d["w1" + kv] = np.ascontiguousarray(w1.reshape(16, 2, 64, 128).transpose(1, 2, 0, 3).reshape(128, 16, 128))
        pe = np.asarray(w["pe_" + kv], dtype=np.float32)
        d["pe2" + kv] = np.ascontiguousarray(pe.reshape(16, 2, 64).transpose(1, 2, 0).reshape(128, 16))
        d["w2" + kv] = np.ascontiguousarray(np.asarray(w["w2_" + kv], dtype=np.float32))
    d["kaugc"] = consts["kaugc"]
    d["poolm"] = consts["poolm"]
    d["M1"] = consts["M1"]; d["A1"] = consts["A1"]
    h0 = 4 * g + horder[0]
    d["graw"] = np.ascontiguousarray(graw[:, h0 * 3:(h0 + 2) * 3])
    return d


T_SEQ = 8192
NTOK = 2048
NCORE = 8


def _launch(nc, in_maps):
    res = run_bass_kernel_spmd(nc, in_maps, core_ids=list(range(NCORE)))
    return res.results


def _mk(nc, d, name, shape, dt, out=False):
    d[name] = nc.dram_tensor(name, list(shape), dt, kind="ExternalOutput" if out else "ExternalInput").ap()
    return d[name]


def _build_dense(kind):
    nc = bass.Bass("TRN2", target_bir_lowering=False)
    d = {}
    NT = NTOK
    _mk(nc, d, "x", [NT, 1024], F32)
    if kind in ("B", "C"):
        _mk(nc, d, "oT", [1024, NT], BF16)
        _mk(nc, d, "wo", [1024, 1024], F32)
    nffn = {"A": 1, "B": 2, "C": 1}[kind]
    for i in range(nffn):
        _mk(nc, d, f"fg{i}", [1024], F32)
        _mk(nc, d, f"fwi{i}", [1024, 5632], F32)
        _mk(nc, d, f"fwo{i}", [2816, 1024], F32)
    if kind == "A":
        _mk(nc, d, "pg", [1024], F32); _mk(nc, d, "pw", [1024, 2328], F32)
        _mk(nc, d, "xo", [NT, 1024], F32, True)
        _mk(nc, d, "aT", [1024, NT], F32, True); _mk(nc, d, "qT", [512, NT], BF16, True)
        for n in ("kcT", "vcT", "ksT", "kwT"):
            _mk(nc, d, n, [128, NT], BF16, True)
        _mk(nc, d, "vs", [NT, 128], BF16, True); _mk(nc, d, "vw", [NT, 128], BF16, True)
        _mk(nc, d, "gg", [NT, 24], F32, True)
    elif kind == "B":
        _mk(nc, d, "pg", [1024], F32); _mk(nc, d, "pw", [1024, 3072], F32)
        _mk(nc, d, "xo", [NT, 1024], F32, True)
        _mk(nc, d, "cT", [1536, NT], F32, True); _mk(nc, d, "qT", [512, NT], BF16, True)
        _mk(nc, d, "kT", [512, NT], BF16, True); _mk(nc, d, "v", [NT, 512], BF16, True)
    else:
        _mk(nc, d, "gfin", [1024], F32)
        _mk(nc, d, "out", [NT, 1024], F32, True)
    with ExitStack() as es:
        P = Prog(nc, es)
        dn = Dense(nc, P, NT)
        outs = []
        dn.load_x(d["x"])
        if kind in ("B", "C"):
            dn.outproj(d["oT"], d["wo"])
        for i in range(nffn):
            dn.ffn(d[f"fg{i}"], d[f"fwi{i}"], d[f"fwo{i}"])
        if kind == "A":
            names = [(0, 1024, "F", "aT"), (1024, 1536, "F", "qT"), (1536, 1664, "F", "kcT"), (1664, 1792, "F", "vcT"),
                     (1792, 1920, "F", "ksT"), (1920, 2048, "T", "vs"), (2048, 2176, "F", "kwT"), (2176, 2304, "T", "vw"),
                     (2304, 2328, "T", "gg")]
        elif kind == "B":
            names = [(0, 1536, "F", "cT"), (1536, 2048, "F", "qT"), (2048, 2560, "F", "kT"), (2560, 3072, "T", "v")]
        if kind in ("A", "B"):
            bx = P.buf("xo_out")
            dn.store_x(d["xo"], bx)
            outs.append(bx)
            specs = []
            for (c0, c1, lay, n) in names:
                b = P.buf("o_" + n)
                outs.append(b)
                specs.append((c0, c1, lay, d[n], b))
            dn.proj(d["pg"], d["pw"], specs)
        else:
            dn.alloc_final()
            bo = P.buf("out_out")
            dn.final(d["gfin"], d["out"], bo)
            outs.append(bo)
        P.finish("sp", outs)
        P.emit()
    return nc


def _build_conv0():
    nc = bass.Bass("TRN2", target_bir_lowering=False)
    d = {}
    _mk(nc, d, "aT", [2, 512, NTOK + 30], F32); _mk(nc, d, "dww", [128, 4, 31], F32)
    for n in ("dwb", "lng", "lnb"):
        _mk(nc, d, n, [128, 4], F32)
    _mk(nc, d, "coutT", [512, NTOK], BF16, True)
    with ExitStack() as es:
        P = Prog(nc, es)
        banks = alloc_banks(P)
        ident, b_id = make_ident(nc, P, "identc")
        outs = build_conv(nc, P, NTOK, d, banks, ident, b_id)
        P.finish("sp", outs)
        P.emit()
    return nc


def _build_nsa():
    nc = bass.Bass("TRN2", target_bir_lowering=False)
    d = {}
    T = T_SEQ
    _mk(nc, d, "QA", [4, 68, T], BF16); _mk(nc, d, "KSA", [68, T], BF16); _mk(nc, d, "KWA", [68, T], BF16)
    _mk(nc, d, "VSA", [128, T // 128, 65], BF16); _mk(nc, d, "VWA", [128, T // 128, 65], BF16)
    _mk(nc, d, "c2k", [128, T], BF16); _mk(nc, d, "c2v", [128, T], BF16)
    for kv in "kv":
        _mk(nc, d, "w1" + kv, [128, 16, 128], F32); _mk(nc, d, "pe2" + kv, [128, 16], F32); _mk(nc, d, "w2" + kv, [128, 64], F32)
    _mk(nc, d, "kaugc", [4, T // 16], BF16); _mk(nc, d, "poolm", [T // 16, 128], BF16)
    _mk(nc, d, "M1", [T, 128], F32); _mk(nc, d, "A1", [T, 128], F32); _mk(nc, d, "graw", [T, 6], F32)
    _mk(nc, d, "o_nsa", [T, 128], BF16, True)
    with ExitStack() as es:
        P = Prog(nc, es)
        outs = build_nsa(nc, P, T, list(range(T // 512)), d, alloc_banks(P), NOWN=2)
        P.finish("sp", outs)
        P.emit()
    return nc


def _build_m1():
    nc = bass.Bass("TRN2", target_bir_lowering=False)
    d = {}
    T = T_SEQ
    _mk(nc, d, "qT", [2, 64, T], BF16); _mk(nc, d, "kT", [2, 64, T], BF16); _mk(nc, d, "v", [T, 2, 64], BF16)
    _mk(nc, d, "convin", [3, 512, NTOK + 2], F32); _mk(nc, d, "scw", [128, 12], F32)
    _mk(nc, d, "o_sbT", [2, 64, T], BF16, True); _mk(nc, d, "coutT", [512, NTOK], BF16, True)
    with ExitStack() as es:
        P = Prog(nc, es)
        outs = build_mixer1(nc, P, T, NTOK, d)
        P.finish("sp", outs)
        P.emit()
    return nc


def _cat_tok(res, name, axis):
    return [np.concatenate([np.asarray(res[b * 4 + j][name]) for j in range(4)], axis=axis) for b in range(2)]


def kernel(x, ffn1_norm, ffn1_w_in, ffn1_w_out, mix_norm, ffn2_norm, ffn2_w_in, ffn2_w_out,
           ab_w_in, conv_dw_w, conv_dw_b, conv_ln_g, conv_ln_b,
           nsa_pe_k, nsa_w1_k, nsa_w2_k, nsa_pe_v, nsa_w1_v, nsa_w2_v, ab_w_out,
           cd_w_in, sc_conv_w, cd_w_out, final_norm):
    f32 = lambda a: np.ascontiguousarray(np.asarray(a, dtype=np.float32))
    x = f32(x)
    T = T_SEQ
    xs = [np.ascontiguousarray(x[c // 4, (c % 4) * NTOK:(c % 4 + 1) * NTOK]) for c in range(NCORE)]
    common = {"fg0": f32(ffn1_norm[0]), "fwi0": f32(ffn1_w_in[0]), "fwo0": f32(ffn1_w_out[0]), "pg": f32(mix_norm[0]), "pw": f32(ab_w_in[0])}
    rA = _launch(_build_dense("A"), [dict(common, x=xs[c]) for c in range(NCORE)])
    aT = _cat_tok(rA, "aT", 1); qT = _cat_tok(rA, "qT", 1)
    kcT = _cat_tok(rA, "kcT", 1); vcT = _cat_tok(rA, "vcT", 1); ksT = _cat_tok(rA, "ksT", 1); kwT = _cat_tok(rA, "kwT", 1)
    vs = _cat_tok(rA, "vs", 0); vw = _cat_tok(rA, "vw", 0); gg = _cat_tok(rA, "gg", 0)
    lay4 = lambda v: np.ascontiguousarray(f32(v).reshape(4, 128).T)
    cc = {"dww": np.ascontiguousarray(f32(conv_dw_w[0]).reshape(31, 4, 128).transpose(2, 1, 0)),
          "dwb": lay4(conv_dw_b[0]), "lng": lay4(conv_ln_g[0]), "lnb": lay4(conv_ln_b[0])}
    maps = []
    for c in range(NCORE):
        b, j = c // 4, c % 4
        a = np.zeros((2, 512, NTOK + 30), dtype=np.float32)
        lo = j * NTOK - 30
        src = aT[b].reshape(2, 512, T)
        if lo < 0:
            a[:, :, 30:] = src[:, :, 0:NTOK]
        else:
            a[:] = src[:, :, lo:lo + NTOK + 30]
        maps.append(dict(cc, aT=a))
    rC0 = _launch(_build_conv0(), maps)
    consts = nsa_consts(T)
    w = dict(pe_k=nsa_pe_k[0], w1_k=nsa_w1_k[0], w2_k=nsa_w2_k[0], pe_v=nsa_pe_v[0], w1_v=nsa_w1_v[0], w2_v=nsa_w2_v[0])
    maps = []
    for c in range(NCORE):
        b, g, hh = c // 4, (c % 4) // 2, c % 2
        horder = [2 * hh, 2 * hh + 1, 2 * (1 - hh), 2 * (1 - hh) + 1]
        dd = nsa_inputs(T, g, qT[b], kcT[b], vcT[b], ksT[b], kwT[b], vs[b], vw[b], gg[b], w, consts, horder)
        maps.append(dd)
    rN = _launch(_build_nsa(), maps)
    oT = []
    for c in range(NCORE):
        b, j = c // 4, c % 4
        o = np.zeros((1024, NTOK), dtype=bf)
        o[0:512] = np.asarray(rC0[c]["coutT"])
        for g in range(2):
            for hh in range(2):
                src = np.asarray(rN[b * 4 + g * 2 + hh]["o_nsa"])[j * NTOK:(j + 1) * NTOK]
                r0 = 512 + (4 * g + 2 * hh) * 64
                o[r0:r0 + 128] = src.T
        oT.append(o)
    common = {"wo": f32(ab_w_out[0]), "fg0": f32(ffn2_norm[0]), "fwi0": f32(ffn2_w_in[0]), "fwo0": f32(ffn2_w_out[0]),
              "fg1": f32(ffn1_norm[1]), "fwi1": f32(ffn1_w_in[1]), "fwo1": f32(ffn1_w_out[1]), "pg": f32(mix_norm[1]), "pw": f32(cd_w_in[0])}
    rB = _launch(_build_dense("B"), [dict(common, x=np.asarray(rA[c]["xo"]), oT=oT[c]) for c in range(NCORE)])
    cT = _cat_tok(rB, "cT", 1); q1 = _cat_tok(rB, "qT", 1); k1 = _cat_tok(rB, "kT", 1); v1 = _cat_tok(rB, "v", 0)
    scw = np.ascontiguousarray(f32(sc_conv_w[0]).reshape(3, 4, 128).transpose(2, 1, 0).reshape(128, 12))
    maps = []
    for c in range(NCORE):
        b, j = c // 4, c % 4
        ci = np.zeros((3, 512, NTOK + 2), dtype=np.float32)
        src = cT[b].reshape(3, 512, T)
        lo = j * NTOK - 2
        if lo < 0:
            ci[:, :, 2:] = src[:, :, 0:NTOK]
        else:
            ci[:] = src[:, :, lo:lo + NTOK + 2]
        hp = j
        maps.append({"qT": np.ascontiguousarray(q1[b][hp * 128:(hp + 1) * 128].reshape(2, 64, T)),
                     "kT": np.ascontiguousarray(k1[b][hp * 128:(hp + 1) * 128].reshape(2, 64, T)),
                     "v": np.ascontiguousarray(v1[b][:, hp * 128:(hp + 1) * 128].reshape(T, 2, 64)),
                     "convin": ci, "scw": scw})
    rM1 = _launch(_build_m1(), maps)
    oT = []
    for c in range(NCORE):
        b, j = c // 4, c % 4
        o = np.zeros((1024, NTOK), dtype=bf)
        o[0:512] = np.asarray(rM1[c]["coutT"])
        for hp in range(4):
            src = np.asarray(rM1[b * 4 + hp]["o_sbT"]).reshape(128, T)[:, j * NTOK:(j + 1) * NTOK]
            o[512 + hp * 128:512 + (hp + 1) * 128] = src
        oT.append(o)
    common = {"wo": f32(cd_w_out[0]), "fg0": f32(ffn2_norm[1]), "fwi0": f32(ffn2_w_in[1]), "fwo0": f32(ffn2_w_out[1]), "gfin": f32(final_norm)}
    rC = _launch(_build_dense("C"), [dict(common, x=np.asarray(rB[c]["xo"]), oT=oT[c]) for c in range(NCORE)])
    out = np.zeros((2, T, 1024), dtype=np.float32)
    for c in range(NCORE):
        out[c // 4, (c % 4) * NTOK:(c % 4 + 1) * NTOK] = np.asarray(rC[c]["out"])
    return out
```

```python
import numpy as np
import math
from contextlib import ExitStack
import concourse.bass as bass
import concourse.mybir as mybir
from concourse.bass_utils import run_bass_kernel_spmd
import ml_dtypes

F32 = mybir.dt.float32
BF16 = mybir.dt.bfloat16
AF = mybir.ActivationFunctionType
ALU = mybir.AluOpType
AX = mybir.AxisListType

SEM_EPOCH = 30000


class Buf:
    __slots__ = ("name", "w", "r", "dsem", "dcnt")

    def __init__(self, name):
        self.name = name
        self.w = []
        self.r = []
        self.dsem = None
        self.dcnt = 0


class Prog:
    def __init__(self, nc, es):
        self.nc = nc
        self.es = es
        self.eng = {"pe": nc.tensor, "act": nc.scalar, "dve": nc.vector, "pool": nc.gpsimd, "sp": nc.sync}
        self.sem = {}
        self.cnt = {}
        self.waited = {k: {} for k in self.eng}
        self.nsem = 0
        for k in self.eng:
            self._new_eng_sem(k)
        self.n_inst = 0
        self.n_wait = 0
        self.q = {k: [] for k in self.eng}

    def _new_sem(self, name):
        self.nsem += 1
        return self.es.enter_context(self.nc.semaphore(f"{name}_{self.nsem}"))

    def _new_eng_sem(self, k):
        self.sem[k] = self._new_sem("e" + k)
        self.cnt[k] = 0

    def buf(self, name):
        return Buf(name)

    def sb(self, name, shape, dtype):
        t = self.es.enter_context(self.nc.sbuf_tensor(name, list(shape), dtype))
        return t

    def ps(self, name, shape, dtype):
        t = self.es.enter_context(self.nc.psum_tensor(name, list(shape), dtype))
        return t

    def _wait(self, e, conds, skip_self=False):
        eng = self.eng[e]
        wd = self.waited[e]
        best = {}
        for (s, v, owner) in conds:
            if skip_self and owner == e:
                continue
            key = id(s)
            if wd.get(key, 0) >= v:
                continue
            if key not in best or best[key][1] < v:
                best[key] = (s, v)
        for key, (s, v) in best.items():
            self.q[e].append(("w", s, v))
            wd[key] = v
            self.n_wait += 1

    def op(self, e, fn, reads=(), writes=(), skip_self=False):
        conds = []
        for b in reads:
            conds += b.w
        for b in writes:
            conds += b.w
            conds += b.r
        self._wait(e, conds, skip_self=skip_self)
        if self.cnt[e] >= SEM_EPOCH:
            self._new_eng_sem(e)
        self.cnt[e] += 1
        self.q[e].append(("i", fn, self.sem[e], 1))
        c = (self.sem[e], self.cnt[e], e)
        for b in reads:
            b.r = [x for x in b.r if x[0] is not c[0]] + [c]
        for b in writes:
            b.w = [c]
            b.r = []
        self.n_inst += 1

    def dma(self, e, out, in_, reads=(), writes=(), **kw):
        conds = []
        for b in reads:
            conds += b.w
        for b in writes:
            conds += b.w
            conds += b.r
        self._wait(e, conds)
        tgt = writes[0] if writes else reads[0]
        if tgt.dsem is None:
            tgt.dsem = self._new_sem("d" + tgt.name)
        tgt.dcnt += 1
        eng = self.eng[e]
        self.q[e].append(("i", (lambda: eng.dma_start(out=out, in_=in_, **kw)), tgt.dsem, 16))
        c = (tgt.dsem, 16 * tgt.dcnt, "dma")
        for b in reads:
            b.r = [x for x in b.r if x[0] is not c[0]] + [c]
        for b in writes:
            b.w = [x for x in b.w if x[0] is not c[0]] + [c]
            b.r = []
        self.n_inst += 1

    def dma_fn(self, e, fn, reads=(), writes=()):
        conds = []
        for b in reads:
            conds += b.w
        for b in writes:
            conds += b.w
            conds += b.r
        self._wait(e, conds)
        tgt = writes[0] if writes else reads[0]
        if tgt.dsem is None:
            tgt.dsem = self._new_sem("d" + tgt.name)
        tgt.dcnt += 1
        self.q[e].append(("i", fn, tgt.dsem, 16))
        c = (tgt.dsem, 16 * tgt.dcnt, "dma")
        for b in reads:
            b.r = [x for x in b.r if x[0] is not c[0]] + [c]
        for b in writes:
            b.w = [x for x in b.w if x[0] is not c[0]] + [c]
            b.r = []
        self.n_inst += 1

    def cc(self, kind, in_ap, out_ap, groups, reads=(), writes=()):
        nc = self.nc
        fn = lambda: nc.gpsimd.collective_compute(kind, mybir.AluOpType.bypass, replica_groups=groups, ins=[in_ap], outs=[out_ap])
        self.dma_fn("pool", fn, reads=reads, writes=writes)

    def finish(self, e, bufs):
        conds = []
        for b in bufs:
            conds += b.w
        self._wait(e, conds)

    def emit(self):
        nc = self.nc
        with nc.Block() as block:
            def run(e):
                eng = self.eng[e]
                for it in self.q[e]:
                    if it[0] == "w":
                        eng.wait_ge(it[1], it[2])
                    else:
                        it[1]().then_inc(it[2], it[3])

            @block.tensor
            def _(x):
                run("pe")

            @block.scalar
            def _(x):
                run("act")

            @block.vector
            def _(x):
                run("dve")

            @block.gpsimd
            def _(x):
                run("pool")

            @block.sync
            def _(x):
                run("sp")


D = 1024
DFF = 2816
NFC = DFF // 128


def make_ident(nc, P, name="ident"):
    ident = P.sb(name, [128, 128], BF16)
    b = P.buf(name)
    P.op("pool", lambda: nc.gpsimd.memset(ident[:], 0.0), writes=[b])
    P.op("pool", lambda: nc.gpsimd.affine_select(out=ident[:], in_=ident[:], pattern=[[-1, 128]],
                                                   compare_op=ALU.not_equal, fill=1.0, base=0,
                                                   channel_multiplier=1), reads=[b], writes=[b])
    return ident, b


class Dense:
    def __init__(self, nc, P, NT):
        self.nc, self.P, self.NT = nc, P, NT
        self.NTILE = NT // 128
        self.NST = NT // 512
        nt = self.NTILE
        self.x = P.sb("x_res", [128, nt, D], F32)
        self.b_x = [P.buf(f"x{t}") for t in range(nt)]
        self.xnT = P.sb("xnT", [128, 8, NT], BF16)
        self.b_xnT = [P.buf(f"xnT{t}") for t in range(nt)]
        self.ident, self.b_id = make_ident(nc, P)
        self.sq = P.sb("sq", [128, D], F32); self.b_sq = P.buf("sq")
        self.ss = P.sb("ss", [128, nt], F32); self.b_ss = [P.buf(f"ss{g}") for g in range(nt // 4)]
        self.rstd = P.sb("rstd", [128, nt], F32); self.b_rstd = [P.buf(f"rstd{g}") for g in range(nt // 4)]
        self.sq2 = P.sb("sq2", [128, D], F32); self.b_sq2 = P.buf("sq2")
        self.xs = [P.sb(f"xs{i}", [128, D], BF16) for i in range(2)]
        self.b_xs = [P.buf(f"xs{i}") for i in range(2)]
        self.gt = P.sb("gt", [128, 8], F32); self.b_gt = P.buf("gt")
        self.NWB = 6
        self.wb = [P.sb(f"wb{i}", [128, 8 * 512], BF16) for i in range(self.NWB)]
        self.b_wb = [P.buf(f"wb{i}") for i in range(self.NWB)]
        self.wi = 0
        self.tp = [P.ps(f"tp{i}", [128, 8, 128], BF16) for i in range(2)]; self.b_tp = [P.buf(f"tp{i}") for i in range(2)]
        self.pg = [P.ps(f"pg{i}", [128, 512], F32) for i in range(2)]; self.b_pg = [P.buf(f"pg{i}") for i in range(2)]
        self.pu = [P.ps(f"pu{i}", [128, 512], F32) for i in range(2)]; self.b_pu = [P.buf(f"pu{i}") for i in range(2)]
        self.py = [P.ps(f"py{i}", [128, 512], F32) for i in range(2)]; self.b_py = [P.buf(f"py{i}") for i in range(2)]
        self.ipg = 0
        self.ipy = 0
        self.sg = [P.sb(f"sg{i}", [128, 512], F32) for i in range(2)]; self.b_sg = [P.buf(f"sg{i}") for i in range(2)]
        self.act = [P.sb(f"actT{i}", [128, 4, 512], BF16) for i in range(2)]
        self.b_act = [P.buf(f"actT{i}") for i in range(2)]
        self.iact = 0
        self.stg = [P.sb(f"stg{i}", [128, 512], F32) for i in range(3)]
        self.b_stg = [P.buf(f"stg{i}") for i in range(3)]
        self.istg = 0
        self.gfull = None

    def next_wb(self):
        i = self.wi % self.NWB
        self.wi += 1
        return self.wb[i], self.b_wb[i]

    def load_w(self, src_ap, rc, cols):
        wb, b = self.next_wb()
        view = wb[:, 0:rc * cols].rearrange("p (c n) -> p c n", c=rc)
        self.P.dma("pool", view, src_ap.rearrange("(c p) n -> p c n", p=128), writes=[b])
        return view, b

    def load_x(self, x_dram):
        for t in range(self.NTILE):
            self.P.dma("sp", self.x[:, t, :], x_dram[t * 128:(t + 1) * 128, :], writes=[self.b_x[t]])

    def store_x(self, out_dram, b_out):
        for t in range(self.NTILE):
            self.P.dma("sp", out_dram[t * 128:(t + 1) * 128, :], self.x[:, t, :], reads=[self.b_x[t]], writes=[b_out])

    def stats_group(self, g):
        nc, P = self.nc, self.P
        for t in range(4 * g, 4 * g + 4):
            sq, b_sq = (self.sq, self.b_sq) if t % 2 == 0 else (self.sq2, self.b_sq2)
            P.op("act", lambda t=t, sq=sq: nc.scalar.activation(out=sq[:], in_=self.x[:, t, :], func=AF.Square,
                                                                 accum_out=self.ss[:, t:t + 1]),
                 reads=[self.b_x[t]], writes=[b_sq, self.b_ss[g]])
        P.op("act", lambda g=g: nc.scalar.activation(out=self.rstd[:, 4 * g:4 * g + 4], in_=self.ss[:, 4 * g:4 * g + 4], func=AF.Sqrt,
                                                     scale=1.0 / D, bias=1e-6), reads=[self.b_ss[g]], writes=[self.b_rstd[g]])
        P.op("dve", lambda g=g: nc.vector.reciprocal(out=self.rstd[:, 4 * g:4 * g + 4], in_=self.rstd[:, 4 * g:4 * g + 4]),
             reads=[self.b_rstd[g]], writes=[self.b_rstd[g]])

    def stats(self):
        for g in range(self.NTILE // 4):
            self.stats_group(g)

    def norm_T(self, g_dram):
        nc, P = self.nc, self.P
        P.dma("sp", self.gt[:], g_dram.rearrange("(c p) -> p c", p=128), writes=[self.b_gt], allow_slow_non_contiguous=True)
        self.stats_group(0)
        for t in range(self.NTILE):
            xs, b_xs = self.xs[t % 2], self.b_xs[t % 2]
            tp, b_tp = self.tp[t % 2], self.b_tp[t % 2]
            g = t // 4
            if t % 4 == 0 and g + 1 < self.NTILE // 4:
                self.stats_group(g + 1)
            if t % 2 == 0:
                P.op("act", lambda t=t, xs=xs: nc.scalar.activation(out=xs[:], in_=self.x[:, t, :], func=AF.Identity, scale=self.rstd[:, t:t + 1]),
                     reads=[self.b_x[t], self.b_rstd[g]], writes=[b_xs])
            else:
                P.op("dve", lambda t=t, xs=xs: nc.vector.tensor_scalar(out=xs[:], in0=self.x[:, t, :], scalar1=self.rstd[:, t:t + 1],
                                                                        scalar2=None, op0=ALU.mult),
                     reads=[self.b_x[t], self.b_rstd[g]], writes=[b_xs])
            for c in range(8):
                P.op("pe", lambda c=c, xs=xs, tp=tp: nc.tensor.transpose(out=tp[:, c, :], in_=xs[:, c * 128:(c + 1) * 128],
                                                                          identity=self.ident[:]),
                     reads=[b_xs, self.b_id], writes=[b_tp], skip_self=True)
            P.op("dve", lambda t=t, tp=tp: nc.vector.tensor_tensor(out=self.xnT[:, :, t * 128:(t + 1) * 128], in0=tp[:],
                                                                    in1=self.gt[:].unsqueeze(2).to_broadcast([128, 8, 128]), op=ALU.mult),
                 reads=[b_tp, self.b_gt], writes=[self.b_xnT[t]])

    def ffn(self, g_dram, w_in, w_out):
        nc, P = self.nc, self.P
        self.norm_T(g_dram)
        groups = [(s, min(4, NFC - s)) for s in range(0, NFC, 4)]
        for (fc0, nfc) in groups:
            ncol = nfc * 128
            wg, b_wg = self.load_w(w_in[:, fc0 * 128: fc0 * 128 + ncol], 8, ncol)
            wu, b_wu = self.load_w(w_in[:, DFF + fc0 * 128: DFF + fc0 * 128 + ncol], 8, ncol)
            wo, b_wo = self.load_w(w_out[fc0 * 128: fc0 * 128 + ncol, :], nfc, D)
            for st in range(self.NST):
                tiles = list(range(st * 4, st * 4 + 4))
                xb = [self.b_xnT[t] for t in tiles]
                act, b_act = self.act[self.iact % 2], self.b_act[self.iact % 2]
                self.iact += 1
                for j in range(nfc):
                    i = self.ipg % 2
                    self.ipg += 1
                    pg, b_pg, pu, b_pu = self.pg[i], self.b_pg[i], self.pu[i], self.b_pu[i]
                    sg, b_sg = self.sg[i], self.b_sg[i]
                    for k in range(8):
                        P.op("pe", lambda k=k, j=j, pg=pg, wg=wg, st=st: nc.tensor.matmul(
                            pg[:], lhsT=wg[:, k, j * 128:(j + 1) * 128], rhs=self.xnT[:, k, st * 512:(st + 1) * 512],
                            start=(k == 0), stop=(k == 7)), reads=xb + [b_wg], writes=[b_pg], skip_self=True)
                    for k in range(8):
                        P.op("pe", lambda k=k, j=j, pu=pu, wu=wu, st=st: nc.tensor.matmul(
                            pu[:], lhsT=wu[:, k, j * 128:(j + 1) * 128], rhs=self.xnT[:, k, st * 512:(st + 1) * 512],
                            start=(k == 0), stop=(k == 7)), reads=xb + [b_wu], writes=[b_pu], skip_self=True)
                    P.op("act", lambda pg=pg, sg=sg: nc.scalar.activation(out=sg[:], in_=pg[:], func=AF.Silu),
                         reads=[b_pg], writes=[b_sg])
                    P.op("dve", lambda j=j, pu=pu, sg=sg, act=act: nc.vector.tensor_tensor(out=act[:, j, :], in0=pu[:], in1=sg[:], op=ALU.mult),
                         reads=[b_pu, b_sg], writes=[b_act])
                for sub in range(4):
                    t = st * 4 + sub
                    for dh in range(2):
                        i = self.ipy % 2
                        self.ipy += 1
                        py, b_py = self.py[i], self.b_py[i]
                        for j in range(nfc):
                            P.op("pe", lambda j=j, py=py, act=act, wo=wo, sub=sub, dh=dh, nfc=nfc: nc.tensor.matmul(
                                py[:], lhsT=act[:, j, sub * 128:(sub + 1) * 128], rhs=wo[:, j, dh * 512:(dh + 1) * 512],
                                start=(j == 0), stop=(j == nfc - 1)), reads=[b_act, b_wo], writes=[b_py], skip_self=True)
                        P.op("dve", lambda t=t, dh=dh, py=py: nc.vector.scalar_tensor_tensor(
                            out=self.x[:, t, dh * 512:(dh + 1) * 512], in0=py[:], scalar=0.5, in1=self.x[:, t, dh * 512:(dh + 1) * 512],
                            op0=ALU.mult, op1=ALU.add), reads=[b_py, self.b_x[t]], writes=[self.b_x[t]])

    def outproj(self, oT_dram, w_dram):
        nc, P = self.nc, self.P
        for t in range(self.NTILE):
            P.dma("sp", self.xnT[:, :, t * 128:(t + 1) * 128],
                  oT_dram[:, t * 128:(t + 1) * 128].rearrange("(c p) n -> p c n", p=128), writes=[self.b_xnT[t]])
        for dh in range(2):
            w, b_w = self.load_w(w_dram[:, dh * 512:(dh + 1) * 512], 8, 512)
            for t in range(self.NTILE):
                i = self.ipy % 2
                self.ipy += 1
                py, b_py = self.py[i], self.b_py[i]
                for k in range(8):
                    P.op("pe", lambda k=k, py=py, w=w, t=t: nc.tensor.matmul(
                        py[:], lhsT=self.xnT[:, k, t * 128:(t + 1) * 128], rhs=w[:, k, :], start=(k == 0), stop=(k == 7)),
                        reads=[self.b_xnT[t], b_w], writes=[b_py], skip_self=True)
                P.op("dve", lambda t=t, dh=dh, py=py: nc.vector.tensor_tensor(
                    out=self.x[:, t, dh * 512:(dh + 1) * 512], in0=py[:], in1=self.x[:, t, dh * 512:(dh + 1) * 512], op=ALU.add),
                    reads=[b_py, self.b_x[t]], writes=[self.b_x[t]])

    def proj(self, g_dram, w_dram, outs):
        nc, P = self.nc, self.P
        self.norm_T(g_dram)
        for (c0, c1, layout, o_ap, b_o) in outs:
            for cs in range(c0, c1, 512):
                ce = min(cs + 512, c1)
                ncol = ce - cs
                w, b_w = self.load_w(w_dram[:, cs:ce], 8, ncol)
                if layout == "F":
                    assert ncol % 128 == 0
                    for j in range(ncol // 128):
                        for st in range(self.NST):
                            i = self.ipy % 2
                            self.ipy += 1
                            py, b_py = self.py[i], self.b_py[i]
                            xb = [self.b_xnT[t] for t in range(st * 4, st * 4 + 4)]
                            for k in range(8):
                                P.op("pe", lambda k=k, j=j, py=py, w=w, st=st: nc.tensor.matmul(
                                    py[:], lhsT=w[:, k, j * 128:(j + 1) * 128], rhs=self.xnT[:, k, st * 512:(st + 1) * 512],
                                    start=(k == 0), stop=(k == 7)), reads=xb + [b_w], writes=[b_py], skip_self=True)
                            si = self.istg % 3
                            self.istg += 1
                            stg, b_stg = self.stg[si], self.b_stg[si]
                            if o_ap.dtype == BF16:
                                sv = stg[:].bitcast(BF16)[:, 0:512]
                            else:
                                sv = stg[:]
                            eng = "act" if (self.istg % 2) else "dve"
                            if eng == "act":
                                P.op("act", lambda sv=sv, py=py: nc.scalar.copy(out=sv, in_=py[:]), reads=[b_py], writes=[b_stg])
                            else:
                                P.op("dve", lambda sv=sv, py=py: nc.vector.tensor_copy(out=sv, in_=py[:]), reads=[b_py], writes=[b_stg])
                            r0 = cs - c0 + j * 128
                            P.dma("sp", o_ap[r0:r0 + 128, st * 512:(st + 1) * 512], sv, reads=[b_stg], writes=[b_o])
                else:
                    for t in range(self.NTILE):
                        i = self.ipy % 2
                        self.ipy += 1
                        py, b_py = self.py[i], self.b_py[i]
                        for k in range(8):
                            P.op("pe", lambda k=k, py=py, w=w, t=t, ncol=ncol: nc.tensor.matmul(
                                py[:, 0:ncol], lhsT=self.xnT[:, k, t * 128:(t + 1) * 128], rhs=w[:, k, :],
                                start=(k == 0), stop=(k == 7)), reads=[self.b_xnT[t], b_w], writes=[b_py], skip_self=True)
                        si = self.istg % 3
                        self.istg += 1
                        stg, b_stg = self.stg[si], self.b_stg[si]
                        if o_ap.dtype == BF16:
                            sv = stg[:].bitcast(BF16)[:, 0:ncol]
                        else:
                            sv = stg[:, 0:ncol]
                        P.op("dve", lambda sv=sv, py=py, ncol=ncol: nc.vector.tensor_copy(out=sv, in_=py[:, 0:ncol]), reads=[b_py], writes=[b_stg])
                        P.dma("sp", o_ap[t * 128:(t + 1) * 128, cs - c0:ce - c0], sv, reads=[b_stg], writes=[b_o])

    def final(self, g_dram, out_dram, b_out):
        nc, P = self.nc, self.P
        gfull = P.sb("gfull", [128, D], F32)
        b_g = P.buf("gfull")
        P.dma("sp", gfull[:], g_dram.partition_broadcast(128), writes=[b_g])
        self.stats()
        for t in range(self.NTILE):
            si = t % 2
            o = self.fin[si]
            b_o = self.b_fin[si]
            P.op("dve", lambda t=t, o=o: nc.vector.scalar_tensor_tensor(out=o[:], in0=self.x[:, t, :], scalar=self.rstd[:, t:t + 1],
                                                                        in1=gfull[:], op0=ALU.mult, op1=ALU.mult),
                 reads=[self.b_x[t], self.b_rstd[t // 4], b_g], writes=[b_o])
            P.dma("sp", out_dram[t * 128:(t + 1) * 128, :], o[:], reads=[b_o], writes=[b_out])

    def alloc_final(self):
        P = self.P
        self.fin = [self.sq, self.sq]
        self.b_fin = [self.b_sq, self.b_sq]


def tri_consts(nc, P):
    triu = P.sb("triu", [128, 128], BF16); b_u = P.buf("triu")
    tril = P.sb("tril", [128, 128], BF16); b_l = P.buf("tril")
    P.op("pool", lambda: nc.gpsimd.memset(triu[:], 1.0), writes=[b_u])
    P.op("pool", lambda: nc.gpsimd.affine_select(out=triu[:], in_=triu[:], pattern=[[-1, 128]], compare_op=ALU.is_ge,
                                                   fill=0.0, base=0, channel_multiplier=1), reads=[b_u], writes=[b_u])
    P.op("pool", lambda: nc.gpsimd.memset(tril[:], 0.0), writes=[b_l])
    P.op("pool", lambda: nc.gpsimd.affine_select(out=tril[:], in_=tril[:], pattern=[[-1, 128]], compare_op=ALU.is_ge,
                                                   fill=1.0, base=0, channel_multiplier=1), reads=[b_l], writes=[b_l])
    return triu, b_u, tril, b_l


def causal_masks(nc, P, strict=True, dtype=BF16, name="cm"):
    m = P.sb(name, [128, 4, 512], dtype); b = P.buf(name)
    P.op("pool", lambda: nc.gpsimd.memset(m[:], 1.0), writes=[b])
    for o in range(4):
        P.op("pool", lambda o=o: nc.gpsimd.affine_select(out=m[:, o, :], in_=m[:, o, :], pattern=[[1, 512]],
                                                          compare_op=(ALU.is_gt if strict else ALU.is_ge), fill=0.0,
                                                          base=-128 * o, channel_multiplier=-1), reads=[b], writes=[b])
    return m, b


def build_mixer1(nc, P, T, NTC, d):
    scale = 64 ** -0.5
    NQB = T // 512
    NKT = T // 128
    qT = P.sb("qT_sb", [64, 2, T], BF16); b_q = P.buf("qT")
    kT = P.sb("kT_sb", [64, 2, T], BF16); b_k = P.buf("kT")
    v = P.sb("v_sb", [128, NKT, 2, 64], BF16); b_v = P.buf("v")
    for h in range(2):
        P.dma("sp", qT[:, h, :], d["qT"][h], writes=[b_q])
        P.dma("sp", kT[:, h, :], d["kT"][h], writes=[b_k])
    P.dma("sp", v[:], d["v"].rearrange("(n p) h e -> p n h e", p=128), writes=[b_v])
    triu, b_u, tril, b_l = tri_consts(nc, P)
    ident1, b_id1 = make_ident(nc, P, "ident_m1")
    cm = P.sb("negm1", [128, 4, 512], BF16); b_cm = P.buf("negm1")
    P.op("pool", lambda: nc.gpsimd.memset(cm[:], 0.0), writes=[b_cm])
    for o_ in range(4):
        P.op("pool", lambda o_=o_: nc.gpsimd.affine_select(out=cm[:, o_, :], in_=cm[:, o_, :], pattern=[[1, 512]], compare_op=ALU.is_gt,
                                                            fill=-30000.0, base=-128 * o_, channel_multiplier=-1), reads=[b_cm], writes=[b_cm])
    b_out = P.buf("o_sb_out")
    b_cout = P.buf("cout_out")

    N = NTC
    wT = P.sb("scw_sb", [128, 4, 3], F32); b_w = P.buf("scw")
    P.dma("sp", wT[:], d["scw"].rearrange("p (c k) -> p c k", c=4), writes=[b_w])
    cin = [P.sb(f"cin{i}", [128, 3, N + 2], F32) for i in range(2)]
    b_cin = [P.buf(f"cin{i}") for i in range(2)]
    vv = P.sb("cvv", [128, N + 2], F32); b_vv = P.buf("cvv")
    yy = P.sb("cyy", [128, N], F32); b_yy = P.buf("cyy")
    yo = [P.sb(f"cyo{i}", [128, N], BF16) for i in range(2)]
    b_yo = [P.buf(f"cyo{i}") for i in range(2)]
    for c in range(4):
        ci, b_ci = cin[c % 2], b_cin[c % 2]
        P.dma("sp", ci[:], d["convin"][:, c * 128:(c + 1) * 128, :].rearrange("k p n -> p k n"), writes=[b_ci])
        P.op("pool", lambda ci=ci: nc.gpsimd.tensor_tensor(out=vv[:], in0=ci[:, 1, :], in1=ci[:, 2, :], op=ALU.mult),
             reads=[b_ci], writes=[b_vv])
        P.op("dve", lambda c=c: nc.vector.tensor_scalar(out=yy[:], in0=vv[:, 0:N], scalar1=wT[:, c, 0:1], scalar2=None, op0=ALU.mult),
             reads=[b_vv, b_w], writes=[b_yy])
        for k in (1, 2):
            P.op("dve", lambda c=c, k=k: nc.vector.scalar_tensor_tensor(out=yy[:], in0=vv[:, k:N + k], scalar=wT[:, c, k:k + 1],
                                                                        in1=yy[:], op0=ALU.mult, op1=ALU.add),
                 reads=[b_vv, b_w, b_yy], writes=[b_yy])
        o, b_o = yo[c % 2], b_yo[c % 2]
        P.op("dve", lambda ci=ci, o=o: nc.vector.tensor_tensor(out=o[:], in0=yy[:], in1=ci[:, 0, 2:N + 2], op=ALU.mult),
             reads=[b_yy, b_ci], writes=[b_o])
        P.dma("sp", d["coutT"][c * 128:(c + 1) * 128, :], o[:], reads=[b_o], writes=[b_cout])

    S_ps = [P.ps(f"S{i}", [128, 2, 512], F32) for i in range(2)]
    b_S = [P.buf(f"S{i}") for i in range(2)]
    D_ps = [P.ps(f"D{h}", [128, 512], F32) for h in range(2)]
    b_D = [P.buf(f"D{h}") for h in range(2)]
    O_ps = [P.ps(f"O{h}", [64, 512], F32) for h in range(2)]
    b_O = [P.buf(f"O{h}") for h in range(2)]
    NE, NF, NA = 3, 4, 4
    e_sb = [P.sb(f"e{i}", [128, 2, 512], F32) for i in range(NE)]; b_e = [P.buf(f"e{i}") for i in range(NE)]
    sp_sb = [P.sb(f"sp{i}", [128, 2, 512], BF16) for i in range(NE)]; b_sp = [P.buf(f"sp{i}") for i in range(NE)]
    f_sb = [P.sb(f"f{i}", [128, 512], F32) for i in range(NF)]; b_f = [P.buf(f"f{i}") for i in range(NF)]
    a_sb = [P.sb(f"a{i}", [128, 512], BF16) for i in range(NA)]; b_a = [P.buf(f"a{i}") for i in range(NA)]
    oo = [P.sb(f"oo{h}", [64, 512], BF16) for h in range(2)]
    b_oo = [P.buf(f"oo{h}") for h in range(2)]
    zz = P.sb("zz", [128, 512], BF16); b_zz = P.buf("zz")
    P.op("pool", lambda: nc.gpsimd.memset(zz[:], 0.0), writes=[b_zz])
    items = []
    for qb in range(NQB):
        kmax = 4 * qb + 3
        for kb in range(kmax, -1, -1):
            for h in range(2):
                items.append(dict(qb=qb, kb=kb, h=h, diag=(kb >= 4 * qb), o=kb - 4 * qb, first=(kb == kmax), last=(kb == 0)))
    NI = len(items)
    NP = NI // 2

    def stA1(p):
        it = items[2 * p]; kb, qb = it["kb"], it["qb"]
        Sp, bS = S_ps[p % 2], b_S[p % 2]
        e, be = e_sb[p % NE], b_e[p % NE]
        dg, o = it["diag"], it["o"]
        for h in range(2):
            P.op("pe", lambda h=h: nc.tensor.matmul(Sp[:, h, :], lhsT=kT[:, h, kb * 128:(kb + 1) * 128], rhs=qT[:, h, qb * 512:(qb + 1) * 512],
                                                    start=True, stop=not dg), reads=[b_k, b_q], writes=[bS], skip_self=True)
            if dg:
                P.op("pe", lambda h=h: nc.tensor.matmul(Sp[:, h, :], lhsT=ident1[:], rhs=cm[:, o, :], start=False, stop=True),
                     reads=[b_id1, b_cm], writes=[bS], skip_self=True)
        P.op("act", lambda: nc.scalar.activation(out=e[:], in_=Sp[:], func=AF.Exp, scale=scale), reads=[bS], writes=[be])

    def stA2(p):
        it = items[2 * p]; o = it["o"]
        e, be = e_sb[p % NE], b_e[p % NE]
        sp, bsp = sp_sb[p % NE], b_sp[p % NE]
        P.op("act", lambda: nc.scalar.activation(out=sp[:], in_=e[:], func=AF.Ln, bias=1.0), reads=[be], writes=[bsp])

    def stB1(i):
        it = items[i]; h = it["h"]
        sp, bsp = sp_sb[(i // 2) % NE], b_sp[(i // 2) % NE]
        P.op("pe", lambda: nc.tensor.matmul(D_ps[h][:], lhsT=triu[:], rhs=sp[:, h, :], start=it["first"], stop=True, skip_group_check=True),
             reads=[bsp, b_u], writes=[b_D[h]], skip_self=True)

    def stB2(i):
        it = items[i]; h = it["h"]
        f, bf_ = f_sb[i % NF], b_f[i % NF]
        P.op("act", lambda: nc.scalar.activation(out=f[:], in_=D_ps[h][:], func=AF.Exp, scale=-1.0), reads=[b_D[h]], writes=[bf_])

    def stC(i):
        it = items[i]; h = it["h"]
        sp, bsp = sp_sb[(i // 2) % NE], b_sp[(i // 2) % NE]
        e, be = e_sb[(i // 2) % NE], b_e[(i // 2) % NE]
        f, bf_ = f_sb[i % NF], b_f[i % NF]
        a, ba = a_sb[i % NA], b_a[i % NA]
        if not it["last"]:
            P.op("pe", lambda: nc.tensor.matmul(D_ps[h][:], lhsT=tril[:], rhs=sp[:, h, :], start=False, stop=True, skip_group_check=True),
                 reads=[bsp, b_l], writes=[b_D[h]], skip_self=True)
        P.op("dve", lambda: nc.vector.tensor_tensor(out=a[:], in0=e[:, h, :], in1=f[:], op=ALU.mult), reads=[be, bf_], writes=[ba])

    def stD(i):
        it = items[i]; h, kb, qb = it["h"], it["kb"], it["qb"]
        a, ba = a_sb[i % NA], b_a[i % NA]
        if it["first"]:
            P.op("pe", lambda: nc.tensor.matmul(O_ps[h][:], lhsT=zz[:, 0:64], rhs=zz[:], start=True, stop=True),
                 reads=[b_zz], writes=[b_O[h]], skip_self=True)
        P.op("pe", lambda: nc.tensor.matmul(O_ps[h][:], lhsT=v[:, kb, h, :], rhs=a[:], start=False, stop=True, skip_group_check=True),
             reads=[ba, b_v], writes=[b_O[h]], skip_self=True)
        if it["last"]:
            P.op("dve", lambda: nc.vector.tensor_copy(out=oo[h][:], in_=O_ps[h][:]), reads=[b_O[h]], writes=[b_oo[h]])
            P.dma("sp", d["o_sbT"][h, :, qb * 512:(qb + 1) * 512], oo[h][:], reads=[b_oo[h]], writes=[b_out])

    for s_ in range(-4, NI + 1):
        if s_ % 2 == 0 and 0 <= (s_ + 4) // 2 < NP:
            stA1((s_ + 4) // 2)
        if 0 <= s_ + 1 < NI:
            stB1(s_ + 1)
            stB2(s_ + 1)
        if s_ % 2 == 1 and 0 <= (s_ + 3) // 2 < NP:
            stA2((s_ + 3) // 2)
        if 0 <= s_ < NI:
            stC(s_)
        if 0 <= s_ - 1 < NI:
            stD(s_ - 1)
    return [b_out, b_cout]


def build_nsa(nc, P, T, qbs, d, banks, NOWN=4):
    scale = 64 ** -0.5
    NCP = T // 16
    NC = NCP - 1
    NCT = NCP // 128
    NKT = T // 128
    ident, b_id = make_ident(nc, P, "ident0")
    b_out = P.buf("nsa_out")

    QAq = [P.sb(f"QAq{i}", [68, 4, 512], BF16) for i in range(2)]; b_QAq = [P.buf(f"QAq{i}") for i in range(2)]
    cur = {}
    S_ps = [banks[i][0] for i in range(2)]; b_S = [banks[i][1] for i in range(2)]
    MK, b_MK = banks[2]
    A_ps = [banks[3 + i][0] for i in range(4)]; b_A = [banks[3 + i][1] for i in range(4)]
    X, b_X = banks[7]
    zz = P.sb("nzz", [128, 512], BF16); b_zz = P.buf("nzz")
    P.op("pool", lambda: nc.gpsimd.memset(zz[:], 0.0), writes=[b_zz])

    def zero_bank(i):
        P.op("pe", lambda i=i: nc.tensor.matmul(A_ps[i][:], lhsT=zz[:, 0:128], rhs=zz[:], start=True, stop=True),
             reads=[b_zz], writes=[b_A[i]], skip_self=True)

    KcA = P.sb("KcA", [68, NCP], BF16); b_KcA = P.buf("KcA")
    VcX = P.sb("VcX", [128, NCT, 193], BF16); b_VcX = P.buf("VcX")
    P.dma("sp", KcA[64:68, :], d["kaugc"], writes=[b_KcA])
    P.dma("sp", VcX[:, :, 65:193], d["poolm"].rearrange("(n p) j -> p n j", p=128), writes=[b_VcX])
    P.op("pool", lambda: nc.gpsimd.memset(VcX[:, :, 64:65], 1.0), reads=[], writes=[b_VcX])
    w1 = P.sb("w1_sb", [128, 16, 128], BF16); b_w1 = P.buf("w1")
    pe2 = P.sb("pe2_sb", [128, 16], BF16); b_pe2 = P.buf("pe2")
    w2 = P.sb("w2_sb", [128, 64], BF16); b_w2 = P.buf("w2")
    c2 = P.sb("c2_sb", [128, T], BF16); b_c2 = P.buf("c2")
    pb = P.sb("pb_sb", [128, 1], F32); b_pb = P.buf("pb")
    xh = P.sb("xh_sb", [128, NCP], F32); b_xh = P.buf("xh")
    uh = P.sb("uh_sb", [128, NCP], F32); b_uh = P.buf("uh")
    hT = P.sb("hT_sb", [128, NCP], BF16); b_hT = P.buf("hT")
    P.op("pool", lambda: nc.gpsimd.memset(hT[:], 0.0), writes=[b_hT])
    for kv in ("k", "v"):
        P.dma("pool", w1[:], d["w1" + kv], writes=[b_w1])
        P.dma("pool", pe2[:], d["pe2" + kv], writes=[b_pe2])
        P.dma("pool", w2[:], d["w2" + kv], writes=[b_w2])
        P.dma("sp", c2[:], d["c2" + kv], writes=[b_c2])
        c2v = c2[:].rearrange("p (n s) -> p n s", s=16)
        for c in range(16):
            if 2 * c < 16:
                rhs = c2v[:, 0:NC, 2 * c]
            else:
                rhs = c2v[:, 1:NC + 1, 2 * c - 16]
            P.op("pe", lambda c=c, rhs=rhs: nc.tensor.matmul(X[:, 0:NC], lhsT=w1[:, c, :], rhs=rhs, start=(c == 0), stop=(c == 15)),
                 reads=[b_w1, b_c2], writes=[b_X], skip_self=True)
        P.op("act", lambda: nc.scalar.copy(out=xh[:, 0:NC], in_=X[:, 0:NC]), reads=[b_X], writes=[b_xh])
        for c in range(16):
            P.op("pe", lambda c=c: nc.tensor.matmul(X[:, 0:1], lhsT=w1[:, c, :], rhs=pe2[:, c:c + 1], start=(c == 0), stop=(c == 15)),
                 reads=[b_w1, b_pe2], writes=[b_X], skip_self=True)
        P.op("dve", lambda: nc.vector.tensor_copy(out=pb[:], in_=X[:, 0:1]), reads=[b_X], writes=[b_pb])
        P.op("dve", lambda: nc.vector.tensor_scalar(out=xh[:, 0:NC], in0=xh[:, 0:NC], scalar1=pb[:, 0:1], scalar2=None, op0=ALU.add),
             reads=[b_xh, b_pb], writes=[b_xh])
        P.op("dve", lambda: nc.vector.tensor_tensor(out=uh[:, 0:NC], in0=xh[:, 0:NC], in1=xh[:, 0:NC], op=ALU.mult),
             reads=[b_xh], writes=[b_uh])
        P.op("dve", lambda: nc.vector.tensor_scalar(out=uh[:, 0:NC], in0=uh[:, 0:NC], scalar1=0.044715, scalar2=1.0, op0=ALU.mult, op1=ALU.add),
             reads=[b_uh], writes=[b_uh])
        P.op("dve", lambda: nc.vector.tensor_tensor(out=uh[:, 0:NC], in0=uh[:, 0:NC], in1=xh[:, 0:NC], op=ALU.mult),
             reads=[b_uh, b_xh], writes=[b_uh])
        P.op("act", lambda: nc.scalar.activation(out=uh[:, 0:NC], in_=uh[:, 0:NC], func=AF.Sigmoid, scale=2.0 * math.sqrt(2.0 / math.pi)),
             reads=[b_uh], writes=[b_uh])
        P.op("dve", lambda: nc.vector.tensor_tensor(out=hT[:, 0:NC], in0=uh[:, 0:NC], in1=xh[:, 0:NC], op=ALU.mult),
             reads=[b_uh, b_xh], writes=[b_hT])
        if kv == "k":
            P.op("pe", lambda: nc.tensor.matmul(X[0:64, 0:NCP], lhsT=w2[:], rhs=hT[:], start=True, stop=True),
                 reads=[b_w2, b_hT], writes=[b_X], skip_self=True)
            P.op("dve", lambda: nc.vector.tensor_copy(out=KcA[0:64, :], in_=X[0:64, 0:NCP]), reads=[b_X], writes=[b_KcA])
        else:
            for n in range(NCT):
                P.op("pe", lambda n=n: nc.tensor.matmul(X[:, n * 64:(n + 1) * 64], lhsT=hT[:, n * 128:(n + 1) * 128], rhs=w2[:],
                                                        start=True, stop=True), reads=[b_w2, b_hT], writes=[b_X], skip_self=True)
            P.op("dve", lambda: nc.vector.tensor_copy(out=VcX[:, :, 0:64], in_=X[:, 0:NCT * 64].rearrange("p (n e) -> p n e", e=64)),
                 reads=[b_X], writes=[b_VcX])

    KSA = P.sb("KSA_sb", [68, T], BF16); b_KSA = P.buf("KSA")
    KWA = P.sb("KWA_sb", [68, T], BF16); b_KWA = P.buf("KWA")
    P.dma("sp", KSA[:], d["KSA"], writes=[b_KSA])
    P.dma("sp", KWA[:], d["KWA"], writes=[b_KWA])
    VSA = P.sb("VSA_sb", [128, NKT, 65], BF16); b_VSA = P.buf("VSA")
    VWA = P.sb("VWA_sb", [128, NKT, 65], BF16); b_VWA = P.buf("VWA")
    P.dma("sp", VSA[:], d["VSA"], writes=[b_VSA])
    P.dma("sp", VWA[:], d["VWA"], writes=[b_VWA])
    Wsel = P.sb("Wsel", [128, T], BF16); b_Wsel = P.buf("Wsel")
    P.op("pool", lambda: nc.gpsimd.memset(Wsel[:], 1.0), writes=[b_Wsel])
    P.op("pool", lambda: nc.gpsimd.affine_select(out=Wsel[:], in_=Wsel[:], pattern=[[1, T]], compare_op=ALU.is_ge, fill=0.0,
                                                   base=0, channel_multiplier=-64), reads=[b_Wsel], writes=[b_Wsel])
    P.op("pool", lambda: nc.gpsimd.affine_select(out=Wsel[:], in_=Wsel[:], pattern=[[-1, T]], compare_op=ALU.is_ge, fill=0.0,
                                                   base=63, channel_multiplier=64), reads=[b_Wsel], writes=[b_Wsel])

    NE = 5
    e_sb = [P.sb(f"ne{i}", [128, 512], F32) for i in range(NE)]; b_e = [P.buf(f"ne{i}") for i in range(NE)]
    p_sb = [P.sb(f"np{i}", [128, 512], BF16) for i in range(NE)]; b_p = [P.buf(f"np{i}") for i in range(NE)]
    NMK = 3
    mk_sb = [P.sb(f"nmk{i}", [128, 512], BF16) for i in range(NMK)]; b_mk = [P.buf(f"nmk{i}") for i in range(NMK)]
    accC = [P.sb(f"accC{r}", [128, 4, 193], F32) for r in range(4)]; b_accC = [P.buf(f"accC{r}") for r in range(4)]
    accS = [P.sb(f"accS{r}", [128, 4, 65], F32) for r in range(NOWN)]; b_accS = [P.buf(f"accS{r}") for r in range(NOWN)]
    accW = [P.sb(f"accW{r}", [128, 4, 65], F32) for r in range(NOWN)]; b_accW = [P.buf(f"accW{r}") for r in range(NOWN)]
    rden = P.sb("rden", [128, 3, 4, 4], F32)
    b_rdenC = [P.buf(f"rdenC{r}") for r in range(4)]
    b_rdenS = [P.buf(f"rdenS{r}") for r in range(4)]
    b_rdenW = [P.buf(f"rdenW{r}") for r in range(4)]
    imp = P.sb("imp", [128, 4, 128], F32); b_imp = P.buf("imp")
    M1 = P.sb("M1_sb", [128, 4, 128], F32); b_M1 = P.buf("M1")
    A1 = P.sb("A1_sb", [128, 4, 128], F32); b_A1 = P.buf("A1")
    score = [P.sb(f"score{i}", [128, 128], F32) for i in range(2)]; b_score = [P.buf(f"score{i}") for i in range(2)]
    work = [P.sb(f"work{i}", [128, 128], F32) for i in range(2)]; b_work = [P.buf(f"work{i}") for i in range(2)]
    m8 = [P.sb(f"m8{i}", [128, 16], F32) for i in range(2)]; b_m8 = [P.buf(f"m8{i}") for i in range(2)]
    sel = P.sb("sel", [128, 4, 128], BF16); b_sel = P.buf("sel")
    selT = P.sb("selT", [128, 512], BF16); b_selT = P.buf("selT")
    graw2 = [P.sb(f"graw_sb{i}", [128, 4, 3 * NOWN], F32) for i in range(2)]; b_graw2 = [P.buf(f"graw{i}") for i in range(2)]
    gsig2 = [P.sb(f"gsig{i}", [128, 4, 3 * NOWN], F32) for i in range(2)]; b_gsig2 = [P.buf(f"gsig{i}") for i in range(2)]
    wgt = P.sb("wgt", [128, 3, 4, 4], F32); b_wgt = P.buf("wgt")
    oacc = P.sb("oacc", [128, 4, 64 * NOWN], F32); b_oacc = P.buf("oacc")
    obf = P.sb("obf", [128, 4, 64 * NOWN], BF16); b_obf = P.buf("obf")

    negm = P.sb("negm", [128, 4, 512], BF16); b_negm = P.buf("negm")
    P.op("pool", lambda: nc.gpsimd.memset(negm[:], 0.0), writes=[b_negm])
    for o_ in range(4):
        P.op("pool", lambda o_=o_: nc.gpsimd.affine_select(out=negm[:, o_, :], in_=negm[:, o_, :], pattern=[[1, 512]], compare_op=ALU.is_ge,
                                                            fill=-30000.0, base=-128 * o_, channel_multiplier=-1), reads=[b_negm], writes=[b_negm])

    negv = P.sb("negv", [128, 5, 512], BF16); b_negv = P.buf("negv")
    P.op("pool", lambda: nc.gpsimd.memset(negv[:], 0.0), writes=[b_negv])
    for m_ in range(5):
        P.op("pool", lambda m_=m_: nc.gpsimd.affine_select(out=negv[:, m_, :], in_=negv[:, m_, :], pattern=[[1, 512]], compare_op=ALU.is_ge,
                                                            fill=-30000.0, base=-(31 + 512 * (m_ - 4)), channel_multiplier=-16),
             reads=[b_negv], writes=[b_negv])

    items = []
    for qb in qbs:
        nct = min(NCT, (32 * (qb + 1) + 127) // 128)
        for r in range(4):
            for kt in range(nct):
                items.append(dict(kind="C", qb=qb, kt=kt, r=r, first=(kt == 0), last=(kt == nct - 1), qbstart=(r == 0 and kt == 0)))
        kw0 = max(0, 4 * qb - 4)
        for kt in range(kw0, 4 * qb + 4):
            for r in range(NOWN):
                items.append(dict(kind="W", qb=qb, kt=kt, r=r, first=(kt == kw0), last=(kt == 4 * qb + 3), qbstart=False))
        for kt in range(0, 4 * qb + 4):
            for r in range(NOWN):
                items.append(dict(kind="S", qb=qb, kt=kt, r=r, first=(kt == 0), last=(kt == 4 * qb + 3), qbstart=False))
    NI = len(items)

    def banks_of(it):
        r = it["r"]
        if it["kind"] == "C":
            return [2 * (r % 2), 2 * (r % 2) + 1]
        if it["kind"] == "W":
            return [r]
        return [2 + r] if NOWN == 2 else [r]

    def load_qb(qb_):
        q0_ = qb_ * 512
        qi_ = qb_ % 2
        for rr in range(4):
            P.dma("sp", QAq[qi_][:, rr, :], d["QA"][rr][:, q0_:q0_ + 512], writes=[b_QAq[qi_]])
        P.dma("sp", M1[:], d["M1"][q0_:q0_ + 512, :].rearrange("(c p) j -> p c j", p=128), writes=[b_M1])
        P.dma("sp", A1[:], d["A1"][q0_:q0_ + 512, :].rearrange("(c p) j -> p c j", p=128), writes=[b_A1])
        graw, b_graw, gsig, b_gsig = graw2[qi_], b_graw2[qi_], gsig2[qi_], b_gsig2[qi_]
        P.dma("sp", graw[:], d["graw"][q0_:q0_ + 512, :].rearrange("(c p) j -> p c j", p=128), writes=[b_graw])
        P.op("act", lambda: nc.scalar.activation(out=gsig[:], in_=graw[:], func=AF.Sigmoid), reads=[b_graw], writes=[b_gsig])

    def stA(i):
        it = items[i]; kind, qb, kt, r = it["kind"], it["qb"], it["kt"], it["r"]
        q0 = qb * 512
        qi = qb % 2
        if it["qbstart"] and qb == qbs[0]:
            load_qb(qb)
        diag = kt >= 4 * qb
        if kind == "S" and r == 0 and kt == 0:
            selection_pe()
            nxt = qbs.index(qb) + 1
            if nxt < len(qbs):
                load_qb(qbs[nxt])
        if kind == "S" and r == 0:
            MKc, bMKc = (MK, b_MK) if kt % 2 == 0 else (X, b_X)
            P.op("pe", lambda: nc.tensor.matmul(MKc[:], lhsT=Wsel[:, kt * 128:(kt + 1) * 128], rhs=selT[:], start=True, stop=True),
                 reads=[b_Wsel, b_selT], writes=[bMKc], skip_self=True)
        KA, bKA = {"C": (KcA, b_KcA), "W": (KWA, b_KWA), "S": (KSA, b_KSA)}[kind]
        clamp = True if kind == "C" else diag
        si = i % 2
        ei = i % NE
        QAc, bQAc = QAq[qi], b_QAq[qi]
        addmask = diag and kind in ("W", "S")
        cmask = None
        if kind == "C":
            delta = 2048 * kt + 31 - q0
            if delta > -2032:
                cmask = (delta - 31) // 512 + 4
                assert 0 <= cmask <= 4 and 31 + 512 * (cmask - 4) == delta, (delta, cmask)
        P.op("pe", lambda: nc.tensor.matmul(S_ps[si][:], lhsT=KA[:, kt * 128:(kt + 1) * 128], rhs=QAc[:, r, :], start=True,
                                            stop=not (addmask or cmask is not None)),
             reads=[bKA, bQAc], writes=[b_S[si]], skip_self=True)
        if kind == "C":
            if cmask is not None:
                P.op("pe", lambda: nc.tensor.matmul(S_ps[si][:], lhsT=ident[:], rhs=negv[:, cmask, :], start=False, stop=True),
                     reads=[b_id, b_negv], writes=[b_S[si]], skip_self=True)
            P.op("act", lambda: nc.scalar.activation(out=p_sb[ei][:], in_=S_ps[si][:], func=AF.Exp, scale=scale), reads=[b_S[si]], writes=[b_p[ei]])
        elif addmask:
            o_ = kt - 4 * qb
            P.op("pe", lambda: nc.tensor.matmul(S_ps[si][:], lhsT=ident[:], rhs=negm[:, o_, :], start=False, stop=True),
                 reads=[b_id, b_negm], writes=[b_S[si]], skip_self=True)
            if kind == "W":
                P.op("act", lambda: nc.scalar.activation(out=p_sb[ei][:], in_=S_ps[si][:], func=AF.Exp, scale=scale), reads=[b_S[si]], writes=[b_p[ei]])
            else:
                P.op("act", lambda: nc.scalar.activation(out=e_sb[ei][:], in_=S_ps[si][:], func=AF.Exp, scale=scale), reads=[b_S[si]], writes=[b_e[ei]])
        elif clamp:
            P.op("dve", lambda: nc.vector.tensor_scalar(out=e_sb[ei][:], in0=S_ps[si][:], scalar1=40.0 / scale, scalar2=None, op0=ALU.min),
                 reads=[b_S[si]], writes=[b_e[ei]])
            P.op("act", lambda: nc.scalar.activation(out=e_sb[ei][:], in_=e_sb[ei][:], func=AF.Exp, scale=scale), reads=[b_e[ei]], writes=[b_e[ei]])
        else:
            P.op("act", lambda: nc.scalar.activation(out=e_sb[ei][:], in_=S_ps[si][:], func=AF.Exp, scale=scale), reads=[b_S[si]], writes=[b_e[ei]])

    def stB(i):
        it = items[i]; kind, qb, kt, r = it["kind"], it["qb"], it["kt"], it["r"]
        q0 = qb * 512
        ei = i % NE
        diag = kt >= 4 * qb
        e, be, p, bp = e_sb[ei], b_e[ei], p_sb[ei], b_p[ei]
        if kind == "C":
            pass
        elif kind == "W":
            if diag:
                pass
            else:
                bs = 511 - (q0 - 128 * kt)
                P.op("pool", lambda: nc.gpsimd.affine_select(out=p[:], in_=e[:], pattern=[[-1, 512]], compare_op=ALU.is_ge, fill=0.0,
                                                              base=bs, channel_multiplier=1), reads=[be], writes=[bp])
        else:
            mi = kt % NMK
            MKc, bMKc = (MK, b_MK) if kt % 2 == 0 else (X, b_X)
            P.op("dve", lambda: nc.vector.tensor_tensor(out=p[:], in0=e[:], in1=MKc[:], op=ALU.mult), reads=[be, bMKc], writes=[bp])

    import collections as _col
    pending = _col.deque()

    def flush(n=None):
        while pending and (n is None or n > 0):
            pending.popleft()()
            if n is not None:
                n -= 1

    def selection():
        for c in range(4):
            pending.append(lambda c=c: sel_chunk(c))

    def sel_chunk(c):
        if True:
            k = c % 2
            sc, bsc, wk, bwk, mm, bmm = score[k], b_score[k], work[k], b_work[k], m8[k], b_m8[k]
            P.op("dve", lambda c=c, sc=sc: nc.vector.tensor_tensor(out=sc[:], in0=imp[:, c, :], in1=M1[:, c, :], op=ALU.mult),
                 reads=[b_imp, b_M1], writes=[bsc])
            P.op("dve", lambda c=c, sc=sc: nc.vector.tensor_tensor(out=sc[:], in0=sc[:], in1=A1[:, c, :], op=ALU.add),
                 reads=[bsc, b_A1], writes=[bsc])
            P.op("dve", lambda sc=sc, mm=mm: nc.vector.max(out=mm[:, 0:8], in_=sc[:]), reads=[bsc], writes=[bmm])
            P.op("dve", lambda sc=sc, mm=mm, wk=wk: nc.vector.match_replace(out=wk[:], in_to_replace=mm[:, 0:8], in_values=sc[:], imm_value=-1e9),
                 reads=[bsc, bmm], writes=[bwk])
            P.op("dve", lambda mm=mm, wk=wk: nc.vector.max(out=mm[:, 8:16], in_=wk[:]), reads=[bwk], writes=[bmm])
            P.op("dve", lambda mm=mm: nc.vector.tensor_scalar(out=mm[:, 15:16], in0=mm[:, 15:16], scalar1=0.0, scalar2=None, op0=ALU.max),
                 reads=[bmm], writes=[bmm])
            P.op("dve", lambda c=c, sc=sc, mm=mm: nc.vector.tensor_scalar(out=sel[:, c, :], in0=sc[:], scalar1=mm[:, 15:16], scalar2=None, op0=ALU.is_ge),
                 reads=[bsc, bmm], writes=[b_sel])

    def selection_pe():
        flush()
        for c in range(4):
            P.op("pe", lambda c=c: nc.tensor.matmul(X[:, c * 128:(c + 1) * 128], lhsT=sel[:, c, :], rhs=ident[:], start=True, stop=True),
                 reads=[b_sel, b_id], writes=[b_X], skip_self=True)
        P.op("act", lambda: nc.scalar.copy(out=selT[:], in_=X[:]), reads=[b_X], writes=[b_selT])

    def combine(qb):
        pending.append(lambda: combine_w(qb))
        for r in range(NOWN):
            pending.append(lambda r=r: combine_r(qb, r))
        pending.append(lambda: combine_out(qb))

    def combine_w(qb):
        gsig, b_gsig = gsig2[qb % 2], b_gsig2[qb % 2]
        for br, brd in ((0, b_rdenC), (1, b_rdenS), (2, b_rdenW)):
            P.op("dve", lambda br=br: nc.vector.tensor_tensor(
                out=wgt[:, br, 0:NOWN, :], in0=rden[:, br, 0:NOWN, :],
                in1=gsig[:].rearrange("p c (r b) -> p r c b", b=3)[:, :, :, br], op=ALU.mult),
                reads=brd[0:NOWN] + [b_gsig], writes=[b_wgt])

    def combine_r(qb, r):
        if True:
            for c in range(4):
                P.op("dve", lambda r=r, c=c: nc.vector.tensor_scalar(out=oacc[:, c, r * 64:(r + 1) * 64], in0=accC[r][:, c, 0:64],
                                                                      scalar1=wgt[:, 0, r, c:c + 1], scalar2=None, op0=ALU.mult),
                     reads=[b_accC[r], b_wgt], writes=[b_oacc])
                P.op("dve", lambda r=r, c=c: nc.vector.scalar_tensor_tensor(out=oacc[:, c, r * 64:(r + 1) * 64], in0=accS[r][:, c, 0:64],
                                                                            scalar=wgt[:, 1, r, c:c + 1], in1=oacc[:, c, r * 64:(r + 1) * 64],
                                                                            op0=ALU.mult, op1=ALU.add),
                     reads=[b_accS[r], b_wgt, b_oacc], writes=[b_oacc])
                P.op("dve", lambda r=r, c=c: nc.vector.scalar_tensor_tensor(out=obf[:, c, r * 64:(r + 1) * 64], in0=accW[r][:, c, 0:64],
                                                                            scalar=wgt[:, 2, r, c:c + 1], in1=oacc[:, c, r * 64:(r + 1) * 64],
                                                                            op0=ALU.mult, op1=ALU.add),
                     reads=[b_accW[r], b_wgt, b_oacc], writes=[b_obf])

    def combine_out(qb):
        q0 = qb * 512
        P.dma("sp", d["o_nsa"][q0:q0 + 512, :].rearrange("(c p) e -> p c e", p=128), obf[:], reads=[b_obf], writes=[b_out])

    def stC(i):
        it = items[i]; kind, qb, kt, r = it["kind"], it["qb"], it["kt"], it["r"]
        ei = i % NE
        p, bp = p_sb[ei], b_p[ei]
        bks = banks_of(it)
        diag = kt >= 4 * qb
        if it["first"]:
            for bk in bks:
                zero_bank(bk)
        if kind == "C":
            for c in range(4):
                bk, off = bks[c // 2], (c % 2) * 193
                P.op("pe", lambda c=c, bk=bk, off=off: nc.tensor.matmul(A_ps[bk][:, off:off + 193], lhsT=p[:, c * 128:(c + 1) * 128], rhs=VcX[:, kt, 0:193],
                                                                        start=False, stop=True, skip_group_check=True),
                     reads=[bp, b_VcX], writes=[b_A[bk]], skip_self=True)
        else:
            VA, bVA = (VWA, b_VWA) if kind == "W" else (VSA, b_VSA)
            bk = bks[0]
            for c in (range(kt - 4 * qb, 4) if diag else range(4)):
                P.op("pe", lambda c=c: nc.tensor.matmul(A_ps[bk][:, c * 65:(c + 1) * 65], lhsT=p[:, c * 128:(c + 1) * 128], rhs=VA[:, kt, 0:65],
                                                        start=False, stop=True, skip_group_check=True),
                     reads=[bp, bVA], writes=[b_A[bk]], skip_self=True)
        if not it["last"]:
            return
        if kind == "C":
            flush()
            P.op("act", lambda: nc.scalar.copy(out=accC[r][:, 0:2, :].rearrange("p c e -> p (c e)"), in_=A_ps[bks[0]][:, 0:386]),
                 reads=[b_A[bks[0]]], writes=[b_accC[r]])
            P.op("dve", lambda: nc.vector.tensor_copy(out=accC[r][:, 2:4, :].rearrange("p c e -> p (c e)"), in_=A_ps[bks[1]][:, 0:386]),
                 reads=[b_A[bks[1]]], writes=[b_accC[r]])
            P.op("dve", lambda: nc.vector.tensor_scalar(out=rden[:, 0, r, :], in0=accC[r][:, :, 64], scalar1=1e-30, scalar2=None, op0=ALU.max),
                 reads=[b_accC[r]], writes=[b_rdenC[r]])
            P.op("dve", lambda: nc.vector.reciprocal(out=rden[:, 0, r, :], in_=rden[:, 0, r, :]), reads=[b_rdenC[r]], writes=[b_rdenC[r]])
            for c in range(4):
                if r == 0:
                    P.op("dve", lambda c=c: nc.vector.tensor_scalar(out=imp[:, c, :], in0=accC[r][:, c, 65:193], scalar1=rden[:, 0, r, c:c + 1],
                                                                     scalar2=None, op0=ALU.mult), reads=[b_accC[r], b_rdenC[r]], writes=[b_imp])
                else:
                    P.op("dve", lambda c=c: nc.vector.scalar_tensor_tensor(out=imp[:, c, :], in0=accC[r][:, c, 65:193], scalar=rden[:, 0, r, c:c + 1],
                                                                           in1=imp[:, c, :], op0=ALU.mult, op1=ALU.add),
                         reads=[b_accC[r], b_rdenC[r], b_imp], writes=[b_imp])
            if r == 3:
                selection()
        else:
            acc, bacc, brd, bri = (accW, b_accW, b_rdenW, 2) if kind == "W" else (accS, b_accS, b_rdenS, 1)
            bk = bks[0]
            if r % 2 == 0:
                P.op("act", lambda: nc.scalar.copy(out=acc[r][:].rearrange("p c e -> p (c e)"), in_=A_ps[bk][:, 0:260]), reads=[b_A[bk]], writes=[bacc[r]])
            else:
                P.op("dve", lambda: nc.vector.tensor_copy(out=acc[r][:].rearrange("p c e -> p (c e)"), in_=A_ps[bk][:, 0:260]), reads=[b_A[bk]], writes=[bacc[r]])
            P.op("dve", lambda: nc.vector.reciprocal(out=rden[:, bri, r, :], in_=acc[r][:, :, 64]), reads=[bacc[r]], writes=[brd[r]])
            if kind == "S" and r == NOWN - 1:
                combine(qb)

    for s_ in range(-2, NI):
        if 0 <= s_ + 2 < NI:
            stA(s_ + 2)
        if 0 <= s_ + 1 < NI:
            stB(s_ + 1)
        if 0 <= s_ < NI:
            stC(s_)
        flush(1)
    flush()
    return [b_out]


def alloc_banks(P):
    return [(P.ps(f"bank{i}", [128, 512], F32), P.buf(f"bank{i}")) for i in range(8)]


def build_conv(nc, P, NTC, d, banks, ident, b_id):
    N = NTC
    NTT = N // 512
    b_out = P.buf("conv_out")
    dww = P.sb("dww_sb", [128, 4, 31], F32); b_dww = P.buf("dww")
    prm = P.sb("cprm_sb", [128, 3, 4], F32); b_prm = P.buf("cprm")
    P.dma("sp", dww[:], d["dww"], writes=[b_dww])
    P.dma("sp", prm[:, 0, :], d["dwb"], writes=[b_prm])
    P.dma("sp", prm[:, 1, :], d["lng"], writes=[b_prm])
    P.dma("sp", prm[:, 2, :], d["lnb"], writes=[b_prm])
    onesF = P.sb("onesF", [128, 128], F32); b_ones = P.buf("onesF")
    P.op("pool", lambda: nc.gpsimd.memset(onesF[:], 1.0 / 512.0), writes=[b_ones])
    ain = [P.sb(f"ain{i}", [128, 2, N + 30], F32) for i in range(2)]; b_ain = [P.buf(f"ain{i}") for i in range(2)]
    abf = [P.sb(f"abf{i}", [128, N + 30], BF16) for i in range(2)]; b_abf = [P.buf(f"abf{i}") for i in range(2)]
    diag = [P.sb(f"diag{i}", [128, 31, 128], BF16) for i in range(2)]; b_diag = [P.buf(f"diag{i}") for i in range(2)]
    y = [P.sb(f"cy{c}", [128, N], F32) for c in range(4)]; b_y = [P.buf(f"cy{c}") for c in range(4)]
    ysq = P.sb("cysq", [128, 512], F32); b_ysq = P.buf("cysq")
    for c in range(4):
        ai, b_ai = ain[c % 2], b_ain[c % 2]
        ab, b_ab = abf[c % 2], b_abf[c % 2]
        dg, b_dg = diag[c % 2], b_diag[c % 2]
        P.dma("sp", ai[:], d["aT"][:, c * 128:(c + 1) * 128, :].rearrange("k p n -> p k n"), writes=[b_ai])
        P.op("act", lambda ai=ai: nc.scalar.activation(out=ai[:, 1, :], in_=ai[:, 1, :], func=AF.Sigmoid), reads=[b_ai], writes=[b_ai])
        P.op("dve", lambda ai=ai, ab=ab: nc.vector.tensor_tensor(out=ab[:], in0=ai[:, 0, :], in1=ai[:, 1, :], op=ALU.mult),
             reads=[b_ai], writes=[b_ab])
        for k in range(31):
            P.op("dve", lambda k=k, c=c, dg=dg: nc.vector.tensor_scalar(out=dg[:, k, :], in0=ident[:], scalar1=dww[:, c, k:k + 1], scalar2=None,
                                                                         op0=ALU.mult), reads=[b_id, b_dww], writes=[b_dg])
        for tt in range(NTT):
            ps, b_ps = banks[tt % 2]
            for k in range(31):
                P.op("pe", lambda k=k, tt=tt, ps=ps, dg=dg, ab=ab: nc.tensor.matmul(ps[:], lhsT=dg[:, k, :], rhs=ab[:, tt * 512 + k: tt * 512 + k + 512],
                                                                                      start=(k == 0), stop=(k == 30)),
                     reads=[b_dg, b_ab], writes=[b_ps], skip_self=True)
            P.op("act", lambda c=c, tt=tt, ps=ps: nc.scalar.activation(out=y[c][:, tt * 512:(tt + 1) * 512], in_=ps[:], func=AF.Identity,
                                                                        bias=prm[:, 0, c:c + 1]), reads=[b_ps, b_prm], writes=[b_y[c]])
    mean = P.sb("cmean", [128, 512], F32); b_mean = P.buf("cmean")
    rstd = P.sb("crstd", [128, 512], F32); b_rstd = P.buf("crstd")
    yn = [P.sb(f"cyn{i}", [128, 512], F32) for i in range(2)]; b_yn = [P.buf(f"cyn{i}") for i in range(2)]
    co = [P.sb(f"cco{i}", [128, 512], BF16) for i in range(2)]; b_co = [P.buf(f"cco{i}") for i in range(2)]
    it = 0
    for tt in range(NTT):
        sl = slice(tt * 512, (tt + 1) * 512)
        pm, b_pm = banks[2]
        pq, b_pq = banks[3]
        for c in range(4):
            P.op("pe", lambda c=c, sl=sl: nc.tensor.matmul(pm[:], lhsT=onesF[:], rhs=y[c][:, sl], start=(c == 0), stop=(c == 3)),
                 reads=[b_ones, b_y[c]], writes=[b_pm], skip_self=True)
        for c in range(4):
            P.op("act", lambda c=c, sl=sl: nc.scalar.activation(out=ysq[:], in_=y[c][:, sl], func=AF.Square), reads=[b_y[c]], writes=[b_ysq])
            P.op("pe", lambda c=c: nc.tensor.matmul(pq[:], lhsT=onesF[:], rhs=ysq[:], start=(c == 0), stop=(c == 3)),
                 reads=[b_ones, b_ysq], writes=[b_pq], skip_self=True)
        P.op("dve", lambda: nc.vector.tensor_copy(out=mean[:], in_=pm[:]), reads=[b_pm], writes=[b_mean])
        P.op("dve", lambda: nc.vector.tensor_tensor(out=rstd[:], in0=mean[:], in1=mean[:], op=ALU.mult), reads=[b_mean], writes=[b_rstd])
        P.op("dve", lambda: nc.vector.tensor_tensor(out=rstd[:], in0=pq[:], in1=rstd[:], op=ALU.subtract), reads=[b_pq, b_rstd], writes=[b_rstd])
        P.op("act", lambda: nc.scalar.activation(out=rstd[:], in_=rstd[:], func=AF.Sqrt, bias=1e-5), reads=[b_rstd], writes=[b_rstd])
        P.op("dve", lambda: nc.vector.reciprocal(out=rstd[:], in_=rstd[:]), reads=[b_rstd], writes=[b_rstd])
        for c in range(4):
            i = it % 2
            it += 1
            P.op("dve", lambda c=c, sl=sl, i=i: nc.vector.tensor_tensor(out=yn[i][:], in0=y[c][:, sl], in1=mean[:], op=ALU.subtract),
                 reads=[b_y[c], b_mean], writes=[b_yn[i]])
            P.op("dve", lambda i=i: nc.vector.tensor_tensor(out=yn[i][:], in0=yn[i][:], in1=rstd[:], op=ALU.mult),
                 reads=[b_yn[i], b_rstd], writes=[b_yn[i]])
            P.op("act", lambda c=c, i=i: nc.scalar.activation(out=co[i][:], in_=yn[i][:], func=AF.Silu, scale=prm[:, 1, c:c + 1],
                                                               bias=prm[:, 2, c:c + 1]), reads=[b_yn[i], b_prm], writes=[b_co[i]])
            P.dma("sp", d["coutT"][c * 128:(c + 1) * 128, sl], co[i][:], reads=[b_co[i]], writes=[b_out])
    return [b_out]

bf = ml_dtypes.bfloat16

def nsa_consts(T):
    t = np.arange(T)
    j = np.arange(128)
    vis = (j[None, :] * 64 <= t[:, None])
    cur = t // 64
    forced = (j[None, :] == 0) | (j[None, :] == cur[:, None]) | (j[None, :] == cur[:, None] - 1)
    M1 = (vis & ~forced).astype(np.float32)
    A1 = np.where(vis, np.where(forced, 1e4, 0.0), -1.0).astype(np.float32)
    NS = T // 64
    M1[:, NS:] = 0.0; A1[:, NS:] = -1.0
    n = np.arange(512)
    poolm = ((n[:, None] >= 4 * j[None, :] - 1) & (n[:, None] <= 4 * j[None, :] + 3)).astype(np.float32).astype(bf)
    kaug_tok = np.stack([t // 64, t % 64, np.ones(T), np.ones(T)]).astype(np.float32).astype(bf)
    kaugc = np.stack([n // 4, 16 * (n % 4) + 15.5, np.ones(512), np.ones(512)]).astype(np.float32).astype(bf)[:, :T // 16]
    return dict(M1=M1, A1=A1, poolm=poolm[:T // 16], kaug_tok=kaug_tok, kaugc=kaugc)

def q_aug(T, h):
    t = np.arange(T)
    c = (2.0 ** (-(h + 1))) * 8.0
    return np.stack([np.full(T, 64 * c), np.full(T, c), -64 * c * (t // 64), -c * (t % 64)]).astype(np.float32).astype(bf)

def nsa_inputs(T, g, qT, kcT, vcT, ksT, kwT, vs, vw, graw, w, consts, horder=(0, 1, 2, 3)):
    d = {}
    QA = np.zeros((4, 68, T), dtype=bf)
    for r in range(4):
        h = 4 * g + horder[r]
        QA[r, :64] = qT[h * 64:(h + 1) * 64]
        QA[r, 64:] = q_aug(T, h)
    d["QA"] = QA
    for nm, src in (("KSA", ksT), ("KWA", kwT)):
        a = np.zeros((68, T), dtype=bf)
        a[:64] = src[g * 64:(g + 1) * 64]
        a[64:] = consts["kaug_tok"]
        d[nm] = a
    for nm, src in (("VSA", vs), ("VWA", vw)):
        a = np.ones((T, 65), dtype=bf)
        a[:, :64] = src[:, g * 64:(g + 1) * 64]
        d[nm] = np.ascontiguousarray(a.reshape(T // 128, 128, 65).transpose(1, 0, 2))
    for nm, src in (("c2k", kcT), ("c2v", vcT)):
        a = np.zeros((128, T), dtype=bf)
        a[:64] = src[g * 64:(g + 1) * 64]
        a[64:, :T - 1] = src[g * 64:(g + 1) * 64, 1:]
        d[nm] = a
    for kv in ("k", "v"):
        w1 = np.asarray(w["w1_" + kv], dtype=np.float32)
        d["w1" + kv] = np.ascontiguousarray(w1.reshape(16, 2, 64, 128).transpose(1, 2, 0, 3).reshape(128, 16, 128))
        pe = np.asarray(w["pe_" + kv], dtype=np.float32)
        d["pe2" + kv] = np.ascontiguousarray(pe.reshape(16, 2, 64).transpose(1, 2, 0).reshape(128, 16))
        d["w2" + kv] = np.ascontiguousarray(np.asarray(w["w2_" + kv], dtype=np.float32))
    d["kaugc"] = consts["kaugc"]
    d["poolm"] = consts["poolm"]
    d["M1"] = consts["M1"]; d["A1"] = consts["A1"]
    h0 = 4 * g + horder[0]
    d["graw"] = np.ascontiguousarray(graw[:, h0 * 3:(h0 + 2) * 3])
    return d


T_SEQ = 8192
NTOK = 2048
NCORE = 8


def _launch(nc, in_maps):
    res = run_bass_kernel_spmd(nc, in_maps, core_ids=list(range(NCORE)))
    return res.results


def _mk(nc, d, name, shape, dt, out=False):
    d[name] = nc.dram_tensor(name, list(shape), dt, kind="ExternalOutput" if out else "ExternalInput").ap()
    return d[name]


def _build_dense(kind):
    nc = bass.Bass("TRN2", target_bir_lowering=False)
    d = {}
    NT = NTOK
    _mk(nc, d, "x", [NT, 1024], F32)
    if kind in ("B", "C"):
        _mk(nc, d, "oT", [1024, NT], BF16)
        _mk(nc, d, "wo", [1024, 1024], F32)
    nffn = {"A": 1, "B": 2, "C": 1}[kind]
    for i in range(nffn):
        _mk(nc, d, f"fg{i}", [1024], F32)
        _mk(nc, d, f"fwi{i}", [1024, 5632], F32)
        _mk(nc, d, f"fwo{i}", [2816, 1024], F32)
    if kind == "A":
        _mk(nc, d, "pg", [1024], F32); _mk(nc, d, "pw", [1024, 2328], F32)
        _mk(nc, d, "xo", [NT, 1024], F32, True)
        _mk(nc, d, "aT", [1024, NT], F32, True); _mk(nc, d, "qT", [512, NT], BF16, True)
        for n in ("kcT", "vcT", "ksT", "kwT"):
            _mk(nc, d, n, [128, NT], BF16, True)
        _mk(nc, d, "vs", [NT, 128], BF16, True); _mk(nc, d, "vw", [NT, 128], BF16, True)
        _mk(nc, d, "gg", [NT, 24], F32, True)
    elif kind == "B":
        _mk(nc, d, "pg", [1024], F32); _mk(nc, d, "pw", [1024, 3072], F32)
        _mk(nc, d, "xo", [NT, 1024], F32, True)
        _mk(nc, d, "cT", [1536, NT], F32, True); _mk(nc, d, "qT", [512, NT], BF16, True)
        _mk(nc, d, "kT", [512, NT], BF16, True); _mk(nc, d, "v", [NT, 512], BF16, True)
    else:
        _mk(nc, d, "gfin", [1024], F32)
        _mk(nc, d, "out", [NT, 1024], F32, True)
    with ExitStack() as es:
        P = Prog(nc, es)
        dn = Dense(nc, P, NT)
        outs = []
        dn.load_x(d["x"])
        if kind in ("B", "C"):
            dn.outproj(d["oT"], d["wo"])
        for i in range(nffn):
            dn.ffn(d[f"fg{i}"], d[f"fwi{i}"], d[f"fwo{i}"])
        if kind == "A":
            names = [(0, 1024, "F", "aT"), (1024, 1536, "F", "qT"), (1536, 1664, "F", "kcT"), (1664, 1792, "F", "vcT"),
                     (1792, 1920, "F", "ksT"), (1920, 2048, "T", "vs"), (2048, 2176, "F", "kwT"), (2176, 2304, "T", "vw"),
                     (2304, 2328, "T", "gg")]
        elif kind == "B":
            names = [(0, 1536, "F", "cT"), (1536, 2048, "F", "qT"), (2048, 2560, "F", "kT"), (2560, 3072, "T", "v")]
        if kind in ("A", "B"):
            bx = P.buf("xo_out")
            dn.store_x(d["xo"], bx)
            outs.append(bx)
            specs = []
            for (c0, c1, lay, n) in names:
                b = P.buf("o_" + n)
                outs.append(b)
                specs.append((c0, c1, lay, d[n], b))
            dn.proj(d["pg"], d["pw"], specs)
        else:
            dn.alloc_final()
            bo = P.buf("out_out")
            dn.final(d["gfin"], d["out"], bo)
            outs.append(bo)
        P.finish("sp", outs)
        P.emit()
    return nc


def _build_conv0():
    nc = bass.Bass("TRN2", target_bir_lowering=False)
    d = {}
    _mk(nc, d, "aT", [2, 512, NTOK + 30], F32); _mk(nc, d, "dww", [128, 4, 31], F32)
    for n in ("dwb", "lng", "lnb"):
        _mk(nc, d, n, [128, 4], F32)
    _mk(nc, d, "coutT", [512, NTOK], BF16, True)
    with ExitStack() as es:
        P = Prog(nc, es)
        banks = alloc_banks(P)
        ident, b_id = make_ident(nc, P, "identc")
        outs = build_conv(nc, P, NTOK, d, banks, ident, b_id)
        P.finish("sp", outs)
        P.emit()
    return nc


def _build_nsa():
    nc = bass.Bass("TRN2", target_bir_lowering=False)
    d = {}
    T = T_SEQ
    _mk(nc, d, "QA", [4, 68, T], BF16); _mk(nc, d, "KSA", [68, T], BF16); _mk(nc, d, "KWA", [68, T], BF16)
    _mk(nc, d, "VSA", [128, T // 128, 65], BF16); _mk(nc, d, "VWA", [128, T // 128, 65], BF16)
    _mk(nc, d, "c2k", [128, T], BF16); _mk(nc, d, "c2v", [128, T], BF16)
    for kv in "kv":
        _mk(nc, d, "w1" + kv, [128, 16, 128], F32); _mk(nc, d, "pe2" + kv, [128, 16], F32); _mk(nc, d, "w2" + kv, [128, 64], F32)
    _mk(nc, d, "kaugc", [4, T // 16], BF16); _mk(nc, d, "poolm", [T // 16, 128], BF16)
    _mk(nc, d, "M1", [T, 128], F32); _mk(nc, d, "A1", [T, 128], F32); _mk(nc, d, "graw", [T, 6], F32)
    _mk(nc, d, "o_nsa", [T, 128], BF16, True)
    with ExitStack() as es:
        P = Prog(nc, es)
        outs = build_nsa(nc, P, T, list(range(T // 512)), d, alloc_banks(P), NOWN=2)
        P.finish("sp", outs)
        P.emit()
    return nc


def _build_m1():
    nc = bass.Bass("TRN2", target_bir_lowering=False)
    d = {}
    T = T_SEQ
    _mk(nc, d, "qT", [2, 64, T], BF16); _mk(nc, d, "kT", [2, 64, T], BF16); _mk(nc, d, "v", [T, 2, 64], BF16)
    _mk(nc, d, "convin", [3, 512, NTOK + 2], F32); _mk(nc, d, "scw", [128, 12], F32)
    _mk(nc, d, "o_sbT", [2, 64, T], BF16, True); _mk(nc, d, "coutT", [512, NTOK], BF16, True)
    with ExitStack() as es:
        P = Prog(nc, es)
        outs = build_mixer1(nc, P, T, NTOK, d)
        P.finish("sp", outs)
        P.emit()
    return nc


def _cat_tok(res, name, axis):
    return [np.concatenate([np.asarray(res[b * 4 + j][name]) for j in range(4)], axis=axis) for b in range(2)]


def kernel(x, ffn1_norm, ffn1_w_in, ffn1_w_out, mix_norm, ffn2_norm, ffn2_w_in, ffn2_w_out,
           ab_w_in, conv_dw_w, conv_dw_b, conv_ln_g, conv_ln_b,
           nsa_pe_k, nsa_w1_k, nsa_w2_k, nsa_pe_v, nsa_w1_v, nsa_w2_v, ab_w_out,
           cd_w_in, sc_conv_w, cd_w_out, final_norm):
    f32 = lambda a: np.ascontiguousarray(np.asarray(a, dtype=np.float32))
    x = f32(x)
    T = T_SEQ
    xs = [np.ascontiguousarray(x[c // 4, (c % 4) * NTOK:(c % 4 + 1) * NTOK]) for c in range(NCORE)]
    common = {"fg0": f32(ffn1_norm[0]), "fwi0": f32(ffn1_w_in[0]), "fwo0": f32(ffn1_w_out[0]), "pg": f32(mix_norm[0]), "pw": f32(ab_w_in[0])}
    rA = _launch(_build_dense("A"), [dict(common, x=xs[c]) for c in range(NCORE)])
    aT = _cat_tok(rA, "aT", 1); qT = _cat_tok(rA, "qT", 1)
    kcT = _cat_tok(rA, "kcT", 1); vcT = _cat_tok(rA, "vcT", 1); ksT = _cat_tok(rA, "ksT", 1); kwT = _cat_tok(rA, "kwT", 1)
    vs = _cat_tok(rA, "vs", 0); vw = _cat_tok(rA, "vw", 0); gg = _cat_tok(rA, "gg", 0)
    lay4 = lambda v: np.ascontiguousarray(f32(v).reshape(4, 128).T)
    cc = {"dww": np.ascontiguousarray(f32(conv_dw_w[0]).reshape(31, 4, 128).transpose(2, 1, 0)),
          "dwb": lay4(conv_dw_b[0]), "lng": lay4(conv_ln_g[0]), "lnb": lay4(conv_ln_b[0])}
    maps = []
    for c in range(NCORE):
        b, j = c // 4, c % 4
        a = np.zeros((2, 512, NTOK + 30), dtype=np.float32)
        lo = j * NTOK - 30
        src = aT[b].reshape(2, 512, T)
        if lo < 0:
            a[:, :, 30:] = src[:, :, 0:NTOK]
        else:
            a[:] = src[:, :, lo:lo + NTOK + 30]
        maps.append(dict(cc, aT=a))
    rC0 = _launch(_build_conv0(), maps)
    consts = nsa_consts(T)
    w = dict(pe_k=nsa_pe_k[0], w1_k=nsa_w1_k[0], w2_k=nsa_w2_k[0], pe_v=nsa_pe_v[0], w1_v=nsa_w1_v[0], w2_v=nsa_w2_v[0])
    maps = []
    for c in range(NCORE):
        b, g, hh = c // 4, (c % 4) // 2, c % 2
        horder = [2 * hh, 2 * hh + 1, 2 * (1 - hh), 2 * (1 - hh) + 1]
        dd = nsa_inputs(T, g, qT[b], kcT[b], vcT[b], ksT[b], kwT[b], vs[b], vw[b], gg[b], w, consts, horder)
        maps.append(dd)
    rN = _launch(_build_nsa(), maps)
    oT = []
    for c in range(NCORE):
        b, j = c // 4, c % 4
        o = np.zeros((1024, NTOK), dtype=bf)
        o[0:512] = np.asarray(rC0[c]["coutT"])
        for g in range(2):
            for hh in range(2):
                src = np.asarray(rN[b * 4 + g * 2 + hh]["o_nsa"])[j * NTOK:(j + 1) * NTOK]
                r0 = 512 + (4 * g + 2 * hh) * 64
                o[r0:r0 + 128] = src.T
        oT.append(o)
    common = {"wo": f32(ab_w_out[0]), "fg0": f32(ffn2_norm[0]), "fwi0": f32(ffn2_w_in[0]), "fwo0": f32(ffn2_w_out[0]),
              "fg1": f32(ffn1_norm[1]), "fwi1": f32(ffn1_w_in[1]), "fwo1": f32(ffn1_w_out[1]), "pg": f32(mix_norm[1]), "pw": f32(cd_w_in[0])}
    rB = _launch(_build_dense("B"), [dict(common, x=np.asarray(rA[c]["xo"]), oT=oT[c]) for c in range(NCORE)])
    cT = _cat_tok(rB, "cT", 1); q1 = _cat_tok(rB, "qT", 1); k1 = _cat_tok(rB, "kT", 1); v1 = _cat_tok(rB, "v", 0)
    scw = np.ascontiguousarray(f32(sc_conv_w[0]).reshape(3, 4, 128).transpose(2, 1, 0).reshape(128, 12))
    maps = []
    for c in range(NCORE):
        b, j = c // 4, c % 4
        ci = np.zeros((3, 512, NTOK + 2), dtype=np.float32)
        src = cT[b].reshape(3, 512, T)
        lo = j * NTOK - 2
        if lo < 0:
            ci[:, :, 2:] = src[:, :, 0:NTOK]
        else:
            ci[:] = src[:, :, lo:lo + NTOK + 2]
        hp = j
        maps.append({"qT": np.ascontiguousarray(q1[b][hp * 128:(hp + 1) * 128].reshape(2, 64, T)),
                     "kT": np.ascontiguousarray(k1[b][hp * 128:(hp + 1) * 128].reshape(2, 64, T)),
                     "v": np.ascontiguousarray(v1[b][:, hp * 128:(hp + 1) * 128].reshape(T, 2, 64)),
                     "convin": ci, "scw": scw})
    rM1 = _launch(_build_m1(), maps)
    oT = []
    for c in range(NCORE):
        b, j = c // 4, c % 4
        o = np.zeros((1024, NTOK), dtype=bf)
        o[0:512] = np.asarray(rM1[c]["coutT"])
        for hp in range(4):
            src = np.asarray(rM1[b * 4 + hp]["o_sbT"]).reshape(128, T)[:, j * NTOK:(j + 1) * NTOK]
            o[512 + hp * 128:512 + (hp + 1) * 128] = src
        oT.append(o)
    common = {"wo": f32(cd_w_out[0]), "fg0": f32(ffn2_norm[1]), "fwi0": f32(ffn2_w_in[1]), "fwo0": f32(ffn2_w_out[1]), "gfin": f32(final_norm)}
    rC = _launch(_build_dense("C"), [dict(common, x=np.asarray(rB[c]["xo"]), oT=oT[c]) for c in range(NCORE)])
    out = np.zeros((2, T, 1024), dtype=np.float32)
    for c in range(NCORE):
        out[c // 4, (c % 4) * NTOK:(c % 4 + 1) * NTOK] = np.asarray(rC[c]["out"])
    return out
```
